# Optimizing a Trainium2 kernel written in Bass

```python
import math
import jax
import jax.numpy as jnp
from jax import lax
import numpy as np

D_MODEL = 2048
BATCH = 4
SEQ = 2048
DEPTH = 2

GRID_W = 64
CTX_LEN = 256
N_MOD = 9
N_NORMS = 6
D_FF = 5504
FFN_RES_WEIGHT = 0.5
EPS = 1e-6

ATT_HEADS = 8
ATT_QK = 64
ATT_V = 2 * ATT_QK
ATT_BLOCK = 128
ROPE_THETA = 10000.0
ROPE_PAIRS = ATT_QK // 4
LAMBDA_INIT_BASE = 0.8
LAMBDA_INIT_AMP = 0.6
LAMBDA_INIT_RATE = 0.3

REC_HEADS = 8
REC_DK = 128
REC_DV = 128
REC_CHUNK = 64

CONV_WIDTH = D_MODEL
CONV_K = 3

N_EVEN = (DEPTH + 1) // 2
N_ODD = DEPTH // 2
ATT_QK_COLS = ATT_HEADS * 2 * ATT_QK
ATT_V_COLS = ATT_HEADS * ATT_V
REC_K_COLS = REC_HEADS * REC_DK
REC_V_COLS = REC_HEADS * REC_DV
MIX_SPLITS = (ATT_QK_COLS, ATT_QK_COLS, ATT_V_COLS, REC_K_COLS, REC_K_COLS, REC_K_COLS, REC_V_COLS, REC_V_COLS)
MIX_IN = 2 * ATT_QK_COLS + ATT_V_COLS + 3 * REC_K_COLS + 2 * REC_V_COLS
MIX_OUT = ATT_V_COLS + REC_V_COLS

kernel_name = 'hybrid_diffattn_hgrn2_shortconv_dit'


def _rms_norm(x, g):
    xf = x.astype(jnp.float32)
    y = xf * lax.rsqrt(jnp.mean(xf * xf, axis=-1, keepdims=True) + EPS)
    return (y * g.astype(jnp.float32)).astype(x.dtype)


def _modulation(cond, w, b):
    m = jax.nn.silu(cond) @ w + b
    return jnp.split(m[..., None, :], N_MOD, axis=-1)


def _modulate(x, g, shift, scale):
    return _rms_norm(x, g) * (1 + scale) + shift


def _residual(x, y, g, gate, weight):
    return x + weight * gate * _rms_norm(y, g)


def _swiglu(h, w_gate, w_up, w_down):
    return (jax.nn.silu(h @ w_gate) * (h @ w_up)) @ w_down


def _ffn_half(x, mods, g_pre, g_post, w_gate, w_up, w_down):
    shift, scale, gate = mods
    y = _swiglu(_modulate(x, g_pre, shift, scale), w_gate, w_up, w_down)
    return _residual(x, y, g_post, gate, FFN_RES_WEIGHT)


def _axial_rope(t, n):
    rows = n // GRID_W
    row = jnp.broadcast_to(jnp.arange(rows, dtype=jnp.float32)[:, None], (rows, GRID_W)).reshape(n)
    col = jnp.broadcast_to(jnp.arange(GRID_W, dtype=jnp.float32)[None, :], (rows, GRID_W)).reshape(n)
    freqs = ROPE_THETA ** (-jnp.arange(ROPE_PAIRS, dtype=jnp.float32) / ROPE_PAIRS)

    def rot(part, pos):
        ang = (pos[:, None] * freqs)[:, None, None, :]
        cos, sin = jnp.cos(ang), jnp.sin(ang)
        p = part.astype(jnp.float32)
        p1, p2 = p[..., :ROPE_PAIRS], p[..., ROPE_PAIRS:]
        return jnp.concatenate([p1 * cos - p2 * sin, p2 * cos + p1 * sin], axis=-1)

    half = ATT_QK // 2
    out = jnp.concatenate([rot(t[..., :half], row), rot(t[..., half:], col)], axis=-1)
    return out.astype(t.dtype)


def _diff_attend(q, k, v, lam):
    s = jnp.einsum('bqhmd,bkhmd->bhmqk', q, k).astype(jnp.float32) * (ATT_QK ** -0.5)
    p = jax.nn.softmax(s, axis=-1)
    w = (p[:, :, 0] - lam * p[:, :, 1]).astype(v.dtype)
    return jnp.einsum('bhqk,bkhv->bqhv', w, v)


def _diff_attention(q_lat, k_lat, v_lat, q_ctx, k_ctx, v_ctx, lam, with_ctx_out):
    k_all = jnp.concatenate([k_ctx, k_lat], axis=1)
    v_all = jnp.concatenate([v_ctx, v_lat], axis=1)
    b, n = q_lat.shape[:2]
    nblk = n // ATT_BLOCK
    qb = jnp.moveaxis(q_lat.reshape(b, nblk, ATT_BLOCK, ATT_HEADS, 2, ATT_QK), 1, 0)
    o = lax.map(lambda qi: _diff_attend(qi, k_all, v_all, lam), qb)
    o_lat = jnp.moveaxis(o, 0, 1).reshape(b, n, ATT_HEADS, ATT_V)
    o_ctx = _diff_attend(q_ctx, k_ctx, v_ctx, lam) if with_ctx_out else None
    return o_lat, o_ctx


def _gates(z, lb):
    lb = lb.reshape(REC_HEADS, REC_DK)
    f = lb + (1.0 - lb) * jax.nn.sigmoid(z)
    return 1.0 - f, jnp.log(f)


def _gla_chunked(q, k, v, log_f, state, with_output):
    b, n, h, dk = q.shape
    nc = n // REC_CHUNK

    def chunks(a):
        return jnp.moveaxis(a.reshape(b, nc, REC_CHUNK, h, a.shape[-1]), 1, 0)

    lower = jnp.tril(jnp.ones((REC_CHUNK, REC_CHUNK), dtype=bool))[None, :, :, None, None]

    def step(s, inp):
        qc, kc, vc, gc = inp
        cum = jnp.cumsum(gc, axis=1)
        last = cum[:, -1]
        s_new = s * jnp.exp(last)[..., None] + jnp.einsum('bshk,bshv->bhkv', kc * jnp.exp(last[:, None] - cum), vc)
        if not with_output:
            return s_new, None
        o_inter = jnp.einsum('bthk,bhkv->bthv', qc * jnp.exp(cum), s)
        decay = jnp.exp(jnp.where(lower, cum[:, :, None] - cum[:, None], -jnp.inf))
        att = jnp.einsum('bthk,bshk,btshk->bhts', qc, kc, decay)
        return s_new, o_inter + jnp.einsum('bhts,bshv->bthv', att, vc)

    s_final, o = lax.scan(step, state, (chunks(q), chunks(k), chunks(v), chunks(log_f)))
    if with_output:
        o = jnp.moveaxis(o, 0, 1).reshape(b, n, h, v.shape[-1])
    return o, s_final


def _hgrn2_bidir(rec_lat, rec_ctx, lb_fwd, lb_bwd, with_ctx_out):
    q_l, zf_l, zb_l, i_l = rec_lat
    q_c, zf_c, zb_c, i_c = rec_ctx
    s0 = jnp.zeros((q_l.shape[0], REC_HEADS, REC_DK, REC_DV), jnp.float32)
    flip = lambda a: jnp.flip(a, axis=1)
    k_c, g_c = _gates(zf_c, lb_fwd)
    k_l, g_l = _gates(zf_l, lb_fwd)
    oc_f, s_f = _gla_chunked(q_c, k_c, i_c, g_c, s0, with_ctx_out)
    ol_f, _ = _gla_chunked(q_l, k_l, i_l, g_l, s_f, True)
    k_c, g_c = _gates(flip(zb_c), lb_bwd)
    k_l, g_l = _gates(flip(zb_l), lb_bwd)
    oc_b, s_b = _gla_chunked(flip(q_c), k_c, flip(i_c), g_c, s0, with_ctx_out)
    ol_b, _ = _gla_chunked(flip(q_l), k_l, flip(i_l), g_l, s_b, True)
    o_lat = ol_f + flip(ol_b)
    o_ctx = oc_f + flip(oc_b) if with_ctx_out else None
    return o_lat, o_ctx


def _mixer_even(h_lat, h_ctx, w_in, w_out, lam_vecs, att_norm_g, rec_norm_g, lb_fwd, lb_bwd, lam_init, with_ctx_out):
    split_at = np.cumsum(MIX_SPLITS)[:-1].tolist()

    def project(h):
        b, n = h.shape[:2]
        qa, ka, va, qr, zf, zb, ir, gr = jnp.split(h @ w_in, split_at, axis=-1)
        att = (qa.reshape(b, n, ATT_HEADS, 2, ATT_QK), ka.reshape(b, n, ATT_HEADS, 2, ATT_QK),
               va.reshape(b, n, ATT_HEADS, ATT_V))
        f32 = lambda a, d: a.reshape(b, n, REC_HEADS, d).astype(jnp.float32)
        rec = (f32(qr, REC_DK) * (REC_DK ** -0.5), f32(zf, REC_DK), f32(zb, REC_DK), f32(ir, REC_DV))
        return att, rec, gr.reshape(b, n, REC_HEADS, REC_DV)

    (qa_l, ka_l, va_l), rec_l, g_l = project(h_lat)
    (qa_c, ka_c, va_c), rec_c, g_c = project(h_ctx)
    n = h_lat.shape[1]
    qa_l = _axial_rope(qa_l, n)
    ka_l = _axial_rope(ka_l, n)
    lv = lam_vecs.astype(jnp.float32)
    lam = jnp.exp(jnp.sum(lv[0] * lv[1])) - jnp.exp(jnp.sum(lv[2] * lv[3])) + lam_init
    oa_l, oa_c = _diff_attention(qa_l, ka_l, va_l, qa_c, ka_c, va_c, lam, with_ctx_out)
    or_l, or_c = _hgrn2_bidir(rec_l, rec_c, lb_fwd, lb_bwd, with_ctx_out)

    def merge(oa, orec, g):
        b, n = g.shape[:2]
        oa = _rms_norm(oa, att_norm_g) * (1.0 - lam_init)
        orec = (_rms_norm(orec, rec_norm_g) * jax.nn.silu(g.astype(jnp.float32))).astype(g.dtype)
        return jnp.concatenate([oa.reshape(b, n, ATT_V_COLS), orec.reshape(b, n, REC_V_COLS)], axis=-1) @ w_out

    y_lat = merge(oa_l, or_l, g_l)
    y_ctx = merge(oa_c, or_c, g_c) if with_ctx_out else None
    return y_lat, y_ctx


def _short_conv(h, w_in, conv_w, w_out):
    b_gate, c_gate, v = jnp.split(h @ w_in, 3, axis=-1)
    u = lax.conv_general_dilated(c_gate * v, conv_w[:, None, :], window_strides=(1,), padding='SAME',
                                 dimension_numbers=('NWC', 'WIO', 'NWC'), feature_group_count=CONV_WIDTH)
    return (b_gate * u) @ w_out


def setup_inputs(seed: int = 0) -> dict:
    key = jax.random.key(seed)
    ks = jax.random.split(key, 20)
    nrm = lambda k, shape, std: std * jax.random.normal(k, shape, jnp.float32)
    return {
        'x': nrm(ks[0], (BATCH, SEQ, D_MODEL), 1.0),
        'c': nrm(ks[1], (BATCH, D_MODEL), 1.0),
        'ctx': nrm(ks[2], (BATCH, CTX_LEN, D_MODEL), 1.0),
        'c_ctx': nrm(ks[3], (D_MODEL,), 1.0),
        'ada_w': nrm(ks[4], (DEPTH, D_MODEL, N_MOD * D_MODEL), 0.5 * D_MODEL ** -0.5),
        'ada_b': nrm(ks[5], (DEPTH, N_MOD * D_MODEL), 0.02),
        'norm_g': 1.0 + nrm(ks[6], (DEPTH, N_NORMS, D_MODEL), 0.02),
        'ffn_w_gate': nrm(ks[7], (DEPTH, 2, D_MODEL, D_FF), D_MODEL ** -0.5),
        'ffn_w_up': nrm(ks[8], (DEPTH, 2, D_MODEL, D_FF), D_MODEL ** -0.5),
        'ffn_w_down': nrm(ks[9], (DEPTH, 2, D_FF, D_MODEL), D_FF ** -0.5),
        'mix_w_in': nrm(ks[10], (N_EVEN, D_MODEL, MIX_IN), D_MODEL ** -0.5),
        'mix_w_out': nrm(ks[11], (N_EVEN, MIX_OUT, D_MODEL), MIX_OUT ** -0.5),
        'diff_lambda': nrm(ks[12], (N_EVEN, 4, ATT_QK), 0.1),
        'diff_norm_g': 1.0 + nrm(ks[13], (N_EVEN, ATT_V), 0.02),
        'rec_norm_g': 1.0 + nrm(ks[14], (N_EVEN, REC_DV), 0.02),
        'rec_lb': nrm(ks[15], (2, N_EVEN + 1, REC_K_COLS), 0.1),
        'conv_w_in': nrm(ks[16], (N_ODD, D_MODEL, 3 * CONV_WIDTH), D_MODEL ** -0.5),
        'conv_w': nrm(ks[17], (N_ODD, CONV_K, CONV_WIDTH), CONV_K ** -0.5),
        'conv_w_out': nrm(ks[18], (N_ODD, CONV_WIDTH, D_MODEL), CONV_WIDTH ** -0.5),
    }


def reference(x, c, ctx, c_ctx, ada_w, ada_b, norm_g, ffn_w_gate, ffn_w_up, ffn_w_down, mix_w_in, mix_w_out,
              diff_lambda, diff_norm_g, rec_norm_g, rec_lb, conv_w_in, conv_w, conv_w_out):
    x_ctx = ctx
    lb_all = jnp.cumsum(jax.nn.softmax(rec_lb.astype(jnp.float32), axis=1), axis=1)
    for l in range(DEPTH):
        even = l % 2 == 0
        ctx_out = any(j % 2 == 0 for j in range(l + 1, DEPTH))
        ctx_in = even or ctx_out
        g_n = norm_g[l]
        m_lat = _modulation(c, ada_w[l], ada_b[l])
        ffn1 = (ffn_w_gate[l, 0], ffn_w_up[l, 0], ffn_w_down[l, 0])
        ffn2 = (ffn_w_gate[l, 1], ffn_w_up[l, 1], ffn_w_down[l, 1])
        x = _ffn_half(x, m_lat[0:3], g_n[0], g_n[1], *ffn1)
        if ctx_in:
            m_ctx = _modulation(c_ctx, ada_w[l], ada_b[l])
            x_ctx = _ffn_half(x_ctx, m_ctx[0:3], g_n[0], g_n[1], *ffn1)
        h_lat = _modulate(x, g_n[2], m_lat[3], m_lat[4])
        if even:
            e = l // 2
            h_ctx = _modulate(x_ctx, g_n[2], m_ctx[3], m_ctx[4])
            lam_init = LAMBDA_INIT_BASE - LAMBDA_INIT_AMP * math.exp(-LAMBDA_INIT_RATE * l)
            y_lat, y_ctx = _mixer_even(h_lat, h_ctx, mix_w_in[e], mix_w_out[e], diff_lambda[e], diff_norm_g[e],
                                       rec_norm_g[e], lb_all[0, e], lb_all[1, e], lam_init, ctx_out)
        else:
            o = l // 2
            y_lat = _short_conv(h_lat, conv_w_in[o], conv_w[o], conv_w_out[o])
            y_ctx = None
            if ctx_out:
                h_ctx = _modulate(x_ctx, g_n[2], m_ctx[3], m_ctx[4])
                y_ctx = _short_conv(h_ctx, conv_w_in[o], conv_w[o], conv_w_out[o])
        x = _residual(x, y_lat, g_n[3], m_lat[5], 1.0)
        x = _ffn_half(x, m_lat[6:9], g_n[4], g_n[5], *ffn2)
        if ctx_out:
            x_ctx = _residual(x_ctx, y_ctx, g_n[3], m_ctx[5], 1.0)
            x_ctx = _ffn_half(x_ctx, m_ctx[6:9], g_n[4], g_n[5], *ffn2)
    return x
```

```python
import contextlib
import types
import os
import math
import numpy as np
import ml_dtypes
import concourse.bass as bass
import concourse.mybir as mybir
from concourse.bass_utils import run_bass_kernel_spmd


F32 = mybir.dt.float32
BF16 = mybir.dt.bfloat16
ALU = mybir.AluOpType
AF = mybir.ActivationFunctionType
AX = mybir.AxisListType


class Dep:
    __slots__ = ("name", "w", "r", "excl")

    def __init__(self, name, excl=False):
        self.name = name
        self.w = {}
        self.r = {}
        self.excl = excl


class Ins:
    __slots__ = ("eng", "fn", "deps", "signal", "semkey", "semval", "is_dma", "idx", "inc")
    _n = 0

    def __init__(self, eng, fn, is_dma=False, semkey=None):
        self.eng = eng
        self.fn = fn
        self.deps = []
        self.signal = False
        self.is_dma = is_dma
        self.semkey = semkey
        self.semval = None
        self.inc = 16
        Ins._n += 1
        self.idx = Ins._n


def _freeze(fn):
    if getattr(fn, "__closure__", None) is None:
        return fn
    cells = []
    for c in fn.__closure__:
        try:
            cells.append(types.CellType(c.cell_contents))
        except ValueError:
            cells.append(c)
    return types.FunctionType(fn.__code__, fn.__globals__, fn.__name__, fn.__defaults__, tuple(cells))


class Prog:
    ENGS = ("pe", "act", "dve", "pool", "sp")

    def __init__(self, nc):
        self.nc = nc
        self.streams = {e: [] for e in self.ENGS}
        self.stack = contextlib.ExitStack()
        self.dma_keys = {}
        self.n_sb = 0
        self.arena = None
        self.aoff = 0
        self.pending = {e: [] for e in self.ENGS}
        self.open_dmas = []
        self.ps = None
        self.d_ps = None

    def use_arena(self, nbytes):
        self.arena = self.stack.enter_context(self.nc.sbuf_tensor("arena", [128, nbytes], mybir.dt.uint8))
        self.asize = nbytes
        self.aoff = 0

    def shared_psum(self):
        if self.ps is None:
            self.ps = [self.psum(f"ps{i}", [128, 512]) for i in range(8)]
            self.d_ps = [Dep(f"ps{i}", excl=True) for i in range(8)]
        return self.ps, self.d_ps

    def barrier(self):
        lasts = []
        for e in ("pe", "act", "dve", "pool"):
            for ins in reversed(self.streams[e]):
                if not ins.is_dma:
                    lasts.append(ins)
                    break
        lasts += self.open_dmas
        self.open_dmas = []
        for d in lasts:
            d.signal = True
        for e in self.ENGS:
            self.pending[e] = list(lasts)

    def sbuf(self, name, shape, dtype):
        if self.arena is None:
            return self.stack.enter_context(self.nc.sbuf_tensor(name, list(shape), dtype))
        esz = {F32: 4, BF16: 2}[dtype]
        nel = 1
        for s_ in shape[1:]:
            nel *= s_
        nbytes = nel * esz
        off = (self.aoff + 63) // 64 * 64
        if off + nbytes > self.asize:
            raise MemoryError(f"SBUF arena overflow allocating {name}: {off}+{nbytes} > {self.asize}")
        self.aoff = off + nbytes
        v = self.arena[0:shape[0], off:off + nbytes].bitcast(dtype)
        if len(shape) == 3:
            v = v.rearrange("p (a b) -> p a b", a=shape[1])
        elif len(shape) == 4:
            v = v.rearrange("p (a b c) -> p a b c", a=shape[1], b=shape[2])
        return v

    def psum(self, name, shape, dtype=F32):
        return self.stack.enter_context(self.nc.psum_tensor(name, list(shape), dtype))

    def dram(self, name, shape, dtype, kind="Internal"):
        return self.nc.dram_tensor(name, list(shape), dtype, kind=kind)

    def op(self, eng, fn, reads=(), writes=(), is_dma=False, semkey=None, inc=16):
        ins = Ins(eng, _freeze(fn), is_dma, semkey)
        ins.inc = inc
        key = ("d", id(ins)) if is_dma else eng
        deps = {}
        for t in reads:
            for k, d in t.w.items():
                deps[id(d)] = d
            if t.excl:
                for k, d in t.r.items():
                    if not is_dma and k == eng:
                        continue
                    deps[id(d)] = d
        for t in writes:
            for k, d in t.r.items():
                if not is_dma and k == eng:
                    continue
                deps[id(d)] = d
            for k, d in t.w.items():
                if not is_dma and k == eng:
                    continue
                deps[id(d)] = d
        if self.pending[eng]:
            for d in self.pending[eng]:
                if d.is_dma or d.eng != eng:
                    deps[id(d)] = d
            self.pending[eng] = []
        for d in deps.values():
            d.signal = True
        ins.deps = list(deps.values())
        for t in reads:
            t.r[key] = ins
        for t in writes:
            if t.r:
                t.r = {}
                t.w = {}
            t.w[key] = ins
        if is_dma:
            ins.signal = True
            if semkey is None:
                raise ValueError("dma needs semkey")
            self.dma_keys.setdefault(semkey, 0)
            self.open_dmas.append(ins)
        self.streams[eng].append(ins)
        return ins

    def fence(self, srcs, dsts):
        for sd in srcs:
            for dd in dsts:
                for k, i in sd.w.items():
                    if k not in dd.w or dd.w[k].idx < i.idx:
                        dd.w[k] = i
                for k, i in sd.r.items():
                    if k not in dd.r or dd.r[k].idx < i.idx:
                        dd.r[k] = i

    def pe(self, fn, reads=(), writes=()):
        return self.op("pe", fn, reads, writes)

    def act(self, fn, reads=(), writes=()):
        return self.op("act", fn, reads, writes)

    def dve(self, fn, reads=(), writes=()):
        return self.op("dve", fn, reads, writes)

    def pool(self, fn, reads=(), writes=()):
        return self.op("pool", fn, reads, writes)

    def dma(self, out, in_, reads=(), writes=(), semkey=None, eng="sp", **kw):
        return self.op(eng, lambda e: e.dma_start(out=out, in_=in_, **kw), reads, writes,
                       is_dma=True, semkey=semkey)

    def allgather_pairs(self, out_t, in_t, reads=(), writes=(), semkey="cc"):
        return self.op("pool", lambda e: e.collective_compute(
            "AllGather", ALU.bypass, replica_groups=[[0, 1], [2, 3], [4, 5], [6, 7]],
            ins=[in_t.ap().opt()], outs=[out_t.ap().opt()]), reads, writes, is_dma=True, semkey=semkey, inc=1)

    def allreduce_all(self, out_t, in_t, reads=(), writes=(), semkey="ar"):
        return self.op("pool", lambda e: e.collective_compute(
            "AllReduce", ALU.add, replica_groups=[list(range(8))],
            ins=[in_t.ap().opt()], outs=[out_t.ap().opt()]), reads, writes, is_dma=True, semkey=semkey, inc=1)

    def emit(self):
        nc = self.nc
        st = self.stack
        sems = {}
        for e in ("pe", "act", "dve", "pool"):
            sems[e] = st.enter_context(nc.semaphore("s_" + e))
        for k in self.dma_keys:
            sems[("d", k)] = st.enter_context(nc.semaphore("d_" + str(k)))
        for e in self.ENGS:
            cnt = 0
            for ins in self.streams[e]:
                if ins.is_dma:
                    self.dma_keys[ins.semkey] += ins.inc
                    ins.semval = self.dma_keys[ins.semkey]
                elif ins.signal:
                    cnt += 1
                    ins.semval = cnt
        final_dma = dict(self.dma_keys)

        def run(eng_name, e):
            seen = {}
            for ins in self.streams[eng_name]:
                need = {}
                for d in ins.deps:
                    sk = ("d", d.semkey) if d.is_dma else d.eng
                    if d.semval > need.get(sk, 0):
                        need[sk] = d.semval
                for sk, v in need.items():
                    if seen.get(sk, 0) >= v:
                        continue
                    e.wait_ge(sems[sk], v)
                    seen[sk] = v
                bi = ins.fn(e)
                if ins.is_dma:
                    bi.then_inc(sems[("d", ins.semkey)], ins.inc)
                elif ins.signal:
                    bi.then_inc(sems[eng_name], 1)
            if eng_name == "sp":
                for k, v in final_dma.items():
                    if v > 0 and seen.get(("d", k), 0) < v:
                        e.wait_ge(sems[("d", k)], v)

        with nc.Block() as block:
            @block.tensor
            def _(e):
                run("pe", e)

            @block.scalar
            def _(e):
                run("act", e)

            @block.vector
            def _(e):
                run("dve", e)

            @block.gpsimd
            def _(e):
                run("pool", e)

            @block.sync
            def _(e):
                run("sp", e)
        st.close()


def simulate_sync(P):
    keys = dict.fromkeys(P.dma_keys, 0)
    for e in P.ENGS:
        cnt = 0
        for ins in P.streams[e]:
            if ins.is_dma:
                keys[ins.semkey] += ins.inc
                ins.semval = keys[ins.semkey]
            elif ins.signal:
                cnt += 1
                ins.semval = cnt
    sem = {}
    pc = {e: 0 for e in P.ENGS}
    progress = True
    while progress:
        progress = False
        for e in P.ENGS:
            st = P.streams[e]
            while pc[e] < len(st):
                ins = st[pc[e]]
                ok = True
                for d in ins.deps:
                    sk = ("d", d.semkey) if d.is_dma else d.eng
                    if sem.get(sk, 0) < d.semval:
                        ok = False
                        break
                if not ok:
                    break
                if ins.is_dma:
                    sem[("d", ins.semkey)] = sem.get(("d", ins.semkey), 0) + ins.inc
                elif ins.signal:
                    sem[e] = sem.get(e, 0) + 1
                pc[e] += 1
                progress = True
    stuck = {e: (pc[e], len(P.streams[e])) for e in P.ENGS if pc[e] < len(P.streams[e])}
    return stuck


EPS = 1e-6
D = 2048
DFF = 5504
NJ = 43
NC16 = 16


class Ctx:
    def __init__(self, P, ntiles):
        self.P = P
        self.tiles = []
        off = 0
        for n in ntiles:
            self.tiles.append((off, n))
            off += n
        self.NT = off
        NT = off
        self.hy = P.sbuf("hy", [128, 16, NT], BF16)
        self.A = P.sbuf("A", [128, NJ, NT], BF16)
        self.d_h = [Dep(f"h{i}") for i in range(len(ntiles))]
        self.d_A = [Dep(f"A{i}") for i in range(len(ntiles))]
        aflat = self.A[:].rearrange("p j t -> p (j t)")
        self.xt = []
        self.d_xt = []
        nx = min(3, (NJ * NT) // 16384)
        for i in range(nx):
            v = aflat[:, i * 16384:(i + 1) * 16384].bitcast(F32).rearrange("p (c t) -> p c t", c=16)
            self.xt.append(v)
            self.d_xt.append(Dep(f"xt{i}"))
        self.wgu = [P.sbuf(f"wgu{i}", [128, 2, 16, 256], BF16) for i in range(2)]
        self.d_wgu = [Dep(f"wgu{i}") for i in range(2)]
        self.wd = [P.sbuf(f"wd{i}", [128, NJ, 128], BF16) for i in range(2)]
        self.d_wd = [Dep(f"wd{i}") for i in range(2)]
        self.ps, self.d_ps = P.shared_psum()
        self.ones = P.sbuf("ones", [128, 128], BF16)
        self.d_ones = Dep("ones")
        P.dve(lambda e: e.memset(self.ones[:], 1.0), writes=[self.d_ones])
        self.sq = [P.sbuf(f"sq{i}", [128, 512], BF16) for i in range(4)]
        self.d_sq = [Dep(f"sq{i}") for i in range(4)]
        self.tmp = [P.sbuf(f"tmp{i}", [128, 512], F32) for i in range(4)]
        self.d_tmp = [Dep(f"tmp{i}") for i in range(4)]
        self.rstd = P.sbuf("rstd", [128, NT], F32)
        self.d_rstd = [Dep(f"rstd{i}") for i in range(len(ntiles))]
        self.mT = P.sbuf("mT", [128, 144, 2], F32)
        self.d_mT = Dep("mT")
        self.gT = P.sbuf("gT", [128, 6, 16], F32)
        self.d_gT = Dep("gT")
        self.coef = P.sbuf("coef", [128, 4, 16], F32)
        self.d_coef = Dep("coef")
        self.sqi = 0
        self.tmpi = 0
        self.psi = 0

    def next_sq(self):
        i = self.sqi % 4
        self.sqi += 1
        return self.sq[i], self.d_sq[i]

    def next_tmp(self):
        i = self.tmpi % 4
        self.tmpi += 1
        return self.tmp[i], self.d_tmp[i]


def emit_modulation(P, C, ada_w, ada_bT, cvecT, scr):
    cv = scr["cv"]
    sc = scr["sc"]
    bT = scr["bT"]
    d_cv, d_sc, d_bT = Dep("cv"), Dep("sc"), Dep("bT")
    P.dma(cv[:], cvecT, writes=[d_cv], semkey="small")
    P.dma(bT[:], ada_bT, writes=[d_bT], semkey="small2")
    P.act(lambda e: e.activation(out=sc[:], in_=cv[:], func=AF.Silu), reads=[d_cv], writes=[d_sc])
    mp = C.ps[7]
    d_mp = C.d_ps[7]
    mpv = mp[:, 0:288].rearrange("p (j r) -> p j r", r=2)
    for jb in range(36):
        s = jb % 2
        slot = C.wgu[s][:].rearrange("p a c n -> p c (a n)") if False else None
        sl = C.wgu[s][:].rearrange("p a c n -> p (a c n)").rearrange("p (c n) -> p c n", c=16)
        src = ada_w[:, jb * 512:(jb + 1) * 512].rearrange("(c p) n -> p c n", p=128)
        P.dma(sl, src, writes=[C.d_wgu[s]], semkey=f"wgu{s}", eng="pool")
        for j4 in range(4):
            j = jb * 4 + j4
            for k in range(16):
                P.pe(lambda e, sl=sl, j4=j4, k=k, j=j: e.matmul(
                    mpv[:, j, :], lhsT=sl[:, k, j4 * 128:(j4 + 1) * 128], rhs=sc[:, k, :],
                    start=(k == 0), stop=(k == 15)),
                    reads=[C.d_wgu[s], d_sc], writes=[d_mp])
    for r in range(2):
        P.dve(lambda e, r=r: e.tensor_tensor(out=C.mT[:, :, r], in0=mpv[:, :, r], in1=bT[:], op=ALU.add),
              reads=[d_mp, d_bT], writes=[C.d_mT])


def emit_coefs(P, C, sl_scale, sl_gate, gi_pre, gi_post, wres, cols):
    for col in cols:
        P.dve(lambda e, col=col: e.scalar_tensor_tensor(
            out=C.coef[:, col, :], in0=C.mT[:, sl_scale * 16:(sl_scale + 1) * 16, col], scalar=1.0,
            in1=C.gT[:, gi_pre, :], op0=ALU.add, op1=ALU.mult),
            reads=[C.d_mT, C.d_gT], writes=[C.d_coef])
        if sl_gate is not None:
            P.dve(lambda e, col=col: e.scalar_tensor_tensor(
                out=C.coef[:, 2 + col, :], in0=C.mT[:, sl_gate * 16:(sl_gate + 1) * 16, col], scalar=float(wres),
                in1=C.gT[:, gi_post, :], op0=ALU.mult, op1=ALU.mult),
                reads=[C.d_mT, C.d_gT], writes=[C.d_coef])


def emit_prenorm(P, C, srcs, sl_shift, out_sb, d_out, arena_deps, resident=False):
    nx = len(C.xt)
    for ti, (off, n) in enumerate(C.tiles):
        src, d_src, col = srcs[ti]
        xi = ti % nx
        xt = C.xt[xi][:, :, 0:n]
        if not resident:
            P.dma(xt, src.rearrange("(c p) t -> p c t", p=128), reads=[d_src],
                  writes=[C.d_xt[xi]] + arena_deps, semkey=f"xt{xi}")
        ssp = C.ps[6 + (ti % 2)]
        d_ssp = C.d_ps[6 + (ti % 2)]
        for c in range(16):
            sq, d_sq = C.next_sq()
            P.act(lambda e, sq=sq, c=c, xt=xt, n=n: e.activation(out=sq[:, 0:n], in_=xt[:, c, :], func=AF.Square),
                  reads=[C.d_xt[xi]], writes=[d_sq])
            P.pe(lambda e, sq=sq, c=c, ssp=ssp, n=n: e.matmul(ssp[:, 0:n], lhsT=C.ones[:], rhs=sq[:, 0:n],
                                                              start=(c == 0), stop=(c == 15)),
                 reads=[d_sq, C.d_ones], writes=[d_ssp])
        tmp, d_tmp = C.next_tmp()
        P.act(lambda e, tmp=tmp, ssp=ssp, n=n: e.activation(out=tmp[:, 0:n], in_=ssp[:, 0:n], func=AF.Sqrt,
                                                            scale=1.0 / D, bias=EPS),
              reads=[d_ssp], writes=[d_tmp])
        rs = C.rstd[:, off:off + n]
        P.dve(lambda e, tmp=tmp, rs=rs, n=n: e.reciprocal(out=rs, in_=tmp[:, 0:n]),
              reads=[d_tmp], writes=[C.d_rstd[ti]])
        dst = out_sb(ti)
        for c in range(16):
            tmp, d_tmp = C.next_tmp()
            P.dve(lambda e, tmp=tmp, c=c, xt=xt, rs=rs, n=n, col=col: e.scalar_tensor_tensor(
                out=tmp[:, 0:n], in0=xt[:, c, :], scalar=C.coef[:, col, c:c + 1], in1=rs,
                op0=ALU.mult, op1=ALU.mult),
                reads=[C.d_xt[xi], C.d_rstd[ti], C.d_coef], writes=[d_tmp])
            P.act(lambda e, tmp=tmp, c=c, dst=dst, n=n, col=col: e.activation(
                out=dst[:, c, :], in_=tmp[:, 0:n], func=AF.Identity,
                bias=C.mT[:, sl_shift * 16 + c, col:col + 1], scale=1.0),
                reads=[d_tmp, C.d_mT], writes=[d_out[ti]])


def emit_ffn(P, C, srcs, dsts, wg, wu, wd, sl_shift, resident=False):
    nt = len(C.tiles)
    arena = C.d_A
    emit_prenorm(P, C, srcs, sl_shift, lambda ti: C.hy[:, :, C.tiles[ti][0]:C.tiles[ti][0] + C.tiles[ti][1]],
                 C.d_h, arena, resident)
    emit_gateup(P, C, wg, wu)
    emit_down_residual(P, C, NJ, wd, srcs, dsts)


def emit_gateup(P, C, wg, wu):
    it = 0
    for jj in range(22):
        s = jj % 2
        ncol = 256 if jj < 21 else 128
        for a, w in enumerate((wg, wu)):
            src = w[:, jj * 256:jj * 256 + ncol].rearrange("(c p) n -> p c n", p=128)
            P.dma(C.wgu[s][:, a, :, 0:ncol], src, writes=[C.d_wgu[s]], semkey=f"wgu{s}", eng="pool")
        for jl in range(ncol // 128):
            j = jj * 2 + jl
            for ti, (off, n) in enumerate(C.tiles):
                pg = (it % 4) * 2
                it += 1
                G, U = C.ps[pg], C.ps[pg + 1]
                for a, pt in enumerate((G, U)):
                    for k in range(16):
                        P.pe(lambda e, pt=pt, a=a, k=k, s=s, jl=jl, off=off, n=n: e.matmul(
                            pt[:, 0:n], lhsT=C.wgu[s][:, a, k, jl * 128:(jl + 1) * 128],
                            rhs=C.hy[:, k, off:off + n], start=(k == 0), stop=(k == 15)),
                            reads=[C.d_wgu[s], C.d_h[ti]], writes=[C.d_ps[pg + a]])
                tmp, d_tmp = C.next_tmp()
                P.act(lambda e, tmp=tmp, G=G, n=n: e.activation(out=tmp[:, 0:n], in_=G[:, 0:n], func=AF.Silu),
                      reads=[C.d_ps[pg]], writes=[d_tmp])
                P.dve(lambda e, tmp=tmp, U=U, j=j, off=off, n=n: e.tensor_tensor(
                    out=C.A[:, j, off:off + n], in0=tmp[:, 0:n], in1=U[:, 0:n], op=ALU.mult),
                    reads=[d_tmp, C.d_ps[pg + 1]], writes=[C.d_A[ti]] + C.d_xt)


def emit_down_residual(P, C, nch, wd, srcs, dsts):
    NJ = nch
    pend = []
    it = 0
    for dc in range(16):
        s = dc % 2
        src = wd[:, dc * 128:(dc + 1) * 128].rearrange("(j p) n -> p j n", p=128)
        P.dma(C.wd[s][:, 0:nch, :], src, writes=[C.d_wd[s]], semkey=f"wd{s}", eng="pool")
        for ti, (off, n) in enumerate(C.tiles):
            pi = it % 4
            it += 1
            Y = C.ps[pi]
            for j in range(NJ):
                P.pe(lambda e, Y=Y, j=j, s=s, off=off, n=n: e.matmul(
                    Y[:, 0:n], lhsT=C.wd[s][:, j, :], rhs=C.A[:, j, off:off + n],
                    start=(j == 0), stop=(j == NJ - 1)),
                    reads=[C.d_wd[s], C.d_A[ti]], writes=[C.d_ps[pi]])
            for f in pend:
                f()
            pend = []
            P.act(lambda e, Y=Y, dc=dc, off=off, n=n: e.activation(out=C.hy[:, dc, off:off + n], in_=Y[:, 0:n],
                                                                  func=AF.Copy),
                  reads=[C.d_ps[pi]], writes=[C.d_h[ti]])
            sq, d_sq = C.next_sq()
            P.act(lambda e, Y=Y, sq=sq, n=n: e.activation(out=sq[:, 0:n], in_=Y[:, 0:n], func=AF.Square),
                  reads=[C.d_ps[pi]], writes=[d_sq])
            SS = C.ps[4 + ti]

            def ssmm(sq=sq, d_sq=d_sq, SS=SS, ti=ti, dc=dc, n=n):
                P.pe(lambda e: e.matmul(SS[:, 0:n], lhsT=C.ones[:], rhs=sq[:, 0:n],
                                        start=(dc == 0), stop=(dc == 15)),
                     reads=[d_sq, C.d_ones], writes=[C.d_ps[4 + ti]])
            pend.append(ssmm)
    for f in pend:
        f()
    nx = len(C.xt)
    for ti, (off, n) in enumerate(C.tiles):
        src, d_src, col = srcs[ti]
        dst, d_dst, _ = dsts[ti]
        if dst is None:
            continue
        SS = C.ps[4 + ti]
        tmp, d_tmp = C.next_tmp()
        P.act(lambda e, tmp=tmp, SS=SS, n=n: e.activation(out=tmp[:, 0:n], in_=SS[:, 0:n], func=AF.Sqrt,
                                                          scale=1.0 / D, bias=EPS),
              reads=[C.d_ps[4 + ti]], writes=[d_tmp])
        rs = C.rstd[:, off:off + n]
        P.dve(lambda e, tmp=tmp, rs=rs, n=n: e.reciprocal(out=rs, in_=tmp[:, 0:n]),
              reads=[d_tmp], writes=[C.d_rstd[ti]])
        xi = ti % nx
        xt = C.xt[xi][:, :, 0:n]
        P.dma(xt, src.rearrange("(c p) t -> p c t", p=128), reads=[d_src],
              writes=[C.d_xt[xi]] + C.d_A, semkey=f"xt{xi}")
        for c in range(16):
            tmp, d_tmp = C.next_tmp()
            P.dve(lambda e, tmp=tmp, c=c, rs=rs, off=off, n=n, col=col: e.scalar_tensor_tensor(
                out=tmp[:, 0:n], in0=C.hy[:, c, off:off + n], scalar=C.coef[:, 2 + col, c:c + 1], in1=rs,
                op0=ALU.mult, op1=ALU.mult),
                reads=[C.d_h[ti], C.d_rstd[ti], C.d_coef], writes=[d_tmp])
            P.dve(lambda e, tmp=tmp, c=c, xt=xt, n=n: e.tensor_tensor(
                out=xt[:, c, :], in0=xt[:, c, :], in1=tmp[:, 0:n], op=ALU.add),
                reads=[d_tmp, C.d_xt[xi]], writes=[C.d_xt[xi]])
        P.dma(dst.rearrange("(c p) t -> p c t", p=128), xt, reads=[C.d_xt[xi]], writes=[d_dst],
              semkey=f"xt{xi}")


def emit_modulation_sharded(P, C, ada_sl, cvecT, bT0, bT1, mp_own, mp_g, mTd, d_mTd):
    cv = P.sbuf("mcv", [128, 16, 2], F32)
    sc = P.sbuf("msc", [128, 16, 2], BF16)
    bT = P.sbuf("mbT", [128, 2, 144], F32)
    part = P.sbuf("mpart", [128, 288], F32)
    mo = P.sbuf("mo", [128, 2, 144, 2], F32)
    d_cv, d_sc, d_bT, d_part, d_mo = [Dep(n) for n in ("mcv", "msc", "mbT", "mpart", "mo")]
    P.dma(cv[:], cvecT, writes=[d_cv], semkey="small")
    P.dma(bT[:, 0, :], bT0, writes=[d_bT], semkey="small2")
    P.dma(bT[:, 1, :], bT1, writes=[d_bT], semkey="small2")
    P.act(lambda e: e.activation(out=sc[:], in_=cv[:], func=AF.Silu), reads=[d_cv], writes=[d_sc])
    mp = C.ps[7]
    d_mp = C.d_ps[7]
    mpv = mp[:, 0:288].rearrange("p (l j r) -> p l j r", l=2, j=72)
    it = 0
    for l in range(2):
        for jb in range(18):
            s = it % 2
            it += 1
            sl = C.wgu[s][:].rearrange("p a c n -> p (a c n)").rearrange("p (c n) -> p c n", c=16)
            src = ada_sl[l, :, jb * 512:(jb + 1) * 512].rearrange("(c p) n -> p c n", p=128)
            P.dma(sl, src, writes=[C.d_wgu[s]], semkey=f"wgu{s}", eng="pool")
            for j4 in range(4):
                j = jb * 4 + j4
                for k in range(16):
                    P.pe(lambda e, sl=sl, j4=j4, k=k, j=j, l=l: e.matmul(
                        mpv[:, l, j, :], lhsT=sl[:, k, j4 * 128:(j4 + 1) * 128], rhs=sc[:, k, :],
                        start=(k == 0), stop=(k == 15)),
                        reads=[C.d_wgu[s], d_sc], writes=[d_mp])
    P.dve(lambda e: e.tensor_copy(out=part[:], in_=mp[:, 0:288]), reads=[d_mp], writes=[d_part])
    d_mi, d_mr = Dep("mp_own"), Dep("mp_g")
    P.dma(mp_own[:, :], part[:], reads=[d_part], writes=[d_mi], semkey="mred")
    P.allgather_pairs(mp_g, mp_own, reads=[d_mi], writes=[d_mr], semkey="cc0")
    for l in range(2):
        for r in range(2):
            P.dma(mo[:, l, r * 72:(r + 1) * 72, :],
                  mp_g[r * 128:(r + 1) * 128, l * 144:(l + 1) * 144].rearrange("p (j c) -> p j c", c=2),
                  reads=[d_mr], writes=[d_mo], semkey="mred")
        for col in range(2):
            P.dve(lambda e, l=l, col=col: e.tensor_tensor(out=mo[:, l, :, col], in0=mo[:, l, :, col], in1=bT[:, l, :],
                                                          op=ALU.add), reads=[d_mo, d_bT], writes=[d_mo])
        P.dma(mTd[l][:], mo[:, l, :, :], reads=[d_mo], writes=[d_mTd[l]], semkey="mTo")


EPS = 1e-6
NTOK = 2304
NCTX = 256
NLAT = 2048
LAM_INIT0 = 0.8 - 0.6 * math.exp(-0.3 * 0)


DBG = {}


class MixCtx:
    def __init__(self, P):
        self.P = P
        self.hs = P.sbuf("hs", [128, 16, NTOK], BF16)
        self.d_hs = Dep("hs")
        self.ropeC = P.sbuf("ropeC", [128, NLAT], F32)
        self.ropeS = P.sbuf("ropeS", [128, NLAT], F32)
        self.d_rope = Dep("rope")
        self.wt = [P.sbuf(f"wt{i}", [128, 16, 128], BF16) for i in range(8)]
        self.d_wt = [Dep(f"wt{i}") for i in range(8)]
        self.ps, self.d_ps = P.shared_psum()
        self.d_ph = [[Dep(f"ph{i}_{h}") for h in range(2)] for i in range(8)]
        self.ones = P.sbuf("ones", [128, 128], BF16)
        self.onesf = P.sbuf("onesf", [128, 128], F32)
        self.d_ones = Dep("ones")
        P.dve(lambda e: e.memset(self.ones[:], 1.0), writes=[self.d_ones])
        P.dve(lambda e: e.memset(self.onesf[:], 1.0), writes=[self.d_ones])
        self.perm = P.sbuf("perm", [128, 128], BF16)
        self.ident = P.sbuf("ident", [128, 128], BF16)
        self.maskf = P.sbuf("maskf", [128, 128], F32)
        self.maskb = P.sbuf("maskb", [128, 128], F32)
        self.d_const = Dep("const")
        self.vec = P.sbuf("vec", [128, 32], F32)
        self.d_vec = Dep("vec")
        self.lbr = P.sbuf("lbr", [128, 2, 2, 4], F32)
        self.lb = P.sbuf("lb", [128, 2, 4], F32)
        self.oml = P.sbuf("oml", [128, 2, 4], F32)
        self.d_lb = Dep("lb")
        u0 = P.aoff
        self.qT = P.sbuf("qT", [128, NLAT], BF16)
        self.kT = P.sbuf("kT", [128, NTOK], BF16)
        self.V = P.sbuf("V", [128, 18, 128], BF16)
        self.d_qT, self.d_kT, self.d_V = Dep("qT"), Dep("kT"), Dep("V")
        self.E = [P.sbuf(f"E{i}", [128, 512], BF16) for i in range(4)]
        self.d_E = [Dep(f"E{i}") for i in range(4)]
        u1 = P.aoff
        if P.arena is not None:
            P.aoff = u0
        self.zf_sb = P.sbuf("zf_sb", [128, NTOK], F32)
        self.zb_sb = P.sbuf("zb_sb", [128, NTOK], F32)
        self.q_sb = P.sbuf("q_sb", [128, NTOK], BF16)
        self.i_sb = P.sbuf("i_sb", [128, 18, 128], BF16)
        self.d_rp = Dep("recproj")
        self.d_zf = Dep("zf_sb")
        if P.arena is not None:
            P.aoff = max(u1, P.aoff)
        self.qtF = P.sbuf("qtF", [128, NTOK], BF16)
        self.ktF = P.sbuf("ktF", [128, NTOK], BF16)
        self.zfb = self.zf_sb.bitcast(BF16)
        self.d_zfu = [Dep(f"zfu{u}") for u in range(9)]
        self.d_qk = [Dep("qkF"), Dep("qkB")]
        self.sg_sb = P.sbuf("sg_sb", [128, NLAT], BF16)
        self.svall = P.sbuf("svall", [128, 2, 18, 4], F32)
        self.d_sva = Dep("svall")
        self.d_svad = [Dep("svallF"), Dep("svallB")]
        self.d_tfh = [[Dep(f"tfh{i}_{h}") for h in range(2)] for i in range(6)]
        self.maskR = P.sbuf("maskR", [128, 512], F32)
        P.dve(lambda e: e.memset(self.maskR[:], 1.0), writes=[self.d_ones])
        for cc in range(4):
            P.dve(lambda e, cc=cc: e.memset(self.maskR[:, cc * 128:cc * 128 + 1], 0.0), writes=[self.d_ones])
        self.tf = [P.sbuf(f"tf{i}", [128, 512], F32) for i in range(6)]
        self.d_tf = [Dep(f"tf{i}") for i in range(6)]
        self.tb = [P.sbuf(f"tb{i}", [128, 512], BF16) for i in range(4)]
        self.d_tb = [Dep(f"tb{i}") for i in range(4)]
        self.tfi = 0
        self.tbi = 0
        self.Ei = 0
        self.S = P.sbuf("S", [128, 128], F32)
        self.d_S = Dep("S")
        self.S1 = P.sbuf("S1", [128, 128], F32)
        self.d_S1 = Dep("S1")
        self.ofw = P.sbuf("ofw", [128, NLAT], F32)
        self.d_ofw = Dep("ofw")
        self.obw = P.sbuf("obw", [128, NLAT], F32)
        self.d_obw = Dep("obw")
        self.rf = [P.sbuf(f"rf{i}", [128, 128], F32) for i in range(6)]
        self.d_rf = [Dep(f"rf{i}") for i in range(6)]
        self.rb = [P.sbuf(f"rb{i}", [128, 128], BF16) for i in range(12)]
        self.d_rb = [Dep(f"rb{i}") for i in range(12)]
        self.rfi = 0
        self.rbi = 0
        self.sv = [P.sbuf(f"sv{i}", [128, 8], F32) for i in range(8)]
        self.d_sv = [Dep(f"sv{i}") for i in range(8)]
        self.svi = 0

    def ntf(self):
        i = self.tfi % 6
        self.tfi += 1
        return self.tf[i], self.d_tf[i]

    def ntb(self):
        i = self.tbi % 4
        self.tbi += 1
        return self.tb[i], self.d_tb[i]

    def nE(self):
        i = self.Ei % 4
        self.Ei += 1
        return self.E[i], self.d_E[i]

    def nrf(self):
        i = self.rfi % 6
        self.rfi += 1
        return self.rf[i], self.d_rf[i]

    def nrb(self):
        i = self.rbi % 12
        self.rbi += 1
        return self.rb[i], self.d_rb[i]

    def nsv(self):
        i = self.svi % 8
        self.svi += 1
        return self.sv[i], self.d_sv[i]


def load_w(P, M, slot, src):
    P.dma(M.wt[slot][:], src.rearrange("(c p) n -> p c n", p=128), writes=[M.d_wt[slot]],
          semkey=f"wt{slot}", eng="pool")


def emit_mix_setup(P, M, hTf, ropeC, ropeS, perm, ident, maskf, maskb, lamT, dng, rng, lbraw):
    for q in range(4 if hTf is not None else 0):
        P.dma(M.hs[:, q * 4:(q + 1) * 4, :], hTf[q * 512:(q + 1) * 512, :].rearrange("(c p) t -> p c t", p=128),
              writes=[M.d_hs], semkey="hs")
    P.dma(M.ropeC[:], ropeC, writes=[M.d_rope], semkey="rope")
    P.dma(M.ropeS[:], ropeS, writes=[M.d_rope], semkey="rope")
    P.dma(M.perm[:], perm, writes=[M.d_const], semkey="cst", eng="pool")
    P.dma(M.ident[:], ident, writes=[M.d_const], semkey="cst", eng="pool")
    P.dma(M.maskf[:], maskf, writes=[M.d_const], semkey="cst2")
    P.dma(M.maskb[:], maskb, writes=[M.d_const], semkey="cst2")
    P.dma(M.vec[0:64, 0:4], lamT, writes=[M.d_vec], semkey="vec")
    P.dma(M.vec[:, 4:5], dng, writes=[M.d_vec], semkey="vec")
    P.dma(M.vec[:, 5:6], rng, writes=[M.d_vec], semkey="vec")
    P.dma(M.lbr[:], lbraw, writes=[M.d_lb], semkey="lb")
    P.dve(lambda e: e.tensor_tensor(out=M.vec[0:64, 6:7], in0=M.vec[0:64, 0:1], in1=M.vec[0:64, 1:2], op=ALU.mult),
          reads=[M.d_vec], writes=[M.d_vec])
    P.dve(lambda e: e.tensor_tensor(out=M.vec[0:64, 7:8], in0=M.vec[0:64, 2:3], in1=M.vec[0:64, 3:4], op=ALU.mult),
          reads=[M.d_vec], writes=[M.d_vec])
    lp = M.ps[7]
    P.pe(lambda e: e.matmul(lp[:, 0:2], lhsT=M.onesf[0:64, :], rhs=M.vec[0:64, 6:8], start=True, stop=True),
         reads=[M.d_vec, M.d_ones], writes=[M.d_ps[7]])
    P.act(lambda e: e.activation(out=M.vec[:, 8:10], in_=lp[:, 0:2], func=AF.Exp), reads=[M.d_ps[7]],
          writes=[M.d_vec])
    P.dve(lambda e: e.tensor_tensor(out=M.vec[:, 10:11], in0=M.vec[:, 9:10], in1=M.vec[:, 8:9], op=ALU.subtract),
          reads=[M.d_vec], writes=[M.d_vec])
    P.dve(lambda e: e.tensor_scalar(out=M.vec[:, 10:11], in0=M.vec[:, 10:11], scalar1=-LAM_INIT0, scalar2=None,
                                    op0=ALU.add), reads=[M.d_vec], writes=[M.d_vec])
    P.dve(lambda e: e.tensor_scalar(out=M.vec[:, 11:12], in0=M.vec[:, 4:5], scalar1=1.0 - LAM_INIT0, scalar2=None,
                                    op0=ALU.mult), reads=[M.d_vec], writes=[M.d_vec])
    P.dve(lambda e: e.tensor_tensor(out=M.lb[:], in0=M.lbr[:, :, 1, :], in1=M.lbr[:, :, 0, :], op=ALU.subtract),
          reads=[M.d_lb], writes=[M.d_lb])
    P.act(lambda e: e.activation(out=M.lb[:], in_=M.lb[:], func=AF.Exp), reads=[M.d_lb], writes=[M.d_lb])
    P.dve(lambda e: e.tensor_scalar(out=M.lb[:], in0=M.lb[:], scalar1=1.0, scalar2=None, op0=ALU.add),
          reads=[M.d_lb], writes=[M.d_lb])
    P.dve(lambda e: e.reciprocal(out=M.lb[:], in_=M.lb[:]), reads=[M.d_lb], writes=[M.d_lb])
    P.dve(lambda e: e.tensor_scalar(out=M.oml[:], in0=M.lb[:], scalar1=-1.0, scalar2=1.0, op0=ALU.mult, op1=ALU.add),
          reads=[M.d_lb], writes=[M.d_lb])


def proj_fm(P, M, out_ps, d_out, slot, off, n):
    for k in range(16):
        P.pe(lambda e, k=k: e.matmul(out_ps, lhsT=M.wt[slot][:, k, :], rhs=M.hs[:, k, off:off + n],
                                     start=(k == 0), stop=(k == 15)),
             reads=[M.d_wt[slot], M.d_hs], writes=[d_out])


def proj_tm(P, M, out_ps, d_out, slot, off):
    for k in range(16):
        P.pe(lambda e, k=k: e.matmul(out_ps, lhsT=M.hs[:, k, off:off + 128], rhs=M.wt[slot][:, k, :],
                                     start=(k == 0), stop=(k == 15)),
             reads=[M.d_wt[slot], M.d_hs], writes=[d_out])


def emit_rsqrt(P, M, out_sb, d_o, in_ps, d_in, n, inv_dim):
    P.act(lambda e: e.activation(out=out_sb, in_=in_ps, func=AF.Ln, scale=inv_dim, bias=EPS),
          reads=[d_in], writes=[d_o])
    P.act(lambda e: e.activation(out=out_sb, in_=out_sb, func=AF.Exp, scale=-0.5), reads=[d_o], writes=[d_o])


def emit_attention_head(P, M, hd, mergedT, d_merged):
    sq_, sk_, sv_ = 0, 1, 2
    tiles = [(0, NCTX)] + [(NCTX + i * 512, 512) for i in range(4)]
    bi = 0
    import os
    NSUB = int(os.environ.get("ATT_SUB", "99"))
    for (off, n) in tiles:
        for which in ("k", "q"):
            if bi >= NSUB:
                continue
            if which == "q" and off < NCTX:
                continue
            slot = sk_ if which == "k" else sq_
            dstT = M.kT if which == "k" else M.qT
            d_dst = M.d_kT if which == "k" else M.d_qT
            doff = off if which == "k" else off - NCTX
            pb = bi % 2
            bi += 1
            pp, d_pp = M.ps[pb], M.d_ps[pb]
            proj_fm(P, M, pp[:, 0:n], d_pp, slot, off, n)
            if off < NCTX:
                P.act(lambda e, pp=pp, n=n, dstT=dstT, doff=doff: e.activation(
                    out=dstT[:, doff:doff + n], in_=pp[:, 0:n], func=AF.Copy), reads=[d_pp], writes=[d_dst])
                continue
            loff = off - NCTX
            sb, d_sb = M.ntb()
            P.act(lambda e, pp=pp, sb=sb, n=n: e.activation(out=sb[:, 0:n], in_=pp[:, 0:n], func=AF.Copy),
                  reads=[d_pp], writes=[d_sb])
            rp, d_rp = M.ps[2 + pb], M.d_ps[2 + pb]
            P.pe(lambda e, rp=rp, sb=sb, n=n: e.matmul(rp[:, 0:n], lhsT=M.perm[:], rhs=sb[:, 0:n], start=True,
                                                       stop=True), reads=[d_sb, M.d_const], writes=[d_rp])
            t1, d_t1 = M.ntf()
            P.dve(lambda e, t1=t1, pp=pp, n=n, loff=loff: e.tensor_tensor(
                out=t1[:, 0:n], in0=pp[:, 0:n], in1=M.ropeC[:, loff:loff + n], op=ALU.mult),
                reads=[d_pp, M.d_rope], writes=[d_t1])
            t2, d_t2 = M.ntf()
            P.dve(lambda e, t2=t2, rp=rp, n=n, loff=loff: e.tensor_tensor(
                out=t2[:, 0:n], in0=rp[:, 0:n], in1=M.ropeS[:, loff:loff + n], op=ALU.mult),
                reads=[d_rp, M.d_rope], writes=[d_t2])
            P.dve(lambda e, t1=t1, t2=t2, dstT=dstT, doff=doff, n=n: e.tensor_tensor(
                out=dstT[:, doff:doff + n], in0=t1[:, 0:n], in1=t2[:, 0:n], op=ALU.add),
                reads=[d_t1, d_t2], writes=[d_dst])
    import os
    if DBG.get("qT") is not None and hd == 0:
        P.dma(DBG["qT"][:, :], M.qT[:], reads=[M.d_qT], writes=[Dep("dbgq")], semkey="dbg")
        P.dma(DBG["kT"][:, :], M.kT[:], reads=[M.d_kT], writes=[Dep("dbgk")], semkey="dbg")
    STG = int(os.environ.get("ATT_STAGE", "9"))
    if STG < 2:
        return
    for g4 in range(5):
        pb = 4 + (g4 % 2)
        vp, d_vp = M.ps[pb], M.d_ps[pb]
        nk = 4 if g4 < 4 else 2
        for i in range(nk):
            kt = g4 * 4 + i
            proj_tm(P, M, vp[:, i * 128:(i + 1) * 128], d_vp, sv_, kt * 128)
        P.act(lambda e, vp=vp, g4=g4, nk=nk: e.activation(
            out=M.V[:, g4 * 4:g4 * 4 + nk, :].rearrange("p a n -> p (a n)"), in_=vp[:, 0:nk * 128], func=AF.Copy),
            reads=[d_vp], writes=[M.d_V])
    if STG < 3:
        return
    for qt in [int(c) for c in os.environ.get("ATT_QT", "0123")]:
        qo = qt * 512
        steps = [(kt, m) for kt in range(18) for m in range(2)]
        pend = None
        for si, (kt, m) in enumerate(steps):
            sb_i = si % 4
            ST, d_ST = M.ps[sb_i], M.d_ps[sb_i]
            P.pe(lambda e, ST=ST, kt=kt, m=m: e.matmul(
                ST[:, :], lhsT=M.kT[m * 64:(m + 1) * 64, kt * 128:(kt + 1) * 128],
                rhs=M.qT[m * 64:(m + 1) * 64, qo:qo + 512], start=True, stop=True),
                reads=[M.d_kT, M.d_qT], writes=[d_ST])
            E, d_E = M.nE()
            P.act(lambda e, E=E, ST=ST: e.activation(out=E[:], in_=ST[:, :], func=AF.Exp, scale=0.125),
                  reads=[d_ST], writes=[d_E])
            if pend is not None:
                pend()

            def pv(E=E, d_E=d_E, kt=kt, m=m):
                P.pe(lambda e: e.matmul(M.ps[4 + m][:, :], lhsT=M.V[:, kt, :], rhs=E[:], start=(kt == 0),
                                        stop=(kt == 17)), reads=[M.d_V, d_E], writes=[M.d_ps[4 + m]])
                P.pe(lambda e: e.matmul(M.ps[6 + m][:, :], lhsT=M.ones[:], rhs=E[:], start=(kt == 0),
                                        stop=(kt == 17)), reads=[M.d_ones, d_E], writes=[M.d_ps[6 + m]])
            pend = pv
        pend()
        r0, d_r0 = M.ntf()
        r1, d_r1 = M.ntf()
        P.dve(lambda e, r0=r0: e.reciprocal(out=r0[:], in_=M.ps[6][:, :]), reads=[M.d_ps[6]], writes=[d_r0])
        P.dve(lambda e, r1=r1: e.reciprocal(out=r1[:], in_=M.ps[7][:, :]), reads=[M.d_ps[7]], writes=[d_r1])
        oa, d_oa = M.ntf()
        ob, d_ob = M.ntf()
        P.dve(lambda e, oa=oa, r0=r0: e.tensor_tensor(out=oa[:], in0=M.ps[4][:, :], in1=r0[:], op=ALU.mult),
              reads=[M.d_ps[4], d_r0], writes=[d_oa])
        P.dve(lambda e, ob=ob, r1=r1: e.tensor_tensor(out=ob[:], in0=M.ps[5][:, :], in1=r1[:], op=ALU.mult),
              reads=[M.d_ps[5], d_r1], writes=[d_ob])
        o, d_o = M.ntf()
        P.dve(lambda e, o=o, oa=oa, ob=ob: e.scalar_tensor_tensor(
            out=o[:], in0=ob[:], scalar=M.vec[:, 10:11], in1=oa[:], op0=ALU.mult, op1=ALU.add),
            reads=[d_oa, d_ob, M.d_vec], writes=[d_o])
        sq, d_sq = M.ntb()
        P.act(lambda e, sq=sq, o=o: e.activation(out=sq[:], in_=o[:], func=AF.Square), reads=[d_o], writes=[d_sq])
        P.pe(lambda e, sq=sq: e.matmul(M.ps[0][:, :], lhsT=M.ones[:], rhs=sq[:], start=True, stop=True),
             reads=[d_sq, M.d_ones], writes=[M.d_ps[0]])
        ri, d_ri = M.ntf()
        emit_rsqrt(P, M, ri[:], d_ri, M.ps[0][:, :], M.d_ps[0], 512, 1.0 / 128)
        ob16, d_ob16 = M.ntb()
        P.dve(lambda e, ob16=ob16, o=o, ri=ri: e.scalar_tensor_tensor(
            out=ob16[:], in0=o[:], scalar=M.vec[:, 11:12], in1=ri[:], op0=ALU.mult, op1=ALU.mult),
            reads=[d_o, d_ri, M.d_vec], writes=[d_ob16])
        rows = mergedT("att", hd) if callable(mergedT) else mergedT[hd * 128:(hd + 1) * 128, :]
        P.dma(rows[:, qo:qo + 512], ob16[:], reads=[d_ob16], writes=[d_merged], semkey="mg")


def emit_rec_head(P, M, r, mergedT, d_merged, after_burst=None):
    s_q, s_zf, s_zb, s_i, s_g = 3, 4, 5, 6, 7
    orders = [list(range(18)), [1, 0] + list(range(17, 1, -1))]
    Ss = [M.S, M.S1]
    dSs = [M.d_S, M.d_S1]
    obuf = [M.ofw, M.obw]
    d_obuf = [M.d_ofw, M.d_obw]
    for dr in range(2):
        P.dve(lambda e, dr=dr: e.memset(Ss[dr][:], 0.0), writes=[dSs[dr]])
    ptiles = [(0, NCTX)] + [(NCTX + i * 512, 512) for i in range(4)]
    bi = 0
    for (off, n) in ptiles:
        for (slot, dst, sc_) in ((s_zf, M.zf_sb, 1.0), (s_zb, M.zb_sb, 1.0), (s_q, M.q_sb, float(128 ** -0.5))):
            pb = bi % 4
            bi += 1
            proj_fm(P, M, M.ps[pb][:, 0:n], M.d_ps[pb], slot, off, n)
            if slot != s_q:
                P.act(lambda e, pb=pb, dst=dst, off=off, n=n: e.activation(
                    out=dst[:, off:off + n], in_=M.ps[pb][:, 0:n], func=AF.Sigmoid, scale=-1.0),
                    reads=[M.d_ps[pb]], writes=[M.d_rp])
            else:
                P.dve(lambda e, pb=pb, dst=dst, off=off, n=n, sc_=sc_: e.tensor_scalar(
                    out=dst[:, off:off + n], in0=M.ps[pb][:, 0:n], scalar1=sc_, scalar2=None, op0=ALU.mult),
                    reads=[M.d_ps[pb]], writes=[M.d_rp])
        if off >= NCTX:
            pb = bi % 4
            bi += 1
            proj_fm(P, M, M.ps[pb][:, 0:n], M.d_ps[pb], s_g, off, n)
            P.act(lambda e, pb=pb, off=off, n=n: e.activation(
                out=M.sg_sb[:, off - NCTX:off - NCTX + n], in_=M.ps[pb][:, 0:n], func=AF.Silu),
                reads=[M.d_ps[pb]], writes=[M.d_rp])
    for g4 in range(5):
        pb = 4 + (g4 % 2)
        nk = 4 if g4 < 4 else 2
        for i in range(nk):
            proj_tm(P, M, M.ps[pb][:, i * 128:(i + 1) * 128], M.d_ps[pb], s_i, (g4 * 4 + i) * 128)
        P.act(lambda e, pb=pb, g4=g4, nk=nk: e.activation(
            out=M.i_sb[:, g4 * 4:g4 * 4 + nk, :].rearrange("p a n -> p (a n)"), in_=M.ps[pb][:, 0:nk * 128],
            func=AF.Copy), reads=[M.d_ps[pb]], writes=[M.d_rp])

    if after_burst is not None:
        after_burst()
    RSTG = int(os.environ.get("REC_STAGE", "9"))
    if RSTG < 2:
        return
    def prep_gen(dr):
        zsb = M.zf_sb if dr == 0 else M.zb_sb
        T = [t[:, dr * 256:(dr + 1) * 256] for t in M.tf]
        dT = [M.d_tfh[i][dr] for i in range(6)]
        n = 256
        for off in range(0, NTOK, 256):
            c0 = off // 128
            u = off // 256
            if dr == 0:
                qdst, kdst = M.qtF[:, off:off + n], M.ktF[:, off:off + n]
                wdeps = [M.d_qk[0]]
                rdeps = [M.d_rp, M.d_zfu[u]]
            else:
                qdst, kdst = M.zfb[:, u * 512:u * 512 + 256], M.zfb[:, u * 512 + 256:u * 512 + 512]
                wdeps = [M.d_qk[1], M.d_zfu[u]]
                rdeps = [M.d_rp]
            P.dve(lambda e, off=off: e.tensor_scalar(out=T[2], in0=zsb[:, off:off + n], scalar1=M.oml[:, dr, r:r + 1],
                                                     scalar2=None, op0=ALU.mult), reads=rdeps + [M.d_lb],
                  writes=[dT[2]])
            yield
            P.act(lambda e: e.activation(out=T[3], in_=T[2], func=AF.Ln, scale=-1.0, bias=1.0), reads=[dT[2]],
                  writes=[dT[3]])
            yield
            P.dve(lambda e: e.tensor_tensor_scan(out=T[4], data0=M.maskR[:, 0:n], data1=T[3], initial=0.0,
                                                 op0=ALU.mult, op1=ALU.add), reads=[dT[3], M.d_ones], writes=[dT[4]])
            yield
            pfv = T[4].rearrange("p (c t) -> p c t", t=128)
            if dr == 1:
                P.dve(lambda e: e.tensor_tensor(out=T[0], in0=T[4], in1=T[3], op=ALU.subtract),
                      reads=[dT[4], dT[3]], writes=[dT[0]])
                yield
            for ci in range(2):
                src = T[0] if dr == 1 else T[4]
                P.dve(lambda e, ci=ci, src=src: e.tensor_scalar(
                    out=T[5][:, ci * 128:(ci + 1) * 128], in0=src[:, ci * 128:(ci + 1) * 128],
                    scalar1=T[4][:, ci * 128 + 63:ci * 128 + 64], scalar2=(1.0 if dr == 0 else -1.0),
                    op0=ALU.subtract, op1=ALU.mult), reads=[dT[0], dT[4]], writes=[dT[5]])
                yield
            P.act(lambda e: e.activation(out=T[1], in_=T[5], func=AF.Exp), reads=[dT[5]], writes=[dT[1]])
            yield
            P.act(lambda e: e.activation(out=T[3], in_=T[5], func=AF.Exp, scale=-1.0), reads=[dT[5]], writes=[dT[3]])
            yield
            P.dve(lambda e, off=off, qdst=qdst: e.tensor_tensor(out=qdst, in0=M.q_sb[:, off:off + n], in1=T[1],
                                                                op=ALU.mult), reads=[M.d_rp, dT[1]], writes=wdeps)
            yield
            P.dve(lambda e, kdst=kdst: e.tensor_tensor(out=kdst, in0=T[2], in1=T[3], op=ALU.mult),
                  reads=[dT[2], dT[3]], writes=wdeps)
            yield
            svv = M.svall[:, dr, c0:c0 + 2, :]
            d_sva = M.d_svad[dr]
            P.dve(lambda e, svv=svv, pfv=pfv: e.tensor_tensor(out=svv[:, :, 0], in0=pfv[:, :, 127], in1=pfv[:, :, 63],
                                                              op=ALU.subtract), reads=[dT[4]], writes=[d_sva])
            yield
            P.act(lambda e, svv=svv, pfv=pfv: e.activation(out=svv[:, :, 1], in_=pfv[:, :, 63], func=AF.Exp),
                  reads=[dT[4]], writes=[d_sva])
            yield
            P.act(lambda e, svv=svv: e.activation(out=svv[:, :, 2], in_=svv[:, :, 0], func=AF.Exp),
                  reads=[d_sva], writes=[d_sva])
            yield
            P.act(lambda e, svv=svv, pfv=pfv: e.activation(out=svv[:, :, 3], in_=pfv[:, :, 127], func=AF.Exp),
                  reads=[dT[4]], writes=[d_sva])
            yield

    import itertools
    P.fence(M.d_tf, [d for pair in M.d_tfh for d in pair])
    for _ in itertools.zip_longest(prep_gen(0), prep_gen(1)):
        pass
    P.fence([d for pair in M.d_tfh for d in pair], M.d_tf)

    if RSTG < 3:
        return

    def chunk_gen(dr, step):
        if True:
            c = orders[dr][step]
            mask = M.maskf if dr == 0 else M.maskb
            S_, d_S = Ss[dr], dSs[dr]
            a = c * 128
            lat = c >= 2
            la = a - NCTX
            bx = dr * 4 + (step % 2) * 2
            by = bx + 1
            if dr == 0:
                qt_, kt_ = M.qtF[:, a:a + 128], M.ktF[:, a:a + 128]
            else:
                ub = (c // 2) * 512 + (c % 2) * 128
                qt_, kt_ = M.zfb[:, ub:ub + 128], M.zfb[:, ub + 256:ub + 384]
            d_qt = d_kt = M.d_qk[dr]
            vt, d_vt = M.i_sb[:, c, :], M.d_rp
            sv, d_sv = M.svall[:, dr, c, :], M.d_svad[dr]
            c_e1 = 1 if dr == 0 else 2
            c_e2 = 2 if dr == 0 else 1
            ktp = M.ps[bx][:, 384:448].bitcast(BF16)
            d_ktp = M.d_ps[bx]
            P.pe(lambda e, ktp=ktp, kt_=kt_: e.transpose(ktp, kt_, M.ident[:]), reads=[d_kt, M.d_const],
                 writes=[d_ktp])
            yield
            ktok, d_ktok = M.nrb()
            P.act(lambda e, ktok=ktok, ktp=ktp: e.activation(out=ktok[:], in_=ktp, func=AF.Copy),
                  reads=[d_ktp], writes=[d_ktok])
            yield
            if lat:
                atp, d_atp = M.ps[by][:, 0:128], M.d_ps[by]
                P.pe(lambda e, atp=atp, kt_=kt_, qt_=qt_: e.matmul(atp, lhsT=kt_, rhs=qt_, start=True, stop=True),
                     reads=[d_kt, d_qt], writes=[d_atp])
                yield
                am, d_am = M.nrb()
                P.dve(lambda e, am=am, atp=atp: e.tensor_tensor(out=am[:], in0=atp, in1=mask[:], op=ALU.mult),
                      reads=[d_atp, M.d_const], writes=[d_am])
                yield
                sp, d_sp = M.nrb()
                P.dve(lambda e, sp=sp, sv=sv: e.tensor_scalar(out=sp[:], in0=S_[:], scalar1=sv[:, c_e1:c_e1 + 1],
                                                              scalar2=None, op0=ALU.mult),
                      reads=[d_S, d_sv], writes=[d_sp])
                yield
                op_, d_op = M.ps[by][:, 128:256], M.d_ps[by]
                P.pe(lambda e, op_=op_, sp=sp, qt_=qt_: e.matmul(op_, lhsT=sp[:], rhs=qt_, start=True, stop=False),
                     reads=[d_sp, d_qt], writes=[d_op])
                yield
                P.pe(lambda e, op_=op_, vt=vt, am=am: e.matmul(op_, lhsT=vt, rhs=am[:], start=False, stop=True),
                     reads=[d_vt, d_am], writes=[d_op])
                yield
                P.act(lambda e, op_=op_, la=la: e.activation(out=obuf[dr][:, la:la + 128], in_=op_, func=AF.Copy),
                      reads=[d_op], writes=[d_obuf[dr]])
                yield
            kvp, d_kvp = M.ps[by][:, 256:384], M.d_ps[by]
            P.pe(lambda e, kvp=kvp, ktok=ktok, vt=vt: e.matmul(kvp, lhsT=ktok[:], rhs=vt, start=True, stop=True),
                 reads=[d_ktok, d_vt], writes=[d_kvp])
            yield
            tk, d_tk = M.nrf()
            P.dve(lambda e, tk=tk, kvp=kvp, sv=sv: e.tensor_scalar(out=tk[:], in0=kvp, scalar1=sv[:, c_e2:c_e2 + 1],
                                                                    scalar2=None, op0=ALU.mult),
                  reads=[d_kvp, d_sv], writes=[d_tk])
            yield
            P.dve(lambda e, tk=tk, sv=sv: e.scalar_tensor_tensor(out=S_[:], in0=S_[:], scalar=sv[:, 3:4],
                                                                 in1=tk[:], op0=ALU.mult, op1=ALU.add),
                  reads=[d_tk, d_sv, d_S], writes=[d_S])
            yield
    import itertools
    for step in range(18):
        gens = [chunk_gen(0, step), chunk_gen(1, step)]
        for _ in itertools.zip_longest(*gens):
            pass
    if RSTG < 4:
        return
    for t in range(4):
        lo = t * 512
        o_, d_o = M.ntf()
        P.dve(lambda e, o_=o_: e.tensor_tensor(out=o_[:], in0=M.ofw[:, lo:lo + 512], in1=M.obw[:, lo:lo + 512],
                                               op=ALU.add), reads=[M.d_ofw, M.d_obw], writes=[d_o])
        sq, d_sq = M.ntb()
        P.act(lambda e, sq=sq, o_=o_: e.activation(out=sq[:], in_=o_[:], func=AF.Square), reads=[d_o], writes=[d_sq])
        P.pe(lambda e, sq=sq: e.matmul(M.ps[0][:, :], lhsT=M.ones[:], rhs=sq[:], start=True, stop=True),
             reads=[d_sq, M.d_ones], writes=[M.d_ps[0]])
        ri, d_ri = M.ntf()
        emit_rsqrt(P, M, ri[:], d_ri, M.ps[0][:, :], M.d_ps[0], 512, 1.0 / 128)
        o2, d_o2 = M.ntf()
        P.dve(lambda e, o2=o2, o_=o_, ri=ri: e.scalar_tensor_tensor(
            out=o2[:], in0=o_[:], scalar=M.vec[:, 5:6], in1=ri[:], op0=ALU.mult, op1=ALU.mult),
            reads=[d_o, d_ri, M.d_vec], writes=[d_o2])
        o3, d_o3 = M.ntb()
        P.dve(lambda e, o3=o3, o2=o2: e.tensor_tensor(out=o3[:], in0=o2[:], in1=M.sg_sb[:, lo:lo + 512], op=ALU.mult),
              reads=[d_o2, M.d_rp], writes=[d_o3])
        rows = mergedT("rec", r) if callable(mergedT) else mergedT[512 + r * 128:512 + (r + 1) * 128, :]
        P.dma(rows[:, lo:lo + 512], o3[:], reads=[d_o3], writes=[d_merged], semkey="mg")


def emit_load_act16(P, C, srcT, d_src):
    for q in range(4):
        P.dma(C.A[:, q * 4:(q + 1) * 4, :], srcT[q * 512:(q + 1) * 512, :].rearrange("(c p) t -> p c t", p=128),
              reads=[d_src], writes=C.d_A + C.d_xt, semkey="act16")


def emit_conv_in(P, C, w_in, bgT, cvT, d_out, st, bnd=None, d_bnd=None):
    it = 0
    si = 0
    for fc in range(16):
        s = fc % 2
        wv = C.wgu[s][:].rearrange("p a c n -> p (a c n)").rearrange("p (q c n) -> p q c n", q=4, c=16)
        for q in range(3):
            src = w_in[:, q * 2048 + fc * 128: q * 2048 + (fc + 1) * 128].rearrange("(c p) n -> p c n", p=128)
            P.dma(wv[:, q, :, :], src, writes=[C.d_wgu[s]], semkey=f"wgu{s}", eng="pool")
        for ti, (off, n) in enumerate(C.tiles):
            pb = (it % 2) * 3
            it += 1
            for q in range(3):
                pt = C.ps[pb + q]
                for k in range(16):
                    P.pe(lambda e, pt=pt, q=q, k=k, wv=wv, off=off, n=n: e.matmul(
                        pt[:, 0:n], lhsT=wv[:, q, k, :], rhs=C.hy[:, k, off:off + n],
                        start=(k == 0), stop=(k == 15)),
                        reads=[C.d_wgu[s], C.d_h[ti]], writes=[C.d_ps[pb + q]])
            sb, d_sb = st[si % len(st)]
            si += 1
            P.act(lambda e, sb=sb, pb=pb, n=n: e.activation(out=sb[:, 0:n], in_=C.ps[pb][:, 0:n], func=AF.Copy),
                  reads=[C.d_ps[pb]], writes=[d_sb])
            P.dma(bgT[fc * 128:(fc + 1) * 128, off:off + n], sb[:, 0:n], reads=[d_sb], writes=[d_out],
                  semkey="cvo")
            tmp, d_tmp = C.next_tmp()
            P.act(lambda e, tmp=tmp, pb=pb, n=n: e.activation(out=tmp[:, 0:n], in_=C.ps[pb + 1][:, 0:n],
                                                              func=AF.Copy),
                  reads=[C.d_ps[pb + 1]], writes=[d_tmp])
            sb2, d_sb2 = st[si % len(st)]
            si += 1
            P.dve(lambda e, sb2=sb2, tmp=tmp, pb=pb, n=n: e.tensor_tensor(
                out=sb2[:, 0:n], in0=tmp[:, 0:n], in1=C.ps[pb + 2][:, 0:n], op=ALU.mult),
                reads=[d_tmp, C.d_ps[pb + 2]], writes=[d_sb2])
            P.dma(cvT[fc * 128:(fc + 1) * 128, off:off + n], sb2[:, 0:n], reads=[d_sb2], writes=[d_out],
                  semkey="cvo")
            if bnd is not None and off == 0:
                P.act(lambda e, sb2=sb2, fc=fc: e.activation(out=bnd[:, 0, fc:fc + 1], in_=sb2[:, 0:1], func=AF.Copy),
                      reads=[d_sb2], writes=[d_bnd])
            if bnd is not None and off + n == C.NT:
                P.act(lambda e, sb2=sb2, fc=fc, n=n: e.activation(out=bnd[:, 1, fc:fc + 1], in_=sb2[:, n - 1:n],
                                                                  func=AF.Copy),
                      reads=[d_sb2], writes=[d_bnd])


def emit_conv(P, C, bgT, cvhT, d_in, cw, d_cw, st):
    si = 0
    for c in range(16):
        for ti, (off, n) in enumerate(C.tiles):
            i1 = si % len(st)
            si += 1
            i2 = si % len(st)
            si += 1
            cvt, d_cvt = st[i1]
            bt, d_bt = st[i2]
            P.dma(cvt[:, 0:n + 2], cvhT[c * 128:(c + 1) * 128, off:off + n + 2], reads=[d_in], writes=[d_cvt],
                  semkey=f"cvi{i1}")
            P.dma(bt[:, 0:n], bgT[c * 128:(c + 1) * 128, off:off + n], reads=[d_in], writes=[d_bt],
                  semkey=f"cvi{i2}")
            u, d_u = C.next_tmp()
            P.dve(lambda e, u=u, cvt=cvt, c=c, n=n: e.tensor_scalar(
                out=u[:, 0:n], in0=cvt[:, 0:n], scalar1=cw[:, c, 0:1], scalar2=None, op0=ALU.mult),
                reads=[d_cvt, d_cw], writes=[d_u])
            P.dve(lambda e, u=u, cvt=cvt, c=c, n=n: e.scalar_tensor_tensor(
                out=u[:, 0:n], in0=cvt[:, 1:n + 1], scalar=cw[:, c, 1:2], in1=u[:, 0:n], op0=ALU.mult, op1=ALU.add),
                reads=[d_cvt, d_cw, d_u], writes=[d_u])
            P.dve(lambda e, u=u, cvt=cvt, c=c, n=n: e.scalar_tensor_tensor(
                out=u[:, 0:n], in0=cvt[:, 2:n + 2], scalar=cw[:, c, 2:3], in1=u[:, 0:n], op0=ALU.mult, op1=ALU.add),
                reads=[d_cvt, d_cw, d_u], writes=[d_u])
            P.dve(lambda e, u=u, bt=bt, c=c, off=off, n=n: e.tensor_tensor(
                out=C.A[:, c, off:off + n], in0=u[:, 0:n], in1=bt[:, 0:n], op=ALU.mult),
                reads=[d_u, d_bt], writes=[C.d_A[ti]] + C.d_xt)


def emit_load_mg_sel(P, C, mg_g, d_src, selv, d_sel):
    for hc in range(2):
        for c in range(16):
            kind, rk, hd = c // 8, (c // 4) % 2, c % 4
            k = kind * 2 + hd // 2
            r0 = rk * 256 + (hd % 2) * 128
            P.dma(C.A[:, hc * 16 + c, :], mg_g[k][r0:r0 + 128, hc * 1024:(hc + 1) * 1024],
                  reads=[d_src], writes=C.d_A + C.d_xt, semkey="act16")
    for c in range(16):
        P.dve(lambda e, c=c: e.tensor_scalar(out=C.A[:, c, :], in0=C.A[:, c, :], scalar1=selv[:, 0:1], scalar2=None,
                                             op0=ALU.mult), reads=C.d_A + [d_sel], writes=C.d_A)
        P.dve(lambda e, c=c: e.scalar_tensor_tensor(out=C.A[:, c, :], in0=C.A[:, 16 + c, :], scalar=selv[:, 1:2],
                                                    in1=C.A[:, c, :], op0=ALU.mult, op1=ALU.add),
              reads=C.d_A + [d_sel], writes=C.d_A)


def emit_conv_halo(P, C, bgT, cvT, d_in, bnd_g, d_bnd, selv, d_sel, cw, d_cw, st, hal, d_hal):
    P.dma(hal[:, 0, :], bnd_g[0:128, 16:32], reads=[d_bnd], writes=[d_hal], semkey="hal")
    P.dma(hal[:, 1, :], bnd_g[128:256, 0:16], reads=[d_bnd], writes=[d_hal], semkey="hal")
    P.dve(lambda e: e.tensor_scalar(out=hal[:, 0, :], in0=hal[:, 0, :], scalar1=selv[:, 2:3], scalar2=None,
                                    op0=ALU.mult), reads=[d_hal, d_sel], writes=[d_hal])
    P.dve(lambda e: e.tensor_scalar(out=hal[:, 1, :], in0=hal[:, 1, :], scalar1=selv[:, 3:4], scalar2=None,
                                    op0=ALU.mult), reads=[d_hal, d_sel], writes=[d_hal])
    si = 0
    NT = C.NT
    for c in range(16):
        for ti, (off, n) in enumerate(C.tiles):
            i1 = si % len(st)
            si += 1
            i2 = si % len(st)
            si += 1
            cvt, d_cvt = st[i1]
            bt, d_bt = st[i2]
            lo = max(off - 1, 0)
            hi = min(off + n + 1, NT)
            dlo = lo - (off - 1)
            P.dma(cvt[:, dlo:dlo + (hi - lo)], cvT[c * 128:(c + 1) * 128, lo:hi], reads=[d_in], writes=[d_cvt],
                  semkey=f"cvi{i1}")
            if off == 0:
                P.dve(lambda e, cvt=cvt, c=c: e.tensor_copy(out=cvt[:, 0:1], in_=hal[:, 0, c:c + 1]),
                      reads=[d_hal], writes=[d_cvt])
            if off + n == NT:
                P.dve(lambda e, cvt=cvt, c=c, n=n: e.tensor_copy(out=cvt[:, n + 1:n + 2], in_=hal[:, 1, c:c + 1]),
                      reads=[d_hal], writes=[d_cvt])
            P.dma(bt[:, 0:n], bgT[c * 128:(c + 1) * 128, off:off + n], reads=[d_in], writes=[d_bt],
                  semkey=f"cvi{i2}")
            u, d_u = C.next_tmp()
            P.dve(lambda e, u=u, cvt=cvt, c=c, n=n: e.tensor_scalar(
                out=u[:, 0:n], in0=cvt[:, 0:n], scalar1=cw[:, c, 0:1], scalar2=None, op0=ALU.mult),
                reads=[d_cvt, d_cw], writes=[d_u])
            P.dve(lambda e, u=u, cvt=cvt, c=c, n=n: e.scalar_tensor_tensor(
                out=u[:, 0:n], in0=cvt[:, 1:n + 1], scalar=cw[:, c, 1:2], in1=u[:, 0:n], op0=ALU.mult, op1=ALU.add),
                reads=[d_cvt, d_cw, d_u], writes=[d_u])
            P.dve(lambda e, u=u, cvt=cvt, c=c, n=n: e.scalar_tensor_tensor(
                out=u[:, 0:n], in0=cvt[:, 2:n + 2], scalar=cw[:, c, 2:3], in1=u[:, 0:n], op0=ALU.mult, op1=ALU.add),
                reads=[d_cvt, d_cw, d_u], writes=[d_u])
            P.dve(lambda e, u=u, bt=bt, c=c, off=off, n=n: e.tensor_tensor(
                out=C.A[:, c, off:off + n], in0=u[:, 0:n], in1=bt[:, 0:n], op=ALU.mult),
                reads=[d_u, d_bt], writes=[C.d_A[ti]] + C.d_xt)


BF = ml_dtypes.bfloat16
NCORES = 8


def _ffn_w(nc, tag):
    wg = nc.dram_tensor("wg" + tag, [2048, 5504], F32, kind="ExternalInput")
    wu = nc.dram_tensor("wu" + tag, [2048, 5504], F32, kind="ExternalInput")
    wd = nc.dram_tensor("wd" + tag, [5504, 2048], F32, kind="ExternalInput")
    return wg, wu, wd


def build_A():
    nc = bass.Bass("TRN2", target_bir_lowering=False)
    P = Prog(nc)
    xT = nc.dram_tensor("xT", [2048, 1024], F32, kind="ExternalInput")
    ctxT = nc.dram_tensor("ctxT", [2048, 128], F32, kind="ExternalInput")
    cvecT = nc.dram_tensor("cvecT", [128, 16, 2], F32, kind="ExternalInput")
    ada_w = nc.dram_tensor("ada_w", [2048, 18432], F32, kind="ExternalInput")
    ada_bT = nc.dram_tensor("ada_bT", [128, 144], F32, kind="ExternalInput")
    gTin = nc.dram_tensor("gTin", [128, 6, 16], F32, kind="ExternalInput")
    wg, wu, wd = _ffn_w(nc, "")
    x1T = nc.dram_tensor("x1T", [2048, 1024], F32, kind="ExternalOutput")
    hT = nc.dram_tensor("hT", [2048, 1152], BF16, kind="ExternalOutput")
    mTo = nc.dram_tensor("mTo", [128, 144, 2], F32, kind="ExternalOutput")
    xc1T = nc.dram_tensor("xc1T", [2048, 128], F32, kind="Internal")
    C = Ctx(P, [512, 512, 128])
    scr = {"cv": P.sbuf("cv", [128, 16, 2], F32), "sc": P.sbuf("sc", [128, 16, 2], BF16),
           "bT": P.sbuf("bT", [128, 144], F32)}
    P.dma(C.gT[:], gTin[:], writes=[C.d_gT], semkey="small3")
    emit_modulation(P, C, ada_w, ada_bT[:], cvecT[:], scr)
    d_in = Dep("in")
    d_x1 = [Dep("x1a"), Dep("x1b"), Dep("xc1")]
    emit_coefs(P, C, 1, 2, 0, 1, 0.5, [0, 1])
    srcs = [(xT[:, 0:512], d_in, 0), (xT[:, 512:1024], d_in, 0), (ctxT[:, :], d_in, 1)]
    dsts = [(x1T[:, 0:512], d_x1[0], 0), (x1T[:, 512:1024], d_x1[1], 0), (xc1T[:, :], d_x1[2], 1)]
    emit_ffn(P, C, srcs, dsts, wg, wu, wd, 0)
    emit_coefs(P, C, 4, None, 2, None, 1.0, [0, 1])
    emit_prenorm(P, C, dsts, 3, lambda ti: C.hy[:, :, C.tiles[ti][0]:C.tiles[ti][0] + C.tiles[ti][1]],
                 C.d_h, C.d_A)
    d_hT = Dep("hT")
    for ti, (off, n) in enumerate(C.tiles):
        P.dma(hT[:, off:off + n].rearrange("(c p) t -> p c t", p=128), C.hy[:, :, off:off + n],
              reads=[C.d_h[ti]], writes=[d_hT], semkey="hT")
    P.dma(mTo[:], C.mT[:], reads=[C.d_mT], writes=[Dep("mTo")], semkey="mTo")
    P.emit()
    return nc


def build_B():
    nc = bass.Bass("TRN2", target_bir_lowering=False)
    P = Prog(nc)
    hTf = nc.dram_tensor("hTf", [2048, NTOK], BF16, kind="ExternalInput")
    w_att = nc.dram_tensor("w_att", [4, 3, 2048, 128], F32, kind="ExternalInput")
    w_rec = nc.dram_tensor("w_rec", [4, 5, 2048, 128], F32, kind="ExternalInput")
    ropeC = nc.dram_tensor("ropeCin", [128, 2048], F32, kind="ExternalInput")
    ropeS = nc.dram_tensor("ropeSin", [128, 2048], F32, kind="ExternalInput")
    perm = nc.dram_tensor("permin", [128, 128], F32, kind="ExternalInput")
    ident = nc.dram_tensor("identin", [128, 128], F32, kind="ExternalInput")
    maskf = nc.dram_tensor("maskfin", [128, 128], F32, kind="ExternalInput")
    maskb = nc.dram_tensor("maskbin", [128, 128], F32, kind="ExternalInput")
    lamT = nc.dram_tensor("lamT", [64, 4], F32, kind="ExternalInput")
    dng = nc.dram_tensor("dng", [128, 1], F32, kind="ExternalInput")
    rng = nc.dram_tensor("rng", [128, 1], F32, kind="ExternalInput")
    lbraw = nc.dram_tensor("lbraw", [128, 2, 2, 4], F32, kind="ExternalInput")
    mergedT = nc.dram_tensor("mergedT", [1024, 2048], BF16, kind="ExternalOutput")
    M = MixCtx(P)
    emit_mix_setup(P, M, hTf, ropeC[:], ropeS[:], perm[:], ident[:], maskf[:], maskb[:], lamT[:], dng[:], rng[:],
                   lbraw[:])
    d_merged = Dep("merged")
    for hd in range(4):
        for i in range(3):
            load_w(P, M, i, w_att[hd, i])
        for i in range(5):
            load_w(P, M, 3 + i, w_rec[hd, i])
        emit_attention_head(P, M, hd, mergedT, d_merged)
        emit_rec_head(P, M, hd, mergedT, d_merged)
    P.emit()
    return nc


def build_C():
    nc = bass.Bass("TRN2", target_bir_lowering=False)
    P = Prog(nc)
    x1T = nc.dram_tensor("x1T", [2048, 1024], F32, kind="ExternalInput")
    mgT = nc.dram_tensor("mgT", [2048, 1024], BF16, kind="ExternalInput")
    w_out = nc.dram_tensor("w_out", [2048, 2048], F32, kind="ExternalInput")
    mT0 = nc.dram_tensor("mT0", [128, 144, 2], F32, kind="ExternalInput")
    gT0 = nc.dram_tensor("gT0", [128, 6, 16], F32, kind="ExternalInput")
    gT1 = nc.dram_tensor("gT1", [128, 6, 16], F32, kind="ExternalInput")
    cvecT = nc.dram_tensor("cvecT", [128, 16, 2], F32, kind="ExternalInput")
    ada_w = nc.dram_tensor("ada_w", [2048, 18432], F32, kind="ExternalInput")
    ada_bT = nc.dram_tensor("ada_bT", [128, 144], F32, kind="ExternalInput")
    wgA, wuA, wdA = _ffn_w(nc, "A")
    wgB, wuB, wdB = _ffn_w(nc, "B")
    cw_in = nc.dram_tensor("cw_in", [2048, 6144], F32, kind="ExternalInput")
    x4T = nc.dram_tensor("x4T", [2048, 1024], F32, kind="ExternalOutput")
    bgT = nc.dram_tensor("bgT", [2048, 1024], F32, kind="ExternalOutput")
    cvT = nc.dram_tensor("cvT", [2048, 1024], F32, kind="ExternalOutput")
    mT1 = nc.dram_tensor("mT1", [128, 144, 2], F32, kind="ExternalOutput")
    x2T = nc.dram_tensor("x2T", [2048, 1024], F32, kind="Internal")
    x3T = nc.dram_tensor("x3T", [2048, 1024], F32, kind="Internal")
    C = Ctx(P, [512, 512])
    scr = {"cv": P.sbuf("cv", [128, 16, 2], F32), "sc": P.sbuf("sc", [128, 16, 2], BF16),
           "bT": P.sbuf("bT", [128, 144], F32)}
    st = [(P.sbuf(f"st{i}", [128, 512], F32), Dep(f"st{i}")) for i in range(4)]
    d_in = Dep("in")

    def tl(t, d):
        return [(t[:, 0:512], d[0], 0), (t[:, 512:1024], d[1], 0)]
    d_x1 = [d_in, d_in]
    d_x2 = [Dep("x2a"), Dep("x2b")]
    d_x3 = [Dep("x3a"), Dep("x3b")]
    d_x4 = [Dep("x4a"), Dep("x4b")]
    P.dma(C.gT[:], gT0[:], writes=[C.d_gT], semkey="small3")
    P.dma(C.mT[:], mT0[:], writes=[C.d_mT], semkey="small4")
    emit_load_act16(P, C, mgT, d_in)
    emit_coefs(P, C, 4, 5, 2, 3, 1.0, [0])
    emit_down_residual(P, C, 16, w_out, tl(x1T, d_x1), tl(x2T, d_x2))
    emit_coefs(P, C, 7, 8, 4, 5, 0.5, [0])
    emit_ffn(P, C, tl(x2T, d_x2), tl(x3T, d_x3), wgA, wuA, wdA, 6)
    P.dma(C.gT[:], gT1[:], writes=[C.d_gT], semkey="small3")
    emit_modulation(P, C, ada_w, ada_bT[:], cvecT[:], scr)
    emit_coefs(P, C, 1, 2, 0, 1, 0.5, [0])
    emit_ffn(P, C, tl(x3T, d_x3), tl(x4T, d_x4), wgB, wuB, wdB, 0)
    emit_coefs(P, C, 4, None, 2, None, 1.0, [0])
    emit_prenorm(P, C, tl(x4T, d_x4), 3, lambda ti: C.hy[:, :, C.tiles[ti][0]:C.tiles[ti][0] + C.tiles[ti][1]],
                 C.d_h, C.d_A)
    emit_conv_in(P, C, cw_in, bgT, cvT, Dep("cvout"), st)
    P.dma(mT1[:], C.mT[:], reads=[C.d_mT], writes=[Dep("mT1o")], semkey="mTo")
    P.emit()
    return nc


def build_D():
    nc = bass.Bass("TRN2", target_bir_lowering=False)
    P = Prog(nc)
    x4T = nc.dram_tensor("x4T", [2048, 1024], F32, kind="ExternalInput")
    bgT = nc.dram_tensor("bgT", [2048, 1024], F32, kind="ExternalInput")
    cvhT = nc.dram_tensor("cvhT", [2048, 1026], F32, kind="ExternalInput")
    cwT = nc.dram_tensor("cwT", [128, 16, 3], F32, kind="ExternalInput")
    cw_out = nc.dram_tensor("cw_out", [2048, 2048], F32, kind="ExternalInput")
    mT1 = nc.dram_tensor("mT1", [128, 144, 2], F32, kind="ExternalInput")
    gT1 = nc.dram_tensor("gT1", [128, 6, 16], F32, kind="ExternalInput")
    wg, wu, wd = _ffn_w(nc, "")
    outT = nc.dram_tensor("outT", [2048, 1024], F32, kind="ExternalOutput")
    x5T = nc.dram_tensor("x5T", [2048, 1024], F32, kind="Internal")
    C = Ctx(P, [512, 512])
    st = [(P.sbuf(f"st{i}", [128, 514], F32), Dep(f"st{i}")) for i in range(4)]
    cw = P.sbuf("cw", [128, 16, 3], F32)
    d_cw = Dep("cw")
    d_in = Dep("in")

    def tl(t, d):
        return [(t[:, 0:512], d[0], 0), (t[:, 512:1024], d[1], 0)]
    d_x5 = [Dep("x5a"), Dep("x5b")]
    d_o = [Dep("oa"), Dep("ob")]
    P.dma(C.gT[:], gT1[:], writes=[C.d_gT], semkey="small3")
    P.dma(C.mT[:], mT1[:], writes=[C.d_mT], semkey="small4")
    P.dma(cw[:], cwT[:], writes=[d_cw], semkey="small5")
    emit_conv(P, C, bgT, cvhT, d_in, cw, d_cw, st)
    emit_coefs(P, C, 4, 5, 2, 3, 1.0, [0])
    emit_down_residual(P, C, 16, cw_out, tl(x4T, [d_in, d_in]), tl(x5T, d_x5))
    emit_coefs(P, C, 7, 8, 4, 5, 0.5, [0])
    emit_ffn(P, C, tl(x5T, d_x5), tl(outT, d_o), wg, wu, wd, 6)
    P.emit()
    return nc


ARENA_BYTES = 207 * 1024
FUSE_STOP = int(os.environ.get("FUSE_STOP", "0"))


def build_fused():
    nc = bass.Bass("TRN2", target_bir_lowering=False)
    P = Prog(nc)
    P.use_arena(ARENA_BYTES)
    inp = lambda n, sh, dt=F32: nc.dram_tensor(n, sh, dt, kind="ExternalInput")
    xT = inp("xT", [2048, 1024])
    ctxT = inp("ctxT", [2048, 128])
    ada_sl = inp("ada_sl", [2, 2048, 9216])
    cvecT = inp("cvecT", [128, 16, 2])
    bT0 = inp("bT0", [128, 144])
    bT1 = inp("bT1", [128, 144])
    gT0 = inp("gT0", [128, 6, 16])
    gT1 = inp("gT1", [128, 6, 16])
    W = [_ffn_w(nc, str(i)) for i in range(4)]
    w_att = inp("w_att", [4, 3, 2048, 128])
    w_rec = inp("w_rec", [4, 5, 2048, 128])
    ropeC = inp("ropeCin", [128, 2048])
    ropeS = inp("ropeSin", [128, 2048])
    perm = inp("permin", [128, 128])
    ident = inp("identin", [128, 128])
    maskf = inp("maskfin", [128, 128])
    maskb = inp("maskbin", [128, 128])
    lamT = inp("lamT", [64, 4])
    dng = inp("dng", [128, 1])
    rng = inp("rng", [128, 1])
    lbraw = inp("lbraw", [128, 2, 2, 4])
    w_out = inp("w_out", [2048, 2048])
    cw_in = inp("cw_in", [2048, 6144])
    cwT = inp("cwT", [128, 16, 3])
    cw_out = inp("cw_out", [2048, 2048])
    selin = inp("selin", [128, 4])
    outT = nc.dram_tensor("outT", [2048, 1024], F32, kind="ExternalOutput")
    itn = lambda n, sh, dt=F32: nc.dram_tensor(n, sh, dt, kind="Internal")
    x1T, x2T, x3T, x4T, x5T = [itn(f"x{i}T", [2048, 1024]) for i in (1, 2, 3, 4, 5)]
    xc1T = itn("xc1T", [2048, 128])
    hT_own = [itn(f"hT_own{q}", [512, 1152], BF16) for q in range(4)]
    hT_g = [itn(f"hT_g{q}", [1024, 1152], BF16) for q in range(4)]
    mg_own = [itn(f"mg_own{q}", [256, 2048], BF16) for q in range(4)]
    mg_g = [itn(f"mg_g{q}", [512, 2048], BF16) for q in range(4)]
    bgT = itn("bgT", [2048, 1024])
    cvT = itn("cvT", [2048, 1024])
    bnd_own = itn("bnd_own", [128, 32])
    bnd_g = itn("bnd_g", [256, 32])
    d_in = Dep("in")

    def tl(t, d):
        return [(t[:, 0:512], d[0], 0), (t[:, 512:1024], d[1], 0)]

    def mk_scr():
        return {"cv": P.sbuf("cv", [128, 16, 2], F32), "sc": P.sbuf("sc", [128, 16, 2], BF16),
                "bT": P.sbuf("bT", [128, 144], F32)}
    hyv = lambda C: (lambda ti: C.hy[:, :, C.tiles[ti][0]:C.tiles[ti][0] + C.tiles[ti][1]])

    mp_own = itn("mp_own", [128, 288])
    mp_g = itn("mp_g", [256, 288])
    mTd = [itn("mT0d", [128, 144, 2]), itn("mT1d", [128, 144, 2])]
    d_mTd = [Dep("mT0d"), Dep("mT1d")]
    C = Ctx(P, [512, 512])
    emit_modulation_sharded(P, C, ada_sl, cvecT[:], bT0[:], bT1[:], mp_own, mp_g, mTd, d_mTd)
    P.barrier()
    P.aoff = 0
    C = Ctx(P, [512, 512, 128])
    P.dma(C.gT[:], gT0[:], writes=[C.d_gT], semkey="small3")
    P.dma(C.mT[:], mTd[0][:], reads=[d_mTd[0]], writes=[C.d_mT], semkey="small4")
    d_x1 = [Dep("x1a"), Dep("x1b"), Dep("xc1")]
    emit_coefs(P, C, 1, 2, 0, 1, 0.5, [0, 1])
    srcs = [(xT[:, 0:512], d_in, 0), (xT[:, 512:1024], d_in, 0), (ctxT[:, :], d_in, 1)]
    dsts = [(x1T[:, 0:512], d_x1[0], 0), (x1T[:, 512:1024], d_x1[1], 0), (xc1T[:, :], d_x1[2], 1)]
    emit_ffn(P, C, srcs, dsts, W[0][0], W[0][1], W[0][2], 0)
    emit_coefs(P, C, 4, None, 2, None, 1.0, [0, 1])
    emit_prenorm(P, C, dsts, 3, hyv(C), C.d_h, C.d_A, resident=True)
    d_hT = Dep("hT")
    for q in range(4):
        P.dma(hT_own[q][:, :].rearrange("(c p) t -> p c t", p=128), C.hy[:, q * 4:(q + 1) * 4, :],
              reads=C.d_h, writes=[d_hT], semkey="hT")
    d_hg = Dep("hT_g")
    for q in range(4):
        P.allgather_pairs(hT_g[q], hT_own[q], reads=[d_hT], writes=[d_hg], semkey="cc1")
    if FUSE_STOP == 1:
        dbg = nc.dram_tensor("dbg", [4096, 1152], BF16, kind="ExternalOutput")
        for q in range(4):
            for r in range(2):
                P.dma(dbg[r * 2048 + q * 512: r * 2048 + (q + 1) * 512, :], hT_g[q][r * 512:(r + 1) * 512, :],
                      reads=[d_hg], writes=[Dep("dbg")], semkey="dbg")
        P.emit()
        return nc
    P.barrier()
    P.aoff = 0
    M = MixCtx(P)
    for r in range(2):
        for q in range(4):
            rows = hT_g[q][r * 512:(r + 1) * 512, :]
            P.dma(M.hs[:, q * 4:(q + 1) * 4, r * 128:(r + 1) * 128],
                  rows[:, 1024:1152].rearrange("(c p) t -> p c t", p=128), reads=[d_hg], writes=[M.d_hs],
                  semkey="hs")
            P.dma(M.hs[:, q * 4:(q + 1) * 4, 256 + r * 1024:256 + (r + 1) * 1024],
                  rows[:, 0:1024].rearrange("(c p) t -> p c t", p=128), reads=[d_hg], writes=[M.d_hs],
                  semkey="hs")
    emit_mix_setup(P, M, None, ropeC[:], ropeS[:], perm[:], ident[:], maskf[:], maskb[:], lamT[:], dng[:], rng[:],
                   lbraw[:])
    d_mg = Dep("mg_own")
    d_mgg = Dep("mg_g")
    for i in range(3):
        load_w(P, M, i, w_att[0, i])
    for i in range(5):
        load_w(P, M, 3 + i, w_rec[0, i])
    for hd in range(4):
        mgdst = lambda kind, h: mg_own[(0 if kind == "att" else 2) + h // 2][(h % 2) * 128:(h % 2) * 128 + 128, :]
        emit_attention_head(P, M, hd, mgdst, d_mg)
        P.barrier()

        def prefetch(hd=hd):
            if hd + 1 < 4:
                for i in range(3):
                    load_w(P, M, i, w_att[hd + 1, i])
                for i in range(5):
                    load_w(P, M, 3 + i, w_rec[hd + 1, i])
        emit_rec_head(P, M, hd, mgdst, d_mg, after_burst=prefetch)
        P.barrier()
        if hd % 2 == 1:
            for q in (hd // 2, 2 + hd // 2):
                P.allgather_pairs(mg_g[q], mg_own[q], reads=[d_mg], writes=[d_mgg], semkey="cc2")
    if FUSE_STOP == 2:
        dbg = nc.dram_tensor("dbg", [2048, 2048], BF16, kind="ExternalOutput")
        for q in range(4):
            for r in range(2):
                base = r * 1024 + (q // 2) * 512 + (q % 2) * 256
                P.dma(dbg[base:base + 256, :], mg_g[q][r * 256:(r + 1) * 256, :], reads=[d_mgg],
                      writes=[Dep("dbg")], semkey="dbg")
        P.emit()
        return nc
    P.barrier()
    P.aoff = 0
    C = Ctx(P, [512, 512])
    selv = P.sbuf("selv", [128, 4], F32)
    d_sel = Dep("selv")
    st = [(P.sbuf(f"st{i}", [128, 514], F32), Dep(f"st{i}")) for i in range(4)]
    cw = P.sbuf("cw", [128, 16, 3], F32)
    d_cw = Dep("cw")
    hal = P.sbuf("hal", [128, 2, 16], F32)
    d_hal = Dep("hal")
    P.dma(selv[:], selin[:], writes=[d_sel], semkey="small5")
    P.dma(cw[:], cwT[:], writes=[d_cw], semkey="small5")
    P.dma(C.gT[:], gT0[:], writes=[C.d_gT], semkey="small3")
    P.dma(C.mT[:], mTd[0][:], reads=[d_mTd[0]], writes=[C.d_mT], semkey="small4")
    d_x2 = [Dep("x2a"), Dep("x2b")]
    d_x3 = [Dep("x3a"), Dep("x3b")]
    d_x4 = [Dep("x4a"), Dep("x4b")]
    d_x5 = [Dep("x5a"), Dep("x5b")]
    d_o = [Dep("oa"), Dep("ob")]
    emit_load_mg_sel(P, C, mg_g, d_mgg, selv, d_sel)
    emit_coefs(P, C, 4, 5, 2, 3, 1.0, [0])
    emit_down_residual(P, C, 16, w_out, tl(x1T, d_x1), tl(x2T, d_x2))
    emit_coefs(P, C, 7, 8, 4, 5, 0.5, [0])
    emit_ffn(P, C, tl(x2T, d_x2), tl(x3T, d_x3), W[1][0], W[1][1], W[1][2], 6, resident=True)
    P.dma(C.gT[:], gT1[:], writes=[C.d_gT], semkey="small3")
    P.dma(C.mT[:], mTd[1][:], reads=[d_mTd[1]], writes=[C.d_mT], semkey="small4")
    emit_coefs(P, C, 1, 2, 0, 1, 0.5, [0])
    emit_ffn(P, C, tl(x3T, d_x3), tl(x4T, d_x4), W[2][0], W[2][1], W[2][2], 0, resident=True)
    emit_coefs(P, C, 4, None, 2, None, 1.0, [0])
    emit_prenorm(P, C, tl(x4T, d_x4), 3, hyv(C), C.d_h, C.d_A, resident=True)
    d_cv = Dep("cvout")
    st512 = [(t[:, 0:512], d) for (t, d) in st]
    bnd_sb = P.sbuf("bnd_sb", [128, 2, 16], F32)
    d_bsb = Dep("bnd_sb")
    emit_conv_in(P, C, cw_in, bgT, cvT, d_cv, st512, bnd_sb, d_bsb)
    d_bo = Dep("bnd_own")
    P.dma(bnd_own[:, :], bnd_sb[:].rearrange("p a c -> p (a c)"), reads=[d_bsb], writes=[d_bo], semkey="bnd")
    d_bg = Dep("bnd_g")
    P.allgather_pairs(bnd_g, bnd_own, reads=[d_bo], writes=[d_bg], semkey="cc3")
    emit_conv_halo(P, C, bgT, cvT, d_cv, bnd_g, d_bg, selv, d_sel, cw, d_cw, st, hal, d_hal)
    emit_coefs(P, C, 4, 5, 2, 3, 1.0, [0])
    emit_down_residual(P, C, 16, cw_out, tl(x4T, d_x4), tl(x5T, d_x5))
    emit_coefs(P, C, 7, 8, 4, 5, 0.5, [0])
    emit_ffn(P, C, tl(x5T, d_x5), tl(outT, d_o), W[3][0], W[3][1], W[3][2], 6, resident=True)
    stuck = simulate_sync(P)
    if stuck:
        raise RuntimeError(f"sync deadlock: {stuck}")
    P.emit()
    return nc


def mix_consts():
    p = np.arange(128)
    d = p % 64
    i = d % 16
    freqs = (10000.0 ** (-np.arange(16, dtype=np.float32) / 16)).astype(np.float32)
    t = np.arange(2048)
    row = (t // 64).astype(np.float32)
    col = (t % 64).astype(np.float32)
    pos = np.where((d < 32)[:, None], row[None, :], col[None, :]).astype(np.float32)
    ang = (pos * freqs[i][:, None]).astype(np.float32)
    C = np.cos(ang).astype(np.float32)
    S = np.sin(ang).astype(np.float32)
    perm = np.zeros((128, 128), np.float32)
    for m in range(128):
        if (m % 32) < 16:
            perm[m + 16, m] = -1.0
        else:
            perm[m - 16, m] = 1.0
    ident = np.eye(128, dtype=np.float32)
    s = np.arange(128)[:, None]
    tt = np.arange(128)[None, :]
    return C, S, perm, ident, (s <= tt).astype(np.float32), (s >= tt).astype(np.float32)


_PROGS = {}


def _prog(name, fn):
    if name not in _PROGS:
        _PROGS[name] = fn()
    return _PROGS[name]


def _run(nc, in_maps):
    res = run_bass_kernel_spmd(nc, in_maps, core_ids=list(range(NCORES)))
    return res.results


def kernel(x, c, ctx, c_ctx, ada_w, ada_b, norm_g, ffn_w_gate, ffn_w_up, ffn_w_down, mix_w_in, mix_w_out,
           diff_lambda, diff_norm_g, rec_norm_g, rec_lb, conv_w_in, conv_w, conv_w_out):
    f32 = lambda a: np.ascontiguousarray(np.asarray(a, dtype=np.float32))
    x, c, ctx, c_ctx = f32(x), f32(c), f32(ctx), f32(c_ctx)
    ada_w, ada_b, norm_g = f32(ada_w), f32(ada_b), f32(norm_g)
    ffn_w_gate, ffn_w_up, ffn_w_down = f32(ffn_w_gate), f32(ffn_w_up), f32(ffn_w_down)
    mix_w_in, mix_w_out = f32(mix_w_in), f32(mix_w_out)
    conv_w_in, conv_w, conv_w_out = f32(conv_w_in), f32(conv_w), f32(conv_w_out)
    rec_lb = f32(rec_lb)
    gT = [np.ascontiguousarray(norm_g[l].reshape(6, 16, 128).transpose(2, 0, 1)) for l in range(2)]
    bT = [np.ascontiguousarray(ada_b[l].reshape(144, 128).T) for l in range(2)]
    cvec = [np.ascontiguousarray(np.stack([c[b], c_ctx], -1).reshape(16, 128, 2).transpose(1, 0, 2))
            for b in range(4)]
    Cc, Ss, perm, ident, maskf, maskb = mix_consts()
    w_in = mix_w_in[0]
    cwT = np.ascontiguousarray(conv_w[0].reshape(3, 16, 128).transpose(2, 1, 0))
    lamT = np.ascontiguousarray(f32(diff_lambda)[0].T)
    dng = f32(diff_norm_g)[0].reshape(128, 1).copy()
    rngv = f32(rec_norm_g)[0].reshape(128, 1).copy()
    heads = []
    for hh in range(2):
        w_att = np.empty((4, 3, 2048, 128), np.float32)
        w_rec = np.empty((4, 5, 2048, 128), np.float32)
        lbraw = np.empty((128, 2, 2, 4), np.float32)
        for hd in range(4):
            g = hh * 4 + hd
            for q in range(3):
                w_att[hd, q] = w_in[:, q * 1024 + g * 128: q * 1024 + (g + 1) * 128]
            for q in range(5):
                w_rec[hd, q] = w_in[:, 3072 + q * 1024 + g * 128: 3072 + q * 1024 + (g + 1) * 128]
            lbraw[:, :, :, hd] = rec_lb[:, :, g * 128:(g + 1) * 128].transpose(2, 0, 1)
        heads.append((w_att, w_rec, lbraw))
    ada_sl = [np.ascontiguousarray(ada_w[:, :, hh * 9216:(hh + 1) * 9216]) for hh in range(2)]
    in_maps = []
    for i in range(NCORES):
        b, h = i // 2, i % 2
        sel = np.zeros((128, 4), np.float32)
        sel[:, 0] = 1.0 if h == 0 else 0.0
        sel[:, 1] = 1.0 if h == 1 else 0.0
        sel[:, 2] = 1.0 if h == 1 else 0.0
        sel[:, 3] = 1.0 if h == 0 else 0.0
        m = {
            "xT": np.ascontiguousarray(x[b, h * 1024:(h + 1) * 1024].T),
            "ctxT": np.ascontiguousarray(ctx[b, h * 128:(h + 1) * 128].T),
            "ada_sl": ada_sl[h], "cvecT": cvec[b],
            "bT0": bT[0], "bT1": bT[1],
            "gT0": gT[0], "gT1": gT[1],
            "w_att": heads[h][0], "w_rec": heads[h][1], "lbraw": heads[h][2],
            "ropeCin": Cc, "ropeSin": Ss, "permin": perm, "identin": ident, "maskfin": maskf, "maskbin": maskb,
            "lamT": lamT, "dng": dng, "rng": rngv,
            "w_out": mix_w_out[0], "cw_in": conv_w_in[0], "cwT": cwT, "cw_out": conv_w_out[0], "selin": sel}
        for k, (l, j) in enumerate([(0, 0), (0, 1), (1, 0), (1, 1)]):
            m[f"wg{k}"] = ffn_w_gate[l, j]
            m[f"wu{k}"] = ffn_w_up[l, j]
            m[f"wd{k}"] = ffn_w_down[l, j]
        in_maps.append(m)
    rD = _run(_prog("F", build_fused), in_maps)
    if FUSE_STOP:
        return rD
    out = np.empty((4, 2048, 2048), np.float32)
    for i in range(NCORES):
        b, h = i // 2, i % 2
        out[b, h * 1024:(h + 1) * 1024] = np.asarray(rD[i]["outT"]).T
    return out
```

```python
import contextlib
import types
import os
import math
import numpy as np
import ml_dtypes
import concourse.bass as bass
import concourse.mybir as mybir
from concourse.bass_utils import run_bass_kernel_spmd


F32 = mybir.dt.float32
BF16 = mybir.dt.bfloat16
ALU = mybir.AluOpType
AF = mybir.ActivationFunctionType
AX = mybir.AxisListType


class Dep:
    __slots__ = ("name", "w", "r", "excl")

    def __init__(self, name, excl=False):
        self.name = name
        self.w = {}
        self.r = {}
        self.excl = excl


class Ins:
    __slots__ = ("eng", "fn", "deps", "signal", "semkey", "semval", "is_dma", "idx", "inc")
    _n = 0

    def __init__(self, eng, fn, is_dma=False, semkey=None):
        self.eng = eng
        self.fn = fn
        self.deps = []
        self.signal = False
        self.is_dma = is_dma
        self.semkey = semkey
        self.semval = None
        self.inc = 16
        Ins._n += 1
        self.idx = Ins._n


def _freeze(fn):
    if getattr(fn, "__closure__", None) is None:
        return fn
    cells = []
    for c in fn.__closure__:
        try:
            cells.append(types.CellType(c.cell_contents))
        except ValueError:
            cells.append(c)
    return types.FunctionType(fn.__code__, fn.__globals__, fn.__name__, fn.__defaults__, tuple(cells))


class Prog:
    ENGS = ("pe", "act", "dve", "pool", "sp")

    def __init__(self, nc):
        self.nc = nc
        self.streams = {e: [] for e in self.ENGS}
        self.stack = contextlib.ExitStack()
        self.dma_keys = {}
        self.n_sb = 0
        self.arena = None
        self.aoff = 0
        self.pending = {e: [] for e in self.ENGS}
        self.open_dmas = []
        self.ps = None
        self.d_ps = None

    def use_arena(self, nbytes):
        self.arena = self.stack.enter_context(self.nc.sbuf_tensor("arena", [128, nbytes], mybir.dt.uint8))
        self.asize = nbytes
        self.aoff = 0

    def shared_psum(self):
        if self.ps is None:
            self.ps = [self.psum(f"ps{i}", [128, 512]) for i in range(8)]
            self.d_ps = [Dep(f"ps{i}", excl=True) for i in range(8)]
        return self.ps, self.d_ps

    def barrier(self):
        lasts = []
        for e in ("pe", "act", "dve", "pool"):
            for ins in reversed(self.streams[e]):
                if not ins.is_dma:
                    lasts.append(ins)
                    break
        lasts += self.open_dmas
        self.open_dmas = []
        for d in lasts:
            d.signal = True
        for e in self.ENGS:
            self.pending[e] = list(lasts)

    def sbuf(self, name, shape, dtype):
        if self.arena is None:
            return self.stack.enter_context(self.nc.sbuf_tensor(name, list(shape), dtype))
        esz = {F32: 4, BF16: 2}[dtype]
        nel = 1
        for s_ in shape[1:]:
            nel *= s_
        nbytes = nel * esz
        off = (self.aoff + 63) // 64 * 64
        if off + nbytes > self.asize:
            raise MemoryError(f"SBUF arena overflow allocating {name}: {off}+{nbytes} > {self.asize}")
        self.aoff = off + nbytes
        v = self.arena[0:shape[0], off:off + nbytes].bitcast(dtype)
        if len(shape) == 3:
            v = v.rearrange("p (a b) -> p a b", a=shape[1])
        elif len(shape) == 4:
            v = v.rearrange("p (a b c) -> p a b c", a=shape[1], b=shape[2])
        return v

    def psum(self, name, shape, dtype=F32):
        return self.stack.enter_context(self.nc.psum_tensor(name, list(shape), dtype))

    def dram(self, name, shape, dtype, kind="Internal"):
        return self.nc.dram_tensor(name, list(shape), dtype, kind=kind)

    def op(self, eng, fn, reads=(), writes=(), is_dma=False, semkey=None, inc=16):
        ins = Ins(eng, _freeze(fn), is_dma, semkey)
        ins.inc = inc
        key = ("d", id(ins)) if is_dma else eng
        deps = {}
        for t in reads:
            for k, d in t.w.items():
                deps[id(d)] = d
            if t.excl:
                for k, d in t.r.items():
                    if not is_dma and k == eng:
                        continue
                    deps[id(d)] = d
        for t in writes:
            for k, d in t.r.items():
                if not is_dma and k == eng:
                    continue
                deps[id(d)] = d
            for k, d in t.w.items():
                if not is_dma and k == eng:
                    continue
                deps[id(d)] = d
        if self.pending[eng]:
            for d in self.pending[eng]:
                if d.is_dma or d.eng != eng:
                    deps[id(d)] = d
            self.pending[eng] = []
        for d in deps.values():
            d.signal = True
        ins.deps = list(deps.values())
        for t in reads:
            t.r[key] = ins
        for t in writes:
            if t.r:
                t.r = {}
                t.w = {}
            t.w[key] = ins
        if is_dma:
            ins.signal = True
            if semkey is None:
                raise ValueError("dma needs semkey")
            self.dma_keys.setdefault(semkey, 0)
            self.open_dmas.append(ins)
        self.streams[eng].append(ins)
        return ins

    def fence(self, srcs, dsts):
        for sd in srcs:
            for dd in dsts:
                for k, i in sd.w.items():
                    if k not in dd.w or dd.w[k].idx < i.idx:
                        dd.w[k] = i
                for k, i in sd.r.items():
                    if k not in dd.r or dd.r[k].idx < i.idx:
                        dd.r[k] = i

    def pe(self, fn, reads=(), writes=()):
        return self.op("pe", fn, reads, writes)

    def act(self, fn, reads=(), writes=()):
        return self.op("act", fn, reads, writes)

    def dve(self, fn, reads=(), writes=()):
        return self.op("dve", fn, reads, writes)

    def pool(self, fn, reads=(), writes=()):
        return self.op("pool", fn, reads, writes)

    def dma(self, out, in_, reads=(), writes=(), semkey=None, eng="sp", **kw):
        return self.op(eng, lambda e: e.dma_start(out=out, in_=in_, **kw), reads, writes,
                       is_dma=True, semkey=semkey)

    def allgather_pairs(self, out_t, in_t, reads=(), writes=(), semkey="cc"):
        return self.op("pool", lambda e: e.collective_compute(
            "AllGather", ALU.bypass, replica_groups=[[0, 1], [2, 3], [4, 5], [6, 7]],
            ins=[in_t.ap().opt()], outs=[out_t.ap().opt()]), reads, writes, is_dma=True, semkey=semkey, inc=1)

    def allreduce_all(self, out_t, in_t, reads=(), writes=(), semkey="ar"):
        return self.op("pool", lambda e: e.collective_compute(
            "AllReduce", ALU.add, replica_groups=[list(range(8))],
            ins=[in_t.ap().opt()], outs=[out_t.ap().opt()]), reads, writes, is_dma=True, semkey=semkey, inc=1)

    def emit(self):
        nc = self.nc
        st = self.stack
        sems = {}
        for e in ("pe", "act", "dve", "pool"):
            sems[e] = st.enter_context(nc.semaphore("s_" + e))
        for k in self.dma_keys:
            sems[("d", k)] = st.enter_context(nc.semaphore("d_" + str(k)))
        for e in self.ENGS:
            cnt = 0
            for ins in self.streams[e]:
                if ins.is_dma:
                    self.dma_keys[ins.semkey] += ins.inc
                    ins.semval = self.dma_keys[ins.semkey]
                elif ins.signal:
                    cnt += 1
                    ins.semval = cnt
        final_dma = dict(self.dma_keys)

        def run(eng_name, e):
            seen = {}
            for ins in self.streams[eng_name]:
                need = {}
                for d in ins.deps:
                    sk = ("d", d.semkey) if d.is_dma else d.eng
                    if d.semval > need.get(sk, 0):
                        need[sk] = d.semval
                for sk, v in need.items():
                    if seen.get(sk, 0) >= v:
                        continue
                    e.wait_ge(sems[sk], v)
                    seen[sk] = v
                bi = ins.fn(e)
                if ins.is_dma:
                    bi.then_inc(sems[("d", ins.semkey)], ins.inc)
                elif ins.signal:
                    bi.then_inc(sems[eng_name], 1)
            if eng_name == "sp":
                for k, v in final_dma.items():
                    if v > 0 and seen.get(("d", k), 0) < v:
                        e.wait_ge(sems[("d", k)], v)

        with nc.Block() as block:
            @block.tensor
            def _(e):
                run("pe", e)

            @block.scalar
            def _(e):
                run("act", e)

            @block.vector
            def _(e):
                run("dve", e)

            @block.gpsimd
            def _(e):
                run("pool", e)

            @block.sync
            def _(e):
                run("sp", e)
        st.close()


def simulate_sync(P):
    keys = dict.fromkeys(P.dma_keys, 0)
    for e in P.ENGS:
        cnt = 0
        for ins in P.streams[e]:
            if ins.is_dma:
                keys[ins.semkey] += ins.inc
                ins.semval = keys[ins.semkey]
            elif ins.signal:
                cnt += 1
                ins.semval = cnt
    sem = {}
    pc = {e: 0 for e in P.ENGS}
    progress = True
    while progress:
        progress = False
        for e in P.ENGS:
            st = P.streams[e]
            while pc[e] < len(st):
                ins = st[pc[e]]
                ok = True
                for d in ins.deps:
                    sk = ("d", d.semkey) if d.is_dma else d.eng
                    if sem.get(sk, 0) < d.semval:
                        ok = False
                        break
                if not ok:
                    break
                if ins.is_dma:
                    sem[("d", ins.semkey)] = sem.get(("d", ins.semkey), 0) + ins.inc
                elif ins.signal:
                    sem[e] = sem.get(e, 0) + 1
                pc[e] += 1
                progress = True
    stuck = {e: (pc[e], len(P.streams[e])) for e in P.ENGS if pc[e] < len(P.streams[e])}
    return stuck


EPS = 1e-6
D = 2048
DFF = 5504
NJ = 43
NC16 = 16


class Ctx:
    def __init__(self, P, ntiles):
        self.P = P
        self.tiles = []
        off = 0
        for n in ntiles:
            self.tiles.append((off, n))
            off += n
        self.NT = off
        NT = off
        self.hy = P.sbuf("hy", [128, 16, NT], BF16)
        self.A = P.sbuf("A", [128, NJ, NT], BF16)
        self.d_h = [Dep(f"h{i}") for i in range(len(ntiles))]
        self.d_A = [Dep(f"A{i}") for i in range(len(ntiles))]
        aflat = self.A[:].rearrange("p j t -> p (j t)")
        self.xt = []
        self.d_xt = []
        nx = min(3, (NJ * NT) // 16384)
        for i in range(nx):
            v = aflat[:, i * 16384:(i + 1) * 16384].bitcast(F32).rearrange("p (c t) -> p c t", c=16)
            self.xt.append(v)
            self.d_xt.append(Dep(f"xt{i}"))
        self.wgu = [P.sbuf(f"wgu{i}", [128, 2, 16, 256], BF16) for i in range(2)]
        self.d_wgu = [Dep(f"wgu{i}") for i in range(2)]
        self.wd = [P.sbuf(f"wd{i}", [128, NJ, 128], BF16) for i in range(2)]
        self.d_wd = [Dep(f"wd{i}") for i in range(2)]
        self.ps, self.d_ps = P.shared_psum()
        self.ones = P.sbuf("ones", [128, 128], BF16)
        self.d_ones = Dep("ones")
        P.dve(lambda e: e.memset(self.ones[:], 1.0), writes=[self.d_ones])
        self.sq = [P.sbuf(f"sq{i}", [128, 512], BF16) for i in range(4)]
        self.d_sq = [Dep(f"sq{i}") for i in range(4)]
        self.tmp = [P.sbuf(f"tmp{i}", [128, 512], F32) for i in range(4)]
        self.d_tmp = [Dep(f"tmp{i}") for i in range(4)]
        self.rstd = P.sbuf("rstd", [128, NT], F32)
        self.d_rstd = [Dep(f"rstd{i}") for i in range(len(ntiles))]
        self.mT = P.sbuf("mT", [128, 144, 2], F32)
        self.d_mT = Dep("mT")
        self.gT = P.sbuf("gT", [128, 6, 16], F32)
        self.d_gT = Dep("gT")
        self.coef = P.sbuf("coef", [128, 4, 16], F32)
        self.d_coef = Dep("coef")
        self.sqi = 0
        self.tmpi = 0
        self.psi = 0

    def next_sq(self):
        i = self.sqi % 4
        self.sqi += 1
        return self.sq[i], self.d_sq[i]

    def next_tmp(self):
        i = self.tmpi % 4
        self.tmpi += 1
        return self.tmp[i], self.d_tmp[i]


def emit_modulation(P, C, ada_w, ada_bT, cvecT, scr):
    cv = scr["cv"]
    sc = scr["sc"]
    bT = scr["bT"]
    d_cv, d_sc, d_bT = Dep("cv"), Dep("sc"), Dep("bT")
    P.dma(cv[:], cvecT, writes=[d_cv], semkey="small")
    P.dma(bT[:], ada_bT, writes=[d_bT], semkey="small2")
    P.act(lambda e: e.activation(out=sc[:], in_=cv[:], func=AF.Silu), reads=[d_cv], writes=[d_sc])
    mp = C.ps[7]
    d_mp = C.d_ps[7]
    mpv = mp[:, 0:288].rearrange("p (j r) -> p j r", r=2)
    for jb in range(36):
        s = jb % 2
        slot = C.wgu[s][:].rearrange("p a c n -> p c (a n)") if False else None
        sl = C.wgu[s][:].rearrange("p a c n -> p (a c n)").rearrange("p (c n) -> p c n", c=16)
        src = ada_w[:, jb * 512:(jb + 1) * 512].rearrange("(c p) n -> p c n", p=128)
        P.dma(sl, src, writes=[C.d_wgu[s]], semkey=f"wgu{s}", eng="pool")
        for j4 in range(4):
            j = jb * 4 + j4
            for k in range(16):
                P.pe(lambda e, sl=sl, j4=j4, k=k, j=j: e.matmul(
                    mpv[:, j, :], lhsT=sl[:, k, j4 * 128:(j4 + 1) * 128], rhs=sc[:, k, :],
                    start=(k == 0), stop=(k == 15)),
                    reads=[C.d_wgu[s], d_sc], writes=[d_mp])
    for r in range(2):
        P.dve(lambda e, r=r: e.tensor_tensor(out=C.mT[:, :, r], in0=mpv[:, :, r], in1=bT[:], op=ALU.add),
              reads=[d_mp, d_bT], writes=[C.d_mT])


def emit_coefs(P, C, sl_scale, sl_gate, gi_pre, gi_post, wres, cols):
    for col in cols:
        P.dve(lambda e, col=col: e.scalar_tensor_tensor(
            out=C.coef[:, col, :], in0=C.mT[:, sl_scale * 16:(sl_scale + 1) * 16, col], scalar=1.0,
            in1=C.gT[:, gi_pre, :], op0=ALU.add, op1=ALU.mult),
            reads=[C.d_mT, C.d_gT], writes=[C.d_coef])
        if sl_gate is not None:
            P.dve(lambda e, col=col: e.scalar_tensor_tensor(
                out=C.coef[:, 2 + col, :], in0=C.mT[:, sl_gate * 16:(sl_gate + 1) * 16, col], scalar=float(wres),
                in1=C.gT[:, gi_post, :], op0=ALU.mult, op1=ALU.mult),
                reads=[C.d_mT, C.d_gT], writes=[C.d_coef])


def emit_prenorm(P, C, srcs, sl_shift, out_sb, d_out, arena_deps, resident=False):
    nx = len(C.xt)
    for ti, (off, n) in enumerate(C.tiles):
        src, d_src, col = srcs[ti]
        xi = ti % nx
        xt = C.xt[xi][:, :, 0:n]
        if not resident:
            P.dma(xt, src.rearrange("(c p) t -> p c t", p=128), reads=[d_src],
                  writes=[C.d_xt[xi]] + arena_deps, semkey=f"xt{xi}")
        ssp = C.ps[6 + (ti % 2)]
        d_ssp = C.d_ps[6 + (ti % 2)]
        for c in range(16):
            sq, d_sq = C.next_sq()
            P.act(lambda e, sq=sq, c=c, xt=xt, n=n: e.activation(out=sq[:, 0:n], in_=xt[:, c, :], func=AF.Square),
                  reads=[C.d_xt[xi]], writes=[d_sq])
            P.pe(lambda e, sq=sq, c=c, ssp=ssp, n=n: e.matmul(ssp[:, 0:n], lhsT=C.ones[:], rhs=sq[:, 0:n],
                                                              start=(c == 0), stop=(c == 15)),
                 reads=[d_sq, C.d_ones], writes=[d_ssp])
        tmp, d_tmp = C.next_tmp()
        P.act(lambda e, tmp=tmp, ssp=ssp, n=n: e.activation(out=tmp[:, 0:n], in_=ssp[:, 0:n], func=AF.Sqrt,
                                                            scale=1.0 / D, bias=EPS),
              reads=[d_ssp], writes=[d_tmp])
        rs = C.rstd[:, off:off + n]
        P.dve(lambda e, tmp=tmp, rs=rs, n=n: e.reciprocal(out=rs, in_=tmp[:, 0:n]),
              reads=[d_tmp], writes=[C.d_rstd[ti]])
        dst = out_sb(ti)
        for c in range(16):
            tmp, d_tmp = C.next_tmp()
            P.dve(lambda e, tmp=tmp, c=c, xt=xt, rs=rs, n=n, col=col: e.scalar_tensor_tensor(
                out=tmp[:, 0:n], in0=xt[:, c, :], scalar=C.coef[:, col, c:c + 1], in1=rs,
                op0=ALU.mult, op1=ALU.mult),
                reads=[C.d_xt[xi], C.d_rstd[ti], C.d_coef], writes=[d_tmp])
            P.act(lambda e, tmp=tmp, c=c, dst=dst, n=n, col=col: e.activation(
                out=dst[:, c, :], in_=tmp[:, 0:n], func=AF.Identity,
                bias=C.mT[:, sl_shift * 16 + c, col:col + 1], scale=1.0),
                reads=[d_tmp, C.d_mT], writes=[d_out[ti]])


def emit_ffn(P, C, srcs, dsts, wg, wu, wd, sl_shift, resident=False):
    nt = len(C.tiles)
    arena = C.d_A
    emit_prenorm(P, C, srcs, sl_shift, lambda ti: C.hy[:, :, C.tiles[ti][0]:C.tiles[ti][0] + C.tiles[ti][1]],
                 C.d_h, arena, resident)
    emit_gateup(P, C, wg, wu)
    emit_down_residual(P, C, NJ, wd, srcs, dsts)


def emit_gateup(P, C, wg, wu):
    it = 0
    for jj in range(22):
        s = jj % 2
        ncol = 256 if jj < 21 else 128
        for a, w in enumerate((wg, wu)):
            src = w[:, jj * 256:jj * 256 + ncol].rearrange("(c p) n -> p c n", p=128)
            P.dma(C.wgu[s][:, a, :, 0:ncol], src, writes=[C.d_wgu[s]], semkey=f"wgu{s}", eng="pool")
        for jl in range(ncol // 128):
            j = jj * 2 + jl
            for ti, (off, n) in enumerate(C.tiles):
                pg = (it % 4) * 2
                it += 1
                G, U = C.ps[pg], C.ps[pg + 1]
                for a, pt in enumerate((G, U)):
                    for k in range(16):
                        P.pe(lambda e, pt=pt, a=a, k=k, s=s, jl=jl, off=off, n=n: e.matmul(
                            pt[:, 0:n], lhsT=C.wgu[s][:, a, k, jl * 128:(jl + 1) * 128],
                            rhs=C.hy[:, k, off:off + n], start=(k == 0), stop=(k == 15)),
                            reads=[C.d_wgu[s], C.d_h[ti]], writes=[C.d_ps[pg + a]])
                tmp, d_tmp = C.next_tmp()
                P.act(lambda e, tmp=tmp, G=G, n=n: e.activation(out=tmp[:, 0:n], in_=G[:, 0:n], func=AF.Silu),
                      reads=[C.d_ps[pg]], writes=[d_tmp])
                P.dve(lambda e, tmp=tmp, U=U, j=j, off=off, n=n: e.tensor_tensor(
                    out=C.A[:, j, off:off + n], in0=tmp[:, 0:n], in1=U[:, 0:n], op=ALU.mult),
                    reads=[d_tmp, C.d_ps[pg + 1]], writes=[C.d_A[ti]] + C.d_xt)


def emit_down_residual(P, C, nch, wd, srcs, dsts):
    NJ = nch
    pend = []
    it = 0
    for dc in range(16):
        s = dc % 2
        src = wd[:, dc * 128:(dc + 1) * 128].rearrange("(j p) n -> p j n", p=128)
        P.dma(C.wd[s][:, 0:nch, :], src, writes=[C.d_wd[s]], semkey=f"wd{s}", eng="pool")
        for ti, (off, n) in enumerate(C.tiles):
            pi = it % 4
            it += 1
            Y = C.ps[pi]
            for j in range(NJ):
                P.pe(lambda e, Y=Y, j=j, s=s, off=off, n=n: e.matmul(
                    Y[:, 0:n], lhsT=C.wd[s][:, j, :], rhs=C.A[:, j, off:off + n],
                    start=(j == 0), stop=(j == NJ - 1)),
                    reads=[C.d_wd[s], C.d_A[ti]], writes=[C.d_ps[pi]])
            for f in pend:
                f()
            pend = []
            P.act(lambda e, Y=Y, dc=dc, off=off, n=n: e.activation(out=C.hy[:, dc, off:off + n], in_=Y[:, 0:n],
                                                                  func=AF.Copy),
                  reads=[C.d_ps[pi]], writes=[C.d_h[ti]])
            sq, d_sq = C.next_sq()
            P.act(lambda e, Y=Y, sq=sq, n=n: e.activation(out=sq[:, 0:n], in_=Y[:, 0:n], func=AF.Square),
                  reads=[C.d_ps[pi]], writes=[d_sq])
            SS = C.ps[4 + ti]

            def ssmm(sq=sq, d_sq=d_sq, SS=SS, ti=ti, dc=dc, n=n):
                P.pe(lambda e: e.matmul(SS[:, 0:n], lhsT=C.ones[:], rhs=sq[:, 0:n],
                                        start=(dc == 0), stop=(dc == 15)),
                     reads=[d_sq, C.d_ones], writes=[C.d_ps[4 + ti]])
            pend.append(ssmm)
    for f in pend:
        f()
    nx = len(C.xt)
    for ti, (off, n) in enumerate(C.tiles):
        src, d_src, col = srcs[ti]
        dst, d_dst, _ = dsts[ti]
        if dst is None:
            continue
        SS = C.ps[4 + ti]
        tmp, d_tmp = C.next_tmp()
        P.act(lambda e, tmp=tmp, SS=SS, n=n: e.activation(out=tmp[:, 0:n], in_=SS[:, 0:n], func=AF.Sqrt,
                                                          scale=1.0 / D, bias=EPS),
              reads=[C.d_ps[4 + ti]], writes=[d_tmp])
        rs = C.rstd[:, off:off + n]
        P.dve(lambda e, tmp=tmp, rs=rs, n=n: e.reciprocal(out=rs, in_=tmp[:, 0:n]),
              reads=[d_tmp], writes=[C.d_rstd[ti]])
        xi = ti % nx
        xt = C.xt[xi][:, :, 0:n]
        P.dma(xt, src.rearrange("(c p) t -> p c t", p=128), reads=[d_src],
              writes=[C.d_xt[xi]] + C.d_A, semkey=f"xt{xi}")
        for c in range(16):
            tmp, d_tmp = C.next_tmp()
            P.dve(lambda e, tmp=tmp, c=c, rs=rs, off=off, n=n, col=col: e.scalar_tensor_tensor(
                out=tmp[:, 0:n], in0=C.hy[:, c, off:off + n], scalar=C.coef[:, 2 + col, c:c + 1], in1=rs,
                op0=ALU.mult, op1=ALU.mult),
                reads=[C.d_h[ti], C.d_rstd[ti], C.d_coef], writes=[d_tmp])
            P.dve(lambda e, tmp=tmp, c=c, xt=xt, n=n: e.tensor_tensor(
                out=xt[:, c, :], in0=xt[:, c, :], in1=tmp[:, 0:n], op=ALU.add),
                reads=[d_tmp, C.d_xt[xi]], writes=[C.d_xt[xi]])
        P.dma(dst.rearrange("(c p) t -> p c t", p=128), xt, reads=[C.d_xt[xi]], writes=[d_dst],
              semkey=f"xt{xi}")


def emit_modulation_sharded(P, C, ada_sl, cvecT, bT0, bT1, mp_own, mp_g, mTd, d_mTd):
    cv = P.sbuf("mcv", [128, 16, 2], F32)
    sc = P.sbuf("msc", [128, 16, 2], BF16)
    bT = P.sbuf("mbT", [128, 2, 144], F32)
    part = P.sbuf("mpart", [128, 288], F32)
    mo = P.sbuf("mo", [128, 2, 144, 2], F32)
    d_cv, d_sc, d_bT, d_part, d_mo = [Dep(n) for n in ("mcv", "msc", "mbT", "mpart", "mo")]
    P.dma(cv[:], cvecT, writes=[d_cv], semkey="small")
    P.dma(bT[:, 0, :], bT0, writes=[d_bT], semkey="small2")
    P.dma(bT[:, 1, :], bT1, writes=[d_bT], semkey="small2")
    P.act(lambda e: e.activation(out=sc[:], in_=cv[:], func=AF.Silu), reads=[d_cv], writes=[d_sc])
    mp = C.ps[7]
    d_mp = C.d_ps[7]
    mpv = mp[:, 0:288].rearrange("p (l j r) -> p l j r", l=2, j=72)
    it = 0
    for l in range(2):
        for jb in range(18):
            s = it % 2
            it += 1
            sl = C.wgu[s][:].rearrange("p a c n -> p (a c n)").rearrange("p (c n) -> p c n", c=16)
            src = ada_sl[l, :, jb * 512:(jb + 1) * 512].rearrange("(c p) n -> p c n", p=128)
            P.dma(sl, src, writes=[C.d_wgu[s]], semkey=f"wgu{s}", eng="pool")
            for j4 in range(4):
                j = jb * 4 + j4
                for k in range(16):
                    P.pe(lambda e, sl=sl, j4=j4, k=k, j=j, l=l: e.matmul(
                        mpv[:, l, j, :], lhsT=sl[:, k, j4 * 128:(j4 + 1) * 128], rhs=sc[:, k, :],
                        start=(k == 0), stop=(k == 15)),
                        reads=[C.d_wgu[s], d_sc], writes=[d_mp])
    P.dve(lambda e: e.tensor_copy(out=part[:], in_=mp[:, 0:288]), reads=[d_mp], writes=[d_part])
    d_mi, d_mr = Dep("mp_own"), Dep("mp_g")
    P.dma(mp_own[:, :], part[:], reads=[d_part], writes=[d_mi], semkey="mred")
    P.allgather_pairs(mp_g, mp_own, reads=[d_mi], writes=[d_mr], semkey="cc0")
    for l in range(2):
        for r in range(2):
            P.dma(mo[:, l, r * 72:(r + 1) * 72, :],
                  mp_g[r * 128:(r + 1) * 128, l * 144:(l + 1) * 144].rearrange("p (j c) -> p j c", c=2),
                  reads=[d_mr], writes=[d_mo], semkey="mred")
        for col in range(2):
            P.dve(lambda e, l=l, col=col: e.tensor_tensor(out=mo[:, l, :, col], in0=mo[:, l, :, col], in1=bT[:, l, :],
                                                          op=ALU.add), reads=[d_mo, d_bT], writes=[d_mo])
        P.dma(mTd[l][:], mo[:, l, :, :], reads=[d_mo], writes=[d_mTd[l]], semkey="mTo")


EPS = 1e-6
NTOK = 2304
NCTX = 256
NLAT = 2048
LAM_INIT0 = 0.8 - 0.6 * math.exp(-0.3 * 0)


DBG = {}


class MixCtx:
    def __init__(self, P):
        self.P = P
        self.hs = P.sbuf("hs", [128, 16, NTOK], BF16)
        self.d_hs = Dep("hs")
        self.ropeC = P.sbuf("ropeC", [128, NLAT], F32)
        self.ropeS = P.sbuf("ropeS", [128, NLAT], F32)
        self.d_rope = Dep("rope")
        self.wt = [P.sbuf(f"wt{i}", [128, 16, 128], BF16) for i in range(8)]
        self.d_wt = [Dep(f"wt{i}") for i in range(8)]
        self.ps, self.d_ps = P.shared_psum()
        self.d_ph = [[Dep(f"ph{i}_{h}") for h in range(2)] for i in range(8)]
        self.ones = P.sbuf("ones", [128, 128], BF16)
        self.onesf = P.sbuf("onesf", [128, 128], F32)
        self.d_ones = Dep("ones")
        P.dve(lambda e: e.memset(self.ones[:], 1.0), writes=[self.d_ones])
        P.dve(lambda e: e.memset(self.onesf[:], 1.0), writes=[self.d_ones])
        self.perm = P.sbuf("perm", [128, 128], BF16)
        self.ident = P.sbuf("ident", [128, 128], BF16)
        self.maskf = P.sbuf("maskf", [128, 128], F32)
        self.maskb = P.sbuf("maskb", [128, 128], F32)
        self.d_const = Dep("const")
        self.vec = P.sbuf("vec", [128, 32], F32)
        self.d_vec = Dep("vec")
        self.lbr = P.sbuf("lbr", [128, 2, 2, 4], F32)
        self.lb = P.sbuf("lb", [128, 2, 4], F32)
        self.oml = P.sbuf("oml", [128, 2, 4], F32)
        self.d_lb = Dep("lb")
        u0 = P.aoff
        self.qT = P.sbuf("qT", [128, NLAT], BF16)
        self.qT1 = P.sbuf("qT1", [128, NLAT], BF16)
        self.kT = P.sbuf("kT", [128, NTOK], BF16)
        self.V = P.sbuf("V", [128, 18, 128], BF16)
        self.d_qT, self.d_kT, self.d_V = Dep("qT"), Dep("kT"), Dep("V")
        self.E = [P.sbuf(f"E{i}", [128, 512], BF16) for i in range(4)]
        self.d_E = [Dep(f"E{i}") for i in range(4)]
        u1 = P.aoff
        if P.arena is not None:
            P.aoff = u0
        self.zf_sb = P.sbuf("zf_sb", [128, NTOK], F32)
        self.zb_sb = P.sbuf("zb_sb", [128, NTOK], F32)
        self.q_sb = P.sbuf("q_sb", [128, NTOK], BF16)
        self.i_sb = P.sbuf("i_sb", [128, 18, 128], BF16)
        self.d_rp = Dep("recproj")
        self.d_zf = Dep("zf_sb")
        if P.arena is not None:
            P.aoff = max(u1, P.aoff)
        self.qtF = P.sbuf("qtF", [128, NTOK], BF16)
        self.ktF = P.sbuf("ktF", [128, NTOK], BF16)
        self.zfb = self.zf_sb.bitcast(BF16)
        self.d_zfu = [Dep(f"zfu{u}") for u in range(9)]
        self.d_qk = [Dep("qkF"), Dep("qkB")]
        self.sg_sb = P.sbuf("sg_sb", [128, NLAT], BF16)
        self.svall = P.sbuf("svall", [128, 2, 18, 4], F32)
        self.d_sva = Dep("svall")
        self.d_svad = [Dep("svallF"), Dep("svallB")]
        self.d_tfh = [[Dep(f"tfh{i}_{h}") for h in range(2)] for i in range(6)]
        self.maskR = P.sbuf("maskR", [128, 512], F32)
        P.dve(lambda e: e.memset(self.maskR[:], 1.0), writes=[self.d_ones])
        for cc in range(4):
            P.dve(lambda e, cc=cc: e.memset(self.maskR[:, cc * 128:cc * 128 + 1], 0.0), writes=[self.d_ones])
        self.tf = [P.sbuf(f"tf{i}", [128, 512], F32) for i in range(6)]
        self.d_tf = [Dep(f"tf{i}") for i in range(6)]
        self.tb = [P.sbuf(f"tb{i}", [128, 512], BF16) for i in range(4)]
        self.d_tb = [Dep(f"tb{i}") for i in range(4)]
        self.tfi = 0
        self.tbi = 0
        self.Ei = 0
        self.S = P.sbuf("S", [128, 128], F32)
        self.d_S = Dep("S")
        self.S1 = P.sbuf("S1", [128, 128], F32)
        self.d_S1 = Dep("S1")
        self.ofw = P.sbuf("ofw", [128, NLAT], F32)
        self.d_ofw = Dep("ofw")
        self.obw = P.sbuf("obw", [128, NLAT], F32)
        self.d_obw = Dep("obw")
        self.rf = [P.sbuf(f"rf{i}", [128, 128], F32) for i in range(6)]
        self.d_rf = [Dep(f"rf{i}") for i in range(6)]
        self.rb = [P.sbuf(f"rb{i}", [128, 128], BF16) for i in range(12)]
        self.d_rb = [Dep(f"rb{i}") for i in range(12)]
        self.rfi = 0
        self.rbi = 0
        self.sv = [P.sbuf(f"sv{i}", [128, 8], F32) for i in range(8)]
        self.d_sv = [Dep(f"sv{i}") for i in range(8)]
        self.svi = 0

    def ntf(self):
        i = self.tfi % 6
        self.tfi += 1
        return self.tf[i], self.d_tf[i]

    def ntb(self):
        i = self.tbi % 4
        self.tbi += 1
        return self.tb[i], self.d_tb[i]

    def nE(self):
        i = self.Ei % 4
        self.Ei += 1
        return self.E[i], self.d_E[i]

    def nrf(self):
        i = self.rfi % 6
        self.rfi += 1
        return self.rf[i], self.d_rf[i]

    def nrb(self):
        i = self.rbi % 12
        self.rbi += 1
        return self.rb[i], self.d_rb[i]

    def nsv(self):
        i = self.svi % 8
        self.svi += 1
        return self.sv[i], self.d_sv[i]


def load_w(P, M, slot, src):
    P.dma(M.wt[slot][:], src.rearrange("(c p) n -> p c n", p=128), writes=[M.d_wt[slot]],
          semkey=f"wt{slot}", eng="pool")


def emit_mix_setup(P, M, hTf, ropeC, ropeS, perm, ident, maskf, maskb, lamT, dng, rng, lbraw):
    for q in range(4 if hTf is not None else 0):
        P.dma(M.hs[:, q * 4:(q + 1) * 4, :], hTf[q * 512:(q + 1) * 512, :].rearrange("(c p) t -> p c t", p=128),
              writes=[M.d_hs], semkey="hs")
    P.dma(M.ropeC[:], ropeC, writes=[M.d_rope], semkey="rope")
    P.dma(M.ropeS[:], ropeS, writes=[M.d_rope], semkey="rope")
    P.dma(M.perm[:], perm, writes=[M.d_const], semkey="cst", eng="pool")
    P.dma(M.ident[:], ident, writes=[M.d_const], semkey="cst", eng="pool")
    P.dma(M.maskf[:], maskf, writes=[M.d_const], semkey="cst2")
    P.dma(M.maskb[:], maskb, writes=[M.d_const], semkey="cst2")
    P.dma(M.vec[0:64, 0:4], lamT, writes=[M.d_vec], semkey="vec")
    P.dma(M.vec[:, 4:5], dng, writes=[M.d_vec], semkey="vec")
    P.dma(M.vec[:, 5:6], rng, writes=[M.d_vec], semkey="vec")
    P.dma(M.lbr[:], lbraw, writes=[M.d_lb], semkey="lb")
    P.dve(lambda e: e.tensor_tensor(out=M.vec[0:64, 6:7], in0=M.vec[0:64, 0:1], in1=M.vec[0:64, 1:2], op=ALU.mult),
          reads=[M.d_vec], writes=[M.d_vec])
    P.dve(lambda e: e.tensor_tensor(out=M.vec[0:64, 7:8], in0=M.vec[0:64, 2:3], in1=M.vec[0:64, 3:4], op=ALU.mult),
          reads=[M.d_vec], writes=[M.d_vec])
    lp = M.ps[7]
    P.pe(lambda e: e.matmul(lp[:, 0:2], lhsT=M.onesf[0:64, :], rhs=M.vec[0:64, 6:8], start=True, stop=True),
         reads=[M.d_vec, M.d_ones], writes=[M.d_ps[7]])
    P.act(lambda e: e.activation(out=M.vec[:, 8:10], in_=lp[:, 0:2], func=AF.Exp), reads=[M.d_ps[7]],
          writes=[M.d_vec])
    P.dve(lambda e: e.tensor_tensor(out=M.vec[:, 10:11], in0=M.vec[:, 9:10], in1=M.vec[:, 8:9], op=ALU.subtract),
          reads=[M.d_vec], writes=[M.d_vec])
    P.dve(lambda e: e.tensor_scalar(out=M.vec[:, 10:11], in0=M.vec[:, 10:11], scalar1=-LAM_INIT0, scalar2=None,
                                    op0=ALU.add), reads=[M.d_vec], writes=[M.d_vec])
    P.dve(lambda e: e.tensor_scalar(out=M.vec[:, 11:12], in0=M.vec[:, 4:5], scalar1=1.0 - LAM_INIT0, scalar2=None,
                                    op0=ALU.mult), reads=[M.d_vec], writes=[M.d_vec])
    P.dve(lambda e: e.tensor_tensor(out=M.lb[:], in0=M.lbr[:, :, 1, :], in1=M.lbr[:, :, 0, :], op=ALU.subtract),
          reads=[M.d_lb], writes=[M.d_lb])
    P.act(lambda e: e.activation(out=M.lb[:], in_=M.lb[:], func=AF.Exp), reads=[M.d_lb], writes=[M.d_lb])
    P.dve(lambda e: e.tensor_scalar(out=M.lb[:], in0=M.lb[:], scalar1=1.0, scalar2=None, op0=ALU.add),
          reads=[M.d_lb], writes=[M.d_lb])
    P.dve(lambda e: e.reciprocal(out=M.lb[:], in_=M.lb[:]), reads=[M.d_lb], writes=[M.d_lb])
    P.dve(lambda e: e.tensor_scalar(out=M.oml[:], in0=M.lb[:], scalar1=-1.0, scalar2=1.0, op0=ALU.mult, op1=ALU.add),
          reads=[M.d_lb], writes=[M.d_lb])


def proj_fm(P, M, out_ps, d_out, slot, off, n):
    for k in range(16):
        P.pe(lambda e, k=k: e.matmul(out_ps, lhsT=M.wt[slot][:, k, :], rhs=M.hs[:, k, off:off + n],
                                     start=(k == 0), stop=(k == 15)),
             reads=[M.d_wt[slot], M.d_hs], writes=[d_out])


def proj_tm(P, M, out_ps, d_out, slot, off):
    for k in range(16):
        P.pe(lambda e, k=k: e.matmul(out_ps, lhsT=M.hs[:, k, off:off + 128], rhs=M.wt[slot][:, k, :],
                                     start=(k == 0), stop=(k == 15)),
             reads=[M.d_wt[slot], M.d_hs], writes=[d_out])


def emit_rsqrt(P, M, out_sb, d_o, in_ps, d_in, n, inv_dim):
    P.act(lambda e: e.activation(out=out_sb, in_=in_ps, func=AF.Ln, scale=inv_dim, bias=EPS),
          reads=[d_in], writes=[d_o])
    P.act(lambda e: e.activation(out=out_sb, in_=out_sb, func=AF.Exp, scale=-0.5), reads=[d_o], writes=[d_o])


def emit_attention_head(P, M, hd, mergedT, d_merged):
    sq_, sk_, sv_ = 0, 1, 2
    P.dve(lambda e: e.memset(M.qT[64:128, :], 0.0), writes=[M.d_qT])
    P.dve(lambda e: e.memset(M.qT1[0:64, :], 0.0), writes=[M.d_qT])
    tiles = [(0, NCTX)] + [(NCTX + i * 512, 512) for i in range(4)]
    bi = 0
    import os
    NSUB = int(os.environ.get("ATT_SUB", "99"))
    for (off, n) in tiles:
        for which in ("k", "q"):
            if bi >= NSUB:
                continue
            if which == "q" and off < NCTX:
                continue
            slot = sk_ if which == "k" else sq_
            dstT = M.kT if which == "k" else M.qT
            d_dst = M.d_kT if which == "k" else M.d_qT
            doff = off if which == "k" else off - NCTX
            pb = bi % 2
            bi += 1
            pp, d_pp = M.ps[pb], M.d_ps[pb]
            proj_fm(P, M, pp[:, 0:n], d_pp, slot, off, n)
            if off < NCTX:
                P.act(lambda e, pp=pp, n=n, dstT=dstT, doff=doff: e.activation(
                    out=dstT[:, doff:doff + n], in_=pp[:, 0:n], func=AF.Copy), reads=[d_pp], writes=[d_dst])
                continue
            loff = off - NCTX
            sb, d_sb = M.ntb()
            P.act(lambda e, pp=pp, sb=sb, n=n: e.activation(out=sb[:, 0:n], in_=pp[:, 0:n], func=AF.Copy),
                  reads=[d_pp], writes=[d_sb])
            rp, d_rp = M.ps[2 + pb], M.d_ps[2 + pb]
            P.pe(lambda e, rp=rp, sb=sb, n=n: e.matmul(rp[:, 0:n], lhsT=M.perm[:], rhs=sb[:, 0:n], start=True,
                                                       stop=True), reads=[d_sb, M.d_const], writes=[d_rp])
            t1, d_t1 = M.ntf()
            P.dve(lambda e, t1=t1, pp=pp, n=n, loff=loff: e.tensor_tensor(
                out=t1[:, 0:n], in0=pp[:, 0:n], in1=M.ropeC[:, loff:loff + n], op=ALU.mult),
                reads=[d_pp, M.d_rope], writes=[d_t1])
            t2, d_t2 = M.ntf()
            P.dve(lambda e, t2=t2, rp=rp, n=n, loff=loff: e.tensor_tensor(
                out=t2[:, 0:n], in0=rp[:, 0:n], in1=M.ropeS[:, loff:loff + n], op=ALU.mult),
                reads=[d_rp, M.d_rope], writes=[d_t2])
            if which == "k":
                P.dve(lambda e, t1=t1, t2=t2, dstT=dstT, doff=doff, n=n: e.tensor_tensor(
                    out=dstT[:, doff:doff + n], in0=t1[:, 0:n], in1=t2[:, 0:n], op=ALU.add),
                    reads=[d_t1, d_t2], writes=[d_dst])
            else:
                P.dve(lambda e, t1=t1, t2=t2, doff=doff, n=n: e.tensor_tensor(
                    out=M.qT[0:64, doff:doff + n], in0=t1[0:64, 0:n], in1=t2[0:64, 0:n], op=ALU.add),
                    reads=[d_t1, d_t2], writes=[d_dst])
                P.dve(lambda e, t1=t1, t2=t2, doff=doff, n=n: e.tensor_tensor(
                    out=M.qT1[64:128, doff:doff + n], in0=t1[64:128, 0:n], in1=t2[64:128, 0:n], op=ALU.add),
                    reads=[d_t1, d_t2], writes=[d_dst])
    import os
    if DBG.get("qT") is not None and hd == 0:
        P.dma(DBG["qT"][:, :], M.qT[:], reads=[M.d_qT], writes=[Dep("dbgq")], semkey="dbg")
        P.dma(DBG["kT"][:, :], M.kT[:], reads=[M.d_kT], writes=[Dep("dbgk")], semkey="dbg")
    STG = int(os.environ.get("ATT_STAGE", "9"))
    if STG < 2:
        return
    for g4 in range(5):
        pb = 4 + (g4 % 2)
        vp, d_vp = M.ps[pb], M.d_ps[pb]
        nk = 4 if g4 < 4 else 2
        for i in range(nk):
            kt = g4 * 4 + i
            proj_tm(P, M, vp[:, i * 128:(i + 1) * 128], d_vp, sv_, kt * 128)
        P.act(lambda e, vp=vp, g4=g4, nk=nk: e.activation(
            out=M.V[:, g4 * 4:g4 * 4 + nk, :].rearrange("p a n -> p (a n)"), in_=vp[:, 0:nk * 128], func=AF.Copy),
            reads=[d_vp], writes=[M.d_V])
    if STG < 3:
        return
    for qt in [int(c) for c in os.environ.get("ATT_QT", "0123")]:
        qo = qt * 512
        steps = [(kt, m) for kt in range(18) for m in range(2)]
        pend = []
        for si, (kt, m) in enumerate(steps):
            sb_i = si % 4
            ST, d_ST = M.ps[sb_i], M.d_ps[sb_i]
            P.pe(lambda e, ST=ST, kt=kt, m=m: e.matmul(
                ST[:, :], lhsT=M.kT[:, kt * 128:(kt + 1) * 128],
                rhs=(M.qT if m == 0 else M.qT1)[:, qo:qo + 512], start=True, stop=True),
                reads=[M.d_kT, M.d_qT], writes=[d_ST])
            E, d_E = M.nE()
            P.act(lambda e, E=E, ST=ST: e.activation(out=E[:], in_=ST[:, :], func=AF.Exp, scale=0.125),
                  reads=[d_ST], writes=[d_E])
            if len(pend) >= 2:
                pend.pop(0)()

            def pv(E=E, d_E=d_E, kt=kt, m=m):
                P.pe(lambda e: e.matmul(M.ps[4 + m][:, :], lhsT=M.V[:, kt, :], rhs=E[:], start=(kt == 0),
                                        stop=(kt == 17)), reads=[M.d_V, d_E], writes=[M.d_ps[4 + m]])
                P.pe(lambda e: e.matmul(M.ps[6 + m][:, :], lhsT=M.ones[:], rhs=E[:], start=(kt == 0),
                                        stop=(kt == 17)), reads=[M.d_ones, d_E], writes=[M.d_ps[6 + m]])
            pend.append(pv)
        for f in pend:
            f()
        r0, d_r0 = M.ntf()
        r1, d_r1 = M.ntf()
        P.dve(lambda e, r0=r0: e.reciprocal(out=r0[:], in_=M.ps[6][:, :]), reads=[M.d_ps[6]], writes=[d_r0])
        P.dve(lambda e, r1=r1: e.reciprocal(out=r1[:], in_=M.ps[7][:, :]), reads=[M.d_ps[7]], writes=[d_r1])
        oa, d_oa = M.ntf()
        ob, d_ob = M.ntf()
        P.dve(lambda e, oa=oa, r0=r0: e.tensor_tensor(out=oa[:], in0=M.ps[4][:, :], in1=r0[:], op=ALU.mult),
              reads=[M.d_ps[4], d_r0], writes=[d_oa])
        P.dve(lambda e, ob=ob, r1=r1: e.tensor_tensor(out=ob[:], in0=M.ps[5][:, :], in1=r1[:], op=ALU.mult),
              reads=[M.d_ps[5], d_r1], writes=[d_ob])
        o, d_o = M.ntf()
        P.dve(lambda e, o=o, oa=oa, ob=ob: e.scalar_tensor_tensor(
            out=o[:], in0=ob[:], scalar=M.vec[:, 10:11], in1=oa[:], op0=ALU.mult, op1=ALU.add),
            reads=[d_oa, d_ob, M.d_vec], writes=[d_o])
        sq, d_sq = M.ntb()
        P.act(lambda e, sq=sq, o=o: e.activation(out=sq[:], in_=o[:], func=AF.Square), reads=[d_o], writes=[d_sq])
        P.pe(lambda e, sq=sq: e.matmul(M.ps[0][:, :], lhsT=M.ones[:], rhs=sq[:], start=True, stop=True),
             reads=[d_sq, M.d_ones], writes=[M.d_ps[0]])
        ri, d_ri = M.ntf()
        emit_rsqrt(P, M, ri[:], d_ri, M.ps[0][:, :], M.d_ps[0], 512, 1.0 / 128)
        ob16, d_ob16 = M.ntb()
        P.dve(lambda e, ob16=ob16, o=o, ri=ri: e.scalar_tensor_tensor(
            out=ob16[:], in0=o[:], scalar=M.vec[:, 11:12], in1=ri[:], op0=ALU.mult, op1=ALU.mult),
            reads=[d_o, d_ri, M.d_vec], writes=[d_ob16])
        rows = mergedT("att", hd) if callable(mergedT) else mergedT[hd * 128:(hd + 1) * 128, :]
        P.dma(rows[:, qo:qo + 512], ob16[:], reads=[d_ob16], writes=[d_merged], semkey="mg")


def emit_rec_head(P, M, r, mergedT, d_merged, after_burst=None):
    s_q, s_zf, s_zb, s_i, s_g = 3, 4, 5, 6, 7
    orders = [list(range(18)), [1, 0] + list(range(17, 1, -1))]
    Ss = [M.S, M.S1]
    dSs = [M.d_S, M.d_S1]
    obuf = [M.ofw, M.obw]
    d_obuf = [M.d_ofw, M.d_obw]
    for dr in range(2):
        P.dve(lambda e, dr=dr: e.memset(Ss[dr][:], 0.0), writes=[dSs[dr]])
    ptiles = [(0, NCTX)] + [(NCTX + i * 512, 512) for i in range(4)]
    bi = 0
    for (off, n) in ptiles:
        for (slot, dst, sc_) in ((s_zf, M.zf_sb, 1.0), (s_zb, M.zb_sb, 1.0), (s_q, M.q_sb, float(128 ** -0.5))):
            pb = bi % 4
            bi += 1
            proj_fm(P, M, M.ps[pb][:, 0:n], M.d_ps[pb], slot, off, n)
            if slot != s_q:
                P.act(lambda e, pb=pb, dst=dst, off=off, n=n: e.activation(
                    out=dst[:, off:off + n], in_=M.ps[pb][:, 0:n], func=AF.Sigmoid, scale=-1.0),
                    reads=[M.d_ps[pb]], writes=[M.d_rp])
            else:
                P.dve(lambda e, pb=pb, dst=dst, off=off, n=n, sc_=sc_: e.tensor_scalar(
                    out=dst[:, off:off + n], in0=M.ps[pb][:, 0:n], scalar1=sc_, scalar2=None, op0=ALU.mult),
                    reads=[M.d_ps[pb]], writes=[M.d_rp])
        if off >= NCTX:
            pb = bi % 4
            bi += 1
            proj_fm(P, M, M.ps[pb][:, 0:n], M.d_ps[pb], s_g, off, n)
            P.act(lambda e, pb=pb, off=off, n=n: e.activation(
                out=M.sg_sb[:, off - NCTX:off - NCTX + n], in_=M.ps[pb][:, 0:n], func=AF.Silu),
                reads=[M.d_ps[pb]], writes=[M.d_rp])
    for g4 in range(5):
        pb = 4 + (g4 % 2)
        nk = 4 if g4 < 4 else 2
        for i in range(nk):
            proj_tm(P, M, M.ps[pb][:, i * 128:(i + 1) * 128], M.d_ps[pb], s_i, (g4 * 4 + i) * 128)
        P.act(lambda e, pb=pb, g4=g4, nk=nk: e.activation(
            out=M.i_sb[:, g4 * 4:g4 * 4 + nk, :].rearrange("p a n -> p (a n)"), in_=M.ps[pb][:, 0:nk * 128],
            func=AF.Copy), reads=[M.d_ps[pb]], writes=[M.d_rp])

    if after_burst is not None:
        after_burst()
    RSTG = int(os.environ.get("REC_STAGE", "9"))
    if RSTG < 2:
        return
    def prep_gen(dr):
        zsb = M.zf_sb if dr == 0 else M.zb_sb
        T = [t[:, dr * 256:(dr + 1) * 256] for t in M.tf]
        dT = [M.d_tfh[i][dr] for i in range(6)]
        n = 256
        for off in range(0, NTOK, 256):
            c0 = off // 128
            u = off // 256
            if dr == 0:
                qdst, kdst = M.qtF[:, off:off + n], M.ktF[:, off:off + n]
                wdeps = [M.d_qk[0]]
                rdeps = [M.d_rp, M.d_zfu[u]]
            else:
                qdst, kdst = M.zfb[:, u * 512:u * 512 + 256], M.zfb[:, u * 512 + 256:u * 512 + 512]
                wdeps = [M.d_qk[1], M.d_zfu[u]]
                rdeps = [M.d_rp]
            P.dve(lambda e, off=off: e.tensor_scalar(out=T[2], in0=zsb[:, off:off + n], scalar1=M.oml[:, dr, r:r + 1],
                                                     scalar2=None, op0=ALU.mult), reads=rdeps + [M.d_lb],
                  writes=[dT[2]])
            yield
            P.act(lambda e: e.activation(out=T[3], in_=T[2], func=AF.Ln, scale=-1.0, bias=1.0), reads=[dT[2]],
                  writes=[dT[3]])
            yield
            P.dve(lambda e: e.tensor_tensor_scan(out=T[4], data0=M.maskR[:, 0:n], data1=T[3], initial=0.0,
                                                 op0=ALU.mult, op1=ALU.add), reads=[dT[3], M.d_ones], writes=[dT[4]])
            yield
            pfv = T[4].rearrange("p (c t) -> p c t", t=128)
            if dr == 1:
                P.dve(lambda e: e.tensor_tensor(out=T[0], in0=T[4], in1=T[3], op=ALU.subtract),
                      reads=[dT[4], dT[3]], writes=[dT[0]])
                yield
            for ci in range(2):
                src = T[0] if dr == 1 else T[4]
                P.dve(lambda e, ci=ci, src=src: e.tensor_scalar(
                    out=T[5][:, ci * 128:(ci + 1) * 128], in0=src[:, ci * 128:(ci + 1) * 128],
                    scalar1=T[4][:, ci * 128 + 63:ci * 128 + 64], scalar2=(1.0 if dr == 0 else -1.0),
                    op0=ALU.subtract, op1=ALU.mult), reads=[dT[0], dT[4]], writes=[dT[5]])
                yield
            P.act(lambda e: e.activation(out=T[1], in_=T[5], func=AF.Exp), reads=[dT[5]], writes=[dT[1]])
            yield
            P.act(lambda e: e.activation(out=T[3], in_=T[5], func=AF.Exp, scale=-1.0), reads=[dT[5]], writes=[dT[3]])
            yield
            P.dve(lambda e, off=off, qdst=qdst: e.tensor_tensor(out=qdst, in0=M.q_sb[:, off:off + n], in1=T[1],
                                                                op=ALU.mult), reads=[M.d_rp, dT[1]], writes=wdeps)
            yield
            P.dve(lambda e, kdst=kdst: e.tensor_tensor(out=kdst, in0=T[2], in1=T[3], op=ALU.mult),
                  reads=[dT[2], dT[3]], writes=wdeps)
            yield
            svv = M.svall[:, dr, c0:c0 + 2, :]
            d_sva = M.d_svad[dr]
            P.dve(lambda e, svv=svv, pfv=pfv: e.tensor_tensor(out=svv[:, :, 0], in0=pfv[:, :, 127], in1=pfv[:, :, 63],
                                                              op=ALU.subtract), reads=[dT[4]], writes=[d_sva])
            yield
            P.act(lambda e, svv=svv, pfv=pfv: e.activation(out=svv[:, :, 1], in_=pfv[:, :, 63], func=AF.Exp),
                  reads=[dT[4]], writes=[d_sva])
            yield
            P.act(lambda e, svv=svv: e.activation(out=svv[:, :, 2], in_=svv[:, :, 0], func=AF.Exp),
                  reads=[d_sva], writes=[d_sva])
            yield
            P.act(lambda e, svv=svv, pfv=pfv: e.activation(out=svv[:, :, 3], in_=pfv[:, :, 127], func=AF.Exp),
                  reads=[dT[4]], writes=[d_sva])
            yield

    import itertools
    P.fence(M.d_tf, [d for pair in M.d_tfh for d in pair])
    for _ in itertools.zip_longest(prep_gen(0), prep_gen(1)):
        pass
    P.fence([d for pair in M.d_tfh for d in pair], M.d_tf)

    if RSTG < 3:
        return

    def chunk_gen(dr, step):
        if True:
            c = orders[dr][step]
            mask = M.maskf if dr == 0 else M.maskb
            S_, d_S = Ss[dr], dSs[dr]
            a = c * 128
            lat = c >= 2
            la = a - NCTX
            bx = dr * 4 + (step % 2) * 2
            by = bx + 1
            if dr == 0:
                qt_, kt_ = M.qtF[:, a:a + 128], M.ktF[:, a:a + 128]
            else:
                ub = (c // 2) * 512 + (c % 2) * 128
                qt_, kt_ = M.zfb[:, ub:ub + 128], M.zfb[:, ub + 256:ub + 384]
            d_qt = d_kt = M.d_qk[dr]
            vt, d_vt = M.i_sb[:, c, :], M.d_rp
            sv, d_sv = M.svall[:, dr, c, :], M.d_svad[dr]
            c_e1 = 1 if dr == 0 else 2
            c_e2 = 2 if dr == 0 else 1
            ktp = M.ps[bx][:, 384:448].bitcast(BF16)
            d_ktp = M.d_ps[bx]
            P.pe(lambda e, ktp=ktp, kt_=kt_: e.transpose(ktp, kt_, M.ident[:]), reads=[d_kt, M.d_const],
                 writes=[d_ktp])
            yield
            ktok, d_ktok = M.nrb()
            P.act(lambda e, ktok=ktok, ktp=ktp: e.activation(out=ktok[:], in_=ktp, func=AF.Copy),
                  reads=[d_ktp], writes=[d_ktok])
            yield
            if lat:
                atp, d_atp = M.ps[by][:, 0:128], M.d_ps[by]
                P.pe(lambda e, atp=atp, kt_=kt_, qt_=qt_: e.matmul(atp, lhsT=kt_, rhs=qt_, start=True, stop=True),
                     reads=[d_kt, d_qt], writes=[d_atp])
                yield
                am, d_am = M.nrb()
                P.dve(lambda e, am=am, atp=atp: e.tensor_tensor(out=am[:], in0=atp, in1=mask[:], op=ALU.mult),
                      reads=[d_atp, M.d_const], writes=[d_am])
                yield
                sp, d_sp = M.nrb()
                P.dve(lambda e, sp=sp, sv=sv: e.tensor_scalar(out=sp[:], in0=S_[:], scalar1=sv[:, c_e1:c_e1 + 1],
                                                              scalar2=None, op0=ALU.mult),
                      reads=[d_S, d_sv], writes=[d_sp])
                yield
                op_, d_op = M.ps[by][:, 128:256], M.d_ps[by]
                P.pe(lambda e, op_=op_, sp=sp, qt_=qt_: e.matmul(op_, lhsT=sp[:], rhs=qt_, start=True, stop=False),
                     reads=[d_sp, d_qt], writes=[d_op])
                yield
                P.pe(lambda e, op_=op_, vt=vt, am=am: e.matmul(op_, lhsT=vt, rhs=am[:], start=False, stop=True),
                     reads=[d_vt, d_am], writes=[d_op])
                yield
                P.act(lambda e, op_=op_, la=la: e.activation(out=obuf[dr][:, la:la + 128], in_=op_, func=AF.Copy),
                      reads=[d_op], writes=[d_obuf[dr]])
                yield
            kvp, d_kvp = M.ps[by][:, 256:384], M.d_ps[by]
            P.pe(lambda e, kvp=kvp, ktok=ktok, vt=vt: e.matmul(kvp, lhsT=ktok[:], rhs=vt, start=True, stop=True),
                 reads=[d_ktok, d_vt], writes=[d_kvp])
            yield
            tk, d_tk = M.nrf()
            P.dve(lambda e, tk=tk, kvp=kvp, sv=sv: e.tensor_scalar(out=tk[:], in0=kvp, scalar1=sv[:, c_e2:c_e2 + 1],
                                                                    scalar2=None, op0=ALU.mult),
                  reads=[d_kvp, d_sv], writes=[d_tk])
            yield
            P.dve(lambda e, tk=tk, sv=sv: e.scalar_tensor_tensor(out=S_[:], in0=S_[:], scalar=sv[:, 3:4],
                                                                 in1=tk[:], op0=ALU.mult, op1=ALU.add),
                  reads=[d_tk, d_sv, d_S], writes=[d_S])
            yield
    import itertools
    for step in range(18):
        gens = [chunk_gen(0, step), chunk_gen(1, step)]
        for _ in itertools.zip_longest(*gens):
            pass
    if RSTG < 4:
        return
    for t in range(4):
        lo = t * 512
        o_, d_o = M.ntf()
        P.dve(lambda e, o_=o_: e.tensor_tensor(out=o_[:], in0=M.ofw[:, lo:lo + 512], in1=M.obw[:, lo:lo + 512],
                                               op=ALU.add), reads=[M.d_ofw, M.d_obw], writes=[d_o])
        sq, d_sq = M.ntb()
        P.act(lambda e, sq=sq, o_=o_: e.activation(out=sq[:], in_=o_[:], func=AF.Square), reads=[d_o], writes=[d_sq])
        P.pe(lambda e, sq=sq: e.matmul(M.ps[0][:, :], lhsT=M.ones[:], rhs=sq[:], start=True, stop=True),
             reads=[d_sq, M.d_ones], writes=[M.d_ps[0]])
        ri, d_ri = M.ntf()
        emit_rsqrt(P, M, ri[:], d_ri, M.ps[0][:, :], M.d_ps[0], 512, 1.0 / 128)
        o2, d_o2 = M.ntf()
        P.dve(lambda e, o2=o2, o_=o_, ri=ri: e.scalar_tensor_tensor(
            out=o2[:], in0=o_[:], scalar=M.vec[:, 5:6], in1=ri[:], op0=ALU.mult, op1=ALU.mult),
            reads=[d_o, d_ri, M.d_vec], writes=[d_o2])
        o3, d_o3 = M.ntb()
        P.dve(lambda e, o3=o3, o2=o2: e.tensor_tensor(out=o3[:], in0=o2[:], in1=M.sg_sb[:, lo:lo + 512], op=ALU.mult),
              reads=[d_o2, M.d_rp], writes=[d_o3])
        rows = mergedT("rec", r) if callable(mergedT) else mergedT[512 + r * 128:512 + (r + 1) * 128, :]
        P.dma(rows[:, lo:lo + 512], o3[:], reads=[d_o3], writes=[d_merged], semkey="mg")


def emit_load_act16(P, C, srcT, d_src):
    for q in range(4):
        P.dma(C.A[:, q * 4:(q + 1) * 4, :], srcT[q * 512:(q + 1) * 512, :].rearrange("(c p) t -> p c t", p=128),
              reads=[d_src], writes=C.d_A + C.d_xt, semkey="act16")


def emit_conv_in(P, C, w_in, bgT, cvT, d_out, st, bnd=None, d_bnd=None):
    it = 0
    si = 0
    for fc in range(16):
        s = fc % 2
        wv = C.wgu[s][:].rearrange("p a c n -> p (a c n)").rearrange("p (q c n) -> p q c n", q=4, c=16)
        for q in range(3):
            src = w_in[:, q * 2048 + fc * 128: q * 2048 + (fc + 1) * 128].rearrange("(c p) n -> p c n", p=128)
            P.dma(wv[:, q, :, :], src, writes=[C.d_wgu[s]], semkey=f"wgu{s}", eng="pool")
        for ti, (off, n) in enumerate(C.tiles):
            pb = (it % 2) * 3
            it += 1
            for q in range(3):
                pt = C.ps[pb + q]
                for k in range(16):
                    P.pe(lambda e, pt=pt, q=q, k=k, wv=wv, off=off, n=n: e.matmul(
                        pt[:, 0:n], lhsT=wv[:, q, k, :], rhs=C.hy[:, k, off:off + n],
                        start=(k == 0), stop=(k == 15)),
                        reads=[C.d_wgu[s], C.d_h[ti]], writes=[C.d_ps[pb + q]])
            sb, d_sb = st[si % len(st)]
            si += 1
            P.act(lambda e, sb=sb, pb=pb, n=n: e.activation(out=sb[:, 0:n], in_=C.ps[pb][:, 0:n], func=AF.Copy),
                  reads=[C.d_ps[pb]], writes=[d_sb])
            P.dma(bgT[fc * 128:(fc + 1) * 128, off:off + n], sb[:, 0:n], reads=[d_sb], writes=[d_out],
                  semkey="cvo")
            tmp, d_tmp = C.next_tmp()
            P.act(lambda e, tmp=tmp, pb=pb, n=n: e.activation(out=tmp[:, 0:n], in_=C.ps[pb + 1][:, 0:n],
                                                              func=AF.Copy),
                  reads=[C.d_ps[pb + 1]], writes=[d_tmp])
            sb2, d_sb2 = st[si % len(st)]
            si += 1
            P.dve(lambda e, sb2=sb2, tmp=tmp, pb=pb, n=n: e.tensor_tensor(
                out=sb2[:, 0:n], in0=tmp[:, 0:n], in1=C.ps[pb + 2][:, 0:n], op=ALU.mult),
                reads=[d_tmp, C.d_ps[pb + 2]], writes=[d_sb2])
            P.dma(cvT[fc * 128:(fc + 1) * 128, off:off + n], sb2[:, 0:n], reads=[d_sb2], writes=[d_out],
                  semkey="cvo")
            if bnd is not None and off == 0:
                P.act(lambda e, sb2=sb2, fc=fc: e.activation(out=bnd[:, 0, fc:fc + 1], in_=sb2[:, 0:1], func=AF.Copy),
                      reads=[d_sb2], writes=[d_bnd])
            if bnd is not None and off + n == C.NT:
                P.act(lambda e, sb2=sb2, fc=fc, n=n: e.activation(out=bnd[:, 1, fc:fc + 1], in_=sb2[:, n - 1:n],
                                                                  func=AF.Copy),
                      reads=[d_sb2], writes=[d_bnd])


def emit_conv(P, C, bgT, cvhT, d_in, cw, d_cw, st):
    si = 0
    for c in range(16):
        for ti, (off, n) in enumerate(C.tiles):
            i1 = si % len(st)
            si += 1
            i2 = si % len(st)
            si += 1
            cvt, d_cvt = st[i1]
            bt, d_bt = st[i2]
            P.dma(cvt[:, 0:n + 2], cvhT[c * 128:(c + 1) * 128, off:off + n + 2], reads=[d_in], writes=[d_cvt],
                  semkey=f"cvi{i1}")
            P.dma(bt[:, 0:n], bgT[c * 128:(c + 1) * 128, off:off + n], reads=[d_in], writes=[d_bt],
                  semkey=f"cvi{i2}")
            u, d_u = C.next_tmp()
            P.dve(lambda e, u=u, cvt=cvt, c=c, n=n: e.tensor_scalar(
                out=u[:, 0:n], in0=cvt[:, 0:n], scalar1=cw[:, c, 0:1], scalar2=None, op0=ALU.mult),
                reads=[d_cvt, d_cw], writes=[d_u])
            P.dve(lambda e, u=u, cvt=cvt, c=c, n=n: e.scalar_tensor_tensor(
                out=u[:, 0:n], in0=cvt[:, 1:n + 1], scalar=cw[:, c, 1:2], in1=u[:, 0:n], op0=ALU.mult, op1=ALU.add),
                reads=[d_cvt, d_cw, d_u], writes=[d_u])
            P.dve(lambda e, u=u, cvt=cvt, c=c, n=n: e.scalar_tensor_tensor(
                out=u[:, 0:n], in0=cvt[:, 2:n + 2], scalar=cw[:, c, 2:3], in1=u[:, 0:n], op0=ALU.mult, op1=ALU.add),
                reads=[d_cvt, d_cw, d_u], writes=[d_u])
            P.dve(lambda e, u=u, bt=bt, c=c, off=off, n=n: e.tensor_tensor(
                out=C.A[:, c, off:off + n], in0=u[:, 0:n], in1=bt[:, 0:n], op=ALU.mult),
                reads=[d_u, d_bt], writes=[C.d_A[ti]] + C.d_xt)


def emit_load_mg_sel(P, C, mg_g, d_src, selv, d_sel):
    for hc in range(2):
        for c in range(16):
            kind, rk, hd = c // 8, (c // 4) % 2, c % 4
            k = kind * 2 + hd // 2
            r0 = rk * 256 + (hd % 2) * 128
            P.dma(C.A[:, hc * 16 + c, :], mg_g[k][r0:r0 + 128, hc * 1024:(hc + 1) * 1024],
                  reads=[d_src], writes=C.d_A + C.d_xt, semkey="act16")
    for c in range(16):
        P.dve(lambda e, c=c: e.tensor_scalar(out=C.A[:, c, :], in0=C.A[:, c, :], scalar1=selv[:, 0:1], scalar2=None,
                                             op0=ALU.mult), reads=C.d_A + [d_sel], writes=C.d_A)
        P.dve(lambda e, c=c: e.scalar_tensor_tensor(out=C.A[:, c, :], in0=C.A[:, 16 + c, :], scalar=selv[:, 1:2],
                                                    in1=C.A[:, c, :], op0=ALU.mult, op1=ALU.add),
              reads=C.d_A + [d_sel], writes=C.d_A)


def emit_conv_halo(P, C, bgT, cvT, d_in, bnd_g, d_bnd, selv, d_sel, cw, d_cw, st, hal, d_hal):
    P.dma(hal[:, 0, :], bnd_g[0:128, 16:32], reads=[d_bnd], writes=[d_hal], semkey="hal")
    P.dma(hal[:, 1, :], bnd_g[128:256, 0:16], reads=[d_bnd], writes=[d_hal], semkey="hal")
    P.dve(lambda e: e.tensor_scalar(out=hal[:, 0, :], in0=hal[:, 0, :], scalar1=selv[:, 2:3], scalar2=None,
                                    op0=ALU.mult), reads=[d_hal, d_sel], writes=[d_hal])
    P.dve(lambda e: e.tensor_scalar(out=hal[:, 1, :], in0=hal[:, 1, :], scalar1=selv[:, 3:4], scalar2=None,
                                    op0=ALU.mult), reads=[d_hal, d_sel], writes=[d_hal])
    si = 0
    NT = C.NT
    for c in range(16):
        for ti, (off, n) in enumerate(C.tiles):
            i1 = si % len(st)
            si += 1
            i2 = si % len(st)
            si += 1
            cvt, d_cvt = st[i1]
            bt, d_bt = st[i2]
            lo = max(off - 1, 0)
            hi = min(off + n + 1, NT)
            dlo = lo - (off - 1)
            P.dma(cvt[:, dlo:dlo + (hi - lo)], cvT[c * 128:(c + 1) * 128, lo:hi], reads=[d_in], writes=[d_cvt],
                  semkey=f"cvi{i1}")
            if off == 0:
                P.dve(lambda e, cvt=cvt, c=c: e.tensor_copy(out=cvt[:, 0:1], in_=hal[:, 0, c:c + 1]),
                      reads=[d_hal], writes=[d_cvt])
            if off + n == NT:
                P.dve(lambda e, cvt=cvt, c=c, n=n: e.tensor_copy(out=cvt[:, n + 1:n + 2], in_=hal[:, 1, c:c + 1]),
                      reads=[d_hal], writes=[d_cvt])
            P.dma(bt[:, 0:n], bgT[c * 128:(c + 1) * 128, off:off + n], reads=[d_in], writes=[d_bt],
                  semkey=f"cvi{i2}")
            u, d_u = C.next_tmp()
            P.dve(lambda e, u=u, cvt=cvt, c=c, n=n: e.tensor_scalar(
                out=u[:, 0:n], in0=cvt[:, 0:n], scalar1=cw[:, c, 0:1], scalar2=None, op0=ALU.mult),
                reads=[d_cvt, d_cw], writes=[d_u])
            P.dve(lambda e, u=u, cvt=cvt, c=c, n=n: e.scalar_tensor_tensor(
                out=u[:, 0:n], in0=cvt[:, 1:n + 1], scalar=cw[:, c, 1:2], in1=u[:, 0:n], op0=ALU.mult, op1=ALU.add),
                reads=[d_cvt, d_cw, d_u], writes=[d_u])
            P.dve(lambda e, u=u, cvt=cvt, c=c, n=n: e.scalar_tensor_tensor(
                out=u[:, 0:n], in0=cvt[:, 2:n + 2], scalar=cw[:, c, 2:3], in1=u[:, 0:n], op0=ALU.mult, op1=ALU.add),
                reads=[d_cvt, d_cw, d_u], writes=[d_u])
            P.dve(lambda e, u=u, bt=bt, c=c, off=off, n=n: e.tensor_tensor(
                out=C.A[:, c, off:off + n], in0=u[:, 0:n], in1=bt[:, 0:n], op=ALU.mult),
                reads=[d_u, d_bt], writes=[C.d_A[ti]] + C.d_xt)


BF = ml_dtypes.bfloat16
NCORES = 8


def _ffn_w(nc, tag):
    wg = nc.dram_tensor("wg" + tag, [2048, 5504], F32, kind="ExternalInput")
    wu = nc.dram_tensor("wu" + tag, [2048, 5504], F32, kind="ExternalInput")
    wd = nc.dram_tensor("wd" + tag, [5504, 2048], F32, kind="ExternalInput")
    return wg, wu, wd


def build_A():
    nc = bass.Bass("TRN2", target_bir_lowering=False)
    P = Prog(nc)
    xT = nc.dram_tensor("xT", [2048, 1024], F32, kind="ExternalInput")
    ctxT = nc.dram_tensor("ctxT", [2048, 128], F32, kind="ExternalInput")
    cvecT = nc.dram_tensor("cvecT", [128, 16, 2], F32, kind="ExternalInput")
    ada_w = nc.dram_tensor("ada_w", [2048, 18432], F32, kind="ExternalInput")
    ada_bT = nc.dram_tensor("ada_bT", [128, 144], F32, kind="ExternalInput")
    gTin = nc.dram_tensor("gTin", [128, 6, 16], F32, kind="ExternalInput")
    wg, wu, wd = _ffn_w(nc, "")
    x1T = nc.dram_tensor("x1T", [2048, 1024], F32, kind="ExternalOutput")
    hT = nc.dram_tensor("hT", [2048, 1152], BF16, kind="ExternalOutput")
    mTo = nc.dram_tensor("mTo", [128, 144, 2], F32, kind="ExternalOutput")
    xc1T = nc.dram_tensor("xc1T", [2048, 128], F32, kind="Internal")
    C = Ctx(P, [512, 512, 128])
    scr = {"cv": P.sbuf("cv", [128, 16, 2], F32), "sc": P.sbuf("sc", [128, 16, 2], BF16),
           "bT": P.sbuf("bT", [128, 144], F32)}
    P.dma(C.gT[:], gTin[:], writes=[C.d_gT], semkey="small3")
    emit_modulation(P, C, ada_w, ada_bT[:], cvecT[:], scr)
    d_in = Dep("in")
    d_x1 = [Dep("x1a"), Dep("x1b"), Dep("xc1")]
    emit_coefs(P, C, 1, 2, 0, 1, 0.5, [0, 1])
    srcs = [(xT[:, 0:512], d_in, 0), (xT[:, 512:1024], d_in, 0), (ctxT[:, :], d_in, 1)]
    dsts = [(x1T[:, 0:512], d_x1[0], 0), (x1T[:, 512:1024], d_x1[1], 0), (xc1T[:, :], d_x1[2], 1)]
    emit_ffn(P, C, srcs, dsts, wg, wu, wd, 0)
    emit_coefs(P, C, 4, None, 2, None, 1.0, [0, 1])
    emit_prenorm(P, C, dsts, 3, lambda ti: C.hy[:, :, C.tiles[ti][0]:C.tiles[ti][0] + C.tiles[ti][1]],
                 C.d_h, C.d_A)
    d_hT = Dep("hT")
    for ti, (off, n) in enumerate(C.tiles):
        P.dma(hT[:, off:off + n].rearrange("(c p) t -> p c t", p=128), C.hy[:, :, off:off + n],
              reads=[C.d_h[ti]], writes=[d_hT], semkey="hT")
    P.dma(mTo[:], C.mT[:], reads=[C.d_mT], writes=[Dep("mTo")], semkey="mTo")
    P.emit()
    return nc


def build_B():
    nc = bass.Bass("TRN2", target_bir_lowering=False)
    P = Prog(nc)
    hTf = nc.dram_tensor("hTf", [2048, NTOK], BF16, kind="ExternalInput")
    w_att = nc.dram_tensor("w_att", [4, 3, 2048, 128], F32, kind="ExternalInput")
    w_rec = nc.dram_tensor("w_rec", [4, 5, 2048, 128], F32, kind="ExternalInput")
    ropeC = nc.dram_tensor("ropeCin", [128, 2048], F32, kind="ExternalInput")
    ropeS = nc.dram_tensor("ropeSin", [128, 2048], F32, kind="ExternalInput")
    perm = nc.dram_tensor("permin", [128, 128], F32, kind="ExternalInput")
    ident = nc.dram_tensor("identin", [128, 128], F32, kind="ExternalInput")
    maskf = nc.dram_tensor("maskfin", [128, 128], F32, kind="ExternalInput")
    maskb = nc.dram_tensor("maskbin", [128, 128], F32, kind="ExternalInput")
    lamT = nc.dram_tensor("lamT", [64, 4], F32, kind="ExternalInput")
    dng = nc.dram_tensor("dng", [128, 1], F32, kind="ExternalInput")
    rng = nc.dram_tensor("rng", [128, 1], F32, kind="ExternalInput")
    lbraw = nc.dram_tensor("lbraw", [128, 2, 2, 4], F32, kind="ExternalInput")
    mergedT = nc.dram_tensor("mergedT", [1024, 2048], BF16, kind="ExternalOutput")
    M = MixCtx(P)
    emit_mix_setup(P, M, hTf, ropeC[:], ropeS[:], perm[:], ident[:], maskf[:], maskb[:], lamT[:], dng[:], rng[:],
                   lbraw[:])
    d_merged = Dep("merged")
    for hd in range(4):
        for i in range(3):
            load_w(P, M, i, w_att[hd, i])
        for i in range(5):
            load_w(P, M, 3 + i, w_rec[hd, i])
        emit_attention_head(P, M, hd, mergedT, d_merged)
        emit_rec_head(P, M, hd, mergedT, d_merged)
    P.emit()
    return nc


def build_C():
    nc = bass.Bass("TRN2", target_bir_lowering=False)
    P = Prog(nc)
    x1T = nc.dram_tensor("x1T", [2048, 1024], F32, kind="ExternalInput")
    mgT = nc.dram_tensor("mgT", [2048, 1024], BF16, kind="ExternalInput")
    w_out = nc.dram_tensor("w_out", [2048, 2048], F32, kind="ExternalInput")
    mT0 = nc.dram_tensor("mT0", [128, 144, 2], F32, kind="ExternalInput")
    gT0 = nc.dram_tensor("gT0", [128, 6, 16], F32, kind="ExternalInput")
    gT1 = nc.dram_tensor("gT1", [128, 6, 16], F32, kind="ExternalInput")
    cvecT = nc.dram_tensor("cvecT", [128, 16, 2], F32, kind="ExternalInput")
    ada_w = nc.dram_tensor("ada_w", [2048, 18432], F32, kind="ExternalInput")
    ada_bT = nc.dram_tensor("ada_bT", [128, 144], F32, kind="ExternalInput")
    wgA, wuA, wdA = _ffn_w(nc, "A")
    wgB, wuB, wdB = _ffn_w(nc, "B")
    cw_in = nc.dram_tensor("cw_in", [2048, 6144], F32, kind="ExternalInput")
    x4T = nc.dram_tensor("x4T", [2048, 1024], F32, kind="ExternalOutput")
    bgT = nc.dram_tensor("bgT", [2048, 1024], F32, kind="ExternalOutput")
    cvT = nc.dram_tensor("cvT", [2048, 1024], F32, kind="ExternalOutput")
    mT1 = nc.dram_tensor("mT1", [128, 144, 2], F32, kind="ExternalOutput")
    x2T = nc.dram_tensor("x2T", [2048, 1024], F32, kind="Internal")
    x3T = nc.dram_tensor("x3T", [2048, 1024], F32, kind="Internal")
    C = Ctx(P, [512, 512])
    scr = {"cv": P.sbuf("cv", [128, 16, 2], F32), "sc": P.sbuf("sc", [128, 16, 2], BF16),
           "bT": P.sbuf("bT", [128, 144], F32)}
    st = [(P.sbuf(f"st{i}", [128, 512], F32), Dep(f"st{i}")) for i in range(4)]
    d_in = Dep("in")

    def tl(t, d):
        return [(t[:, 0:512], d[0], 0), (t[:, 512:1024], d[1], 0)]
    d_x1 = [d_in, d_in]
    d_x2 = [Dep("x2a"), Dep("x2b")]
    d_x3 = [Dep("x3a"), Dep("x3b")]
    d_x4 = [Dep("x4a"), Dep("x4b")]
    P.dma(C.gT[:], gT0[:], writes=[C.d_gT], semkey="small3")
    P.dma(C.mT[:], mT0[:], writes=[C.d_mT], semkey="small4")
    emit_load_act16(P, C, mgT, d_in)
    emit_coefs(P, C, 4, 5, 2, 3, 1.0, [0])
    emit_down_residual(P, C, 16, w_out, tl(x1T, d_x1), tl(x2T, d_x2))
    emit_coefs(P, C, 7, 8, 4, 5, 0.5, [0])
    emit_ffn(P, C, tl(x2T, d_x2), tl(x3T, d_x3), wgA, wuA, wdA, 6)
    P.dma(C.gT[:], gT1[:], writes=[C.d_gT], semkey="small3")
    emit_modulation(P, C, ada_w, ada_bT[:], cvecT[:], scr)
    emit_coefs(P, C, 1, 2, 0, 1, 0.5, [0])
    emit_ffn(P, C, tl(x3T, d_x3), tl(x4T, d_x4), wgB, wuB, wdB, 0)
    emit_coefs(P, C, 4, None, 2, None, 1.0, [0])
    emit_prenorm(P, C, tl(x4T, d_x4), 3, lambda ti: C.hy[:, :, C.tiles[ti][0]:C.tiles[ti][0] + C.tiles[ti][1]],
                 C.d_h, C.d_A)
    emit_conv_in(P, C, cw_in, bgT, cvT, Dep("cvout"), st)
    P.dma(mT1[:], C.mT[:], reads=[C.d_mT], writes=[Dep("mT1o")], semkey="mTo")
    P.emit()
    return nc


def build_D():
    nc = bass.Bass("TRN2", target_bir_lowering=False)
    P = Prog(nc)
    x4T = nc.dram_tensor("x4T", [2048, 1024], F32, kind="ExternalInput")
    bgT = nc.dram_tensor("bgT", [2048, 1024], F32, kind="ExternalInput")
    cvhT = nc.dram_tensor("cvhT", [2048, 1026], F32, kind="ExternalInput")
    cwT = nc.dram_tensor("cwT", [128, 16, 3], F32, kind="ExternalInput")
    cw_out = nc.dram_tensor("cw_out", [2048, 2048], F32, kind="ExternalInput")
    mT1 = nc.dram_tensor("mT1", [128, 144, 2], F32, kind="ExternalInput")
    gT1 = nc.dram_tensor("gT1", [128, 6, 16], F32, kind="ExternalInput")
    wg, wu, wd = _ffn_w(nc, "")
    outT = nc.dram_tensor("outT", [2048, 1024], F32, kind="ExternalOutput")
    x5T = nc.dram_tensor("x5T", [2048, 1024], F32, kind="Internal")
    C = Ctx(P, [512, 512])
    st = [(P.sbuf(f"st{i}", [128, 514], F32), Dep(f"st{i}")) for i in range(4)]
    cw = P.sbuf("cw", [128, 16, 3], F32)
    d_cw = Dep("cw")
    d_in = Dep("in")

    def tl(t, d):
        return [(t[:, 0:512], d[0], 0), (t[:, 512:1024], d[1], 0)]
    d_x5 = [Dep("x5a"), Dep("x5b")]
    d_o = [Dep("oa"), Dep("ob")]
    P.dma(C.gT[:], gT1[:], writes=[C.d_gT], semkey="small3")
    P.dma(C.mT[:], mT1[:], writes=[C.d_mT], semkey="small4")
    P.dma(cw[:], cwT[:], writes=[d_cw], semkey="small5")
    emit_conv(P, C, bgT, cvhT, d_in, cw, d_cw, st)
    emit_coefs(P, C, 4, 5, 2, 3, 1.0, [0])
    emit_down_residual(P, C, 16, cw_out, tl(x4T, [d_in, d_in]), tl(x5T, d_x5))
    emit_coefs(P, C, 7, 8, 4, 5, 0.5, [0])
    emit_ffn(P, C, tl(x5T, d_x5), tl(outT, d_o), wg, wu, wd, 6)
    P.emit()
    return nc


ARENA_BYTES = 207 * 1024
FUSE_STOP = int(os.environ.get("FUSE_STOP", "0"))


def build_fused():
    nc = bass.Bass("TRN2", target_bir_lowering=False)
    P = Prog(nc)
    P.use_arena(ARENA_BYTES)
    inp = lambda n, sh, dt=F32: nc.dram_tensor(n, sh, dt, kind="ExternalInput")
    xT = inp("xT", [2048, 1024])
    ctxT = inp("ctxT", [2048, 128])
    ada_sl = inp("ada_sl", [2, 2048, 9216])
    cvecT = inp("cvecT", [128, 16, 2])
    bT0 = inp("bT0", [128, 144])
    bT1 = inp("bT1", [128, 144])
    gT0 = inp("gT0", [128, 6, 16])
    gT1 = inp("gT1", [128, 6, 16])
    W = [_ffn_w(nc, str(i)) for i in range(4)]
    w_att = inp("w_att", [4, 3, 2048, 128])
    w_rec = inp("w_rec", [4, 5, 2048, 128])
    ropeC = inp("ropeCin", [128, 2048])
    ropeS = inp("ropeSin", [128, 2048])
    perm = inp("permin", [128, 128])
    ident = inp("identin", [128, 128])
    maskf = inp("maskfin", [128, 128])
    maskb = inp("maskbin", [128, 128])
    lamT = inp("lamT", [64, 4])
    dng = inp("dng", [128, 1])
    rng = inp("rng", [128, 1])
    lbraw = inp("lbraw", [128, 2, 2, 4])
    w_out = inp("w_out", [2048, 2048])
    cw_in = inp("cw_in", [2048, 6144])
    cwT = inp("cwT", [128, 16, 3])
    cw_out = inp("cw_out", [2048, 2048])
    selin = inp("selin", [128, 4])
    outT = nc.dram_tensor("outT", [2048, 1024], F32, kind="ExternalOutput")
    itn = lambda n, sh, dt=F32: nc.dram_tensor(n, sh, dt, kind="Internal")
    x1T, x2T, x3T, x4T, x5T = [itn(f"x{i}T", [2048, 1024]) for i in (1, 2, 3, 4, 5)]
    xc1T = itn("xc1T", [2048, 128])
    hT_own = [itn(f"hT_own{q}", [512, 1152], BF16) for q in range(4)]
    hT_g = [itn(f"hT_g{q}", [1024, 1152], BF16) for q in range(4)]
    mg_own = [itn(f"mg_own{q}", [256, 2048], BF16) for q in range(4)]
    mg_g = [itn(f"mg_g{q}", [512, 2048], BF16) for q in range(4)]
    bgT = itn("bgT", [2048, 1024])
    cvT = itn("cvT", [2048, 1024])
    bnd_own = itn("bnd_own", [128, 32])
    bnd_g = itn("bnd_g", [256, 32])
    d_in = Dep("in")

    def tl(t, d):
        return [(t[:, 0:512], d[0], 0), (t[:, 512:1024], d[1], 0)]

    def mk_scr():
        return {"cv": P.sbuf("cv", [128, 16, 2], F32), "sc": P.sbuf("sc", [128, 16, 2], BF16),
                "bT": P.sbuf("bT", [128, 144], F32)}
    hyv = lambda C: (lambda ti: C.hy[:, :, C.tiles[ti][0]:C.tiles[ti][0] + C.tiles[ti][1]])

    mp_own = itn("mp_own", [128, 288])
    mp_g = itn("mp_g", [256, 288])
    mTd = [itn("mT0d", [128, 144, 2]), itn("mT1d", [128, 144, 2])]
    d_mTd = [Dep("mT0d"), Dep("mT1d")]
    C = Ctx(P, [512, 512])
    emit_modulation_sharded(P, C, ada_sl, cvecT[:], bT0[:], bT1[:], mp_own, mp_g, mTd, d_mTd)
    P.barrier()
    P.aoff = 0
    C = Ctx(P, [512, 512, 128])
    P.dma(C.gT[:], gT0[:], writes=[C.d_gT], semkey="small3")
    P.dma(C.mT[:], mTd[0][:], reads=[d_mTd[0]], writes=[C.d_mT], semkey="small4")
    d_x1 = [Dep("x1a"), Dep("x1b"), Dep("xc1")]
    emit_coefs(P, C, 1, 2, 0, 1, 0.5, [0, 1])
    srcs = [(xT[:, 0:512], d_in, 0), (xT[:, 512:1024], d_in, 0), (ctxT[:, :], d_in, 1)]
    dsts = [(x1T[:, 0:512], d_x1[0], 0), (x1T[:, 512:1024], d_x1[1], 0), (xc1T[:, :], d_x1[2], 1)]
    emit_ffn(P, C, srcs, dsts, W[0][0], W[0][1], W[0][2], 0)
    emit_coefs(P, C, 4, None, 2, None, 1.0, [0, 1])
    emit_prenorm(P, C, dsts, 3, hyv(C), C.d_h, C.d_A, resident=True)
    d_hT = Dep("hT")
    for q in range(4):
        P.dma(hT_own[q][:, :].rearrange("(c p) t -> p c t", p=128), C.hy[:, q * 4:(q + 1) * 4, :],
              reads=C.d_h, writes=[d_hT], semkey="hT")
    d_hg = Dep("hT_g")
    for q in range(4):
        P.allgather_pairs(hT_g[q], hT_own[q], reads=[d_hT], writes=[d_hg], semkey="cc1")
    if FUSE_STOP == 1:
        dbg = nc.dram_tensor("dbg", [4096, 1152], BF16, kind="ExternalOutput")
        for q in range(4):
            for r in range(2):
                P.dma(dbg[r * 2048 + q * 512: r * 2048 + (q + 1) * 512, :], hT_g[q][r * 512:(r + 1) * 512, :],
                      reads=[d_hg], writes=[Dep("dbg")], semkey="dbg")
        P.emit()
        return nc
    P.barrier()
    P.aoff = 0
    M = MixCtx(P)
    for r in range(2):
        for q in range(4):
            rows = hT_g[q][r * 512:(r + 1) * 512, :]
            P.dma(M.hs[:, q * 4:(q + 1) * 4, r * 128:(r + 1) * 128],
                  rows[:, 1024:1152].rearrange("(c p) t -> p c t", p=128), reads=[d_hg], writes=[M.d_hs],
                  semkey="hs")
            P.dma(M.hs[:, q * 4:(q + 1) * 4, 256 + r * 1024:256 + (r + 1) * 1024],
                  rows[:, 0:1024].rearrange("(c p) t -> p c t", p=128), reads=[d_hg], writes=[M.d_hs],
                  semkey="hs")
    emit_mix_setup(P, M, None, ropeC[:], ropeS[:], perm[:], ident[:], maskf[:], maskb[:], lamT[:], dng[:], rng[:],
                   lbraw[:])
    d_mg = Dep("mg_own")
    d_mgg = Dep("mg_g")
    for i in range(3):
        load_w(P, M, i, w_att[0, i])
    for i in range(5):
        load_w(P, M, 3 + i, w_rec[0, i])
    for hd in range(4):
        mgdst = lambda kind, h: mg_own[(0 if kind == "att" else 2) + h // 2][(h % 2) * 128:(h % 2) * 128 + 128, :]
        emit_attention_head(P, M, hd, mgdst, d_mg)
        P.barrier()

        def prefetch(hd=hd):
            if hd + 1 < 4:
                for i in range(3):
                    load_w(P, M, i, w_att[hd + 1, i])
                for i in range(5):
                    load_w(P, M, 3 + i, w_rec[hd + 1, i])
        emit_rec_head(P, M, hd, mgdst, d_mg, after_burst=prefetch)
        P.barrier()
        if hd % 2 == 1:
            for q in (hd // 2, 2 + hd // 2):
                P.allgather_pairs(mg_g[q], mg_own[q], reads=[d_mg], writes=[d_mgg], semkey="cc2")
    if FUSE_STOP == 2:
        dbg = nc.dram_tensor("dbg", [2048, 2048], BF16, kind="ExternalOutput")
        for q in range(4):
            for r in range(2):
                base = r * 1024 + (q // 2) * 512 + (q % 2) * 256
                P.dma(dbg[base:base + 256, :], mg_g[q][r * 256:(r + 1) * 256, :], reads=[d_mgg],
                      writes=[Dep("dbg")], semkey="dbg")
        P.emit()
        return nc
    P.barrier()
    P.aoff = 0
    C = Ctx(P, [512, 512])
    selv = P.sbuf("selv", [128, 4], F32)
    d_sel = Dep("selv")
    st = [(P.sbuf(f"st{i}", [128, 514], F32), Dep(f"st{i}")) for i in range(4)]
    cw = P.sbuf("cw", [128, 16, 3], F32)
    d_cw = Dep("cw")
    hal = P.sbuf("hal", [128, 2, 16], F32)
    d_hal = Dep("hal")
    P.dma(selv[:], selin[:], writes=[d_sel], semkey="small5")
    P.dma(cw[:], cwT[:], writes=[d_cw], semkey="small5")
    P.dma(C.gT[:], gT0[:], writes=[C.d_gT], semkey="small3")
    P.dma(C.mT[:], mTd[0][:], reads=[d_mTd[0]], writes=[C.d_mT], semkey="small4")
    d_x2 = [Dep("x2a"), Dep("x2b")]
    d_x3 = [Dep("x3a"), Dep("x3b")]
    d_x4 = [Dep("x4a"), Dep("x4b")]
    d_x5 = [Dep("x5a"), Dep("x5b")]
    d_o = [Dep("oa"), Dep("ob")]
    emit_load_mg_sel(P, C, mg_g, d_mgg, selv, d_sel)
    emit_coefs(P, C, 4, 5, 2, 3, 1.0, [0])
    emit_down_residual(P, C, 16, w_out, tl(x1T, d_x1), tl(x2T, d_x2))
    emit_coefs(P, C, 7, 8, 4, 5, 0.5, [0])
    emit_ffn(P, C, tl(x2T, d_x2), tl(x3T, d_x3), W[1][0], W[1][1], W[1][2], 6, resident=True)
    P.dma(C.gT[:], gT1[:], writes=[C.d_gT], semkey="small3")
    P.dma(C.mT[:], mTd[1][:], reads=[d_mTd[1]], writes=[C.d_mT], semkey="small4")
    emit_coefs(P, C, 1, 2, 0, 1, 0.5, [0])
    emit_ffn(P, C, tl(x3T, d_x3), tl(x4T, d_x4), W[2][0], W[2][1], W[2][2], 0, resident=True)
    emit_coefs(P, C, 4, None, 2, None, 1.0, [0])
    emit_prenorm(P, C, tl(x4T, d_x4), 3, hyv(C), C.d_h, C.d_A, resident=True)
    d_cv = Dep("cvout")
    st512 = [(t[:, 0:512], d) for (t, d) in st]
    bnd_sb = P.sbuf("bnd_sb", [128, 2, 16], F32)
    d_bsb = Dep("bnd_sb")
    emit_conv_in(P, C, cw_in, bgT, cvT, d_cv, st512, bnd_sb, d_bsb)
    d_bo = Dep("bnd_own")
    P.dma(bnd_own[:, :], bnd_sb[:].rearrange("p a c -> p (a c)"), reads=[d_bsb], writes=[d_bo], semkey="bnd")
    d_bg = Dep("bnd_g")
    P.allgather_pairs(bnd_g, bnd_own, reads=[d_bo], writes=[d_bg], semkey="cc3")
    emit_conv_halo(P, C, bgT, cvT, d_cv, bnd_g, d_bg, selv, d_sel, cw, d_cw, st, hal, d_hal)
    emit_coefs(P, C, 4, 5, 2, 3, 1.0, [0])
    emit_down_residual(P, C, 16, cw_out, tl(x4T, d_x4), tl(x5T, d_x5))
    emit_coefs(P, C, 7, 8, 4, 5, 0.5, [0])
    emit_ffn(P, C, tl(x5T, d_x5), tl(outT, d_o), W[3][0], W[3][1], W[3][2], 6, resident=True)
    stuck = simulate_sync(P)
    if stuck:
        raise RuntimeError(f"sync deadlock: {stuck}")
    P.emit()
    return nc


def mix_consts():
    p = np.arange(128)
    d = p % 64
    i = d % 16
    freqs = (10000.0 ** (-np.arange(16, dtype=np.float32) / 16)).astype(np.float32)
    t = np.arange(2048)
    row = (t // 64).astype(np.float32)
    col = (t % 64).astype(np.float32)
    pos = np.where((d < 32)[:, None], row[None, :], col[None, :]).astype(np.float32)
    ang = (pos * freqs[i][:, None]).astype(np.float32)
    C = np.cos(ang).astype(np.float32)
    S = np.sin(ang).astype(np.float32)
    perm = np.zeros((128, 128), np.float32)
    for m in range(128):
        if (m % 32) < 16:
            perm[m + 16, m] = -1.0
        else:
            perm[m - 16, m] = 1.0
    ident = np.eye(128, dtype=np.float32)
    s = np.arange(128)[:, None]
    tt = np.arange(128)[None, :]
    return C, S, perm, ident, (s <= tt).astype(np.float32), (s >= tt).astype(np.float32)


_PROGS = {}


def _prog(name, fn):
    if name not in _PROGS:
        _PROGS[name] = fn()
    return _PROGS[name]


def _run(nc, in_maps):
    res = run_bass_kernel_spmd(nc, in_maps, core_ids=list(range(NCORES)))
    return res.results


def kernel(x, c, ctx, c_ctx, ada_w, ada_b, norm_g, ffn_w_gate, ffn_w_up, ffn_w_down, mix_w_in, mix_w_out,
           diff_lambda, diff_norm_g, rec_norm_g, rec_lb, conv_w_in, conv_w, conv_w_out):
    f32 = lambda a: np.ascontiguousarray(np.asarray(a, dtype=np.float32))
    x, c, ctx, c_ctx = f32(x), f32(c), f32(ctx), f32(c_ctx)
    ada_w, ada_b, norm_g = f32(ada_w), f32(ada_b), f32(norm_g)
    ffn_w_gate, ffn_w_up, ffn_w_down = f32(ffn_w_gate), f32(ffn_w_up), f32(ffn_w_down)
    mix_w_in, mix_w_out = f32(mix_w_in), f32(mix_w_out)
    conv_w_in, conv_w, conv_w_out = f32(conv_w_in), f32(conv_w), f32(conv_w_out)
    rec_lb = f32(rec_lb)
    gT = [np.ascontiguousarray(norm_g[l].reshape(6, 16, 128).transpose(2, 0, 1)) for l in range(2)]
    bT = [np.ascontiguousarray(ada_b[l].reshape(144, 128).T) for l in range(2)]
    cvec = [np.ascontiguousarray(np.stack([c[b], c_ctx], -1).reshape(16, 128, 2).transpose(1, 0, 2))
            for b in range(4)]
    Cc, Ss, perm, ident, maskf, maskb = mix_consts()
    w_in = mix_w_in[0]
    cwT = np.ascontiguousarray(conv_w[0].reshape(3, 16, 128).transpose(2, 1, 0))
    lamT = np.ascontiguousarray(f32(diff_lambda)[0].T)
    dng = f32(diff_norm_g)[0].reshape(128, 1).copy()
    rngv = f32(rec_norm_g)[0].reshape(128, 1).copy()
    heads = []
    for hh in range(2):
        w_att = np.empty((4, 3, 2048, 128), np.float32)
        w_rec = np.empty((4, 5, 2048, 128), np.float32)
        lbraw = np.empty((128, 2, 2, 4), np.float32)
        for hd in range(4):
            g = hh * 4 + hd
            for q in range(3):
                w_att[hd, q] = w_in[:, q * 1024 + g * 128: q * 1024 + (g + 1) * 128]
            for q in range(5):
                w_rec[hd, q] = w_in[:, 3072 + q * 1024 + g * 128: 3072 + q * 1024 + (g + 1) * 128]
            lbraw[:, :, :, hd] = rec_lb[:, :, g * 128:(g + 1) * 128].transpose(2, 0, 1)
        heads.append((w_att, w_rec, lbraw))
    ada_sl = [np.ascontiguousarray(ada_w[:, :, hh * 9216:(hh + 1) * 9216]) for hh in range(2)]
    in_maps = []
    for i in range(NCORES):
        b, h = i // 2, i % 2
        sel = np.zeros((128, 4), np.float32)
        sel[:, 0] = 1.0 if h == 0 else 0.0
        sel[:, 1] = 1.0 if h == 1 else 0.0
        sel[:, 2] = 1.0 if h == 1 else 0.0
        sel[:, 3] = 1.0 if h == 0 else 0.0
        m = {
            "xT": np.ascontiguousarray(x[b, h * 1024:(h + 1) * 1024].T),
            "ctxT": np.ascontiguousarray(ctx[b, h * 128:(h + 1) * 128].T),
            "ada_sl": ada_sl[h], "cvecT": cvec[b],
            "bT0": bT[0], "bT1": bT[1],
            "gT0": gT[0], "gT1": gT[1],
            "w_att": heads[h][0], "w_rec": heads[h][1], "lbraw": heads[h][2],
            "ropeCin": Cc, "ropeSin": Ss, "permin": perm, "identin": ident, "maskfin": maskf, "maskbin": maskb,
            "lamT": lamT, "dng": dng, "rng": rngv,
            "w_out": mix_w_out[0], "cw_in": conv_w_in[0], "cwT": cwT, "cw_out": conv_w_out[0], "selin": sel}
        for k, (l, j) in enumerate([(0, 0), (0, 1), (1, 0), (1, 1)]):
            m[f"wg{k}"] = ffn_w_gate[l, j]
            m[f"wu{k}"] = ffn_w_up[l, j]
            m[f"wd{k}"] = ffn_w_down[l, j]
        in_maps.append(m)
    rD = _run(_prog("F", build_fused), in_maps)
    if FUSE_STOP:
        return rD
    out = np.empty((4, 2048, 2048), np.float32)
    for i in range(NCORES):
        b, h = i // 2, i % 2
        out[b, h * 1024:(h + 1) * 1024] = np.asarray(rD[i]["outT"]).T
    return out
```

```python
import contextlib
import types
import os
import math
import numpy as np
import ml_dtypes
import concourse.bass as bass
import concourse.mybir as mybir
from concourse.bass_utils import run_bass_kernel_spmd


F32 = mybir.dt.float32
BF16 = mybir.dt.bfloat16
ALU = mybir.AluOpType
AF = mybir.ActivationFunctionType
AX = mybir.AxisListType


class Dep:
    __slots__ = ("name", "w", "r", "excl")

    def __init__(self, name, excl=False):
        self.name = name
        self.w = {}
        self.r = {}
        self.excl = excl


class Ins:
    __slots__ = ("eng", "fn", "deps", "signal", "semkey", "semval", "is_dma", "idx", "inc")
    _n = 0

    def __init__(self, eng, fn, is_dma=False, semkey=None):
        self.eng = eng
        self.fn = fn
        self.deps = []
        self.signal = False
        self.is_dma = is_dma
        self.semkey = semkey
        self.semval = None
        self.inc = 16
        Ins._n += 1
        self.idx = Ins._n


def _freeze(fn):
    if getattr(fn, "__closure__", None) is None:
        return fn
    cells = []
    for c in fn.__closure__:
        try:
            cells.append(types.CellType(c.cell_contents))
        except ValueError:
            cells.append(c)
    return types.FunctionType(fn.__code__, fn.__globals__, fn.__name__, fn.__defaults__, tuple(cells))


class Prog:
    ENGS = ("pe", "act", "dve", "pool", "sp")

    def __init__(self, nc):
        self.nc = nc
        self.streams = {e: [] for e in self.ENGS}
        self.stack = contextlib.ExitStack()
        self.dma_keys = {}
        self.n_sb = 0
        self.arena = None
        self.aoff = 0
        self.pending = {e: [] for e in self.ENGS}
        self.open_dmas = []
        self.ps = None
        self.d_ps = None

    def use_arena(self, nbytes):
        self.arena = self.stack.enter_context(self.nc.sbuf_tensor("arena", [128, nbytes], mybir.dt.uint8))
        self.asize = nbytes
        self.aoff = 0

    def shared_psum(self):
        if self.ps is None:
            self.ps = [self.psum(f"ps{i}", [128, 512]) for i in range(8)]
            self.d_ps = [Dep(f"ps{i}", excl=True) for i in range(8)]
        return self.ps, self.d_ps

    def barrier(self):
        lasts = []
        for e in ("pe", "act", "dve", "pool"):
            for ins in reversed(self.streams[e]):
                if not ins.is_dma:
                    lasts.append(ins)
                    break
        lasts += self.open_dmas
        self.open_dmas = []
        for d in lasts:
            d.signal = True
        for e in self.ENGS:
            self.pending[e] = list(lasts)

    def sbuf(self, name, shape, dtype):
        if self.arena is None:
            return self.stack.enter_context(self.nc.sbuf_tensor(name, list(shape), dtype))
        esz = {F32: 4, BF16: 2}[dtype]
        nel = 1
        for s_ in shape[1:]:
            nel *= s_
        nbytes = nel * esz
        off = (self.aoff + 63) // 64 * 64
        if off + nbytes > self.asize:
            raise MemoryError(f"SBUF arena overflow allocating {name}: {off}+{nbytes} > {self.asize}")
        self.aoff = off + nbytes
        v = self.arena[0:shape[0], off:off + nbytes].bitcast(dtype)
        if len(shape) == 3:
            v = v.rearrange("p (a b) -> p a b", a=shape[1])
        elif len(shape) == 4:
            v = v.rearrange("p (a b c) -> p a b c", a=shape[1], b=shape[2])
        return v

    def psum(self, name, shape, dtype=F32):
        return self.stack.enter_context(self.nc.psum_tensor(name, list(shape), dtype))

    def dram(self, name, shape, dtype, kind="Internal"):
        return self.nc.dram_tensor(name, list(shape), dtype, kind=kind)

    def op(self, eng, fn, reads=(), writes=(), is_dma=False, semkey=None, inc=16):
        ins = Ins(eng, _freeze(fn), is_dma, semkey)
        ins.inc = inc
        key = ("d", id(ins)) if is_dma else eng
        deps = {}
        for t in reads:
            for k, d in t.w.items():
                deps[id(d)] = d
            if t.excl:
                for k, d in t.r.items():
                    if not is_dma and k == eng:
                        continue
                    deps[id(d)] = d
        for t in writes:
            for k, d in t.r.items():
                if not is_dma and k == eng:
                    continue
                deps[id(d)] = d
            for k, d in t.w.items():
                if not is_dma and k == eng:
                    continue
                deps[id(d)] = d
        if self.pending[eng]:
            for d in self.pending[eng]:
                if d.is_dma or d.eng != eng:
                    deps[id(d)] = d
            self.pending[eng] = []
        for d in deps.values():
            d.signal = True
        ins.deps = list(deps.values())
        for t in reads:
            t.r[key] = ins
        for t in writes:
            if t.r:
                t.r = {}
                t.w = {}
            t.w[key] = ins
        if is_dma:
            ins.signal = True
            if semkey is None:
                raise ValueError("dma needs semkey")
            self.dma_keys.setdefault(semkey, 0)
            self.open_dmas.append(ins)
        self.streams[eng].append(ins)
        return ins

    def fence(self, srcs, dsts):
        for sd in srcs:
            for dd in dsts:
                for k, i in sd.w.items():
                    if k not in dd.w or dd.w[k].idx < i.idx:
                        dd.w[k] = i
                for k, i in sd.r.items():
                    if k not in dd.r or dd.r[k].idx < i.idx:
                        dd.r[k] = i

    def pe(self, fn, reads=(), writes=()):
        return self.op("pe", fn, reads, writes)

    def act(self, fn, reads=(), writes=()):
        return self.op("act", fn, reads, writes)

    def dve(self, fn, reads=(), writes=()):
        return self.op("dve", fn, reads, writes)

    def pool(self, fn, reads=(), writes=()):
        return self.op("pool", fn, reads, writes)

    def dma(self, out, in_, reads=(), writes=(), semkey=None, eng="sp", **kw):
        return self.op(eng, lambda e: e.dma_start(out=out, in_=in_, **kw), reads, writes,
                       is_dma=True, semkey=semkey)

    def allgather_pairs(self, out_t, in_t, reads=(), writes=(), semkey="cc"):
        return self.op("pool", lambda e: e.collective_compute(
            "AllGather", ALU.bypass, replica_groups=[[0, 1], [2, 3], [4, 5], [6, 7]],
            ins=[in_t.ap().opt()], outs=[out_t.ap().opt()]), reads, writes, is_dma=True, semkey=semkey, inc=1)

    def allreduce_all(self, out_t, in_t, reads=(), writes=(), semkey="ar"):
        return self.op("pool", lambda e: e.collective_compute(
            "AllReduce", ALU.add, replica_groups=[list(range(8))],
            ins=[in_t.ap().opt()], outs=[out_t.ap().opt()]), reads, writes, is_dma=True, semkey=semkey, inc=1)

    def emit(self):
        nc = self.nc
        st = self.stack
        sems = {}
        for e in ("pe", "act", "dve", "pool"):
            sems[e] = st.enter_context(nc.semaphore("s_" + e))
        for k in self.dma_keys:
            sems[("d", k)] = st.enter_context(nc.semaphore("d_" + str(k)))
        for e in self.ENGS:
            cnt = 0
            for ins in self.streams[e]:
                if ins.is_dma:
                    self.dma_keys[ins.semkey] += ins.inc
                    ins.semval = self.dma_keys[ins.semkey]
                elif ins.signal:
                    cnt += 1
                    ins.semval = cnt
        final_dma = dict(self.dma_keys)

        def run(eng_name, e):
            seen = {}
            for ins in self.streams[eng_name]:
                need = {}
                for d in ins.deps:
                    sk = ("d", d.semkey) if d.is_dma else d.eng
                    if d.semval > need.get(sk, 0):
                        need[sk] = d.semval
                for sk, v in need.items():
                    if seen.get(sk, 0) >= v:
                        continue
                    e.wait_ge(sems[sk], v)
                    seen[sk] = v
                bi = ins.fn(e)
                if ins.is_dma:
                    bi.then_inc(sems[("d", ins.semkey)], ins.inc)
                elif ins.signal:
                    bi.then_inc(sems[eng_name], 1)
            if eng_name == "sp":
                for k, v in final_dma.items():
                    if v > 0 and seen.get(("d", k), 0) < v:
                        e.wait_ge(sems[("d", k)], v)

        with nc.Block() as block:
            @block.tensor
            def _(e):
                run("pe", e)

            @block.scalar
            def _(e):
                run("act", e)

            @block.vector
            def _(e):
                run("dve", e)

            @block.gpsimd
            def _(e):
                run("pool", e)

            @block.sync
            def _(e):
                run("sp", e)
        st.close()


def simulate_sync(P):
    keys = dict.fromkeys(P.dma_keys, 0)
    for e in P.ENGS:
        cnt = 0
        for ins in P.streams[e]:
            if ins.is_dma:
                keys[ins.semkey] += ins.inc
                ins.semval = keys[ins.semkey]
            elif ins.signal:
                cnt += 1
                ins.semval = cnt
    sem = {}
    pc = {e: 0 for e in P.ENGS}
    progress = True
    while progress:
        progress = False
        for e in P.ENGS:
            st = P.streams[e]
            while pc[e] < len(st):
                ins = st[pc[e]]
                ok = True
                for d in ins.deps:
                    sk = ("d", d.semkey) if d.is_dma else d.eng
                    if sem.get(sk, 0) < d.semval:
                        ok = False
                        break
                if not ok:
                    break
                if ins.is_dma:
                    sem[("d", ins.semkey)] = sem.get(("d", ins.semkey), 0) + ins.inc
                elif ins.signal:
                    sem[e] = sem.get(e, 0) + 1
                pc[e] += 1
                progress = True
    stuck = {e: (pc[e], len(P.streams[e])) for e in P.ENGS if pc[e] < len(P.streams[e])}
    return stuck


EPS = 1e-6
D = 2048
DFF = 5504
NJ = 43
NC16 = 16


class Ctx:
    def __init__(self, P, ntiles):
        self.P = P
        self.tiles = []
        off = 0
        for n in ntiles:
            self.tiles.append((off, n))
            off += n
        self.NT = off
        NT = off
        self.hy = P.sbuf("hy", [128, 16, NT], BF16)
        self.A = P.sbuf("A", [128, NJ, NT], BF16)
        self.d_h = [Dep(f"h{i}") for i in range(len(ntiles))]
        self.d_A = [Dep(f"A{i}") for i in range(len(ntiles))]
        aflat = self.A[:].rearrange("p j t -> p (j t)")
        self.xt = []
        self.d_xt = []
        nx = min(3, (NJ * NT) // 16384)
        for i in range(nx):
            v = aflat[:, i * 16384:(i + 1) * 16384].bitcast(F32).rearrange("p (c t) -> p c t", c=16)
            self.xt.append(v)
            self.d_xt.append(Dep(f"xt{i}"))
        self.wgu = [P.sbuf(f"wgu{i}", [128, 2, 16, 256], BF16) for i in range(2)]
        self.d_wgu = [Dep(f"wgu{i}") for i in range(2)]
        self.wd = [P.sbuf(f"wd{i}", [128, NJ, 128], BF16) for i in range(2)]
        self.d_wd = [Dep(f"wd{i}") for i in range(2)]
        self.ps, self.d_ps = P.shared_psum()
        self.ones = P.sbuf("ones", [128, 128], BF16)
        self.d_ones = Dep("ones")
        P.dve(lambda e: e.memset(self.ones[:], 1.0), writes=[self.d_ones])
        self.sq = [P.sbuf(f"sq{i}", [128, 512], BF16) for i in range(4)]
        self.d_sq = [Dep(f"sq{i}") for i in range(4)]
        self.tmp = [P.sbuf(f"tmp{i}", [128, 512], F32) for i in range(4)]
        self.d_tmp = [Dep(f"tmp{i}") for i in range(4)]
        self.rstd = P.sbuf("rstd", [128, NT], F32)
        self.d_rstd = [Dep(f"rstd{i}") for i in range(len(ntiles))]
        self.mT = P.sbuf("mT", [128, 144, 2], F32)
        self.d_mT = Dep("mT")
        self.gT = P.sbuf("gT", [128, 6, 16], F32)
        self.d_gT = Dep("gT")
        self.coef = P.sbuf("coef", [128, 4, 16], F32)
        self.d_coef = Dep("coef")
        self.sqi = 0
        self.tmpi = 0
        self.psi = 0

    def next_sq(self):
        i = self.sqi % 4
        self.sqi += 1
        return self.sq[i], self.d_sq[i]

    def next_tmp(self):
        i = self.tmpi % 4
        self.tmpi += 1
        return self.tmp[i], self.d_tmp[i]


def emit_modulation(P, C, ada_w, ada_bT, cvecT, scr):
    cv = scr["cv"]
    sc = scr["sc"]
    bT = scr["bT"]
    d_cv, d_sc, d_bT = Dep("cv"), Dep("sc"), Dep("bT")
    P.dma(cv[:], cvecT, writes=[d_cv], semkey="small")
    P.dma(bT[:], ada_bT, writes=[d_bT], semkey="small2")
    P.act(lambda e: e.activation(out=sc[:], in_=cv[:], func=AF.Silu), reads=[d_cv], writes=[d_sc])
    mp = C.ps[7]
    d_mp = C.d_ps[7]
    mpv = mp[:, 0:288].rearrange("p (j r) -> p j r", r=2)
    for jb in range(36):
        s = jb % 2
        slot = C.wgu[s][:].rearrange("p a c n -> p c (a n)") if False else None
        sl = C.wgu[s][:].rearrange("p a c n -> p (a c n)").rearrange("p (c n) -> p c n", c=16)
        src = ada_w[:, jb * 512:(jb + 1) * 512].rearrange("(c p) n -> p c n", p=128)
        P.dma(sl, src, writes=[C.d_wgu[s]], semkey=f"wgu{s}", eng="pool")
        for j4 in range(4):
            j = jb * 4 + j4
            for k in range(16):
                P.pe(lambda e, sl=sl, j4=j4, k=k, j=j: e.matmul(
                    mpv[:, j, :], lhsT=sl[:, k, j4 * 128:(j4 + 1) * 128], rhs=sc[:, k, :],
                    start=(k == 0), stop=(k == 15)),
                    reads=[C.d_wgu[s], d_sc], writes=[d_mp])
    for r in range(2):
        P.dve(lambda e, r=r: e.tensor_tensor(out=C.mT[:, :, r], in0=mpv[:, :, r], in1=bT[:], op=ALU.add),
              reads=[d_mp, d_bT], writes=[C.d_mT])


def emit_coefs(P, C, sl_scale, sl_gate, gi_pre, gi_post, wres, cols):
    for col in cols:
        P.dve(lambda e, col=col: e.scalar_tensor_tensor(
            out=C.coef[:, col, :], in0=C.mT[:, sl_scale * 16:(sl_scale + 1) * 16, col], scalar=1.0,
            in1=C.gT[:, gi_pre, :], op0=ALU.add, op1=ALU.mult),
            reads=[C.d_mT, C.d_gT], writes=[C.d_coef])
        if sl_gate is not None:
            P.dve(lambda e, col=col: e.scalar_tensor_tensor(
                out=C.coef[:, 2 + col, :], in0=C.mT[:, sl_gate * 16:(sl_gate + 1) * 16, col], scalar=float(wres),
                in1=C.gT[:, gi_post, :], op0=ALU.mult, op1=ALU.mult),
                reads=[C.d_mT, C.d_gT], writes=[C.d_coef])


def emit_prenorm(P, C, srcs, sl_shift, out_sb, d_out, arena_deps, resident=False, after_tile=None):
    nx = len(C.xt)
    for ti, (off, n) in enumerate(C.tiles):
        src, d_src, col = srcs[ti]
        xi = ti % nx
        xt = C.xt[xi][:, :, 0:n]
        if not resident:
            P.dma(xt, src.rearrange("(c p) t -> p c t", p=128), reads=[d_src],
                  writes=[C.d_xt[xi]] + arena_deps, semkey=f"xt{xi}")
        ssp = C.ps[6 + (ti % 2)]
        d_ssp = C.d_ps[6 + (ti % 2)]
        for c in range(16):
            sq, d_sq = C.next_sq()
            P.act(lambda e, sq=sq, c=c, xt=xt, n=n: e.activation(out=sq[:, 0:n], in_=xt[:, c, :], func=AF.Square),
                  reads=[C.d_xt[xi]], writes=[d_sq])
            P.pe(lambda e, sq=sq, c=c, ssp=ssp, n=n: e.matmul(ssp[:, 0:n], lhsT=C.ones[:], rhs=sq[:, 0:n],
                                                              start=(c == 0), stop=(c == 15)),
                 reads=[d_sq, C.d_ones], writes=[d_ssp])
        tmp, d_tmp = C.next_tmp()
        P.act(lambda e, tmp=tmp, ssp=ssp, n=n: e.activation(out=tmp[:, 0:n], in_=ssp[:, 0:n], func=AF.Sqrt,
                                                            scale=1.0 / D, bias=EPS),
              reads=[d_ssp], writes=[d_tmp])
        rs = C.rstd[:, off:off + n]
        P.dve(lambda e, tmp=tmp, rs=rs, n=n: e.reciprocal(out=rs, in_=tmp[:, 0:n]),
              reads=[d_tmp], writes=[C.d_rstd[ti]])
        dst = out_sb(ti)
        for c in range(16):
            tmp, d_tmp = C.next_tmp()
            P.dve(lambda e, tmp=tmp, c=c, xt=xt, rs=rs, n=n, col=col: e.scalar_tensor_tensor(
                out=tmp[:, 0:n], in0=xt[:, c, :], scalar=C.coef[:, col, c:c + 1], in1=rs,
                op0=ALU.mult, op1=ALU.mult),
                reads=[C.d_xt[xi], C.d_rstd[ti], C.d_coef], writes=[d_tmp])
            P.act(lambda e, tmp=tmp, c=c, dst=dst, n=n, col=col: e.activation(
                out=dst[:, c, :], in_=tmp[:, 0:n], func=AF.Identity,
                bias=C.mT[:, sl_shift * 16 + c, col:col + 1], scale=1.0),
                reads=[d_tmp, C.d_mT], writes=[d_out[ti]])
        if after_tile is not None:
            after_tile(ti, off, n)


def emit_ffn(P, C, srcs, dsts, wg, wu, wd, sl_shift, resident=False):
    nt = len(C.tiles)
    arena = C.d_A
    emit_prenorm(P, C, srcs, sl_shift, lambda ti: C.hy[:, :, C.tiles[ti][0]:C.tiles[ti][0] + C.tiles[ti][1]],
                 C.d_h, arena, resident)
    emit_gateup(P, C, wg, wu)
    emit_down_residual(P, C, NJ, wd, srcs, dsts)


def emit_gateup(P, C, wg, wu):
    it = 0
    for jj in range(22):
        s = jj % 2
        ncol = 256 if jj < 21 else 128
        for a, w in enumerate((wg, wu)):
            src = w[:, jj * 256:jj * 256 + ncol].rearrange("(c p) n -> p c n", p=128)
            P.dma(C.wgu[s][:, a, :, 0:ncol], src, writes=[C.d_wgu[s]], semkey=f"wgu{s}", eng="pool")
        for jl in range(ncol // 128):
            j = jj * 2 + jl
            for ti, (off, n) in enumerate(C.tiles):
                pg = (it % 4) * 2
                it += 1
                G, U = C.ps[pg], C.ps[pg + 1]
                for a, pt in enumerate((G, U)):
                    for k in range(16):
                        P.pe(lambda e, pt=pt, a=a, k=k, s=s, jl=jl, off=off, n=n: e.matmul(
                            pt[:, 0:n], lhsT=C.wgu[s][:, a, k, jl * 128:(jl + 1) * 128],
                            rhs=C.hy[:, k, off:off + n], start=(k == 0), stop=(k == 15)),
                            reads=[C.d_wgu[s], C.d_h[ti]], writes=[C.d_ps[pg + a]])
                tmp, d_tmp = C.next_tmp()
                P.act(lambda e, tmp=tmp, G=G, n=n: e.activation(out=tmp[:, 0:n], in_=G[:, 0:n], func=AF.Silu),
                      reads=[C.d_ps[pg]], writes=[d_tmp])
                P.dve(lambda e, tmp=tmp, U=U, j=j, off=off, n=n: e.tensor_tensor(
                    out=C.A[:, j, off:off + n], in0=tmp[:, 0:n], in1=U[:, 0:n], op=ALU.mult),
                    reads=[d_tmp, C.d_ps[pg + 1]], writes=[C.d_A[ti]] + C.d_xt)


def emit_down_residual(P, C, nch, wd, srcs, dsts):
    NJ = nch
    pend = []
    it = 0
    for dc in range(16):
        s = dc % 2
        src = wd[:, dc * 128:(dc + 1) * 128].rearrange("(j p) n -> p j n", p=128)
        P.dma(C.wd[s][:, 0:nch, :], src, writes=[C.d_wd[s]], semkey=f"wd{s}", eng="pool")
        for ti, (off, n) in enumerate(C.tiles):
            pi = it % 4
            it += 1
            Y = C.ps[pi]
            for j in range(NJ):
                P.pe(lambda e, Y=Y, j=j, s=s, off=off, n=n: e.matmul(
                    Y[:, 0:n], lhsT=C.wd[s][:, j, :], rhs=C.A[:, j, off:off + n],
                    start=(j == 0), stop=(j == NJ - 1)),
                    reads=[C.d_wd[s], C.d_A[ti]], writes=[C.d_ps[pi]])
            for f in pend:
                f()
            pend = []
            P.act(lambda e, Y=Y, dc=dc, off=off, n=n: e.activation(out=C.hy[:, dc, off:off + n], in_=Y[:, 0:n],
                                                                  func=AF.Copy),
                  reads=[C.d_ps[pi]], writes=[C.d_h[ti]])
            sq, d_sq = C.next_sq()
            P.act(lambda e, Y=Y, sq=sq, n=n: e.activation(out=sq[:, 0:n], in_=Y[:, 0:n], func=AF.Square),
                  reads=[C.d_ps[pi]], writes=[d_sq])
            SS = C.ps[4 + ti]

            def ssmm(sq=sq, d_sq=d_sq, SS=SS, ti=ti, dc=dc, n=n):
                P.pe(lambda e: e.matmul(SS[:, 0:n], lhsT=C.ones[:], rhs=sq[:, 0:n],
                                        start=(dc == 0), stop=(dc == 15)),
                     reads=[d_sq, C.d_ones], writes=[C.d_ps[4 + ti]])
            pend.append(ssmm)
    for f in pend:
        f()
    nx = len(C.xt)
    for ti, (off, n) in enumerate(C.tiles):
        src, d_src, col = srcs[ti]
        dst, d_dst, _ = dsts[ti]
        if dst is None:
            continue
        SS = C.ps[4 + ti]
        tmp, d_tmp = C.next_tmp()
        P.act(lambda e, tmp=tmp, SS=SS, n=n: e.activation(out=tmp[:, 0:n], in_=SS[:, 0:n], func=AF.Sqrt,
                                                          scale=1.0 / D, bias=EPS),
              reads=[C.d_ps[4 + ti]], writes=[d_tmp])
        rs = C.rstd[:, off:off + n]
        P.dve(lambda e, tmp=tmp, rs=rs, n=n: e.reciprocal(out=rs, in_=tmp[:, 0:n]),
              reads=[d_tmp], writes=[C.d_rstd[ti]])
        xi = ti % nx
        xt = C.xt[xi][:, :, 0:n]
        P.dma(xt, src.rearrange("(c p) t -> p c t", p=128), reads=[d_src],
              writes=[C.d_xt[xi]] + C.d_A, semkey=f"xt{xi}")
        for c in range(16):
            tmp, d_tmp = C.next_tmp()
            P.dve(lambda e, tmp=tmp, c=c, rs=rs, off=off, n=n, col=col: e.scalar_tensor_tensor(
                out=tmp[:, 0:n], in0=C.hy[:, c, off:off + n], scalar=C.coef[:, 2 + col, c:c + 1], in1=rs,
                op0=ALU.mult, op1=ALU.mult),
                reads=[C.d_h[ti], C.d_rstd[ti], C.d_coef], writes=[d_tmp])
            P.dve(lambda e, tmp=tmp, c=c, xt=xt, n=n: e.tensor_tensor(
                out=xt[:, c, :], in0=xt[:, c, :], in1=tmp[:, 0:n], op=ALU.add),
                reads=[d_tmp, C.d_xt[xi]], writes=[C.d_xt[xi]])
        P.dma(dst.rearrange("(c p) t -> p c t", p=128), xt, reads=[C.d_xt[xi]], writes=[d_dst],
              semkey=f"xt{xi}")


def emit_modulation_sharded(P, C, ada_sl, cvecT, bT0, bT1, mp_own, mp_g, mTd, d_mTd):
    cv = P.sbuf("mcv", [128, 16, 2], F32)
    sc = P.sbuf("msc", [128, 16, 2], BF16)
    bT = P.sbuf("mbT", [128, 2, 144], F32)
    part = P.sbuf("mpart", [128, 288], F32)
    mo = P.sbuf("mo", [128, 2, 144, 2], F32)
    d_cv, d_sc, d_bT, d_part, d_mo = [Dep(n) for n in ("mcv", "msc", "mbT", "mpart", "mo")]
    P.dma(cv[:], cvecT, writes=[d_cv], semkey="small")
    P.dma(bT[:, 0, :], bT0, writes=[d_bT], semkey="small2")
    P.dma(bT[:, 1, :], bT1, writes=[d_bT], semkey="small2")
    P.act(lambda e: e.activation(out=sc[:], in_=cv[:], func=AF.Silu), reads=[d_cv], writes=[d_sc])
    mp = C.ps[7]
    d_mp = C.d_ps[7]
    mpv = mp[:, 0:288].rearrange("p (l j r) -> p l j r", l=2, j=72)
    it = 0
    for l in range(2):
        for jb in range(18):
            s = it % 2
            it += 1
            sl = C.wgu[s][:].rearrange("p a c n -> p (a c n)").rearrange("p (c n) -> p c n", c=16)
            src = ada_sl[l, :, jb * 512:(jb + 1) * 512].rearrange("(c p) n -> p c n", p=128)
            P.dma(sl, src, writes=[C.d_wgu[s]], semkey=f"wgu{s}", eng="pool")
            for j4 in range(4):
                j = jb * 4 + j4
                for k in range(16):
                    P.pe(lambda e, sl=sl, j4=j4, k=k, j=j, l=l: e.matmul(
                        mpv[:, l, j, :], lhsT=sl[:, k, j4 * 128:(j4 + 1) * 128], rhs=sc[:, k, :],
                        start=(k == 0), stop=(k == 15)),
                        reads=[C.d_wgu[s], d_sc], writes=[d_mp])
    P.dve(lambda e: e.tensor_copy(out=part[:], in_=mp[:, 0:288]), reads=[d_mp], writes=[d_part])
    d_mi, d_mr = Dep("mp_own"), Dep("mp_g")
    P.dma(mp_own[:, :], part[:], reads=[d_part], writes=[d_mi], semkey="mred")
    P.allgather_pairs(mp_g, mp_own, reads=[d_mi], writes=[d_mr], semkey="cc0")
    for l in range(2):
        for r in range(2):
            P.dma(mo[:, l, r * 72:(r + 1) * 72, :],
                  mp_g[r * 128:(r + 1) * 128, l * 144:(l + 1) * 144].rearrange("p (j c) -> p j c", c=2),
                  reads=[d_mr], writes=[d_mo], semkey="mred")
        for col in range(2):
            P.dve(lambda e, l=l, col=col: e.tensor_tensor(out=mo[:, l, :, col], in0=mo[:, l, :, col], in1=bT[:, l, :],
                                                          op=ALU.add), reads=[d_mo, d_bT], writes=[d_mo])
        P.dma(mTd[l][:], mo[:, l, :, :], reads=[d_mo], writes=[d_mTd[l]], semkey="mTo")


EPS = 1e-6
NTOK = 2304
NCTX = 256
NLAT = 2048
LAM_INIT0 = 0.8 - 0.6 * math.exp(-0.3 * 0)


DBG = {}


class MixCtx:
    def __init__(self, P):
        self.P = P
        self.hs = P.sbuf("hs", [128, 16, NTOK], BF16)
        self.d_hs = Dep("hs")
        self.ropeC = P.sbuf("ropeC", [128, NLAT], F32)
        self.ropeS = P.sbuf("ropeS", [128, NLAT], F32)
        self.d_rope = Dep("rope")
        self.wt = [P.sbuf(f"wt{i}", [128, 16, 128], BF16) for i in range(8)]
        self.d_wt = [Dep(f"wt{i}") for i in range(8)]
        self.ps, self.d_ps = P.shared_psum()
        self.d_ph = [[Dep(f"ph{i}_{h}") for h in range(2)] for i in range(8)]
        self.ones = P.sbuf("ones", [128, 128], BF16)
        self.onesf = P.sbuf("onesf", [128, 128], F32)
        self.d_ones = Dep("ones")
        P.dve(lambda e: e.memset(self.ones[:], 1.0), writes=[self.d_ones])
        P.dve(lambda e: e.memset(self.onesf[:], 1.0), writes=[self.d_ones])
        self.perm = P.sbuf("perm", [128, 128], BF16)
        self.ident = P.sbuf("ident", [128, 128], BF16)
        self.maskf = P.sbuf("maskf", [128, 128], F32)
        self.maskb = P.sbuf("maskb", [128, 128], F32)
        self.d_const = Dep("const")
        self.vec = P.sbuf("vec", [128, 32], F32)
        self.d_vec = Dep("vec")
        self.lbr = P.sbuf("lbr", [128, 2, 2, 4], F32)
        self.lb = P.sbuf("lb", [128, 2, 4], F32)
        self.oml = P.sbuf("oml", [128, 2, 4], F32)
        self.d_lb = Dep("lb")
        u0 = P.aoff
        self.qT = P.sbuf("qT", [128, NLAT], BF16)
        self.qT1 = P.sbuf("qT1", [128, NLAT], BF16)
        self.kT = P.sbuf("kT", [128, NTOK], BF16)
        self.V = P.sbuf("V", [128, 18, 128], BF16)
        self.d_qT, self.d_kT, self.d_V = Dep("qT"), Dep("kT"), Dep("V")
        self.E = [P.sbuf(f"E{i}", [128, 512], BF16) for i in range(4)]
        self.d_E = [Dep(f"E{i}") for i in range(4)]
        u1 = P.aoff
        if P.arena is not None:
            P.aoff = u0
        self.zf_sb = P.sbuf("zf_sb", [128, NTOK], F32)
        self.zb_sb = P.sbuf("zb_sb", [128, NTOK], F32)
        self.q_sb = P.sbuf("q_sb", [128, NTOK], BF16)
        self.i_sb = P.sbuf("i_sb", [128, 18, 128], BF16)
        self.d_rp = Dep("recproj")
        self.d_zf = Dep("zf_sb")
        if P.arena is not None:
            P.aoff = max(u1, P.aoff)
        self.qtF = P.sbuf("qtF", [128, NTOK], BF16)
        self.ktF = P.sbuf("ktF", [128, NTOK], BF16)
        self.zfb = self.zf_sb.bitcast(BF16)
        self.d_zfu = [Dep(f"zfu{u}") for u in range(9)]
        self.d_qk = [Dep("qkF"), Dep("qkB")]
        self.sg_sb = P.sbuf("sg_sb", [128, NLAT], BF16)
        self.svall = P.sbuf("svall", [128, 2, 18, 4], F32)
        self.d_sva = Dep("svall")
        self.d_svad = [Dep("svallF"), Dep("svallB")]
        self.d_tfh = [[Dep(f"tfh{i}_{h}") for h in range(2)] for i in range(6)]
        self.maskR = P.sbuf("maskR", [128, 512], F32)
        P.dve(lambda e: e.memset(self.maskR[:], 1.0), writes=[self.d_ones])
        for cc in range(4):
            P.dve(lambda e, cc=cc: e.memset(self.maskR[:, cc * 128:cc * 128 + 1], 0.0), writes=[self.d_ones])
        self.tf = [P.sbuf(f"tf{i}", [128, 512], F32) for i in range(6)]
        self.d_tf = [Dep(f"tf{i}") for i in range(6)]
        self.tb = [P.sbuf(f"tb{i}", [128, 512], BF16) for i in range(4)]
        self.d_tb = [Dep(f"tb{i}") for i in range(4)]
        self.tfi = 0
        self.tbi = 0
        self.Ei = 0
        self.S = P.sbuf("S", [128, 128], F32)
        self.d_S = Dep("S")
        self.S1 = P.sbuf("S1", [128, 128], F32)
        self.d_S1 = Dep("S1")
        self.ofw = P.sbuf("ofw", [128, NLAT], F32)
        self.d_ofw = Dep("ofw")
        self.obw = P.sbuf("obw", [128, NLAT], F32)
        self.d_obw = Dep("obw")
        self.rf = [P.sbuf(f"rf{i}", [128, 128], F32) for i in range(6)]
        self.d_rf = [Dep(f"rf{i}") for i in range(6)]
        self.rb = [P.sbuf(f"rb{i}", [128, 128], BF16) for i in range(12)]
        self.d_rb = [Dep(f"rb{i}") for i in range(12)]
        self.rfi = 0
        self.rbi = 0
        self.sv = [P.sbuf(f"sv{i}", [128, 8], F32) for i in range(8)]
        self.d_sv = [Dep(f"sv{i}") for i in range(8)]
        self.svi = 0

    def ntf(self):
        i = self.tfi % 6
        self.tfi += 1
        return self.tf[i], self.d_tf[i]

    def ntb(self):
        i = self.tbi % 4
        self.tbi += 1
        return self.tb[i], self.d_tb[i]

    def nE(self):
        i = self.Ei % 4
        self.Ei += 1
        return self.E[i], self.d_E[i]

    def nrf(self):
        i = self.rfi % 6
        self.rfi += 1
        return self.rf[i], self.d_rf[i]

    def nrb(self):
        i = self.rbi % 12
        self.rbi += 1
        return self.rb[i], self.d_rb[i]

    def nsv(self):
        i = self.svi % 8
        self.svi += 1
        return self.sv[i], self.d_sv[i]


def load_w(P, M, slot, src):
    P.dma(M.wt[slot][:], src.rearrange("(c p) n -> p c n", p=128), writes=[M.d_wt[slot]],
          semkey=f"wt{slot}", eng="pool")


def emit_mix_setup(P, M, hTf, ropeC, ropeS, perm, ident, maskf, maskb, lamT, dng, rng, lbraw):
    for q in range(4 if hTf is not None else 0):
        P.dma(M.hs[:, q * 4:(q + 1) * 4, :], hTf[q * 512:(q + 1) * 512, :].rearrange("(c p) t -> p c t", p=128),
              writes=[M.d_hs], semkey="hs")
    P.dma(M.ropeC[:], ropeC, writes=[M.d_rope], semkey="rope")
    P.dma(M.ropeS[:], ropeS, writes=[M.d_rope], semkey="rope")
    P.dma(M.perm[:], perm, writes=[M.d_const], semkey="cst", eng="pool")
    P.dma(M.ident[:], ident, writes=[M.d_const], semkey="cst", eng="pool")
    P.dma(M.maskf[:], maskf, writes=[M.d_const], semkey="cst2")
    P.dma(M.maskb[:], maskb, writes=[M.d_const], semkey="cst2")
    P.dma(M.vec[0:64, 0:4], lamT, writes=[M.d_vec], semkey="vec")
    P.dma(M.vec[:, 4:5], dng, writes=[M.d_vec], semkey="vec")
    P.dma(M.vec[:, 5:6], rng, writes=[M.d_vec], semkey="vec")
    P.dma(M.lbr[:], lbraw, writes=[M.d_lb], semkey="lb")
    P.dve(lambda e: e.tensor_tensor(out=M.vec[0:64, 6:7], in0=M.vec[0:64, 0:1], in1=M.vec[0:64, 1:2], op=ALU.mult),
          reads=[M.d_vec], writes=[M.d_vec])
    P.dve(lambda e: e.tensor_tensor(out=M.vec[0:64, 7:8], in0=M.vec[0:64, 2:3], in1=M.vec[0:64, 3:4], op=ALU.mult),
          reads=[M.d_vec], writes=[M.d_vec])
    lp = M.ps[7]
    P.pe(lambda e: e.matmul(lp[:, 0:2], lhsT=M.onesf[0:64, :], rhs=M.vec[0:64, 6:8], start=True, stop=True),
         reads=[M.d_vec, M.d_ones], writes=[M.d_ps[7]])
    P.act(lambda e: e.activation(out=M.vec[:, 8:10], in_=lp[:, 0:2], func=AF.Exp), reads=[M.d_ps[7]],
          writes=[M.d_vec])
    P.dve(lambda e: e.tensor_tensor(out=M.vec[:, 10:11], in0=M.vec[:, 9:10], in1=M.vec[:, 8:9], op=ALU.subtract),
          reads=[M.d_vec], writes=[M.d_vec])
    P.dve(lambda e: e.tensor_scalar(out=M.vec[:, 10:11], in0=M.vec[:, 10:11], scalar1=-LAM_INIT0, scalar2=None,
                                    op0=ALU.add), reads=[M.d_vec], writes=[M.d_vec])
    P.dve(lambda e: e.tensor_scalar(out=M.vec[:, 11:12], in0=M.vec[:, 4:5], scalar1=1.0 - LAM_INIT0, scalar2=None,
                                    op0=ALU.mult), reads=[M.d_vec], writes=[M.d_vec])
    P.dve(lambda e: e.tensor_tensor(out=M.lb[:], in0=M.lbr[:, :, 1, :], in1=M.lbr[:, :, 0, :], op=ALU.subtract),
          reads=[M.d_lb], writes=[M.d_lb])
    P.act(lambda e: e.activation(out=M.lb[:], in_=M.lb[:], func=AF.Exp), reads=[M.d_lb], writes=[M.d_lb])
    P.dve(lambda e: e.tensor_scalar(out=M.lb[:], in0=M.lb[:], scalar1=1.0, scalar2=None, op0=ALU.add),
          reads=[M.d_lb], writes=[M.d_lb])
    P.dve(lambda e: e.reciprocal(out=M.lb[:], in_=M.lb[:]), reads=[M.d_lb], writes=[M.d_lb])
    P.dve(lambda e: e.tensor_scalar(out=M.oml[:], in0=M.lb[:], scalar1=-1.0, scalar2=1.0, op0=ALU.mult, op1=ALU.add),
          reads=[M.d_lb], writes=[M.d_lb])


def proj_fm(P, M, out_ps, d_out, slot, off, n):
    for k in range(16):
        P.pe(lambda e, k=k: e.matmul(out_ps, lhsT=M.wt[slot][:, k, :], rhs=M.hs[:, k, off:off + n],
                                     start=(k == 0), stop=(k == 15)),
             reads=[M.d_wt[slot], M.d_hs], writes=[d_out])


def proj_tm(P, M, out_ps, d_out, slot, off):
    for k in range(16):
        P.pe(lambda e, k=k: e.matmul(out_ps, lhsT=M.hs[:, k, off:off + 128], rhs=M.wt[slot][:, k, :],
                                     start=(k == 0), stop=(k == 15)),
             reads=[M.d_wt[slot], M.d_hs], writes=[d_out])


def emit_rsqrt(P, M, out_sb, d_o, in_ps, d_in, n, inv_dim):
    P.act(lambda e: e.activation(out=out_sb, in_=in_ps, func=AF.Ln, scale=inv_dim, bias=EPS),
          reads=[d_in], writes=[d_o])
    P.act(lambda e: e.activation(out=out_sb, in_=out_sb, func=AF.Exp, scale=-0.5), reads=[d_o], writes=[d_o])


def emit_attention_head(P, M, hd, mergedT, d_merged):
    sq_, sk_, sv_ = 0, 1, 2
    P.dve(lambda e: e.memset(M.qT[64:128, :], 0.0), writes=[M.d_qT])
    P.dve(lambda e: e.memset(M.qT1[0:64, :], 0.0), writes=[M.d_qT])
    tiles = [(0, NCTX)] + [(NCTX + i * 512, 512) for i in range(4)]
    bi = 0
    import os
    NSUB = int(os.environ.get("ATT_SUB", "99"))
    for (off, n) in tiles:
        for which in ("k", "q"):
            if bi >= NSUB:
                continue
            if which == "q" and off < NCTX:
                continue
            slot = sk_ if which == "k" else sq_
            dstT = M.kT if which == "k" else M.qT
            d_dst = M.d_kT if which == "k" else M.d_qT
            doff = off if which == "k" else off - NCTX
            pb = bi % 2
            bi += 1
            pp, d_pp = M.ps[pb], M.d_ps[pb]
            proj_fm(P, M, pp[:, 0:n], d_pp, slot, off, n)
            if off < NCTX:
                P.act(lambda e, pp=pp, n=n, dstT=dstT, doff=doff: e.activation(
                    out=dstT[:, doff:doff + n], in_=pp[:, 0:n], func=AF.Copy), reads=[d_pp], writes=[d_dst])
                continue
            loff = off - NCTX
            sb, d_sb = M.ntb()
            P.act(lambda e, pp=pp, sb=sb, n=n: e.activation(out=sb[:, 0:n], in_=pp[:, 0:n], func=AF.Copy),
                  reads=[d_pp], writes=[d_sb])
            rp, d_rp = M.ps[2 + pb], M.d_ps[2 + pb]
            P.pe(lambda e, rp=rp, sb=sb, n=n: e.matmul(rp[:, 0:n], lhsT=M.perm[:], rhs=sb[:, 0:n], start=True,
                                                       stop=True), reads=[d_sb, M.d_const], writes=[d_rp])
            t1, d_t1 = M.ntf()
            P.dve(lambda e, t1=t1, pp=pp, n=n, loff=loff: e.tensor_tensor(
                out=t1[:, 0:n], in0=pp[:, 0:n], in1=M.ropeC[:, loff:loff + n], op=ALU.mult),
                reads=[d_pp, M.d_rope], writes=[d_t1])
            t2, d_t2 = M.ntf()
            P.dve(lambda e, t2=t2, rp=rp, n=n, loff=loff: e.tensor_tensor(
                out=t2[:, 0:n], in0=rp[:, 0:n], in1=M.ropeS[:, loff:loff + n], op=ALU.mult),
                reads=[d_rp, M.d_rope], writes=[d_t2])
            if which == "k":
                P.dve(lambda e, t1=t1, t2=t2, dstT=dstT, doff=doff, n=n: e.tensor_tensor(
                    out=dstT[:, doff:doff + n], in0=t1[:, 0:n], in1=t2[:, 0:n], op=ALU.add),
                    reads=[d_t1, d_t2], writes=[d_dst])
            else:
                P.dve(lambda e, t1=t1, t2=t2, doff=doff, n=n: e.tensor_tensor(
                    out=M.qT[0:64, doff:doff + n], in0=t1[0:64, 0:n], in1=t2[0:64, 0:n], op=ALU.add),
                    reads=[d_t1, d_t2], writes=[d_dst])
                P.dve(lambda e, t1=t1, t2=t2, doff=doff, n=n: e.tensor_tensor(
                    out=M.qT1[64:128, doff:doff + n], in0=t1[64:128, 0:n], in1=t2[64:128, 0:n], op=ALU.add),
                    reads=[d_t1, d_t2], writes=[d_dst])
    import os
    if DBG.get("qT") is not None and hd == 0:
        P.dma(DBG["qT"][:, :], M.qT[:], reads=[M.d_qT], writes=[Dep("dbgq")], semkey="dbg")
        P.dma(DBG["kT"][:, :], M.kT[:], reads=[M.d_kT], writes=[Dep("dbgk")], semkey="dbg")
    STG = int(os.environ.get("ATT_STAGE", "9"))
    if STG < 2:
        return
    for g4 in range(5):
        pb = 4 + (g4 % 2)
        vp, d_vp = M.ps[pb], M.d_ps[pb]
        nk = 4 if g4 < 4 else 2
        for i in range(nk):
            kt = g4 * 4 + i
            proj_tm(P, M, vp[:, i * 128:(i + 1) * 128], d_vp, sv_, kt * 128)
        P.act(lambda e, vp=vp, g4=g4, nk=nk: e.activation(
            out=M.V[:, g4 * 4:g4 * 4 + nk, :].rearrange("p a n -> p (a n)"), in_=vp[:, 0:nk * 128], func=AF.Copy),
            reads=[d_vp], writes=[M.d_V])
    if STG < 3:
        return
    for qt in [int(c) for c in os.environ.get("ATT_QT", "0123")]:
        qo = qt * 512
        steps = [(kt, m) for kt in range(18) for m in range(2)]
        pend = []
        for si, (kt, m) in enumerate(steps):
            sb_i = si % 4
            ST, d_ST = M.ps[sb_i], M.d_ps[sb_i]
            P.pe(lambda e, ST=ST, kt=kt, m=m: e.matmul(
                ST[:, :], lhsT=M.kT[:, kt * 128:(kt + 1) * 128],
                rhs=(M.qT if m == 0 else M.qT1)[:, qo:qo + 512], start=True, stop=True),
                reads=[M.d_kT, M.d_qT], writes=[d_ST])
            E, d_E = M.nE()
            P.act(lambda e, E=E, ST=ST: e.activation(out=E[:], in_=ST[:, :], func=AF.Exp, scale=0.125),
                  reads=[d_ST], writes=[d_E])
            if len(pend) >= 2:
                pend.pop(0)()

            def pv(E=E, d_E=d_E, kt=kt, m=m):
                P.pe(lambda e: e.matmul(M.ps[4 + m][:, :], lhsT=M.V[:, kt, :], rhs=E[:], start=(kt == 0),
                                        stop=(kt == 17)), reads=[M.d_V, d_E], writes=[M.d_ps[4 + m]])
                P.pe(lambda e: e.matmul(M.ps[6 + m][:, :], lhsT=M.ones[:], rhs=E[:], start=(kt == 0),
                                        stop=(kt == 17)), reads=[M.d_ones, d_E], writes=[M.d_ps[6 + m]])
            pend.append(pv)
        for f in pend:
            f()
        r0, d_r0 = M.ntf()
        r1, d_r1 = M.ntf()
        P.dve(lambda e, r0=r0: e.reciprocal(out=r0[:], in_=M.ps[6][:, :]), reads=[M.d_ps[6]], writes=[d_r0])
        P.dve(lambda e, r1=r1: e.reciprocal(out=r1[:], in_=M.ps[7][:, :]), reads=[M.d_ps[7]], writes=[d_r1])
        oa, d_oa = M.ntf()
        ob, d_ob = M.ntf()
        P.dve(lambda e, oa=oa, r0=r0: e.tensor_tensor(out=oa[:], in0=M.ps[4][:, :], in1=r0[:], op=ALU.mult),
              reads=[M.d_ps[4], d_r0], writes=[d_oa])
        P.dve(lambda e, ob=ob, r1=r1: e.tensor_tensor(out=ob[:], in0=M.ps[5][:, :], in1=r1[:], op=ALU.mult),
              reads=[M.d_ps[5], d_r1], writes=[d_ob])
        o, d_o = M.ntf()
        P.dve(lambda e, o=o, oa=oa, ob=ob: e.scalar_tensor_tensor(
            out=o[:], in0=ob[:], scalar=M.vec[:, 10:11], in1=oa[:], op0=ALU.mult, op1=ALU.add),
            reads=[d_oa, d_ob, M.d_vec], writes=[d_o])
        sq, d_sq = M.ntb()
        P.act(lambda e, sq=sq, o=o: e.activation(out=sq[:], in_=o[:], func=AF.Square), reads=[d_o], writes=[d_sq])
        P.pe(lambda e, sq=sq: e.matmul(M.ps[0][:, :], lhsT=M.ones[:], rhs=sq[:], start=True, stop=True),
             reads=[d_sq, M.d_ones], writes=[M.d_ps[0]])
        ri, d_ri = M.ntf()
        emit_rsqrt(P, M, ri[:], d_ri, M.ps[0][:, :], M.d_ps[0], 512, 1.0 / 128)
        ob16, d_ob16 = M.ntb()
        P.dve(lambda e, ob16=ob16, o=o, ri=ri: e.scalar_tensor_tensor(
            out=ob16[:], in0=o[:], scalar=M.vec[:, 11:12], in1=ri[:], op0=ALU.mult, op1=ALU.mult),
            reads=[d_o, d_ri, M.d_vec], writes=[d_ob16])
        rows = mergedT("att", hd) if callable(mergedT) else mergedT[hd * 128:(hd + 1) * 128, :]
        P.dma(rows[:, qo:qo + 512], ob16[:], reads=[d_ob16], writes=[d_merged], semkey="mg")


def emit_rec_head(P, M, r, mergedT, d_merged, after_burst=None):
    s_q, s_zf, s_zb, s_i, s_g = 3, 4, 5, 6, 7
    orders = [list(range(18)), [1, 0] + list(range(17, 1, -1))]
    Ss = [M.S, M.S1]
    dSs = [M.d_S, M.d_S1]
    obuf = [M.ofw, M.obw]
    d_obuf = [M.d_ofw, M.d_obw]
    for dr in range(2):
        P.dve(lambda e, dr=dr: e.memset(Ss[dr][:], 0.0), writes=[dSs[dr]])
    ptiles = [(0, NCTX)] + [(NCTX + i * 512, 512) for i in range(4)]
    bi = 0
    for (off, n) in ptiles:
        for (slot, dst, sc_) in ((s_zf, M.zf_sb, 1.0), (s_zb, M.zb_sb, 1.0), (s_q, M.q_sb, float(128 ** -0.5))):
            pb = bi % 4
            bi += 1
            proj_fm(P, M, M.ps[pb][:, 0:n], M.d_ps[pb], slot, off, n)
            if slot != s_q:
                P.act(lambda e, pb=pb, dst=dst, off=off, n=n: e.activation(
                    out=dst[:, off:off + n], in_=M.ps[pb][:, 0:n], func=AF.Sigmoid, scale=-1.0),
                    reads=[M.d_ps[pb]], writes=[M.d_rp])
            else:
                P.dve(lambda e, pb=pb, dst=dst, off=off, n=n, sc_=sc_: e.tensor_scalar(
                    out=dst[:, off:off + n], in0=M.ps[pb][:, 0:n], scalar1=sc_, scalar2=None, op0=ALU.mult),
                    reads=[M.d_ps[pb]], writes=[M.d_rp])
        if off >= NCTX:
            pb = bi % 4
            bi += 1
            proj_fm(P, M, M.ps[pb][:, 0:n], M.d_ps[pb], s_g, off, n)
            P.act(lambda e, pb=pb, off=off, n=n: e.activation(
                out=M.sg_sb[:, off - NCTX:off - NCTX + n], in_=M.ps[pb][:, 0:n], func=AF.Silu),
                reads=[M.d_ps[pb]], writes=[M.d_rp])
    for g4 in range(5):
        pb = 4 + (g4 % 2)
        nk = 4 if g4 < 4 else 2
        for i in range(nk):
            proj_tm(P, M, M.ps[pb][:, i * 128:(i + 1) * 128], M.d_ps[pb], s_i, (g4 * 4 + i) * 128)
        P.act(lambda e, pb=pb, g4=g4, nk=nk: e.activation(
            out=M.i_sb[:, g4 * 4:g4 * 4 + nk, :].rearrange("p a n -> p (a n)"), in_=M.ps[pb][:, 0:nk * 128],
            func=AF.Copy), reads=[M.d_ps[pb]], writes=[M.d_rp])

    if after_burst is not None:
        after_burst()
    RSTG = int(os.environ.get("REC_STAGE", "9"))
    if RSTG < 2:
        return
    def prep_gen(dr):
        zsb = M.zf_sb if dr == 0 else M.zb_sb
        T = [t[:, dr * 256:(dr + 1) * 256] for t in M.tf]
        dT = [M.d_tfh[i][dr] for i in range(6)]
        n = 256
        for off in range(0, NTOK, 256):
            c0 = off // 128
            u = off // 256
            if dr == 0:
                qdst, kdst = M.qtF[:, off:off + n], M.ktF[:, off:off + n]
                wdeps = [M.d_qk[0]]
                rdeps = [M.d_rp, M.d_zfu[u]]
            else:
                qdst, kdst = M.zfb[:, u * 512:u * 512 + 256], M.zfb[:, u * 512 + 256:u * 512 + 512]
                wdeps = [M.d_qk[1], M.d_zfu[u]]
                rdeps = [M.d_rp]
            P.dve(lambda e, off=off: e.tensor_scalar(out=T[2], in0=zsb[:, off:off + n], scalar1=M.oml[:, dr, r:r + 1],
                                                     scalar2=None, op0=ALU.mult), reads=rdeps + [M.d_lb],
                  writes=[dT[2]])
            yield
            P.act(lambda e: e.activation(out=T[3], in_=T[2], func=AF.Ln, scale=-1.0, bias=1.0), reads=[dT[2]],
                  writes=[dT[3]])
            yield
            P.dve(lambda e: e.tensor_tensor_scan(out=T[4], data0=M.maskR[:, 0:n], data1=T[3], initial=0.0,
                                                 op0=ALU.mult, op1=ALU.add), reads=[dT[3], M.d_ones], writes=[dT[4]])
            yield
            pfv = T[4].rearrange("p (c t) -> p c t", t=128)
            if dr == 1:
                P.dve(lambda e: e.tensor_tensor(out=T[0], in0=T[4], in1=T[3], op=ALU.subtract),
                      reads=[dT[4], dT[3]], writes=[dT[0]])
                yield
            for ci in range(2):
                src = T[0] if dr == 1 else T[4]
                P.dve(lambda e, ci=ci, src=src: e.tensor_scalar(
                    out=T[5][:, ci * 128:(ci + 1) * 128], in0=src[:, ci * 128:(ci + 1) * 128],
                    scalar1=T[4][:, ci * 128 + 63:ci * 128 + 64], scalar2=(1.0 if dr == 0 else -1.0),
                    op0=ALU.subtract, op1=ALU.mult), reads=[dT[0], dT[4]], writes=[dT[5]])
                yield
            P.act(lambda e: e.activation(out=T[1], in_=T[5], func=AF.Exp), reads=[dT[5]], writes=[dT[1]])
            yield
            P.act(lambda e: e.activation(out=T[3], in_=T[5], func=AF.Exp, scale=-1.0), reads=[dT[5]], writes=[dT[3]])
            yield
            P.dve(lambda e, off=off, qdst=qdst: e.tensor_tensor(out=qdst, in0=M.q_sb[:, off:off + n], in1=T[1],
                                                                op=ALU.mult), reads=[M.d_rp, dT[1]], writes=wdeps)
            yield
            P.dve(lambda e, kdst=kdst: e.tensor_tensor(out=kdst, in0=T[2], in1=T[3], op=ALU.mult),
                  reads=[dT[2], dT[3]], writes=wdeps)
            yield
            svv = M.svall[:, dr, c0:c0 + 2, :]
            d_sva = M.d_svad[dr]
            P.dve(lambda e, svv=svv, pfv=pfv: e.tensor_tensor(out=svv[:, :, 0], in0=pfv[:, :, 127], in1=pfv[:, :, 63],
                                                              op=ALU.subtract), reads=[dT[4]], writes=[d_sva])
            yield
            P.act(lambda e, svv=svv, pfv=pfv: e.activation(out=svv[:, :, 1], in_=pfv[:, :, 63], func=AF.Exp),
                  reads=[dT[4]], writes=[d_sva])
            yield
            P.act(lambda e, svv=svv: e.activation(out=svv[:, :, 2], in_=svv[:, :, 0], func=AF.Exp),
                  reads=[d_sva], writes=[d_sva])
            yield
            P.act(lambda e, svv=svv, pfv=pfv: e.activation(out=svv[:, :, 3], in_=pfv[:, :, 127], func=AF.Exp),
                  reads=[dT[4]], writes=[d_sva])
            yield

    import itertools
    P.fence(M.d_tf, [d for pair in M.d_tfh for d in pair])
    for _ in itertools.zip_longest(prep_gen(0), prep_gen(1)):
        pass
    P.fence([d for pair in M.d_tfh for d in pair], M.d_tf)

    if RSTG < 3:
        return

    def chunk_gen(dr, step):
        if True:
            c = orders[dr][step]
            mask = M.maskf if dr == 0 else M.maskb
            S_, d_S = Ss[dr], dSs[dr]
            a = c * 128
            lat = c >= 2
            la = a - NCTX
            bx = dr * 4 + (step % 2) * 2
            by = bx + 1
            if dr == 0:
                qt_, kt_ = M.qtF[:, a:a + 128], M.ktF[:, a:a + 128]
            else:
                ub = (c // 2) * 512 + (c % 2) * 128
                qt_, kt_ = M.zfb[:, ub:ub + 128], M.zfb[:, ub + 256:ub + 384]
            d_qt = d_kt = M.d_qk[dr]
            vt, d_vt = M.i_sb[:, c, :], M.d_rp
            sv, d_sv = M.svall[:, dr, c, :], M.d_svad[dr]
            c_e1 = 1 if dr == 0 else 2
            c_e2 = 2 if dr == 0 else 1
            ktp = M.ps[bx][:, 384:448].bitcast(BF16)
            d_ktp = M.d_ps[bx]
            P.pe(lambda e, ktp=ktp, kt_=kt_: e.transpose(ktp, kt_, M.ident[:]), reads=[d_kt, M.d_const],
                 writes=[d_ktp])
            yield
            ktok, d_ktok = M.nrb()
            P.act(lambda e, ktok=ktok, ktp=ktp: e.activation(out=ktok[:], in_=ktp, func=AF.Copy),
                  reads=[d_ktp], writes=[d_ktok])
            yield
            if lat:
                atp, d_atp = M.ps[by][:, 0:128], M.d_ps[by]
                P.pe(lambda e, atp=atp, kt_=kt_, qt_=qt_: e.matmul(atp, lhsT=kt_, rhs=qt_, start=True, stop=True),
                     reads=[d_kt, d_qt], writes=[d_atp])
                yield
                am, d_am = M.nrb()
                P.dve(lambda e, am=am, atp=atp: e.tensor_tensor(out=am[:], in0=atp, in1=mask[:], op=ALU.mult),
                      reads=[d_atp, M.d_const], writes=[d_am])
                yield
                sp, d_sp = M.nrb()
                P.dve(lambda e, sp=sp, sv=sv: e.tensor_scalar(out=sp[:], in0=S_[:], scalar1=sv[:, c_e1:c_e1 + 1],
                                                              scalar2=None, op0=ALU.mult),
                      reads=[d_S, d_sv], writes=[d_sp])
                yield
                op_, d_op = M.ps[by][:, 128:256], M.d_ps[by]
                P.pe(lambda e, op_=op_, sp=sp, qt_=qt_: e.matmul(op_, lhsT=sp[:], rhs=qt_, start=True, stop=False),
                     reads=[d_sp, d_qt], writes=[d_op])
                yield
                P.pe(lambda e, op_=op_, vt=vt, am=am: e.matmul(op_, lhsT=vt, rhs=am[:], start=False, stop=True),
                     reads=[d_vt, d_am], writes=[d_op])
                yield
                P.act(lambda e, op_=op_, la=la: e.activation(out=obuf[dr][:, la:la + 128], in_=op_, func=AF.Copy),
                      reads=[d_op], writes=[d_obuf[dr]])
                yield
            kvp, d_kvp = M.ps[by][:, 256:384], M.d_ps[by]
            P.pe(lambda e, kvp=kvp, ktok=ktok, vt=vt: e.matmul(kvp, lhsT=ktok[:], rhs=vt, start=True, stop=True),
                 reads=[d_ktok, d_vt], writes=[d_kvp])
            yield
            tk, d_tk = M.nrf()
            P.dve(lambda e, tk=tk, kvp=kvp, sv=sv: e.tensor_scalar(out=tk[:], in0=kvp, scalar1=sv[:, c_e2:c_e2 + 1],
                                                                    scalar2=None, op0=ALU.mult),
                  reads=[d_kvp, d_sv], writes=[d_tk])
            yield
            P.dve(lambda e, tk=tk, sv=sv: e.scalar_tensor_tensor(out=S_[:], in0=S_[:], scalar=sv[:, 3:4],
                                                                 in1=tk[:], op0=ALU.mult, op1=ALU.add),
                  reads=[d_tk, d_sv, d_S], writes=[d_S])
            yield
    import itertools
    for step in range(18):
        gens = [chunk_gen(0, step), chunk_gen(1, step)]
        for _ in itertools.zip_longest(*gens):
            pass
    if RSTG < 4:
        return
    for t in range(4):
        lo = t * 512
        o_, d_o = M.ntf()
        P.dve(lambda e, o_=o_: e.tensor_tensor(out=o_[:], in0=M.ofw[:, lo:lo + 512], in1=M.obw[:, lo:lo + 512],
                                               op=ALU.add), reads=[M.d_ofw, M.d_obw], writes=[d_o])
        sq, d_sq = M.ntb()
        P.act(lambda e, sq=sq, o_=o_: e.activation(out=sq[:], in_=o_[:], func=AF.Square), reads=[d_o], writes=[d_sq])
        P.pe(lambda e, sq=sq: e.matmul(M.ps[0][:, :], lhsT=M.ones[:], rhs=sq[:], start=True, stop=True),
             reads=[d_sq, M.d_ones], writes=[M.d_ps[0]])
        ri, d_ri = M.ntf()
        emit_rsqrt(P, M, ri[:], d_ri, M.ps[0][:, :], M.d_ps[0], 512, 1.0 / 128)
        o2, d_o2 = M.ntf()
        P.dve(lambda e, o2=o2, o_=o_, ri=ri: e.scalar_tensor_tensor(
            out=o2[:], in0=o_[:], scalar=M.vec[:, 5:6], in1=ri[:], op0=ALU.mult, op1=ALU.mult),
            reads=[d_o, d_ri, M.d_vec], writes=[d_o2])
        o3, d_o3 = M.ntb()
        P.dve(lambda e, o3=o3, o2=o2: e.tensor_tensor(out=o3[:], in0=o2[:], in1=M.sg_sb[:, lo:lo + 512], op=ALU.mult),
              reads=[d_o2, M.d_rp], writes=[d_o3])
        rows = mergedT("rec", r) if callable(mergedT) else mergedT[512 + r * 128:512 + (r + 1) * 128, :]
        P.dma(rows[:, lo:lo + 512], o3[:], reads=[d_o3], writes=[d_merged], semkey="mg")


def emit_load_act16(P, C, srcT, d_src):
    for q in range(4):
        P.dma(C.A[:, q * 4:(q + 1) * 4, :], srcT[q * 512:(q + 1) * 512, :].rearrange("(c p) t -> p c t", p=128),
              reads=[d_src], writes=C.d_A + C.d_xt, semkey="act16")


def emit_conv_in(P, C, w_in, bgT, cvT, d_out, st, bnd=None, d_bnd=None):
    it = 0
    si = 0
    for fc in range(16):
        s = fc % 2
        wv = C.wgu[s][:].rearrange("p a c n -> p (a c n)").rearrange("p (q c n) -> p q c n", q=4, c=16)
        for q in range(3):
            src = w_in[:, q * 2048 + fc * 128: q * 2048 + (fc + 1) * 128].rearrange("(c p) n -> p c n", p=128)
            P.dma(wv[:, q, :, :], src, writes=[C.d_wgu[s]], semkey=f"wgu{s}", eng="pool")
        for ti, (off, n) in enumerate(C.tiles):
            pb = (it % 2) * 3
            it += 1
            for q in range(3):
                pt = C.ps[pb + q]
                for k in range(16):
                    P.pe(lambda e, pt=pt, q=q, k=k, wv=wv, off=off, n=n: e.matmul(
                        pt[:, 0:n], lhsT=wv[:, q, k, :], rhs=C.hy[:, k, off:off + n],
                        start=(k == 0), stop=(k == 15)),
                        reads=[C.d_wgu[s], C.d_h[ti]], writes=[C.d_ps[pb + q]])
            sb, d_sb = st[si % len(st)]
            si += 1
            P.act(lambda e, sb=sb, pb=pb, n=n: e.activation(out=sb[:, 0:n], in_=C.ps[pb][:, 0:n], func=AF.Copy),
                  reads=[C.d_ps[pb]], writes=[d_sb])
            P.dma(bgT[fc * 128:(fc + 1) * 128, off:off + n], sb[:, 0:n], reads=[d_sb], writes=[d_out],
                  semkey="cvo")
            tmp, d_tmp = C.next_tmp()
            P.act(lambda e, tmp=tmp, pb=pb, n=n: e.activation(out=tmp[:, 0:n], in_=C.ps[pb + 1][:, 0:n],
                                                              func=AF.Copy),
                  reads=[C.d_ps[pb + 1]], writes=[d_tmp])
            sb2, d_sb2 = st[si % len(st)]
            si += 1
            P.dve(lambda e, sb2=sb2, tmp=tmp, pb=pb, n=n: e.tensor_tensor(
                out=sb2[:, 0:n], in0=tmp[:, 0:n], in1=C.ps[pb + 2][:, 0:n], op=ALU.mult),
                reads=[d_tmp, C.d_ps[pb + 2]], writes=[d_sb2])
            P.dma(cvT[fc * 128:(fc + 1) * 128, off:off + n], sb2[:, 0:n], reads=[d_sb2], writes=[d_out],
                  semkey="cvo")
            if bnd is not None and off == 0:
                P.act(lambda e, sb2=sb2, fc=fc: e.activation(out=bnd[:, 0, fc:fc + 1], in_=sb2[:, 0:1], func=AF.Copy),
                      reads=[d_sb2], writes=[d_bnd])
            if bnd is not None and off + n == C.NT:
                P.act(lambda e, sb2=sb2, fc=fc, n=n: e.activation(out=bnd[:, 1, fc:fc + 1], in_=sb2[:, n - 1:n],
                                                                  func=AF.Copy),
                      reads=[d_sb2], writes=[d_bnd])


def emit_conv(P, C, bgT, cvhT, d_in, cw, d_cw, st):
    si = 0
    for c in range(16):
        for ti, (off, n) in enumerate(C.tiles):
            i1 = si % len(st)
            si += 1
            i2 = si % len(st)
            si += 1
            cvt, d_cvt = st[i1]
            bt, d_bt = st[i2]
            P.dma(cvt[:, 0:n + 2], cvhT[c * 128:(c + 1) * 128, off:off + n + 2], reads=[d_in], writes=[d_cvt],
                  semkey=f"cvi{i1}")
            P.dma(bt[:, 0:n], bgT[c * 128:(c + 1) * 128, off:off + n], reads=[d_in], writes=[d_bt],
                  semkey=f"cvi{i2}")
            u, d_u = C.next_tmp()
            P.dve(lambda e, u=u, cvt=cvt, c=c, n=n: e.tensor_scalar(
                out=u[:, 0:n], in0=cvt[:, 0:n], scalar1=cw[:, c, 0:1], scalar2=None, op0=ALU.mult),
                reads=[d_cvt, d_cw], writes=[d_u])
            P.dve(lambda e, u=u, cvt=cvt, c=c, n=n: e.scalar_tensor_tensor(
                out=u[:, 0:n], in0=cvt[:, 1:n + 1], scalar=cw[:, c, 1:2], in1=u[:, 0:n], op0=ALU.mult, op1=ALU.add),
                reads=[d_cvt, d_cw, d_u], writes=[d_u])
            P.dve(lambda e, u=u, cvt=cvt, c=c, n=n: e.scalar_tensor_tensor(
                out=u[:, 0:n], in0=cvt[:, 2:n + 2], scalar=cw[:, c, 2:3], in1=u[:, 0:n], op0=ALU.mult, op1=ALU.add),
                reads=[d_cvt, d_cw, d_u], writes=[d_u])
            P.dve(lambda e, u=u, bt=bt, c=c, off=off, n=n: e.tensor_tensor(
                out=C.A[:, c, off:off + n], in0=u[:, 0:n], in1=bt[:, 0:n], op=ALU.mult),
                reads=[d_u, d_bt], writes=[C.d_A[ti]] + C.d_xt)


def emit_load_mg_sel(P, C, mg_g, d_src, selv, d_sel):
    for hc in range(2):
        for c in range(16):
            kind, rk, hd = c // 8, (c // 4) % 2, c % 4
            k = kind * 2 + hd // 2
            r0 = rk * 256 + (hd % 2) * 128
            P.dma(C.A[:, hc * 16 + c, :], mg_g[k][r0:r0 + 128, hc * 1024:(hc + 1) * 1024],
                  reads=[d_src], writes=C.d_A + C.d_xt, semkey="act16")
    for c in range(16):
        P.dve(lambda e, c=c: e.tensor_scalar(out=C.A[:, c, :], in0=C.A[:, c, :], scalar1=selv[:, 0:1], scalar2=None,
                                             op0=ALU.mult), reads=C.d_A + [d_sel], writes=C.d_A)
        P.dve(lambda e, c=c: e.scalar_tensor_tensor(out=C.A[:, c, :], in0=C.A[:, 16 + c, :], scalar=selv[:, 1:2],
                                                    in1=C.A[:, c, :], op0=ALU.mult, op1=ALU.add),
              reads=C.d_A + [d_sel], writes=C.d_A)


def emit_conv_halo(P, C, bgT, cvT, d_in, bnd_g, d_bnd, selv, d_sel, cw, d_cw, st, hal, d_hal):
    P.dma(hal[:, 0, :], bnd_g[0:128, 16:32], reads=[d_bnd], writes=[d_hal], semkey="hal")
    P.dma(hal[:, 1, :], bnd_g[128:256, 0:16], reads=[d_bnd], writes=[d_hal], semkey="hal")
    P.dve(lambda e: e.tensor_scalar(out=hal[:, 0, :], in0=hal[:, 0, :], scalar1=selv[:, 2:3], scalar2=None,
                                    op0=ALU.mult), reads=[d_hal, d_sel], writes=[d_hal])
    P.dve(lambda e: e.tensor_scalar(out=hal[:, 1, :], in0=hal[:, 1, :], scalar1=selv[:, 3:4], scalar2=None,
                                    op0=ALU.mult), reads=[d_hal, d_sel], writes=[d_hal])
    si = 0
    NT = C.NT
    for c in range(16):
        for ti, (off, n) in enumerate(C.tiles):
            i1 = si % len(st)
            si += 1
            i2 = si % len(st)
            si += 1
            cvt, d_cvt = st[i1]
            bt, d_bt = st[i2]
            lo = max(off - 1, 0)
            hi = min(off + n + 1, NT)
            dlo = lo - (off - 1)
            P.dma(cvt[:, dlo:dlo + (hi - lo)], cvT[c * 128:(c + 1) * 128, lo:hi], reads=[d_in], writes=[d_cvt],
                  semkey=f"cvi{i1}")
            if off == 0:
                P.dve(lambda e, cvt=cvt, c=c: e.tensor_copy(out=cvt[:, 0:1], in_=hal[:, 0, c:c + 1]),
                      reads=[d_hal], writes=[d_cvt])
            if off + n == NT:
                P.dve(lambda e, cvt=cvt, c=c, n=n: e.tensor_copy(out=cvt[:, n + 1:n + 2], in_=hal[:, 1, c:c + 1]),
                      reads=[d_hal], writes=[d_cvt])
            P.dma(bt[:, 0:n], bgT[c * 128:(c + 1) * 128, off:off + n], reads=[d_in], writes=[d_bt],
                  semkey=f"cvi{i2}")
            u, d_u = C.next_tmp()
            P.dve(lambda e, u=u, cvt=cvt, c=c, n=n: e.tensor_scalar(
                out=u[:, 0:n], in0=cvt[:, 0:n], scalar1=cw[:, c, 0:1], scalar2=None, op0=ALU.mult),
                reads=[d_cvt, d_cw], writes=[d_u])
            P.dve(lambda e, u=u, cvt=cvt, c=c, n=n: e.scalar_tensor_tensor(
                out=u[:, 0:n], in0=cvt[:, 1:n + 1], scalar=cw[:, c, 1:2], in1=u[:, 0:n], op0=ALU.mult, op1=ALU.add),
                reads=[d_cvt, d_cw, d_u], writes=[d_u])
            P.dve(lambda e, u=u, cvt=cvt, c=c, n=n: e.scalar_tensor_tensor(
                out=u[:, 0:n], in0=cvt[:, 2:n + 2], scalar=cw[:, c, 2:3], in1=u[:, 0:n], op0=ALU.mult, op1=ALU.add),
                reads=[d_cvt, d_cw, d_u], writes=[d_u])
            P.dve(lambda e, u=u, bt=bt, c=c, off=off, n=n: e.tensor_tensor(
                out=C.A[:, c, off:off + n], in0=u[:, 0:n], in1=bt[:, 0:n], op=ALU.mult),
                reads=[d_u, d_bt], writes=[C.d_A[ti]] + C.d_xt)


BF = ml_dtypes.bfloat16
NCORES = 8


def _ffn_w(nc, tag):
    wg = nc.dram_tensor("wg" + tag, [2048, 5504], F32, kind="ExternalInput")
    wu = nc.dram_tensor("wu" + tag, [2048, 5504], F32, kind="ExternalInput")
    wd = nc.dram_tensor("wd" + tag, [5504, 2048], F32, kind="ExternalInput")
    return wg, wu, wd


def build_A():
    nc = bass.Bass("TRN2", target_bir_lowering=False)
    P = Prog(nc)
    xT = nc.dram_tensor("xT", [2048, 1024], F32, kind="ExternalInput")
    ctxT = nc.dram_tensor("ctxT", [2048, 128], F32, kind="ExternalInput")
    cvecT = nc.dram_tensor("cvecT", [128, 16, 2], F32, kind="ExternalInput")
    ada_w = nc.dram_tensor("ada_w", [2048, 18432], F32, kind="ExternalInput")
    ada_bT = nc.dram_tensor("ada_bT", [128, 144], F32, kind="ExternalInput")
    gTin = nc.dram_tensor("gTin", [128, 6, 16], F32, kind="ExternalInput")
    wg, wu, wd = _ffn_w(nc, "")
    x1T = nc.dram_tensor("x1T", [2048, 1024], F32, kind="ExternalOutput")
    hT = nc.dram_tensor("hT", [2048, 1152], BF16, kind="ExternalOutput")
    mTo = nc.dram_tensor("mTo", [128, 144, 2], F32, kind="ExternalOutput")
    xc1T = nc.dram_tensor("xc1T", [2048, 128], F32, kind="Internal")
    C = Ctx(P, [512, 512, 128])
    scr = {"cv": P.sbuf("cv", [128, 16, 2], F32), "sc": P.sbuf("sc", [128, 16, 2], BF16),
           "bT": P.sbuf("bT", [128, 144], F32)}
    P.dma(C.gT[:], gTin[:], writes=[C.d_gT], semkey="small3")
    emit_modulation(P, C, ada_w, ada_bT[:], cvecT[:], scr)
    d_in = Dep("in")
    d_x1 = [Dep("x1a"), Dep("x1b"), Dep("xc1")]
    emit_coefs(P, C, 1, 2, 0, 1, 0.5, [0, 1])
    srcs = [(xT[:, 0:512], d_in, 0), (xT[:, 512:1024], d_in, 0), (ctxT[:, :], d_in, 1)]
    dsts = [(x1T[:, 0:512], d_x1[0], 0), (x1T[:, 512:1024], d_x1[1], 0), (xc1T[:, :], d_x1[2], 1)]
    emit_ffn(P, C, srcs, dsts, wg, wu, wd, 0)
    emit_coefs(P, C, 4, None, 2, None, 1.0, [0, 1])
    emit_prenorm(P, C, dsts, 3, lambda ti: C.hy[:, :, C.tiles[ti][0]:C.tiles[ti][0] + C.tiles[ti][1]],
                 C.d_h, C.d_A)
    d_hT = Dep("hT")
    for ti, (off, n) in enumerate(C.tiles):
        P.dma(hT[:, off:off + n].rearrange("(c p) t -> p c t", p=128), C.hy[:, :, off:off + n],
              reads=[C.d_h[ti]], writes=[d_hT], semkey="hT")
    P.dma(mTo[:], C.mT[:], reads=[C.d_mT], writes=[Dep("mTo")], semkey="mTo")
    P.emit()
    return nc


def build_B():
    nc = bass.Bass("TRN2", target_bir_lowering=False)
    P = Prog(nc)
    hTf = nc.dram_tensor("hTf", [2048, NTOK], BF16, kind="ExternalInput")
    w_att = nc.dram_tensor("w_att", [4, 3, 2048, 128], F32, kind="ExternalInput")
    w_rec = nc.dram_tensor("w_rec", [4, 5, 2048, 128], F32, kind="ExternalInput")
    ropeC = nc.dram_tensor("ropeCin", [128, 2048], F32, kind="ExternalInput")
    ropeS = nc.dram_tensor("ropeSin", [128, 2048], F32, kind="ExternalInput")
    perm = nc.dram_tensor("permin", [128, 128], F32, kind="ExternalInput")
    ident = nc.dram_tensor("identin", [128, 128], F32, kind="ExternalInput")
    maskf = nc.dram_tensor("maskfin", [128, 128], F32, kind="ExternalInput")
    maskb = nc.dram_tensor("maskbin", [128, 128], F32, kind="ExternalInput")
    lamT = nc.dram_tensor("lamT", [64, 4], F32, kind="ExternalInput")
    dng = nc.dram_tensor("dng", [128, 1], F32, kind="ExternalInput")
    rng = nc.dram_tensor("rng", [128, 1], F32, kind="ExternalInput")
    lbraw = nc.dram_tensor("lbraw", [128, 2, 2, 4], F32, kind="ExternalInput")
    mergedT = nc.dram_tensor("mergedT", [1024, 2048], BF16, kind="ExternalOutput")
    M = MixCtx(P)
    emit_mix_setup(P, M, hTf, ropeC[:], ropeS[:], perm[:], ident[:], maskf[:], maskb[:], lamT[:], dng[:], rng[:],
                   lbraw[:])
    d_merged = Dep("merged")
    for hd in range(4):
        for i in range(3):
            load_w(P, M, i, w_att[hd, i])
        for i in range(5):
            load_w(P, M, 3 + i, w_rec[hd, i])
        emit_attention_head(P, M, hd, mergedT, d_merged)
        emit_rec_head(P, M, hd, mergedT, d_merged)
    P.emit()
    return nc


def build_C():
    nc = bass.Bass("TRN2", target_bir_lowering=False)
    P = Prog(nc)
    x1T = nc.dram_tensor("x1T", [2048, 1024], F32, kind="ExternalInput")
    mgT = nc.dram_tensor("mgT", [2048, 1024], BF16, kind="ExternalInput")
    w_out = nc.dram_tensor("w_out", [2048, 2048], F32, kind="ExternalInput")
    mT0 = nc.dram_tensor("mT0", [128, 144, 2], F32, kind="ExternalInput")
    gT0 = nc.dram_tensor("gT0", [128, 6, 16], F32, kind="ExternalInput")
    gT1 = nc.dram_tensor("gT1", [128, 6, 16], F32, kind="ExternalInput")
    cvecT = nc.dram_tensor("cvecT", [128, 16, 2], F32, kind="ExternalInput")
    ada_w = nc.dram_tensor("ada_w", [2048, 18432], F32, kind="ExternalInput")
    ada_bT = nc.dram_tensor("ada_bT", [128, 144], F32, kind="ExternalInput")
    wgA, wuA, wdA = _ffn_w(nc, "A")
    wgB, wuB, wdB = _ffn_w(nc, "B")
    cw_in = nc.dram_tensor("cw_in", [2048, 6144], F32, kind="ExternalInput")
    x4T = nc.dram_tensor("x4T", [2048, 1024], F32, kind="ExternalOutput")
    bgT = nc.dram_tensor("bgT", [2048, 1024], F32, kind="ExternalOutput")
    cvT = nc.dram_tensor("cvT", [2048, 1024], F32, kind="ExternalOutput")
    mT1 = nc.dram_tensor("mT1", [128, 144, 2], F32, kind="ExternalOutput")
    x2T = nc.dram_tensor("x2T", [2048, 1024], F32, kind="Internal")
    x3T = nc.dram_tensor("x3T", [2048, 1024], F32, kind="Internal")
    C = Ctx(P, [512, 512])
    scr = {"cv": P.sbuf("cv", [128, 16, 2], F32), "sc": P.sbuf("sc", [128, 16, 2], BF16),
           "bT": P.sbuf("bT", [128, 144], F32)}
    st = [(P.sbuf(f"st{i}", [128, 512], F32), Dep(f"st{i}")) for i in range(4)]
    d_in = Dep("in")

    def tl(t, d):
        return [(t[:, 0:512], d[0], 0), (t[:, 512:1024], d[1], 0)]
    d_x1 = [d_in, d_in]
    d_x2 = [Dep("x2a"), Dep("x2b")]
    d_x3 = [Dep("x3a"), Dep("x3b")]
    d_x4 = [Dep("x4a"), Dep("x4b")]
    P.dma(C.gT[:], gT0[:], writes=[C.d_gT], semkey="small3")
    P.dma(C.mT[:], mT0[:], writes=[C.d_mT], semkey="small4")
    emit_load_act16(P, C, mgT, d_in)
    emit_coefs(P, C, 4, 5, 2, 3, 1.0, [0])
    emit_down_residual(P, C, 16, w_out, tl(x1T, d_x1), tl(x2T, d_x2))
    emit_coefs(P, C, 7, 8, 4, 5, 0.5, [0])
    emit_ffn(P, C, tl(x2T, d_x2), tl(x3T, d_x3), wgA, wuA, wdA, 6)
    P.dma(C.gT[:], gT1[:], writes=[C.d_gT], semkey="small3")
    emit_modulation(P, C, ada_w, ada_bT[:], cvecT[:], scr)
    emit_coefs(P, C, 1, 2, 0, 1, 0.5, [0])
    emit_ffn(P, C, tl(x3T, d_x3), tl(x4T, d_x4), wgB, wuB, wdB, 0)
    emit_coefs(P, C, 4, None, 2, None, 1.0, [0])
    emit_prenorm(P, C, tl(x4T, d_x4), 3, lambda ti: C.hy[:, :, C.tiles[ti][0]:C.tiles[ti][0] + C.tiles[ti][1]],
                 C.d_h, C.d_A)
    emit_conv_in(P, C, cw_in, bgT, cvT, Dep("cvout"), st)
    P.dma(mT1[:], C.mT[:], reads=[C.d_mT], writes=[Dep("mT1o")], semkey="mTo")
    P.emit()
    return nc


def build_D():
    nc = bass.Bass("TRN2", target_bir_lowering=False)
    P = Prog(nc)
    x4T = nc.dram_tensor("x4T", [2048, 1024], F32, kind="ExternalInput")
    bgT = nc.dram_tensor("bgT", [2048, 1024], F32, kind="ExternalInput")
    cvhT = nc.dram_tensor("cvhT", [2048, 1026], F32, kind="ExternalInput")
    cwT = nc.dram_tensor("cwT", [128, 16, 3], F32, kind="ExternalInput")
    cw_out = nc.dram_tensor("cw_out", [2048, 2048], F32, kind="ExternalInput")
    mT1 = nc.dram_tensor("mT1", [128, 144, 2], F32, kind="ExternalInput")
    gT1 = nc.dram_tensor("gT1", [128, 6, 16], F32, kind="ExternalInput")
    wg, wu, wd = _ffn_w(nc, "")
    outT = nc.dram_tensor("outT", [2048, 1024], F32, kind="ExternalOutput")
    x5T = nc.dram_tensor("x5T", [2048, 1024], F32, kind="Internal")
    C = Ctx(P, [512, 512])
    st = [(P.sbuf(f"st{i}", [128, 514], F32), Dep(f"st{i}")) for i in range(4)]
    cw = P.sbuf("cw", [128, 16, 3], F32)
    d_cw = Dep("cw")
    d_in = Dep("in")

    def tl(t, d):
        return [(t[:, 0:512], d[0], 0), (t[:, 512:1024], d[1], 0)]
    d_x5 = [Dep("x5a"), Dep("x5b")]
    d_o = [Dep("oa"), Dep("ob")]
    P.dma(C.gT[:], gT1[:], writes=[C.d_gT], semkey="small3")
    P.dma(C.mT[:], mT1[:], writes=[C.d_mT], semkey="small4")
    P.dma(cw[:], cwT[:], writes=[d_cw], semkey="small5")
    emit_conv(P, C, bgT, cvhT, d_in, cw, d_cw, st)
    emit_coefs(P, C, 4, 5, 2, 3, 1.0, [0])
    emit_down_residual(P, C, 16, cw_out, tl(x4T, [d_in, d_in]), tl(x5T, d_x5))
    emit_coefs(P, C, 7, 8, 4, 5, 0.5, [0])
    emit_ffn(P, C, tl(x5T, d_x5), tl(outT, d_o), wg, wu, wd, 6)
    P.emit()
    return nc


ARENA_BYTES = 207 * 1024
FUSE_STOP = int(os.environ.get("FUSE_STOP", "0"))


def build_fused():
    nc = bass.Bass("TRN2", target_bir_lowering=False)
    P = Prog(nc)
    P.use_arena(ARENA_BYTES)
    inp = lambda n, sh, dt=F32: nc.dram_tensor(n, sh, dt, kind="ExternalInput")
    xT = inp("xT", [2048, 1024])
    ctxT = inp("ctxT", [2048, 128])
    ada_sl = inp("ada_sl", [2, 2048, 9216])
    cvecT = inp("cvecT", [128, 16, 2])
    bT0 = inp("bT0", [128, 144])
    bT1 = inp("bT1", [128, 144])
    gT0 = inp("gT0", [128, 6, 16])
    gT1 = inp("gT1", [128, 6, 16])
    W = [_ffn_w(nc, str(i)) for i in range(4)]
    w_att = inp("w_att", [4, 3, 2048, 128])
    w_rec = inp("w_rec", [4, 5, 2048, 128])
    ropeC = inp("ropeCin", [128, 2048])
    ropeS = inp("ropeSin", [128, 2048])
    perm = inp("permin", [128, 128])
    ident = inp("identin", [128, 128])
    maskf = inp("maskfin", [128, 128])
    maskb = inp("maskbin", [128, 128])
    lamT = inp("lamT", [64, 4])
    dng = inp("dng", [128, 1])
    rng = inp("rng", [128, 1])
    lbraw = inp("lbraw", [128, 2, 2, 4])
    w_out = inp("w_out", [2048, 2048])
    cw_in = inp("cw_in", [2048, 6144])
    cwT = inp("cwT", [128, 16, 3])
    cw_out = inp("cw_out", [2048, 2048])
    selin = inp("selin", [128, 4])
    outT = nc.dram_tensor("outT", [2048, 1024], F32, kind="ExternalOutput")
    itn = lambda n, sh, dt=F32: nc.dram_tensor(n, sh, dt, kind="Internal")
    x1T, x2T, x3T, x4T, x5T = [itn(f"x{i}T", [2048, 1024]) for i in (1, 2, 3, 4, 5)]
    xc1T = itn("xc1T", [2048, 128])
    hT_own = [itn(f"hT_own{t}", [2048, n_], BF16) for t, n_ in enumerate((512, 512, 128))]
    hT_g = [itn(f"hT_g{t}", [4096, n_], BF16) for t, n_ in enumerate((512, 512, 128))]
    mg_own = [itn(f"mg_own{q}", [256, 2048], BF16) for q in range(4)]
    mg_g = [itn(f"mg_g{q}", [512, 2048], BF16) for q in range(4)]
    bgT = itn("bgT", [2048, 1024])
    cvT = itn("cvT", [2048, 1024])
    bnd_own = itn("bnd_own", [128, 32])
    bnd_g = itn("bnd_g", [256, 32])
    d_in = Dep("in")

    def tl(t, d):
        return [(t[:, 0:512], d[0], 0), (t[:, 512:1024], d[1], 0)]

    def mk_scr():
        return {"cv": P.sbuf("cv", [128, 16, 2], F32), "sc": P.sbuf("sc", [128, 16, 2], BF16),
                "bT": P.sbuf("bT", [128, 144], F32)}
    hyv = lambda C: (lambda ti: C.hy[:, :, C.tiles[ti][0]:C.tiles[ti][0] + C.tiles[ti][1]])

    mp_own = itn("mp_own", [128, 288])
    mp_g = itn("mp_g", [256, 288])
    mTd = [itn("mT0d", [128, 144, 2]), itn("mT1d", [128, 144, 2])]
    d_mTd = [Dep("mT0d"), Dep("mT1d")]
    C = Ctx(P, [512, 512])
    emit_modulation_sharded(P, C, ada_sl, cvecT[:], bT0[:], bT1[:], mp_own, mp_g, mTd, d_mTd)
    P.barrier()
    P.aoff = 0
    C = Ctx(P, [512, 512, 128])
    P.dma(C.gT[:], gT0[:], writes=[C.d_gT], semkey="small3")
    P.dma(C.mT[:], mTd[0][:], reads=[d_mTd[0]], writes=[C.d_mT], semkey="small4")
    d_x1 = [Dep("x1a"), Dep("x1b"), Dep("xc1")]
    emit_coefs(P, C, 1, 2, 0, 1, 0.5, [0, 1])
    srcs = [(xT[:, 0:512], d_in, 0), (xT[:, 512:1024], d_in, 0), (ctxT[:, :], d_in, 1)]
    dsts = [(x1T[:, 0:512], d_x1[0], 0), (x1T[:, 512:1024], d_x1[1], 0), (xc1T[:, :], d_x1[2], 1)]
    emit_ffn(P, C, srcs, dsts, W[0][0], W[0][1], W[0][2], 0)
    emit_coefs(P, C, 4, None, 2, None, 1.0, [0, 1])
    d_hg = Dep("hT_g")

    def ship_tile(ti, off, n):
        d_t = Dep(f"hT_own{ti}")
        P.dma(hT_own[ti][:, :].rearrange("(c p) t -> p c t", p=128), C.hy[:, :, off:off + n],
              reads=[C.d_h[ti]], writes=[d_t], semkey=f"hT{ti}")
        P.allgather_pairs(hT_g[ti], hT_own[ti], reads=[d_t], writes=[d_hg], semkey="cc1")
    emit_prenorm(P, C, dsts, 3, hyv(C), C.d_h, C.d_A, resident=True, after_tile=ship_tile)
    P.barrier()
    P.aoff = 0
    M = MixCtx(P)
    for r in range(2):
        for q in range(4):
            rs = slice(r * 2048 + q * 512, r * 2048 + (q + 1) * 512)
            P.dma(M.hs[:, q * 4:(q + 1) * 4, r * 128:(r + 1) * 128],
                  hT_g[2][rs, :].rearrange("(c p) t -> p c t", p=128), reads=[d_hg], writes=[M.d_hs], semkey="hs")
            for ti in range(2):
                c0 = 256 + r * 1024 + ti * 512
                P.dma(M.hs[:, q * 4:(q + 1) * 4, c0:c0 + 512],
                      hT_g[ti][rs, :].rearrange("(c p) t -> p c t", p=128), reads=[d_hg], writes=[M.d_hs],
                      semkey="hs")
    emit_mix_setup(P, M, None, ropeC[:], ropeS[:], perm[:], ident[:], maskf[:], maskb[:], lamT[:], dng[:], rng[:],
                   lbraw[:])
    d_mg = Dep("mg_own")
    d_mgg = Dep("mg_g")
    for i in range(3):
        load_w(P, M, i, w_att[0, i])
    for i in range(5):
        load_w(P, M, 3 + i, w_rec[0, i])
    for hd in range(4):
        mgdst = lambda kind, h: mg_own[(0 if kind == "att" else 2) + h // 2][(h % 2) * 128:(h % 2) * 128 + 128, :]
        emit_attention_head(P, M, hd, mgdst, d_mg)
        P.barrier()

        def prefetch(hd=hd):
            if hd + 1 < 4:
                for i in range(3):
                    load_w(P, M, i, w_att[hd + 1, i])
                for i in range(5):
                    load_w(P, M, 3 + i, w_rec[hd + 1, i])
        emit_rec_head(P, M, hd, mgdst, d_mg, after_burst=prefetch)
        P.barrier()
        if hd % 2 == 1:
            for q in (hd // 2, 2 + hd // 2):
                P.allgather_pairs(mg_g[q], mg_own[q], reads=[d_mg], writes=[d_mgg], semkey="cc2")
    if FUSE_STOP == 2:
        dbg = nc.dram_tensor("dbg", [2048, 2048], BF16, kind="ExternalOutput")
        for q in range(4):
            for r in range(2):
                base = r * 1024 + (q // 2) * 512 + (q % 2) * 256
                P.dma(dbg[base:base + 256, :], mg_g[q][r * 256:(r + 1) * 256, :], reads=[d_mgg],
                      writes=[Dep("dbg")], semkey="dbg")
        P.emit()
        return nc
    P.barrier()
    P.aoff = 0
    C = Ctx(P, [512, 512])
    selv = P.sbuf("selv", [128, 4], F32)
    d_sel = Dep("selv")
    st = [(P.sbuf(f"st{i}", [128, 514], F32), Dep(f"st{i}")) for i in range(4)]
    cw = P.sbuf("cw", [128, 16, 3], F32)
    d_cw = Dep("cw")
    hal = P.sbuf("hal", [128, 2, 16], F32)
    d_hal = Dep("hal")
    P.dma(selv[:], selin[:], writes=[d_sel], semkey="small5")
    P.dma(cw[:], cwT[:], writes=[d_cw], semkey="small5")
    P.dma(C.gT[:], gT0[:], writes=[C.d_gT], semkey="small3")
    P.dma(C.mT[:], mTd[0][:], reads=[d_mTd[0]], writes=[C.d_mT], semkey="small4")
    d_x2 = [Dep("x2a"), Dep("x2b")]
    d_x3 = [Dep("x3a"), Dep("x3b")]
    d_x4 = [Dep("x4a"), Dep("x4b")]
    d_x5 = [Dep("x5a"), Dep("x5b")]
    d_o = [Dep("oa"), Dep("ob")]
    emit_load_mg_sel(P, C, mg_g, d_mgg, selv, d_sel)
    emit_coefs(P, C, 4, 5, 2, 3, 1.0, [0])
    emit_down_residual(P, C, 16, w_out, tl(x1T, d_x1), tl(x2T, d_x2))
    emit_coefs(P, C, 7, 8, 4, 5, 0.5, [0])
    emit_ffn(P, C, tl(x2T, d_x2), tl(x3T, d_x3), W[1][0], W[1][1], W[1][2], 6, resident=True)
    P.dma(C.gT[:], gT1[:], writes=[C.d_gT], semkey="small3")
    P.dma(C.mT[:], mTd[1][:], reads=[d_mTd[1]], writes=[C.d_mT], semkey="small4")
    emit_coefs(P, C, 1, 2, 0, 1, 0.5, [0])
    emit_ffn(P, C, tl(x3T, d_x3), tl(x4T, d_x4), W[2][0], W[2][1], W[2][2], 0, resident=True)
    emit_coefs(P, C, 4, None, 2, None, 1.0, [0])
    emit_prenorm(P, C, tl(x4T, d_x4), 3, hyv(C), C.d_h, C.d_A, resident=True)
    d_cv = Dep("cvout")
    st512 = [(t[:, 0:512], d) for (t, d) in st]
    bnd_sb = P.sbuf("bnd_sb", [128, 2, 16], F32)
    d_bsb = Dep("bnd_sb")
    emit_conv_in(P, C, cw_in, bgT, cvT, d_cv, st512, bnd_sb, d_bsb)
    d_bo = Dep("bnd_own")
    P.dma(bnd_own[:, :], bnd_sb[:].rearrange("p a c -> p (a c)"), reads=[d_bsb], writes=[d_bo], semkey="bnd")
    d_bg = Dep("bnd_g")
    P.allgather_pairs(bnd_g, bnd_own, reads=[d_bo], writes=[d_bg], semkey="cc3")
    emit_conv_halo(P, C, bgT, cvT, d_cv, bnd_g, d_bg, selv, d_sel, cw, d_cw, st, hal, d_hal)
    emit_coefs(P, C, 4, 5, 2, 3, 1.0, [0])
    emit_down_residual(P, C, 16, cw_out, tl(x4T, d_x4), tl(x5T, d_x5))
    emit_coefs(P, C, 7, 8, 4, 5, 0.5, [0])
    emit_ffn(P, C, tl(x5T, d_x5), tl(outT, d_o), W[3][0], W[3][1], W[3][2], 6, resident=True)
    stuck = simulate_sync(P)
    if stuck:
        raise RuntimeError(f"sync deadlock: {stuck}")
    P.emit()
    return nc


def mix_consts():
    p = np.arange(128)
    d = p % 64
    i = d % 16
    freqs = (10000.0 ** (-np.arange(16, dtype=np.float32) / 16)).astype(np.float32)
    t = np.arange(2048)
    row = (t // 64).astype(np.float32)
    col = (t % 64).astype(np.float32)
    pos = np.where((d < 32)[:, None], row[None, :], col[None, :]).astype(np.float32)
    ang = (pos * freqs[i][:, None]).astype(np.float32)
    C = np.cos(ang).astype(np.float32)
    S = np.sin(ang).astype(np.float32)
    perm = np.zeros((128, 128), np.float32)
    for m in range(128):
        if (m % 32) < 16:
            perm[m + 16, m] = -1.0
        else:
            perm[m - 16, m] = 1.0
    ident = np.eye(128, dtype=np.float32)
    s = np.arange(128)[:, None]
    tt = np.arange(128)[None, :]
    return C, S, perm, ident, (s <= tt).astype(np.float32), (s >= tt).astype(np.float32)


_PROGS = {}


def _prog(name, fn):
    if name not in _PROGS:
        _PROGS[name] = fn()
    return _PROGS[name]


def _run(nc, in_maps):
    res = run_bass_kernel_spmd(nc, in_maps, core_ids=list(range(NCORES)))
    return res.results


def kernel(x, c, ctx, c_ctx, ada_w, ada_b, norm_g, ffn_w_gate, ffn_w_up, ffn_w_down, mix_w_in, mix_w_out,
           diff_lambda, diff_norm_g, rec_norm_g, rec_lb, conv_w_in, conv_w, conv_w_out):
    f32 = lambda a: np.ascontiguousarray(np.asarray(a, dtype=np.float32))
    x, c, ctx, c_ctx = f32(x), f32(c), f32(ctx), f32(c_ctx)
    ada_w, ada_b, norm_g = f32(ada_w), f32(ada_b), f32(norm_g)
    ffn_w_gate, ffn_w_up, ffn_w_down = f32(ffn_w_gate), f32(ffn_w_up), f32(ffn_w_down)
    mix_w_in, mix_w_out = f32(mix_w_in), f32(mix_w_out)
    conv_w_in, conv_w, conv_w_out = f32(conv_w_in), f32(conv_w), f32(conv_w_out)
    rec_lb = f32(rec_lb)
    gT = [np.ascontiguousarray(norm_g[l].reshape(6, 16, 128).transpose(2, 0, 1)) for l in range(2)]
    bT = [np.ascontiguousarray(ada_b[l].reshape(144, 128).T) for l in range(2)]
    cvec = [np.ascontiguousarray(np.stack([c[b], c_ctx], -1).reshape(16, 128, 2).transpose(1, 0, 2))
            for b in range(4)]
    Cc, Ss, perm, ident, maskf, maskb = mix_consts()
    w_in = mix_w_in[0]
    cwT = np.ascontiguousarray(conv_w[0].reshape(3, 16, 128).transpose(2, 1, 0))
    lamT = np.ascontiguousarray(f32(diff_lambda)[0].T)
    dng = f32(diff_norm_g)[0].reshape(128, 1).copy()
    rngv = f32(rec_norm_g)[0].reshape(128, 1).copy()
    heads = []
    for hh in range(2):
        w_att = np.empty((4, 3, 2048, 128), np.float32)
        w_rec = np.empty((4, 5, 2048, 128), np.float32)
        lbraw = np.empty((128, 2, 2, 4), np.float32)
        for hd in range(4):
            g = hh * 4 + hd
            for q in range(3):
                w_att[hd, q] = w_in[:, q * 1024 + g * 128: q * 1024 + (g + 1) * 128]
            for q in range(5):
                w_rec[hd, q] = w_in[:, 3072 + q * 1024 + g * 128: 3072 + q * 1024 + (g + 1) * 128]
            lbraw[:, :, :, hd] = rec_lb[:, :, g * 128:(g + 1) * 128].transpose(2, 0, 1)
        heads.append((w_att, w_rec, lbraw))
    ada_sl = [np.ascontiguousarray(ada_w[:, :, hh * 9216:(hh + 1) * 9216]) for hh in range(2)]
    in_maps = []
    for i in range(NCORES):
        b, h = i // 2, i % 2
        sel = np.zeros((128, 4), np.float32)
        sel[:, 0] = 1.0 if h == 0 else 0.0
        sel[:, 1] = 1.0 if h == 1 else 0.0
        sel[:, 2] = 1.0 if h == 1 else 0.0
        sel[:, 3] = 1.0 if h == 0 else 0.0
        m = {
            "xT": np.ascontiguousarray(x[b, h * 1024:(h + 1) * 1024].T),
            "ctxT": np.ascontiguousarray(ctx[b, h * 128:(h + 1) * 128].T),
            "ada_sl": ada_sl[h], "cvecT": cvec[b],
            "bT0": bT[0], "bT1": bT[1],
            "gT0": gT[0], "gT1": gT[1],
            "w_att": heads[h][0], "w_rec": heads[h][1], "lbraw": heads[h][2],
            "ropeCin": Cc, "ropeSin": Ss, "permin": perm, "identin": ident, "maskfin": maskf, "maskbin": maskb,
            "lamT": lamT, "dng": dng, "rng": rngv,
            "w_out": mix_w_out[0], "cw_in": conv_w_in[0], "cwT": cwT, "cw_out": conv_w_out[0], "selin": sel}
        for k, (l, j) in enumerate([(0, 0), (0, 1), (1, 0), (1, 1)]):
            m[f"wg{k}"] = ffn_w_gate[l, j]
            m[f"wu{k}"] = ffn_w_up[l, j]
            m[f"wd{k}"] = ffn_w_down[l, j]
        in_maps.append(m)
    rD = _run(_prog("F", build_fused), in_maps)
    if FUSE_STOP:
        return rD
    out = np.empty((4, 2048, 2048), np.float32)
    for i in range(NCORES):
        b, h = i // 2, i % 2
        out[b, h * 1024:(h + 1) * 1024] = np.asarray(rD[i]["outT"]).T
    return out
```

```python
import contextlib
import types
import os
import math
import numpy as np
import ml_dtypes
import concourse.bass as bass
import concourse.mybir as mybir
from concourse.bass_utils import run_bass_kernel_spmd


F32 = mybir.dt.float32
BF16 = mybir.dt.bfloat16
ALU = mybir.AluOpType
AF = mybir.ActivationFunctionType
AX = mybir.AxisListType


class Dep:
    __slots__ = ("name", "w", "r", "excl")

    def __init__(self, name, excl=False):
        self.name = name
        self.w = {}
        self.r = {}
        self.excl = excl


class Ins:
    __slots__ = ("eng", "fn", "deps", "signal", "semkey", "semval", "is_dma", "idx", "inc")
    _n = 0

    def __init__(self, eng, fn, is_dma=False, semkey=None):
        self.eng = eng
        self.fn = fn
        self.deps = []
        self.signal = False
        self.is_dma = is_dma
        self.semkey = semkey
        self.semval = None
        self.inc = 16
        Ins._n += 1
        self.idx = Ins._n


def _freeze(fn):
    if getattr(fn, "__closure__", None) is None:
        return fn
    cells = []
    for c in fn.__closure__:
        try:
            cells.append(types.CellType(c.cell_contents))
        except ValueError:
            cells.append(c)
    return types.FunctionType(fn.__code__, fn.__globals__, fn.__name__, fn.__defaults__, tuple(cells))


class Prog:
    ENGS = ("pe", "act", "dve", "pool", "sp")

    def __init__(self, nc):
        self.nc = nc
        self.streams = {e: [] for e in self.ENGS}
        self.stack = contextlib.ExitStack()
        self.dma_keys = {}
        self.n_sb = 0
        self.arena = None
        self.aoff = 0
        self.pending = {e: [] for e in self.ENGS}
        self.open_dmas = []
        self.ps = None
        self.d_ps = None

    def use_arena(self, nbytes):
        self.arena = self.stack.enter_context(self.nc.sbuf_tensor("arena", [128, nbytes], mybir.dt.uint8))
        self.asize = nbytes
        self.aoff = 0

    def shared_psum(self):
        if self.ps is None:
            self.ps = [self.psum(f"ps{i}", [128, 512]) for i in range(8)]
            self.d_ps = [Dep(f"ps{i}", excl=True) for i in range(8)]
        return self.ps, self.d_ps

    def barrier(self):
        lasts = []
        for e in ("pe", "act", "dve", "pool"):
            for ins in reversed(self.streams[e]):
                if not ins.is_dma:
                    lasts.append(ins)
                    break
        lasts += self.open_dmas
        self.open_dmas = []
        for d in lasts:
            d.signal = True
        for e in self.ENGS:
            self.pending[e] = list(lasts)

    def sbuf(self, name, shape, dtype):
        if self.arena is None:
            return self.stack.enter_context(self.nc.sbuf_tensor(name, list(shape), dtype))
        esz = {F32: 4, BF16: 2}[dtype]
        nel = 1
        for s_ in shape[1:]:
            nel *= s_
        nbytes = nel * esz
        off = (self.aoff + 63) // 64 * 64
        if off + nbytes > self.asize:
            raise MemoryError(f"SBUF arena overflow allocating {name}: {off}+{nbytes} > {self.asize}")
        self.aoff = off + nbytes
        v = self.arena[0:shape[0], off:off + nbytes].bitcast(dtype)
        if len(shape) == 3:
            v = v.rearrange("p (a b) -> p a b", a=shape[1])
        elif len(shape) == 4:
            v = v.rearrange("p (a b c) -> p a b c", a=shape[1], b=shape[2])
        return v

    def psum(self, name, shape, dtype=F32):
        return self.stack.enter_context(self.nc.psum_tensor(name, list(shape), dtype))

    def dram(self, name, shape, dtype, kind="Internal"):
        return self.nc.dram_tensor(name, list(shape), dtype, kind=kind)

    def op(self, eng, fn, reads=(), writes=(), is_dma=False, semkey=None, inc=16):
        ins = Ins(eng, _freeze(fn), is_dma, semkey)
        ins.inc = inc
        key = ("d", id(ins)) if is_dma else eng
        deps = {}
        for t in reads:
            for k, d in t.w.items():
                deps[id(d)] = d
            if t.excl:
                for k, d in t.r.items():
                    if not is_dma and k == eng:
                        continue
                    deps[id(d)] = d
        for t in writes:
            for k, d in t.r.items():
                if not is_dma and k == eng:
                    continue
                deps[id(d)] = d
            for k, d in t.w.items():
                if not is_dma and k == eng:
                    continue
                deps[id(d)] = d
        if self.pending[eng]:
            for d in self.pending[eng]:
                if d.is_dma or d.eng != eng:
                    deps[id(d)] = d
            self.pending[eng] = []
        for d in deps.values():
            d.signal = True
        ins.deps = list(deps.values())
        for t in reads:
            t.r[key] = ins
        for t in writes:
            if t.r:
                t.r = {}
                t.w = {}
            t.w[key] = ins
        if is_dma:
            ins.signal = True
            if semkey is None:
                raise ValueError("dma needs semkey")
            self.dma_keys.setdefault(semkey, 0)
            self.open_dmas.append(ins)
        self.streams[eng].append(ins)
        return ins

    def fence(self, srcs, dsts):
        for sd in srcs:
            for dd in dsts:
                for k, i in sd.w.items():
                    if k not in dd.w or dd.w[k].idx < i.idx:
                        dd.w[k] = i
                for k, i in sd.r.items():
                    if k not in dd.r or dd.r[k].idx < i.idx:
                        dd.r[k] = i

    def pe(self, fn, reads=(), writes=()):
        return self.op("pe", fn, reads, writes)

    def act(self, fn, reads=(), writes=()):
        return self.op("act", fn, reads, writes)

    def dve(self, fn, reads=(), writes=()):
        return self.op("dve", fn, reads, writes)

    def pool(self, fn, reads=(), writes=()):
        return self.op("pool", fn, reads, writes)

    def dma(self, out, in_, reads=(), writes=(), semkey=None, eng="sp", **kw):
        return self.op(eng, lambda e: e.dma_start(out=out, in_=in_, **kw), reads, writes,
                       is_dma=True, semkey=semkey)

    def allgather_pairs(self, out_t, in_t, reads=(), writes=(), semkey="cc"):
        return self.op("pool", lambda e: e.collective_compute(
            "AllGather", ALU.bypass, replica_groups=[[0, 1], [2, 3], [4, 5], [6, 7]],
            ins=[in_t.ap().opt()], outs=[out_t.ap().opt()]), reads, writes, is_dma=True, semkey=semkey, inc=1)

    def allgather_far(self, out_t, in_t, reads=(), writes=(), semkey="ccf"):
        return self.op("pool", lambda e: e.collective_compute(
            "AllGather", ALU.bypass, replica_groups=[[0, 4], [1, 5], [2, 6], [3, 7]],
            ins=[in_t.ap().opt()], outs=[out_t.ap().opt()]), reads, writes, is_dma=True, semkey=semkey, inc=1)

    def allreduce_all(self, out_t, in_t, reads=(), writes=(), semkey="ar"):
        return self.op("pool", lambda e: e.collective_compute(
            "AllReduce", ALU.add, replica_groups=[list(range(8))],
            ins=[in_t.ap().opt()], outs=[out_t.ap().opt()]), reads, writes, is_dma=True, semkey=semkey, inc=1)

    def emit(self):
        nc = self.nc
        st = self.stack
        sems = {}
        for e in ("pe", "act", "dve", "pool"):
            sems[e] = st.enter_context(nc.semaphore("s_" + e))
        for k in self.dma_keys:
            sems[("d", k)] = st.enter_context(nc.semaphore("d_" + str(k)))
        for e in self.ENGS:
            cnt = 0
            for ins in self.streams[e]:
                if ins.is_dma:
                    self.dma_keys[ins.semkey] += ins.inc
                    ins.semval = self.dma_keys[ins.semkey]
                elif ins.signal:
                    cnt += 1
                    ins.semval = cnt
        final_dma = dict(self.dma_keys)

        def run(eng_name, e):
            seen = {}
            for ins in self.streams[eng_name]:
                need = {}
                for d in ins.deps:
                    sk = ("d", d.semkey) if d.is_dma else d.eng
                    if d.semval > need.get(sk, 0):
                        need[sk] = d.semval
                for sk, v in need.items():
                    if seen.get(sk, 0) >= v:
                        continue
                    e.wait_ge(sems[sk], v)
                    seen[sk] = v
                bi = ins.fn(e)
                if ins.is_dma:
                    bi.then_inc(sems[("d", ins.semkey)], ins.inc)
                elif ins.signal:
                    bi.then_inc(sems[eng_name], 1)
            if eng_name == "sp":
                for k, v in final_dma.items():
                    if v > 0 and seen.get(("d", k), 0) < v:
                        e.wait_ge(sems[("d", k)], v)

        with nc.Block() as block:
            @block.tensor
            def _(e):
                run("pe", e)

            @block.scalar
            def _(e):
                run("act", e)

            @block.vector
            def _(e):
                run("dve", e)

            @block.gpsimd
            def _(e):
                run("pool", e)

            @block.sync
            def _(e):
                run("sp", e)
        st.close()


def simulate_sync(P):
    keys = dict.fromkeys(P.dma_keys, 0)
    for e in P.ENGS:
        cnt = 0
        for ins in P.streams[e]:
            if ins.is_dma:
                keys[ins.semkey] += ins.inc
                ins.semval = keys[ins.semkey]
            elif ins.signal:
                cnt += 1
                ins.semval = cnt
    sem = {}
    pc = {e: 0 for e in P.ENGS}
    progress = True
    while progress:
        progress = False
        for e in P.ENGS:
            st = P.streams[e]
            while pc[e] < len(st):
                ins = st[pc[e]]
                ok = True
                for d in ins.deps:
                    sk = ("d", d.semkey) if d.is_dma else d.eng
                    if sem.get(sk, 0) < d.semval:
                        ok = False
                        break
                if not ok:
                    break
                if ins.is_dma:
                    sem[("d", ins.semkey)] = sem.get(("d", ins.semkey), 0) + ins.inc
                elif ins.signal:
                    sem[e] = sem.get(e, 0) + 1
                pc[e] += 1
                progress = True
    stuck = {e: (pc[e], len(P.streams[e])) for e in P.ENGS if pc[e] < len(P.streams[e])}
    return stuck


EPS = 1e-6
D = 2048
DFF = 5504
NJ = 43
NC16 = 16


class Ctx:
    def __init__(self, P, ntiles):
        self.P = P
        self.tiles = []
        off = 0
        for n in ntiles:
            self.tiles.append((off, n))
            off += n
        self.NT = off
        NT = off
        self.hy = P.sbuf("hy", [128, 16, NT], BF16)
        self.A = P.sbuf("A", [128, NJ, NT], BF16)
        self.d_h = [Dep(f"h{i}") for i in range(len(ntiles))]
        self.d_A = [Dep(f"A{i}") for i in range(len(ntiles))]
        aflat = self.A[:].rearrange("p j t -> p (j t)")
        self.xt = []
        self.d_xt = []
        nx = min(3, (NJ * NT) // 16384)
        for i in range(nx):
            v = aflat[:, i * 16384:(i + 1) * 16384].bitcast(F32).rearrange("p (c t) -> p c t", c=16)
            self.xt.append(v)
            self.d_xt.append(Dep(f"xt{i}"))
        self.wgu = [P.sbuf(f"wgu{i}", [128, 2, 16, 256], BF16) for i in range(2)]
        self.d_wgu = [Dep(f"wgu{i}") for i in range(2)]
        self.wd = [P.sbuf(f"wd{i}", [128, NJ, 128], BF16) for i in range(2)]
        self.d_wd = [Dep(f"wd{i}") for i in range(2)]
        self.ps, self.d_ps = P.shared_psum()
        self.ones = P.sbuf("ones", [128, 128], BF16)
        self.d_ones = Dep("ones")
        P.dve(lambda e: e.memset(self.ones[:], 1.0), writes=[self.d_ones])
        self.sq = [P.sbuf(f"sq{i}", [128, 512], BF16) for i in range(4)]
        self.d_sq = [Dep(f"sq{i}") for i in range(4)]
        self.tmp = [P.sbuf(f"tmp{i}", [128, 512], F32) for i in range(4)]
        self.d_tmp = [Dep(f"tmp{i}") for i in range(4)]
        self.rstd = P.sbuf("rstd", [128, NT], F32)
        self.d_rstd = [Dep(f"rstd{i}") for i in range(len(ntiles))]
        self.mT = P.sbuf("mT", [128, 144, 2], F32)
        self.d_mT = Dep("mT")
        self.gT = P.sbuf("gT", [128, 6, 16], F32)
        self.d_gT = Dep("gT")
        self.coef = P.sbuf("coef", [128, 4, 16], F32)
        self.d_coef = Dep("coef")
        self.sqi = 0
        self.tmpi = 0
        self.psi = 0

    def next_sq(self):
        i = self.sqi % 4
        self.sqi += 1
        return self.sq[i], self.d_sq[i]

    def next_tmp(self):
        i = self.tmpi % 4
        self.tmpi += 1
        return self.tmp[i], self.d_tmp[i]


def emit_modulation(P, C, ada_w, ada_bT, cvecT, scr):
    cv = scr["cv"]
    sc = scr["sc"]
    bT = scr["bT"]
    d_cv, d_sc, d_bT = Dep("cv"), Dep("sc"), Dep("bT")
    P.dma(cv[:], cvecT, writes=[d_cv], semkey="small")
    P.dma(bT[:], ada_bT, writes=[d_bT], semkey="small2")
    P.act(lambda e: e.activation(out=sc[:], in_=cv[:], func=AF.Silu), reads=[d_cv], writes=[d_sc])
    mp = C.ps[7]
    d_mp = C.d_ps[7]
    mpv = mp[:, 0:288].rearrange("p (j r) -> p j r", r=2)
    for jb in range(36):
        s = jb % 2
        slot = C.wgu[s][:].rearrange("p a c n -> p c (a n)") if False else None
        sl = C.wgu[s][:].rearrange("p a c n -> p (a c n)").rearrange("p (c n) -> p c n", c=16)
        src = ada_w[:, jb * 512:(jb + 1) * 512].rearrange("(c p) n -> p c n", p=128)
        P.dma(sl, src, writes=[C.d_wgu[s]], semkey=f"wgu{s}", eng="pool")
        for j4 in range(4):
            j = jb * 4 + j4
            for k in range(16):
                P.pe(lambda e, sl=sl, j4=j4, k=k, j=j: e.matmul(
                    mpv[:, j, :], lhsT=sl[:, k, j4 * 128:(j4 + 1) * 128], rhs=sc[:, k, :],
                    start=(k == 0), stop=(k == 15)),
                    reads=[C.d_wgu[s], d_sc], writes=[d_mp])
    for r in range(2):
        P.dve(lambda e, r=r: e.tensor_tensor(out=C.mT[:, :, r], in0=mpv[:, :, r], in1=bT[:], op=ALU.add),
              reads=[d_mp, d_bT], writes=[C.d_mT])


def emit_coefs(P, C, sl_scale, sl_gate, gi_pre, gi_post, wres, cols):
    for col in cols:
        P.dve(lambda e, col=col: e.scalar_tensor_tensor(
            out=C.coef[:, col, :], in0=C.mT[:, sl_scale * 16:(sl_scale + 1) * 16, col], scalar=1.0,
            in1=C.gT[:, gi_pre, :], op0=ALU.add, op1=ALU.mult),
            reads=[C.d_mT, C.d_gT], writes=[C.d_coef])
        if sl_gate is not None:
            P.dve(lambda e, col=col: e.scalar_tensor_tensor(
                out=C.coef[:, 2 + col, :], in0=C.mT[:, sl_gate * 16:(sl_gate + 1) * 16, col], scalar=float(wres),
                in1=C.gT[:, gi_post, :], op0=ALU.mult, op1=ALU.mult),
                reads=[C.d_mT, C.d_gT], writes=[C.d_coef])


def emit_prenorm(P, C, srcs, sl_shift, out_sb, d_out, arena_deps, resident=False, after_tile=None):
    nx = len(C.xt)
    for ti, (off, n) in enumerate(C.tiles):
        src, d_src, col = srcs[ti]
        xi = ti % nx
        xt = C.xt[xi][:, :, 0:n]
        if not resident:
            P.dma(xt, src.rearrange("(c p) t -> p c t", p=128), reads=[d_src],
                  writes=[C.d_xt[xi]] + arena_deps, semkey=f"xt{xi}")
        ssp = C.ps[6 + (ti % 2)]
        d_ssp = C.d_ps[6 + (ti % 2)]
        for c in range(16):
            sq, d_sq = C.next_sq()
            P.act(lambda e, sq=sq, c=c, xt=xt, n=n: e.activation(out=sq[:, 0:n], in_=xt[:, c, :], func=AF.Square),
                  reads=[C.d_xt[xi]], writes=[d_sq])
            P.pe(lambda e, sq=sq, c=c, ssp=ssp, n=n: e.matmul(ssp[:, 0:n], lhsT=C.ones[:], rhs=sq[:, 0:n],
                                                              start=(c == 0), stop=(c == 15)),
                 reads=[d_sq, C.d_ones], writes=[d_ssp])
        tmp, d_tmp = C.next_tmp()
        P.act(lambda e, tmp=tmp, ssp=ssp, n=n: e.activation(out=tmp[:, 0:n], in_=ssp[:, 0:n], func=AF.Sqrt,
                                                            scale=1.0 / D, bias=EPS),
              reads=[d_ssp], writes=[d_tmp])
        rs = C.rstd[:, off:off + n]
        P.dve(lambda e, tmp=tmp, rs=rs, n=n: e.reciprocal(out=rs, in_=tmp[:, 0:n]),
              reads=[d_tmp], writes=[C.d_rstd[ti]])
        dst = out_sb(ti)
        for c in range(16):
            tmp, d_tmp = C.next_tmp()
            P.dve(lambda e, tmp=tmp, c=c, xt=xt, rs=rs, n=n, col=col: e.scalar_tensor_tensor(
                out=tmp[:, 0:n], in0=xt[:, c, :], scalar=C.coef[:, col, c:c + 1], in1=rs,
                op0=ALU.mult, op1=ALU.mult),
                reads=[C.d_xt[xi], C.d_rstd[ti], C.d_coef], writes=[d_tmp])
            P.act(lambda e, tmp=tmp, c=c, dst=dst, n=n, col=col: e.activation(
                out=dst[:, c, :], in_=tmp[:, 0:n], func=AF.Identity,
                bias=C.mT[:, sl_shift * 16 + c, col:col + 1], scale=1.0),
                reads=[d_tmp, C.d_mT], writes=[d_out[ti]])
        if after_tile is not None:
            after_tile(ti, off, n)


def emit_ffn(P, C, srcs, dsts, wg, wu, wd, sl_shift, resident=False):
    nt = len(C.tiles)
    arena = C.d_A
    emit_prenorm(P, C, srcs, sl_shift, lambda ti: C.hy[:, :, C.tiles[ti][0]:C.tiles[ti][0] + C.tiles[ti][1]],
                 C.d_h, arena, resident)
    emit_gateup(P, C, wg, wu)
    emit_down_residual(P, C, NJ, wd, srcs, dsts)


def emit_gateup(P, C, wg, wu):
    it = 0
    for jj in range(22):
        s = jj % 2
        ncol = 256 if jj < 21 else 128
        for a, w in enumerate((wg, wu)):
            src = w[:, jj * 256:jj * 256 + ncol].rearrange("(c p) n -> p c n", p=128)
            P.dma(C.wgu[s][:, a, :, 0:ncol], src, writes=[C.d_wgu[s]], semkey=f"wgu{s}", eng="pool")
        for jl in range(ncol // 128):
            j = jj * 2 + jl
            for ti, (off, n) in enumerate(C.tiles):
                pg = (it % 4) * 2
                it += 1
                G, U = C.ps[pg], C.ps[pg + 1]
                for a, pt in enumerate((G, U)):
                    for k in range(16):
                        P.pe(lambda e, pt=pt, a=a, k=k, s=s, jl=jl, off=off, n=n: e.matmul(
                            pt[:, 0:n], lhsT=C.wgu[s][:, a, k, jl * 128:(jl + 1) * 128],
                            rhs=C.hy[:, k, off:off + n], start=(k == 0), stop=(k == 15)),
                            reads=[C.d_wgu[s], C.d_h[ti]], writes=[C.d_ps[pg + a]])
                tmp, d_tmp = C.next_tmp()
                P.act(lambda e, tmp=tmp, G=G, n=n: e.activation(out=tmp[:, 0:n], in_=G[:, 0:n], func=AF.Silu),
                      reads=[C.d_ps[pg]], writes=[d_tmp])
                P.dve(lambda e, tmp=tmp, U=U, j=j, off=off, n=n: e.tensor_tensor(
                    out=C.A[:, j, off:off + n], in0=tmp[:, 0:n], in1=U[:, 0:n], op=ALU.mult),
                    reads=[d_tmp, C.d_ps[pg + 1]], writes=[C.d_A[ti]] + C.d_xt)


def emit_down_residual(P, C, nch, wd, srcs, dsts):
    NJ = nch
    pend = []
    it = 0
    for dc in range(16):
        s = dc % 2
        src = wd[:, dc * 128:(dc + 1) * 128].rearrange("(j p) n -> p j n", p=128)
        P.dma(C.wd[s][:, 0:nch, :], src, writes=[C.d_wd[s]], semkey=f"wd{s}", eng="pool")
        for ti, (off, n) in enumerate(C.tiles):
            pi = it % 4
            it += 1
            Y = C.ps[pi]
            for j in range(NJ):
                P.pe(lambda e, Y=Y, j=j, s=s, off=off, n=n: e.matmul(
                    Y[:, 0:n], lhsT=C.wd[s][:, j, :], rhs=C.A[:, j, off:off + n],
                    start=(j == 0), stop=(j == NJ - 1)),
                    reads=[C.d_wd[s], C.d_A[ti]], writes=[C.d_ps[pi]])
            for f in pend:
                f()
            pend = []
            P.act(lambda e, Y=Y, dc=dc, off=off, n=n: e.activation(out=C.hy[:, dc, off:off + n], in_=Y[:, 0:n],
                                                                  func=AF.Copy),
                  reads=[C.d_ps[pi]], writes=[C.d_h[ti]])
            sq, d_sq = C.next_sq()
            P.act(lambda e, Y=Y, sq=sq, n=n: e.activation(out=sq[:, 0:n], in_=Y[:, 0:n], func=AF.Square),
                  reads=[C.d_ps[pi]], writes=[d_sq])
            SS = C.ps[4 + ti]

            def ssmm(sq=sq, d_sq=d_sq, SS=SS, ti=ti, dc=dc, n=n):
                P.pe(lambda e: e.matmul(SS[:, 0:n], lhsT=C.ones[:], rhs=sq[:, 0:n],
                                        start=(dc == 0), stop=(dc == 15)),
                     reads=[d_sq, C.d_ones], writes=[C.d_ps[4 + ti]])
            pend.append(ssmm)
    for f in pend:
        f()
    nx = len(C.xt)
    for ti, (off, n) in enumerate(C.tiles):
        src, d_src, col = srcs[ti]
        dst, d_dst, _ = dsts[ti]
        if dst is None:
            continue
        SS = C.ps[4 + ti]
        tmp, d_tmp = C.next_tmp()
        P.act(lambda e, tmp=tmp, SS=SS, n=n: e.activation(out=tmp[:, 0:n], in_=SS[:, 0:n], func=AF.Sqrt,
                                                          scale=1.0 / D, bias=EPS),
              reads=[C.d_ps[4 + ti]], writes=[d_tmp])
        rs = C.rstd[:, off:off + n]
        P.dve(lambda e, tmp=tmp, rs=rs, n=n: e.reciprocal(out=rs, in_=tmp[:, 0:n]),
              reads=[d_tmp], writes=[C.d_rstd[ti]])
        xi = ti % nx
        xt = C.xt[xi][:, :, 0:n]
        P.dma(xt, src.rearrange("(c p) t -> p c t", p=128), reads=[d_src],
              writes=[C.d_xt[xi]] + C.d_A, semkey=f"xt{xi}")
        for c in range(16):
            tmp, d_tmp = C.next_tmp()
            P.dve(lambda e, tmp=tmp, c=c, rs=rs, off=off, n=n, col=col: e.scalar_tensor_tensor(
                out=tmp[:, 0:n], in0=C.hy[:, c, off:off + n], scalar=C.coef[:, 2 + col, c:c + 1], in1=rs,
                op0=ALU.mult, op1=ALU.mult),
                reads=[C.d_h[ti], C.d_rstd[ti], C.d_coef], writes=[d_tmp])
            P.dve(lambda e, tmp=tmp, c=c, xt=xt, n=n: e.tensor_tensor(
                out=xt[:, c, :], in0=xt[:, c, :], in1=tmp[:, 0:n], op=ALU.add),
                reads=[d_tmp, C.d_xt[xi]], writes=[C.d_xt[xi]])
        P.dma(dst.rearrange("(c p) t -> p c t", p=128), xt, reads=[C.d_xt[xi]], writes=[d_dst],
              semkey=f"xt{xi}")


def emit_modulation_sharded(P, C, ada_sl, cvec3, selb, bT0, bT1, mp_own, mp_g1, mp_g2, mTd, d_mTd):
    cv = P.sbuf("mcv", [128, 16, 3], F32)
    sc = P.sbuf("msc", [128, 16, 3], BF16)
    sb = P.sbuf("mselb", [128, 2], F32)
    bT = P.sbuf("mbT", [128, 2, 144], F32)
    part = P.sbuf("mpart", [128, 216], F32)
    full = P.sbuf("mfull", [128, 4, 216], F32)
    fv = [full[:, g, :].rearrange("p (l j r) -> p l j r", l=2, j=36) for g in range(4)]
    mo = P.sbuf("mo", [128, 2, 144, 2], F32)
    d_cv, d_sc, d_bT, d_part, d_full, d_mo = [Dep(n) for n in ("mcv", "msc", "mbT", "mpart", "mfull", "mo")]
    P.dma(cv[:], cvec3, writes=[d_cv], semkey="small")
    P.dma(sb[:], selb, writes=[d_bT], semkey="small2")
    P.dma(bT[:, 0, :], bT0, writes=[d_bT], semkey="small2")
    P.dma(bT[:, 1, :], bT1, writes=[d_bT], semkey="small2")
    P.act(lambda e: e.activation(out=sc[:], in_=cv[:], func=AF.Silu), reads=[d_cv], writes=[d_sc])
    mp = C.ps[7]
    d_mp = C.d_ps[7]
    mpv = mp[:, 0:216].rearrange("p (l j r) -> p l j r", l=2, j=36)
    it = 0
    for l in range(2):
        for jb in range(9):
            s = it % 2
            it += 1
            sl = C.wgu[s][:].rearrange("p a c n -> p (a c n)").rearrange("p (c n) -> p c n", c=16)
            src = ada_sl[l, :, jb * 512:(jb + 1) * 512].rearrange("(c p) n -> p c n", p=128)
            P.dma(sl, src, writes=[C.d_wgu[s]], semkey=f"wgu{s}", eng="pool")
            for j4 in range(4):
                j = jb * 4 + j4
                for k in range(16):
                    P.pe(lambda e, sl=sl, j4=j4, k=k, j=j, l=l: e.matmul(
                        mpv[:, l, j, :], lhsT=sl[:, k, j4 * 128:(j4 + 1) * 128], rhs=sc[:, k, :],
                        start=(k == 0), stop=(k == 15)),
                        reads=[C.d_wgu[s], d_sc], writes=[d_mp])
    P.dve(lambda e: e.tensor_copy(out=part[:], in_=mp[:, 0:216]), reads=[d_mp], writes=[d_part])
    d_mi, d_m1, d_m2 = Dep("mp_own"), Dep("mp_g1"), Dep("mp_g2")
    P.dma(mp_own[:, :], part[:], reads=[d_part], writes=[d_mi], semkey="mred")
    P.allgather_pairs(mp_g1, mp_own, reads=[d_mi], writes=[d_m1], semkey="cc0")
    P.allgather_far(mp_g2, mp_g1, reads=[d_m1], writes=[d_m2], semkey="ccf")
    for g in range(4):
        P.dma(full[:, g, :], mp_g2[g * 128:(g + 1) * 128, :],
              reads=[d_m2], writes=[d_full], semkey="mred")
    for l in range(2):
        for g in range(4):
            dst = mo[:, l, g * 36:(g + 1) * 36, :]
            P.dve(lambda e, l=l, g=g, dst=dst: e.tensor_scalar(out=dst[:, :, 0], in0=fv[g][:, l, :, 0],
                                                               scalar1=sb[:, 0:1], scalar2=None, op0=ALU.mult),
                  reads=[d_full, d_bT], writes=[d_mo])
            P.dve(lambda e, l=l, g=g, dst=dst: e.scalar_tensor_tensor(out=dst[:, :, 0], in0=fv[g][:, l, :, 1],
                                                                      scalar=sb[:, 1:2], in1=dst[:, :, 0],
                                                                      op0=ALU.mult, op1=ALU.add),
                  reads=[d_full, d_bT, d_mo], writes=[d_mo])
            P.dve(lambda e, l=l, g=g, dst=dst: e.tensor_copy(out=dst[:, :, 1], in_=fv[g][:, l, :, 2]),
                  reads=[d_full], writes=[d_mo])
        for col in range(2):
            P.dve(lambda e, l=l, col=col: e.tensor_tensor(out=mo[:, l, :, col], in0=mo[:, l, :, col], in1=bT[:, l, :],
                                                          op=ALU.add), reads=[d_mo, d_bT], writes=[d_mo])
        P.dma(mTd[l][:], mo[:, l, :, :], reads=[d_mo], writes=[d_mTd[l]], semkey="mTo")


EPS = 1e-6
NTOK = 2304
NCTX = 256
NLAT = 2048
LAM_INIT0 = 0.8 - 0.6 * math.exp(-0.3 * 0)


DBG = {}


class MixCtx:
    def __init__(self, P):
        self.P = P
        self.hs = P.sbuf("hs", [128, 16, NTOK], BF16)
        self.d_hs = Dep("hs")
        self.ropeC = P.sbuf("ropeC", [128, NLAT], F32)
        self.ropeS = P.sbuf("ropeS", [128, NLAT], F32)
        self.d_rope = Dep("rope")
        self.wt = [P.sbuf(f"wt{i}", [128, 16, 128], BF16) for i in range(8)]
        self.d_wt = [Dep(f"wt{i}") for i in range(8)]
        self.ps, self.d_ps = P.shared_psum()
        self.d_ph = [[Dep(f"ph{i}_{h}") for h in range(2)] for i in range(8)]
        self.ones = P.sbuf("ones", [128, 128], BF16)
        self.onesf = P.sbuf("onesf", [128, 128], F32)
        self.d_ones = Dep("ones")
        P.dve(lambda e: e.memset(self.ones[:], 1.0), writes=[self.d_ones])
        P.dve(lambda e: e.memset(self.onesf[:], 1.0), writes=[self.d_ones])
        self.perm = P.sbuf("perm", [128, 128], BF16)
        self.ident = P.sbuf("ident", [128, 128], BF16)
        self.maskf = P.sbuf("maskf", [128, 128], F32)
        self.maskb = P.sbuf("maskb", [128, 128], F32)
        self.d_const = Dep("const")
        self.vec = P.sbuf("vec", [128, 32], F32)
        self.d_vec = Dep("vec")
        self.lbr = P.sbuf("lbr", [128, 2, 2, 4], F32)
        self.lb = P.sbuf("lb", [128, 2, 4], F32)
        self.oml = P.sbuf("oml", [128, 2, 4], F32)
        self.d_lb = Dep("lb")
        u0 = P.aoff
        self.qT = P.sbuf("qT", [128, NLAT], BF16)
        self.qT1 = P.sbuf("qT1", [128, NLAT], BF16)
        self.kT = P.sbuf("kT", [128, NTOK], BF16)
        self.V = P.sbuf("V", [128, 18, 128], BF16)
        self.d_qT, self.d_kT, self.d_V = Dep("qT"), Dep("kT"), Dep("V")
        self.E = [P.sbuf(f"E{i}", [128, 512], BF16) for i in range(4)]
        self.d_E = [Dep(f"E{i}") for i in range(4)]
        u1 = P.aoff
        if P.arena is not None:
            P.aoff = u0
        self.zf_sb = P.sbuf("zf_sb", [128, NTOK], F32)
        self.zb_sb = P.sbuf("zb_sb", [128, NTOK], F32)
        self.q_sb = P.sbuf("q_sb", [128, NTOK], BF16)
        self.i_sb = P.sbuf("i_sb", [128, 18, 128], BF16)
        self.d_rp = Dep("recproj")
        self.d_zf = Dep("zf_sb")
        if P.arena is not None:
            P.aoff = max(u1, P.aoff)
        self.qtF = P.sbuf("qtF", [128, NTOK], BF16)
        self.ktF = P.sbuf("ktF", [128, NTOK], BF16)
        self.zfb = self.zf_sb.bitcast(BF16)
        self.d_zfu = [Dep(f"zfu{u}") for u in range(9)]
        self.d_qk = [Dep("qkF"), Dep("qkB")]
        self.sg_sb = P.sbuf("sg_sb", [128, NLAT], BF16)
        self.svall = P.sbuf("svall", [128, 2, 18, 4], F32)
        self.d_sva = Dep("svall")
        self.d_svad = [Dep("svallF"), Dep("svallB")]
        self.d_tfh = [[Dep(f"tfh{i}_{h}") for h in range(2)] for i in range(6)]
        self.maskR = P.sbuf("maskR", [128, 512], F32)
        P.dve(lambda e: e.memset(self.maskR[:], 1.0), writes=[self.d_ones])
        for cc in range(4):
            P.dve(lambda e, cc=cc: e.memset(self.maskR[:, cc * 128:cc * 128 + 1], 0.0), writes=[self.d_ones])
        self.tf = [P.sbuf(f"tf{i}", [128, 512], F32) for i in range(6)]
        self.d_tf = [Dep(f"tf{i}") for i in range(6)]
        self.tb = [P.sbuf(f"tb{i}", [128, 512], BF16) for i in range(4)]
        self.d_tb = [Dep(f"tb{i}") for i in range(4)]
        self.tfi = 0
        self.tbi = 0
        self.Ei = 0
        self.S = P.sbuf("S", [128, 128], F32)
        self.d_S = Dep("S")
        self.S1 = P.sbuf("S1", [128, 128], F32)
        self.d_S1 = Dep("S1")
        self.ofw = P.sbuf("ofw", [128, NLAT], F32)
        self.d_ofw = Dep("ofw")
        self.obw = P.sbuf("obw", [128, NLAT], F32)
        self.d_obw = Dep("obw")
        self.rf = [P.sbuf(f"rf{i}", [128, 128], F32) for i in range(6)]
        self.d_rf = [Dep(f"rf{i}") for i in range(6)]
        self.rb = [P.sbuf(f"rb{i}", [128, 128], BF16) for i in range(12)]
        self.d_rb = [Dep(f"rb{i}") for i in range(12)]
        self.rfi = 0
        self.rbi = 0
        self.sv = [P.sbuf(f"sv{i}", [128, 8], F32) for i in range(8)]
        self.d_sv = [Dep(f"sv{i}") for i in range(8)]
        self.svi = 0

    def ntf(self):
        i = self.tfi % 6
        self.tfi += 1
        return self.tf[i], self.d_tf[i]

    def ntb(self):
        i = self.tbi % 4
        self.tbi += 1
        return self.tb[i], self.d_tb[i]

    def nE(self):
        i = self.Ei % 4
        self.Ei += 1
        return self.E[i], self.d_E[i]

    def nrf(self):
        i = self.rfi % 6
        self.rfi += 1
        return self.rf[i], self.d_rf[i]

    def nrb(self):
        i = self.rbi % 12
        self.rbi += 1
        return self.rb[i], self.d_rb[i]

    def nsv(self):
        i = self.svi % 8
        self.svi += 1
        return self.sv[i], self.d_sv[i]


def load_w(P, M, slot, src):
    P.dma(M.wt[slot][:], src.rearrange("(c p) n -> p c n", p=128), writes=[M.d_wt[slot]],
          semkey=f"wt{slot}", eng="pool")


def emit_mix_setup(P, M, hTf, ropeC, ropeS, perm, ident, maskf, maskb, lamT, dng, rng, lbraw):
    for q in range(4 if hTf is not None else 0):
        P.dma(M.hs[:, q * 4:(q + 1) * 4, :], hTf[q * 512:(q + 1) * 512, :].rearrange("(c p) t -> p c t", p=128),
              writes=[M.d_hs], semkey="hs")
    P.dma(M.ropeC[:], ropeC, writes=[M.d_rope], semkey="rope")
    P.dma(M.ropeS[:], ropeS, writes=[M.d_rope], semkey="rope")
    P.dma(M.perm[:], perm, writes=[M.d_const], semkey="cst", eng="pool")
    P.dma(M.ident[:], ident, writes=[M.d_const], semkey="cst", eng="pool")
    P.dma(M.maskf[:], maskf, writes=[M.d_const], semkey="cst2")
    P.dma(M.maskb[:], maskb, writes=[M.d_const], semkey="cst2")
    P.dma(M.vec[0:64, 0:4], lamT, writes=[M.d_vec], semkey="vec")
    P.dma(M.vec[:, 4:5], dng, writes=[M.d_vec], semkey="vec")
    P.dma(M.vec[:, 5:6], rng, writes=[M.d_vec], semkey="vec")
    P.dma(M.lbr[:], lbraw, writes=[M.d_lb], semkey="lb")
    P.dve(lambda e: e.tensor_tensor(out=M.vec[0:64, 6:7], in0=M.vec[0:64, 0:1], in1=M.vec[0:64, 1:2], op=ALU.mult),
          reads=[M.d_vec], writes=[M.d_vec])
    P.dve(lambda e: e.tensor_tensor(out=M.vec[0:64, 7:8], in0=M.vec[0:64, 2:3], in1=M.vec[0:64, 3:4], op=ALU.mult),
          reads=[M.d_vec], writes=[M.d_vec])
    lp = M.ps[7]
    P.pe(lambda e: e.matmul(lp[:, 0:2], lhsT=M.onesf[0:64, :], rhs=M.vec[0:64, 6:8], start=True, stop=True),
         reads=[M.d_vec, M.d_ones], writes=[M.d_ps[7]])
    P.act(lambda e: e.activation(out=M.vec[:, 8:10], in_=lp[:, 0:2], func=AF.Exp), reads=[M.d_ps[7]],
          writes=[M.d_vec])
    P.dve(lambda e: e.tensor_tensor(out=M.vec[:, 10:11], in0=M.vec[:, 9:10], in1=M.vec[:, 8:9], op=ALU.subtract),
          reads=[M.d_vec], writes=[M.d_vec])
    P.dve(lambda e: e.tensor_scalar(out=M.vec[:, 10:11], in0=M.vec[:, 10:11], scalar1=-LAM_INIT0, scalar2=None,
                                    op0=ALU.add), reads=[M.d_vec], writes=[M.d_vec])
    P.dve(lambda e: e.tensor_scalar(out=M.vec[:, 11:12], in0=M.vec[:, 4:5], scalar1=1.0 - LAM_INIT0, scalar2=None,
                                    op0=ALU.mult), reads=[M.d_vec], writes=[M.d_vec])
    P.dve(lambda e: e.tensor_tensor(out=M.lb[:], in0=M.lbr[:, :, 1, :], in1=M.lbr[:, :, 0, :], op=ALU.subtract),
          reads=[M.d_lb], writes=[M.d_lb])
    P.act(lambda e: e.activation(out=M.lb[:], in_=M.lb[:], func=AF.Exp), reads=[M.d_lb], writes=[M.d_lb])
    P.dve(lambda e: e.tensor_scalar(out=M.lb[:], in0=M.lb[:], scalar1=1.0, scalar2=None, op0=ALU.add),
          reads=[M.d_lb], writes=[M.d_lb])
    P.dve(lambda e: e.reciprocal(out=M.lb[:], in_=M.lb[:]), reads=[M.d_lb], writes=[M.d_lb])
    P.dve(lambda e: e.tensor_scalar(out=M.oml[:], in0=M.lb[:], scalar1=-1.0, scalar2=1.0, op0=ALU.mult, op1=ALU.add),
          reads=[M.d_lb], writes=[M.d_lb])


def proj_fm(P, M, out_ps, d_out, slot, off, n):
    for k in range(16):
        P.pe(lambda e, k=k: e.matmul(out_ps, lhsT=M.wt[slot][:, k, :], rhs=M.hs[:, k, off:off + n],
                                     start=(k == 0), stop=(k == 15)),
             reads=[M.d_wt[slot], M.d_hs], writes=[d_out])


def proj_tm(P, M, out_ps, d_out, slot, off):
    for k in range(16):
        P.pe(lambda e, k=k: e.matmul(out_ps, lhsT=M.hs[:, k, off:off + 128], rhs=M.wt[slot][:, k, :],
                                     start=(k == 0), stop=(k == 15)),
             reads=[M.d_wt[slot], M.d_hs], writes=[d_out])


def emit_rsqrt(P, M, out_sb, d_o, in_ps, d_in, n, inv_dim):
    P.act(lambda e: e.activation(out=out_sb, in_=in_ps, func=AF.Ln, scale=inv_dim, bias=EPS),
          reads=[d_in], writes=[d_o])
    P.act(lambda e: e.activation(out=out_sb, in_=out_sb, func=AF.Exp, scale=-0.5), reads=[d_o], writes=[d_o])


def emit_attention_head(P, M, hd, mergedT, d_merged):
    sq_, sk_, sv_ = 0, 1, 2
    P.dve(lambda e: e.memset(M.qT[64:128, :], 0.0), writes=[M.d_qT])
    P.dve(lambda e: e.memset(M.qT1[0:64, :], 0.0), writes=[M.d_qT])
    tiles = [(0, NCTX)] + [(NCTX + i * 512, 512) for i in range(4)]
    bi = 0
    import os
    NSUB = int(os.environ.get("ATT_SUB", "99"))
    for (off, n) in tiles:
        for which in ("k", "q"):
            if bi >= NSUB:
                continue
            if which == "q" and off < NCTX:
                continue
            slot = sk_ if which == "k" else sq_
            dstT = M.kT if which == "k" else M.qT
            d_dst = M.d_kT if which == "k" else M.d_qT
            doff = off if which == "k" else off - NCTX
            pb = bi % 2
            bi += 1
            pp, d_pp = M.ps[pb], M.d_ps[pb]
            proj_fm(P, M, pp[:, 0:n], d_pp, slot, off, n)
            if off < NCTX:
                P.act(lambda e, pp=pp, n=n, dstT=dstT, doff=doff: e.activation(
                    out=dstT[:, doff:doff + n], in_=pp[:, 0:n], func=AF.Copy), reads=[d_pp], writes=[d_dst])
                continue
            loff = off - NCTX
            sb, d_sb = M.ntb()
            P.act(lambda e, pp=pp, sb=sb, n=n: e.activation(out=sb[:, 0:n], in_=pp[:, 0:n], func=AF.Copy),
                  reads=[d_pp], writes=[d_sb])
            rp, d_rp = M.ps[2 + pb], M.d_ps[2 + pb]
            P.pe(lambda e, rp=rp, sb=sb, n=n: e.matmul(rp[:, 0:n], lhsT=M.perm[:], rhs=sb[:, 0:n], start=True,
                                                       stop=True), reads=[d_sb, M.d_const], writes=[d_rp])
            t1, d_t1 = M.ntf()
            P.dve(lambda e, t1=t1, pp=pp, n=n, loff=loff: e.tensor_tensor(
                out=t1[:, 0:n], in0=pp[:, 0:n], in1=M.ropeC[:, loff:loff + n], op=ALU.mult),
                reads=[d_pp, M.d_rope], writes=[d_t1])
            t2, d_t2 = M.ntf()
            P.dve(lambda e, t2=t2, rp=rp, n=n, loff=loff: e.tensor_tensor(
                out=t2[:, 0:n], in0=rp[:, 0:n], in1=M.ropeS[:, loff:loff + n], op=ALU.mult),
                reads=[d_rp, M.d_rope], writes=[d_t2])
            if which == "k":
                P.dve(lambda e, t1=t1, t2=t2, dstT=dstT, doff=doff, n=n: e.tensor_tensor(
                    out=dstT[:, doff:doff + n], in0=t1[:, 0:n], in1=t2[:, 0:n], op=ALU.add),
                    reads=[d_t1, d_t2], writes=[d_dst])
            else:
                P.dve(lambda e, t1=t1, t2=t2, doff=doff, n=n: e.tensor_tensor(
                    out=M.qT[0:64, doff:doff + n], in0=t1[0:64, 0:n], in1=t2[0:64, 0:n], op=ALU.add),
                    reads=[d_t1, d_t2], writes=[d_dst])
                P.dve(lambda e, t1=t1, t2=t2, doff=doff, n=n: e.tensor_tensor(
                    out=M.qT1[64:128, doff:doff + n], in0=t1[64:128, 0:n], in1=t2[64:128, 0:n], op=ALU.add),
                    reads=[d_t1, d_t2], writes=[d_dst])
    import os
    if DBG.get("qT") is not None and hd == 0:
        P.dma(DBG["qT"][:, :], M.qT[:], reads=[M.d_qT], writes=[Dep("dbgq")], semkey="dbg")
        P.dma(DBG["kT"][:, :], M.kT[:], reads=[M.d_kT], writes=[Dep("dbgk")], semkey="dbg")
    STG = int(os.environ.get("ATT_STAGE", "9"))
    if STG < 2:
        return
    for g4 in range(5):
        pb = 4 + (g4 % 2)
        vp, d_vp = M.ps[pb], M.d_ps[pb]
        nk = 4 if g4 < 4 else 2
        for i in range(nk):
            kt = g4 * 4 + i
            proj_tm(P, M, vp[:, i * 128:(i + 1) * 128], d_vp, sv_, kt * 128)
        P.act(lambda e, vp=vp, g4=g4, nk=nk: e.activation(
            out=M.V[:, g4 * 4:g4 * 4 + nk, :].rearrange("p a n -> p (a n)"), in_=vp[:, 0:nk * 128], func=AF.Copy),
            reads=[d_vp], writes=[M.d_V])
    if STG < 3:
        return
    for qt in [int(c) for c in os.environ.get("ATT_QT", "0123")]:
        qo = qt * 512
        steps = [(kt, m) for kt in range(18) for m in range(2)]
        pend = []
        for si, (kt, m) in enumerate(steps):
            sb_i = si % 4
            ST, d_ST = M.ps[sb_i], M.d_ps[sb_i]
            P.pe(lambda e, ST=ST, kt=kt, m=m: e.matmul(
                ST[:, :], lhsT=M.kT[:, kt * 128:(kt + 1) * 128],
                rhs=(M.qT if m == 0 else M.qT1)[:, qo:qo + 512], start=True, stop=True),
                reads=[M.d_kT, M.d_qT], writes=[d_ST])
            E, d_E = M.nE()
            P.act(lambda e, E=E, ST=ST: e.activation(out=E[:], in_=ST[:, :], func=AF.Exp, scale=0.125),
                  reads=[d_ST], writes=[d_E])
            if len(pend) >= 2:
                pend.pop(0)()

            def pv(E=E, d_E=d_E, kt=kt, m=m):
                P.pe(lambda e: e.matmul(M.ps[4 + m][:, :], lhsT=M.V[:, kt, :], rhs=E[:], start=(kt == 0),
                                        stop=(kt == 17)), reads=[M.d_V, d_E], writes=[M.d_ps[4 + m]])
                P.pe(lambda e: e.matmul(M.ps[6 + m][:, :], lhsT=M.ones[:], rhs=E[:], start=(kt == 0),
                                        stop=(kt == 17)), reads=[M.d_ones, d_E], writes=[M.d_ps[6 + m]])
            pend.append(pv)
        for f in pend:
            f()
        r0, d_r0 = M.ntf()
        r1, d_r1 = M.ntf()
        P.dve(lambda e, r0=r0: e.reciprocal(out=r0[:], in_=M.ps[6][:, :]), reads=[M.d_ps[6]], writes=[d_r0])
        P.dve(lambda e, r1=r1: e.reciprocal(out=r1[:], in_=M.ps[7][:, :]), reads=[M.d_ps[7]], writes=[d_r1])
        oa, d_oa = M.ntf()
        ob, d_ob = M.ntf()
        P.dve(lambda e, oa=oa, r0=r0: e.tensor_tensor(out=oa[:], in0=M.ps[4][:, :], in1=r0[:], op=ALU.mult),
              reads=[M.d_ps[4], d_r0], writes=[d_oa])
        P.dve(lambda e, ob=ob, r1=r1: e.tensor_tensor(out=ob[:], in0=M.ps[5][:, :], in1=r1[:], op=ALU.mult),
              reads=[M.d_ps[5], d_r1], writes=[d_ob])
        o, d_o = M.ntf()
        P.dve(lambda e, o=o, oa=oa, ob=ob: e.scalar_tensor_tensor(
            out=o[:], in0=ob[:], scalar=M.vec[:, 10:11], in1=oa[:], op0=ALU.mult, op1=ALU.add),
            reads=[d_oa, d_ob, M.d_vec], writes=[d_o])
        sq, d_sq = M.ntb()
        P.act(lambda e, sq=sq, o=o: e.activation(out=sq[:], in_=o[:], func=AF.Square), reads=[d_o], writes=[d_sq])
        P.pe(lambda e, sq=sq: e.matmul(M.ps[0][:, :], lhsT=M.ones[:], rhs=sq[:], start=True, stop=True),
             reads=[d_sq, M.d_ones], writes=[M.d_ps[0]])
        ri, d_ri = M.ntf()
        emit_rsqrt(P, M, ri[:], d_ri, M.ps[0][:, :], M.d_ps[0], 512, 1.0 / 128)
        ob16, d_ob16 = M.ntb()
        P.dve(lambda e, ob16=ob16, o=o, ri=ri: e.scalar_tensor_tensor(
            out=ob16[:], in0=o[:], scalar=M.vec[:, 11:12], in1=ri[:], op0=ALU.mult, op1=ALU.mult),
            reads=[d_o, d_ri, M.d_vec], writes=[d_ob16])
        rows = mergedT("att", hd) if callable(mergedT) else mergedT[hd * 128:(hd + 1) * 128, :]
        P.dma(rows[:, qo:qo + 512], ob16[:], reads=[d_ob16], writes=[d_merged], semkey="mg")


def emit_rec_head(P, M, r, mergedT, d_merged, after_burst=None):
    s_q, s_zf, s_zb, s_i, s_g = 3, 4, 5, 6, 7
    orders = [list(range(18)), [1, 0] + list(range(17, 1, -1))]
    Ss = [M.S, M.S1]
    dSs = [M.d_S, M.d_S1]
    obuf = [M.ofw, M.obw]
    d_obuf = [M.d_ofw, M.d_obw]
    for dr in range(2):
        P.dve(lambda e, dr=dr: e.memset(Ss[dr][:], 0.0), writes=[dSs[dr]])
    ptiles = [(0, NCTX)] + [(NCTX + i * 512, 512) for i in range(4)]
    bi = 0
    for (off, n) in ptiles:
        for (slot, dst, sc_) in ((s_zf, M.zf_sb, 1.0), (s_zb, M.zb_sb, 1.0), (s_q, M.q_sb, float(128 ** -0.5))):
            pb = bi % 4
            bi += 1
            proj_fm(P, M, M.ps[pb][:, 0:n], M.d_ps[pb], slot, off, n)
            if slot != s_q:
                P.act(lambda e, pb=pb, dst=dst, off=off, n=n: e.activation(
                    out=dst[:, off:off + n], in_=M.ps[pb][:, 0:n], func=AF.Sigmoid, scale=-1.0),
                    reads=[M.d_ps[pb]], writes=[M.d_rp])
            else:
                P.dve(lambda e, pb=pb, dst=dst, off=off, n=n, sc_=sc_: e.tensor_scalar(
                    out=dst[:, off:off + n], in0=M.ps[pb][:, 0:n], scalar1=sc_, scalar2=None, op0=ALU.mult),
                    reads=[M.d_ps[pb]], writes=[M.d_rp])
        if off >= NCTX:
            pb = bi % 4
            bi += 1
            proj_fm(P, M, M.ps[pb][:, 0:n], M.d_ps[pb], s_g, off, n)
            P.act(lambda e, pb=pb, off=off, n=n: e.activation(
                out=M.sg_sb[:, off - NCTX:off - NCTX + n], in_=M.ps[pb][:, 0:n], func=AF.Silu),
                reads=[M.d_ps[pb]], writes=[M.d_rp])
    for g4 in range(5):
        pb = 4 + (g4 % 2)
        nk = 4 if g4 < 4 else 2
        for i in range(nk):
            proj_tm(P, M, M.ps[pb][:, i * 128:(i + 1) * 128], M.d_ps[pb], s_i, (g4 * 4 + i) * 128)
        P.act(lambda e, pb=pb, g4=g4, nk=nk: e.activation(
            out=M.i_sb[:, g4 * 4:g4 * 4 + nk, :].rearrange("p a n -> p (a n)"), in_=M.ps[pb][:, 0:nk * 128],
            func=AF.Copy), reads=[M.d_ps[pb]], writes=[M.d_rp])

    if after_burst is not None:
        after_burst()
    RSTG = int(os.environ.get("REC_STAGE", "9"))
    if RSTG < 2:
        return
    def prep_gen(dr):
        zsb = M.zf_sb if dr == 0 else M.zb_sb
        T = [t[:, dr * 256:(dr + 1) * 256] for t in M.tf]
        dT = [M.d_tfh[i][dr] for i in range(6)]
        n = 256
        for off in range(0, NTOK, 256):
            c0 = off // 128
            u = off // 256
            if dr == 0:
                qdst, kdst = M.qtF[:, off:off + n], M.ktF[:, off:off + n]
                wdeps = [M.d_qk[0]]
                rdeps = [M.d_rp, M.d_zfu[u]]
            else:
                qdst, kdst = M.zfb[:, u * 512:u * 512 + 256], M.zfb[:, u * 512 + 256:u * 512 + 512]
                wdeps = [M.d_qk[1], M.d_zfu[u]]
                rdeps = [M.d_rp]
            P.dve(lambda e, off=off: e.tensor_scalar(out=T[2], in0=zsb[:, off:off + n], scalar1=M.oml[:, dr, r:r + 1],
                                                     scalar2=None, op0=ALU.mult), reads=rdeps + [M.d_lb],
                  writes=[dT[2]])
            yield
            P.act(lambda e: e.activation(out=T[3], in_=T[2], func=AF.Ln, scale=-1.0, bias=1.0), reads=[dT[2]],
                  writes=[dT[3]])
            yield
            P.dve(lambda e: e.tensor_tensor_scan(out=T[4], data0=M.maskR[:, 0:n], data1=T[3], initial=0.0,
                                                 op0=ALU.mult, op1=ALU.add), reads=[dT[3], M.d_ones], writes=[dT[4]])
            yield
            pfv = T[4].rearrange("p (c t) -> p c t", t=128)
            if dr == 1:
                P.dve(lambda e: e.tensor_tensor(out=T[0], in0=T[4], in1=T[3], op=ALU.subtract),
                      reads=[dT[4], dT[3]], writes=[dT[0]])
                yield
            for ci in range(2):
                src = T[0] if dr == 1 else T[4]
                P.dve(lambda e, ci=ci, src=src: e.tensor_scalar(
                    out=T[5][:, ci * 128:(ci + 1) * 128], in0=src[:, ci * 128:(ci + 1) * 128],
                    scalar1=T[4][:, ci * 128 + 63:ci * 128 + 64], scalar2=(1.0 if dr == 0 else -1.0),
                    op0=ALU.subtract, op1=ALU.mult), reads=[dT[0], dT[4]], writes=[dT[5]])
                yield
            P.act(lambda e: e.activation(out=T[1], in_=T[5], func=AF.Exp), reads=[dT[5]], writes=[dT[1]])
            yield
            P.act(lambda e: e.activation(out=T[3], in_=T[5], func=AF.Exp, scale=-1.0), reads=[dT[5]], writes=[dT[3]])
            yield
            P.dve(lambda e, off=off, qdst=qdst: e.tensor_tensor(out=qdst, in0=M.q_sb[:, off:off + n], in1=T[1],
                                                                op=ALU.mult), reads=[M.d_rp, dT[1]], writes=wdeps)
            yield
            P.dve(lambda e, kdst=kdst: e.tensor_tensor(out=kdst, in0=T[2], in1=T[3], op=ALU.mult),
                  reads=[dT[2], dT[3]], writes=wdeps)
            yield
            svv = M.svall[:, dr, c0:c0 + 2, :]
            d_sva = M.d_svad[dr]
            P.dve(lambda e, svv=svv, pfv=pfv: e.tensor_tensor(out=svv[:, :, 0], in0=pfv[:, :, 127], in1=pfv[:, :, 63],
                                                              op=ALU.subtract), reads=[dT[4]], writes=[d_sva])
            yield
            P.act(lambda e, svv=svv, pfv=pfv: e.activation(out=svv[:, :, 1], in_=pfv[:, :, 63], func=AF.Exp),
                  reads=[dT[4]], writes=[d_sva])
            yield
            P.act(lambda e, svv=svv: e.activation(out=svv[:, :, 2], in_=svv[:, :, 0], func=AF.Exp),
                  reads=[d_sva], writes=[d_sva])
            yield
            P.act(lambda e, svv=svv, pfv=pfv: e.activation(out=svv[:, :, 3], in_=pfv[:, :, 127], func=AF.Exp),
                  reads=[dT[4]], writes=[d_sva])
            yield

    import itertools
    P.fence(M.d_tf, [d for pair in M.d_tfh for d in pair])
    for _ in itertools.zip_longest(prep_gen(0), prep_gen(1)):
        pass
    P.fence([d for pair in M.d_tfh for d in pair], M.d_tf)

    if RSTG < 3:
        return

    def chunk_gen(dr, step):
        if True:
            c = orders[dr][step]
            mask = M.maskf if dr == 0 else M.maskb
            S_, d_S = Ss[dr], dSs[dr]
            a = c * 128
            lat = c >= 2
            la = a - NCTX
            bx = dr * 4 + (step % 2) * 2
            by = bx + 1
            if dr == 0:
                qt_, kt_ = M.qtF[:, a:a + 128], M.ktF[:, a:a + 128]
            else:
                ub = (c // 2) * 512 + (c % 2) * 128
                qt_, kt_ = M.zfb[:, ub:ub + 128], M.zfb[:, ub + 256:ub + 384]
            d_qt = d_kt = M.d_qk[dr]
            vt, d_vt = M.i_sb[:, c, :], M.d_rp
            sv, d_sv = M.svall[:, dr, c, :], M.d_svad[dr]
            c_e1 = 1 if dr == 0 else 2
            c_e2 = 2 if dr == 0 else 1
            ktp = M.ps[bx][:, 384:448].bitcast(BF16)
            d_ktp = M.d_ps[bx]
            P.pe(lambda e, ktp=ktp, kt_=kt_: e.transpose(ktp, kt_, M.ident[:]), reads=[d_kt, M.d_const],
                 writes=[d_ktp])
            yield
            ktok, d_ktok = M.nrb()
            P.act(lambda e, ktok=ktok, ktp=ktp: e.activation(out=ktok[:], in_=ktp, func=AF.Copy),
                  reads=[d_ktp], writes=[d_ktok])
            yield
            if lat:
                atp, d_atp = M.ps[by][:, 0:128], M.d_ps[by]
                P.pe(lambda e, atp=atp, kt_=kt_, qt_=qt_: e.matmul(atp, lhsT=kt_, rhs=qt_, start=True, stop=True),
                     reads=[d_kt, d_qt], writes=[d_atp])
                yield
                am, d_am = M.nrb()
                P.dve(lambda e, am=am, atp=atp: e.tensor_tensor(out=am[:], in0=atp, in1=mask[:], op=ALU.mult),
                      reads=[d_atp, M.d_const], writes=[d_am])
                yield
                sp, d_sp = M.nrb()
                P.dve(lambda e, sp=sp, sv=sv: e.tensor_scalar(out=sp[:], in0=S_[:], scalar1=sv[:, c_e1:c_e1 + 1],
                                                              scalar2=None, op0=ALU.mult),
                      reads=[d_S, d_sv], writes=[d_sp])
                yield
                op_, d_op = M.ps[by][:, 128:256], M.d_ps[by]
                P.pe(lambda e, op_=op_, sp=sp, qt_=qt_: e.matmul(op_, lhsT=sp[:], rhs=qt_, start=True, stop=False),
                     reads=[d_sp, d_qt], writes=[d_op])
                yield
                P.pe(lambda e, op_=op_, vt=vt, am=am: e.matmul(op_, lhsT=vt, rhs=am[:], start=False, stop=True),
                     reads=[d_vt, d_am], writes=[d_op])
                yield
                P.act(lambda e, op_=op_, la=la: e.activation(out=obuf[dr][:, la:la + 128], in_=op_, func=AF.Copy),
                      reads=[d_op], writes=[d_obuf[dr]])
                yield
            kvp, d_kvp = M.ps[by][:, 256:384], M.d_ps[by]
            P.pe(lambda e, kvp=kvp, ktok=ktok, vt=vt: e.matmul(kvp, lhsT=ktok[:], rhs=vt, start=True, stop=True),
                 reads=[d_ktok, d_vt], writes=[d_kvp])
            yield
            tk, d_tk = M.nrf()
            P.dve(lambda e, tk=tk, kvp=kvp, sv=sv: e.tensor_scalar(out=tk[:], in0=kvp, scalar1=sv[:, c_e2:c_e2 + 1],
                                                                    scalar2=None, op0=ALU.mult),
                  reads=[d_kvp, d_sv], writes=[d_tk])
            yield
            P.dve(lambda e, tk=tk, sv=sv: e.scalar_tensor_tensor(out=S_[:], in0=S_[:], scalar=sv[:, 3:4],
                                                                 in1=tk[:], op0=ALU.mult, op1=ALU.add),
                  reads=[d_tk, d_sv, d_S], writes=[d_S])
            yield
    import itertools
    for step in range(18):
        gens = [chunk_gen(0, step), chunk_gen(1, step)]
        for _ in itertools.zip_longest(*gens):
            pass
    if RSTG < 4:
        return
    for t in range(4):
        lo = t * 512
        o_, d_o = M.ntf()
        P.dve(lambda e, o_=o_: e.tensor_tensor(out=o_[:], in0=M.ofw[:, lo:lo + 512], in1=M.obw[:, lo:lo + 512],
                                               op=ALU.add), reads=[M.d_ofw, M.d_obw], writes=[d_o])
        sq, d_sq = M.ntb()
        P.act(lambda e, sq=sq, o_=o_: e.activation(out=sq[:], in_=o_[:], func=AF.Square), reads=[d_o], writes=[d_sq])
        P.pe(lambda e, sq=sq: e.matmul(M.ps[0][:, :], lhsT=M.ones[:], rhs=sq[:], start=True, stop=True),
             reads=[d_sq, M.d_ones], writes=[M.d_ps[0]])
        ri, d_ri = M.ntf()
        emit_rsqrt(P, M, ri[:], d_ri, M.ps[0][:, :], M.d_ps[0], 512, 1.0 / 128)
        o2, d_o2 = M.ntf()
        P.dve(lambda e, o2=o2, o_=o_, ri=ri: e.scalar_tensor_tensor(
            out=o2[:], in0=o_[:], scalar=M.vec[:, 5:6], in1=ri[:], op0=ALU.mult, op1=ALU.mult),
            reads=[d_o, d_ri, M.d_vec], writes=[d_o2])
        o3, d_o3 = M.ntb()
        P.dve(lambda e, o3=o3, o2=o2: e.tensor_tensor(out=o3[:], in0=o2[:], in1=M.sg_sb[:, lo:lo + 512], op=ALU.mult),
              reads=[d_o2, M.d_rp], writes=[d_o3])
        rows = mergedT("rec", r) if callable(mergedT) else mergedT[512 + r * 128:512 + (r + 1) * 128, :]
        P.dma(rows[:, lo:lo + 512], o3[:], reads=[d_o3], writes=[d_merged], semkey="mg")


def emit_load_act16(P, C, srcT, d_src):
    for q in range(4):
        P.dma(C.A[:, q * 4:(q + 1) * 4, :], srcT[q * 512:(q + 1) * 512, :].rearrange("(c p) t -> p c t", p=128),
              reads=[d_src], writes=C.d_A + C.d_xt, semkey="act16")


def emit_conv_in(P, C, w_in, bgT, cvT, d_out, st, bnd=None, d_bnd=None):
    it = 0
    si = 0
    for fc in range(16):
        s = fc % 2
        wv = C.wgu[s][:].rearrange("p a c n -> p (a c n)").rearrange("p (q c n) -> p q c n", q=4, c=16)
        for q in range(3):
            src = w_in[:, q * 2048 + fc * 128: q * 2048 + (fc + 1) * 128].rearrange("(c p) n -> p c n", p=128)
            P.dma(wv[:, q, :, :], src, writes=[C.d_wgu[s]], semkey=f"wgu{s}", eng="pool")
        for ti, (off, n) in enumerate(C.tiles):
            pb = (it % 2) * 3
            it += 1
            for q in range(3):
                pt = C.ps[pb + q]
                for k in range(16):
                    P.pe(lambda e, pt=pt, q=q, k=k, wv=wv, off=off, n=n: e.matmul(
                        pt[:, 0:n], lhsT=wv[:, q, k, :], rhs=C.hy[:, k, off:off + n],
                        start=(k == 0), stop=(k == 15)),
                        reads=[C.d_wgu[s], C.d_h[ti]], writes=[C.d_ps[pb + q]])
            sb, d_sb = st[si % len(st)]
            si += 1
            P.act(lambda e, sb=sb, pb=pb, n=n: e.activation(out=sb[:, 0:n], in_=C.ps[pb][:, 0:n], func=AF.Copy),
                  reads=[C.d_ps[pb]], writes=[d_sb])
            P.dma(bgT[fc * 128:(fc + 1) * 128, off:off + n], sb[:, 0:n], reads=[d_sb], writes=[d_out],
                  semkey="cvo")
            tmp, d_tmp = C.next_tmp()
            P.act(lambda e, tmp=tmp, pb=pb, n=n: e.activation(out=tmp[:, 0:n], in_=C.ps[pb + 1][:, 0:n],
                                                              func=AF.Copy),
                  reads=[C.d_ps[pb + 1]], writes=[d_tmp])
            sb2, d_sb2 = st[si % len(st)]
            si += 1
            P.dve(lambda e, sb2=sb2, tmp=tmp, pb=pb, n=n: e.tensor_tensor(
                out=sb2[:, 0:n], in0=tmp[:, 0:n], in1=C.ps[pb + 2][:, 0:n], op=ALU.mult),
                reads=[d_tmp, C.d_ps[pb + 2]], writes=[d_sb2])
            P.dma(cvT[fc * 128:(fc + 1) * 128, off:off + n], sb2[:, 0:n], reads=[d_sb2], writes=[d_out],
                  semkey="cvo")
            if bnd is not None and off == 0:
                P.act(lambda e, sb2=sb2, fc=fc: e.activation(out=bnd[:, 0, fc:fc + 1], in_=sb2[:, 0:1], func=AF.Copy),
                      reads=[d_sb2], writes=[d_bnd])
            if bnd is not None and off + n == C.NT:
                P.act(lambda e, sb2=sb2, fc=fc, n=n: e.activation(out=bnd[:, 1, fc:fc + 1], in_=sb2[:, n - 1:n],
                                                                  func=AF.Copy),
                      reads=[d_sb2], writes=[d_bnd])


def emit_conv(P, C, bgT, cvhT, d_in, cw, d_cw, st):
    si = 0
    for c in range(16):
        for ti, (off, n) in enumerate(C.tiles):
            i1 = si % len(st)
            si += 1
            i2 = si % len(st)
            si += 1
            cvt, d_cvt = st[i1]
            bt, d_bt = st[i2]
            P.dma(cvt[:, 0:n + 2], cvhT[c * 128:(c + 1) * 128, off:off + n + 2], reads=[d_in], writes=[d_cvt],
                  semkey=f"cvi{i1}")
            P.dma(bt[:, 0:n], bgT[c * 128:(c + 1) * 128, off:off + n], reads=[d_in], writes=[d_bt],
                  semkey=f"cvi{i2}")
            u, d_u = C.next_tmp()
            P.dve(lambda e, u=u, cvt=cvt, c=c, n=n: e.tensor_scalar(
                out=u[:, 0:n], in0=cvt[:, 0:n], scalar1=cw[:, c, 0:1], scalar2=None, op0=ALU.mult),
                reads=[d_cvt, d_cw], writes=[d_u])
            P.dve(lambda e, u=u, cvt=cvt, c=c, n=n: e.scalar_tensor_tensor(
                out=u[:, 0:n], in0=cvt[:, 1:n + 1], scalar=cw[:, c, 1:2], in1=u[:, 0:n], op0=ALU.mult, op1=ALU.add),
                reads=[d_cvt, d_cw, d_u], writes=[d_u])
            P.dve(lambda e, u=u, cvt=cvt, c=c, n=n: e.scalar_tensor_tensor(
                out=u[:, 0:n], in0=cvt[:, 2:n + 2], scalar=cw[:, c, 2:3], in1=u[:, 0:n], op0=ALU.mult, op1=ALU.add),
                reads=[d_cvt, d_cw, d_u], writes=[d_u])
            P.dve(lambda e, u=u, bt=bt, c=c, off=off, n=n: e.tensor_tensor(
                out=C.A[:, c, off:off + n], in0=u[:, 0:n], in1=bt[:, 0:n], op=ALU.mult),
                reads=[d_u, d_bt], writes=[C.d_A[ti]] + C.d_xt)


def emit_load_mg_sel(P, C, mg_g, d_src, selv, d_sel):
    for hc in range(2):
        for c in range(16):
            kind, rk, hd = c // 8, (c // 4) % 2, c % 4
            k = kind * 2 + hd // 2
            r0 = rk * 256 + (hd % 2) * 128
            P.dma(C.A[:, hc * 16 + c, :], mg_g[k][r0:r0 + 128, hc * 1024:(hc + 1) * 1024],
                  reads=[d_src], writes=C.d_A + C.d_xt, semkey="act16")
    for c in range(16):
        P.dve(lambda e, c=c: e.tensor_scalar(out=C.A[:, c, :], in0=C.A[:, c, :], scalar1=selv[:, 0:1], scalar2=None,
                                             op0=ALU.mult), reads=C.d_A + [d_sel], writes=C.d_A)
        P.dve(lambda e, c=c: e.scalar_tensor_tensor(out=C.A[:, c, :], in0=C.A[:, 16 + c, :], scalar=selv[:, 1:2],
                                                    in1=C.A[:, c, :], op0=ALU.mult, op1=ALU.add),
              reads=C.d_A + [d_sel], writes=C.d_A)


def emit_conv_halo(P, C, bgT, cvT, d_in, bnd_g, d_bnd, selv, d_sel, cw, d_cw, st, hal, d_hal):
    P.dma(hal[:, 0, :], bnd_g[0:128, 16:32], reads=[d_bnd], writes=[d_hal], semkey="hal")
    P.dma(hal[:, 1, :], bnd_g[128:256, 0:16], reads=[d_bnd], writes=[d_hal], semkey="hal")
    P.dve(lambda e: e.tensor_scalar(out=hal[:, 0, :], in0=hal[:, 0, :], scalar1=selv[:, 2:3], scalar2=None,
                                    op0=ALU.mult), reads=[d_hal, d_sel], writes=[d_hal])
    P.dve(lambda e: e.tensor_scalar(out=hal[:, 1, :], in0=hal[:, 1, :], scalar1=selv[:, 3:4], scalar2=None,
                                    op0=ALU.mult), reads=[d_hal, d_sel], writes=[d_hal])
    si = 0
    NT = C.NT
    for c in range(16):
        for ti, (off, n) in enumerate(C.tiles):
            i1 = si % len(st)
            si += 1
            i2 = si % len(st)
            si += 1
            cvt, d_cvt = st[i1]
            bt, d_bt = st[i2]
            lo = max(off - 1, 0)
            hi = min(off + n + 1, NT)
            dlo = lo - (off - 1)
            P.dma(cvt[:, dlo:dlo + (hi - lo)], cvT[c * 128:(c + 1) * 128, lo:hi], reads=[d_in], writes=[d_cvt],
                  semkey=f"cvi{i1}")
            if off == 0:
                P.dve(lambda e, cvt=cvt, c=c: e.tensor_copy(out=cvt[:, 0:1], in_=hal[:, 0, c:c + 1]),
                      reads=[d_hal], writes=[d_cvt])
            if off + n == NT:
                P.dve(lambda e, cvt=cvt, c=c, n=n: e.tensor_copy(out=cvt[:, n + 1:n + 2], in_=hal[:, 1, c:c + 1]),
                      reads=[d_hal], writes=[d_cvt])
            P.dma(bt[:, 0:n], bgT[c * 128:(c + 1) * 128, off:off + n], reads=[d_in], writes=[d_bt],
                  semkey=f"cvi{i2}")
            u, d_u = C.next_tmp()
            P.dve(lambda e, u=u, cvt=cvt, c=c, n=n: e.tensor_scalar(
                out=u[:, 0:n], in0=cvt[:, 0:n], scalar1=cw[:, c, 0:1], scalar2=None, op0=ALU.mult),
                reads=[d_cvt, d_cw], writes=[d_u])
            P.dve(lambda e, u=u, cvt=cvt, c=c, n=n: e.scalar_tensor_tensor(
                out=u[:, 0:n], in0=cvt[:, 1:n + 1], scalar=cw[:, c, 1:2], in1=u[:, 0:n], op0=ALU.mult, op1=ALU.add),
                reads=[d_cvt, d_cw, d_u], writes=[d_u])
            P.dve(lambda e, u=u, cvt=cvt, c=c, n=n: e.scalar_tensor_tensor(
                out=u[:, 0:n], in0=cvt[:, 2:n + 2], scalar=cw[:, c, 2:3], in1=u[:, 0:n], op0=ALU.mult, op1=ALU.add),
                reads=[d_cvt, d_cw, d_u], writes=[d_u])
            P.dve(lambda e, u=u, bt=bt, c=c, off=off, n=n: e.tensor_tensor(
                out=C.A[:, c, off:off + n], in0=u[:, 0:n], in1=bt[:, 0:n], op=ALU.mult),
                reads=[d_u, d_bt], writes=[C.d_A[ti]] + C.d_xt)


BF = ml_dtypes.bfloat16
NCORES = 8


def _ffn_w(nc, tag):
    wg = nc.dram_tensor("wg" + tag, [2048, 5504], F32, kind="ExternalInput")
    wu = nc.dram_tensor("wu" + tag, [2048, 5504], F32, kind="ExternalInput")
    wd = nc.dram_tensor("wd" + tag, [5504, 2048], F32, kind="ExternalInput")
    return wg, wu, wd


def build_A():
    nc = bass.Bass("TRN2", target_bir_lowering=False)
    P = Prog(nc)
    xT = nc.dram_tensor("xT", [2048, 1024], F32, kind="ExternalInput")
    ctxT = nc.dram_tensor("ctxT", [2048, 128], F32, kind="ExternalInput")
    cvecT = nc.dram_tensor("cvecT", [128, 16, 2], F32, kind="ExternalInput")
    ada_w = nc.dram_tensor("ada_w", [2048, 18432], F32, kind="ExternalInput")
    ada_bT = nc.dram_tensor("ada_bT", [128, 144], F32, kind="ExternalInput")
    gTin = nc.dram_tensor("gTin", [128, 6, 16], F32, kind="ExternalInput")
    wg, wu, wd = _ffn_w(nc, "")
    x1T = nc.dram_tensor("x1T", [2048, 1024], F32, kind="ExternalOutput")
    hT = nc.dram_tensor("hT", [2048, 1152], BF16, kind="ExternalOutput")
    mTo = nc.dram_tensor("mTo", [128, 144, 2], F32, kind="ExternalOutput")
    xc1T = nc.dram_tensor("xc1T", [2048, 128], F32, kind="Internal")
    C = Ctx(P, [512, 512, 128])
    scr = {"cv": P.sbuf("cv", [128, 16, 2], F32), "sc": P.sbuf("sc", [128, 16, 2], BF16),
           "bT": P.sbuf("bT", [128, 144], F32)}
    P.dma(C.gT[:], gTin[:], writes=[C.d_gT], semkey="small3")
    emit_modulation(P, C, ada_w, ada_bT[:], cvecT[:], scr)
    d_in = Dep("in")
    d_x1 = [Dep("x1a"), Dep("x1b"), Dep("xc1")]
    emit_coefs(P, C, 1, 2, 0, 1, 0.5, [0, 1])
    srcs = [(xT[:, 0:512], d_in, 0), (xT[:, 512:1024], d_in, 0), (ctxT[:, :], d_in, 1)]
    dsts = [(x1T[:, 0:512], d_x1[0], 0), (x1T[:, 512:1024], d_x1[1], 0), (xc1T[:, :], d_x1[2], 1)]
    emit_ffn(P, C, srcs, dsts, wg, wu, wd, 0)
    emit_coefs(P, C, 4, None, 2, None, 1.0, [0, 1])
    emit_prenorm(P, C, dsts, 3, lambda ti: C.hy[:, :, C.tiles[ti][0]:C.tiles[ti][0] + C.tiles[ti][1]],
                 C.d_h, C.d_A)
    d_hT = Dep("hT")
    for ti, (off, n) in enumerate(C.tiles):
        P.dma(hT[:, off:off + n].rearrange("(c p) t -> p c t", p=128), C.hy[:, :, off:off + n],
              reads=[C.d_h[ti]], writes=[d_hT], semkey="hT")
    P.dma(mTo[:], C.mT[:], reads=[C.d_mT], writes=[Dep("mTo")], semkey="mTo")
    P.emit()
    return nc


def build_B():
    nc = bass.Bass("TRN2", target_bir_lowering=False)
    P = Prog(nc)
    hTf = nc.dram_tensor("hTf", [2048, NTOK], BF16, kind="ExternalInput")
    w_att = nc.dram_tensor("w_att", [4, 3, 2048, 128], F32, kind="ExternalInput")
    w_rec = nc.dram_tensor("w_rec", [4, 5, 2048, 128], F32, kind="ExternalInput")
    ropeC = nc.dram_tensor("ropeCin", [128, 2048], F32, kind="ExternalInput")
    ropeS = nc.dram_tensor("ropeSin", [128, 2048], F32, kind="ExternalInput")
    perm = nc.dram_tensor("permin", [128, 128], F32, kind="ExternalInput")
    ident = nc.dram_tensor("identin", [128, 128], F32, kind="ExternalInput")
    maskf = nc.dram_tensor("maskfin", [128, 128], F32, kind="ExternalInput")
    maskb = nc.dram_tensor("maskbin", [128, 128], F32, kind="ExternalInput")
    lamT = nc.dram_tensor("lamT", [64, 4], F32, kind="ExternalInput")
    dng = nc.dram_tensor("dng", [128, 1], F32, kind="ExternalInput")
    rng = nc.dram_tensor("rng", [128, 1], F32, kind="ExternalInput")
    lbraw = nc.dram_tensor("lbraw", [128, 2, 2, 4], F32, kind="ExternalInput")
    mergedT = nc.dram_tensor("mergedT", [1024, 2048], BF16, kind="ExternalOutput")
    M = MixCtx(P)
    emit_mix_setup(P, M, hTf, ropeC[:], ropeS[:], perm[:], ident[:], maskf[:], maskb[:], lamT[:], dng[:], rng[:],
                   lbraw[:])
    d_merged = Dep("merged")
    for hd in range(4):
        for i in range(3):
            load_w(P, M, i, w_att[hd, i])
        for i in range(5):
            load_w(P, M, 3 + i, w_rec[hd, i])
        emit_attention_head(P, M, hd, mergedT, d_merged)
        emit_rec_head(P, M, hd, mergedT, d_merged)
    P.emit()
    return nc


def build_C():
    nc = bass.Bass("TRN2", target_bir_lowering=False)
    P = Prog(nc)
    x1T = nc.dram_tensor("x1T", [2048, 1024], F32, kind="ExternalInput")
    mgT = nc.dram_tensor("mgT", [2048, 1024], BF16, kind="ExternalInput")
    w_out = nc.dram_tensor("w_out", [2048, 2048], F32, kind="ExternalInput")
    mT0 = nc.dram_tensor("mT0", [128, 144, 2], F32, kind="ExternalInput")
    gT0 = nc.dram_tensor("gT0", [128, 6, 16], F32, kind="ExternalInput")
    gT1 = nc.dram_tensor("gT1", [128, 6, 16], F32, kind="ExternalInput")
    cvecT = nc.dram_tensor("cvecT", [128, 16, 2], F32, kind="ExternalInput")
    ada_w = nc.dram_tensor("ada_w", [2048, 18432], F32, kind="ExternalInput")
    ada_bT = nc.dram_tensor("ada_bT", [128, 144], F32, kind="ExternalInput")
    wgA, wuA, wdA = _ffn_w(nc, "A")
    wgB, wuB, wdB = _ffn_w(nc, "B")
    cw_in = nc.dram_tensor("cw_in", [2048, 6144], F32, kind="ExternalInput")
    x4T = nc.dram_tensor("x4T", [2048, 1024], F32, kind="ExternalOutput")
    bgT = nc.dram_tensor("bgT", [2048, 1024], F32, kind="ExternalOutput")
    cvT = nc.dram_tensor("cvT", [2048, 1024], F32, kind="ExternalOutput")
    mT1 = nc.dram_tensor("mT1", [128, 144, 2], F32, kind="ExternalOutput")
    x2T = nc.dram_tensor("x2T", [2048, 1024], F32, kind="Internal")
    x3T = nc.dram_tensor("x3T", [2048, 1024], F32, kind="Internal")
    C = Ctx(P, [512, 512])
    scr = {"cv": P.sbuf("cv", [128, 16, 2], F32), "sc": P.sbuf("sc", [128, 16, 2], BF16),
           "bT": P.sbuf("bT", [128, 144], F32)}
    st = [(P.sbuf(f"st{i}", [128, 512], F32), Dep(f"st{i}")) for i in range(4)]
    d_in = Dep("in")

    def tl(t, d):
        return [(t[:, 0:512], d[0], 0), (t[:, 512:1024], d[1], 0)]
    d_x1 = [d_in, d_in]
    d_x2 = [Dep("x2a"), Dep("x2b")]
    d_x3 = [Dep("x3a"), Dep("x3b")]
    d_x4 = [Dep("x4a"), Dep("x4b")]
    P.dma(C.gT[:], gT0[:], writes=[C.d_gT], semkey="small3")
    P.dma(C.mT[:], mT0[:], writes=[C.d_mT], semkey="small4")
    emit_load_act16(P, C, mgT, d_in)
    emit_coefs(P, C, 4, 5, 2, 3, 1.0, [0])
    emit_down_residual(P, C, 16, w_out, tl(x1T, d_x1), tl(x2T, d_x2))
    emit_coefs(P, C, 7, 8, 4, 5, 0.5, [0])
    emit_ffn(P, C, tl(x2T, d_x2), tl(x3T, d_x3), wgA, wuA, wdA, 6)
    P.dma(C.gT[:], gT1[:], writes=[C.d_gT], semkey="small3")
    emit_modulation(P, C, ada_w, ada_bT[:], cvecT[:], scr)
    emit_coefs(P, C, 1, 2, 0, 1, 0.5, [0])
    emit_ffn(P, C, tl(x3T, d_x3), tl(x4T, d_x4), wgB, wuB, wdB, 0)
    emit_coefs(P, C, 4, None, 2, None, 1.0, [0])
    emit_prenorm(P, C, tl(x4T, d_x4), 3, lambda ti: C.hy[:, :, C.tiles[ti][0]:C.tiles[ti][0] + C.tiles[ti][1]],
                 C.d_h, C.d_A)
    emit_conv_in(P, C, cw_in, bgT, cvT, Dep("cvout"), st)
    P.dma(mT1[:], C.mT[:], reads=[C.d_mT], writes=[Dep("mT1o")], semkey="mTo")
    P.emit()
    return nc


def build_D():
    nc = bass.Bass("TRN2", target_bir_lowering=False)
    P = Prog(nc)
    x4T = nc.dram_tensor("x4T", [2048, 1024], F32, kind="ExternalInput")
    bgT = nc.dram_tensor("bgT", [2048, 1024], F32, kind="ExternalInput")
    cvhT = nc.dram_tensor("cvhT", [2048, 1026], F32, kind="ExternalInput")
    cwT = nc.dram_tensor("cwT", [128, 16, 3], F32, kind="ExternalInput")
    cw_out = nc.dram_tensor("cw_out", [2048, 2048], F32, kind="ExternalInput")
    mT1 = nc.dram_tensor("mT1", [128, 144, 2], F32, kind="ExternalInput")
    gT1 = nc.dram_tensor("gT1", [128, 6, 16], F32, kind="ExternalInput")
    wg, wu, wd = _ffn_w(nc, "")
    outT = nc.dram_tensor("outT", [2048, 1024], F32, kind="ExternalOutput")
    x5T = nc.dram_tensor("x5T", [2048, 1024], F32, kind="Internal")
    C = Ctx(P, [512, 512])
    st = [(P.sbuf(f"st{i}", [128, 514], F32), Dep(f"st{i}")) for i in range(4)]
    cw = P.sbuf("cw", [128, 16, 3], F32)
    d_cw = Dep("cw")
    d_in = Dep("in")

    def tl(t, d):
        return [(t[:, 0:512], d[0], 0), (t[:, 512:1024], d[1], 0)]
    d_x5 = [Dep("x5a"), Dep("x5b")]
    d_o = [Dep("oa"), Dep("ob")]
    P.dma(C.gT[:], gT1[:], writes=[C.d_gT], semkey="small3")
    P.dma(C.mT[:], mT1[:], writes=[C.d_mT], semkey="small4")
    P.dma(cw[:], cwT[:], writes=[d_cw], semkey="small5")
    emit_conv(P, C, bgT, cvhT, d_in, cw, d_cw, st)
    emit_coefs(P, C, 4, 5, 2, 3, 1.0, [0])
    emit_down_residual(P, C, 16, cw_out, tl(x4T, [d_in, d_in]), tl(x5T, d_x5))
    emit_coefs(P, C, 7, 8, 4, 5, 0.5, [0])
    emit_ffn(P, C, tl(x5T, d_x5), tl(outT, d_o), wg, wu, wd, 6)
    P.emit()
    return nc


ARENA_BYTES = 207 * 1024
FUSE_STOP = int(os.environ.get("FUSE_STOP", "0"))


def build_fused():
    nc = bass.Bass("TRN2", target_bir_lowering=False)
    P = Prog(nc)
    P.use_arena(ARENA_BYTES)
    inp = lambda n, sh, dt=F32: nc.dram_tensor(n, sh, dt, kind="ExternalInput")
    xT = inp("xT", [2048, 1024])
    ctxT = inp("ctxT", [2048, 128])
    ada_sl = inp("ada_sl", [2, 2048, 4608])
    cvec3 = inp("cvec3", [128, 16, 3])
    selb = inp("selb", [128, 2])
    bT0 = inp("bT0", [128, 144])
    bT1 = inp("bT1", [128, 144])
    gT0 = inp("gT0", [128, 6, 16])
    gT1 = inp("gT1", [128, 6, 16])
    W = [_ffn_w(nc, str(i)) for i in range(4)]
    w_att = inp("w_att", [4, 3, 2048, 128])
    w_rec = inp("w_rec", [4, 5, 2048, 128])
    ropeC = inp("ropeCin", [128, 2048])
    ropeS = inp("ropeSin", [128, 2048])
    perm = inp("permin", [128, 128])
    ident = inp("identin", [128, 128])
    maskf = inp("maskfin", [128, 128])
    maskb = inp("maskbin", [128, 128])
    lamT = inp("lamT", [64, 4])
    dng = inp("dng", [128, 1])
    rng = inp("rng", [128, 1])
    lbraw = inp("lbraw", [128, 2, 2, 4])
    w_out = inp("w_out", [2048, 2048])
    cw_in = inp("cw_in", [2048, 6144])
    cwT = inp("cwT", [128, 16, 3])
    cw_out = inp("cw_out", [2048, 2048])
    selin = inp("selin", [128, 4])
    outT = nc.dram_tensor("outT", [2048, 1024], F32, kind="ExternalOutput")
    itn = lambda n, sh, dt=F32: nc.dram_tensor(n, sh, dt, kind="Internal")
    x1T, x2T, x3T, x4T, x5T = [itn(f"x{i}T", [2048, 1024]) for i in (1, 2, 3, 4, 5)]
    xc1T = itn("xc1T", [2048, 128])
    hT_own = [itn(f"hT_own{t}", [2048, n_], BF16) for t, n_ in enumerate((512, 512, 128))]
    hT_g = [itn(f"hT_g{t}", [4096, n_], BF16) for t, n_ in enumerate((512, 512, 128))]
    mg_own = [itn(f"mg_own{q}", [256, 2048], BF16) for q in range(4)]
    mg_g = [itn(f"mg_g{q}", [512, 2048], BF16) for q in range(4)]
    bgT = itn("bgT", [2048, 1024])
    cvT = itn("cvT", [2048, 1024])
    bnd_own = itn("bnd_own", [128, 32])
    bnd_g = itn("bnd_g", [256, 32])
    d_in = Dep("in")

    def tl(t, d):
        return [(t[:, 0:512], d[0], 0), (t[:, 512:1024], d[1], 0)]

    def mk_scr():
        return {"cv": P.sbuf("cv", [128, 16, 2], F32), "sc": P.sbuf("sc", [128, 16, 2], BF16),
                "bT": P.sbuf("bT", [128, 144], F32)}
    hyv = lambda C: (lambda ti: C.hy[:, :, C.tiles[ti][0]:C.tiles[ti][0] + C.tiles[ti][1]])

    mp_own = itn("mp_own", [128, 216])
    mp_g1 = itn("mp_g1", [256, 216])
    mp_g2 = itn("mp_g2", [512, 216])
    mTd = [itn("mT0d", [128, 144, 2]), itn("mT1d", [128, 144, 2])]
    d_mTd = [Dep("mT0d"), Dep("mT1d")]
    C = Ctx(P, [512, 512])
    emit_modulation_sharded(P, C, ada_sl, cvec3[:], selb[:], bT0[:], bT1[:], mp_own, mp_g1, mp_g2, mTd, d_mTd)
    P.barrier()
    P.aoff = 0
    C = Ctx(P, [512, 512, 128])
    P.dma(C.gT[:], gT0[:], writes=[C.d_gT], semkey="small3")
    P.dma(C.mT[:], mTd[0][:], reads=[d_mTd[0]], writes=[C.d_mT], semkey="small4")
    d_x1 = [Dep("x1a"), Dep("x1b"), Dep("xc1")]
    emit_coefs(P, C, 1, 2, 0, 1, 0.5, [0, 1])
    srcs = [(xT[:, 0:512], d_in, 0), (xT[:, 512:1024], d_in, 0), (ctxT[:, :], d_in, 1)]
    dsts = [(x1T[:, 0:512], d_x1[0], 0), (x1T[:, 512:1024], d_x1[1], 0), (xc1T[:, :], d_x1[2], 1)]
    emit_ffn(P, C, srcs, dsts, W[0][0], W[0][1], W[0][2], 0)
    emit_coefs(P, C, 4, None, 2, None, 1.0, [0, 1])
    d_hg = Dep("hT_g")

    def ship_tile(ti, off, n):
        d_t = Dep(f"hT_own{ti}")
        P.dma(hT_own[ti][:, :].rearrange("(c p) t -> p c t", p=128), C.hy[:, :, off:off + n],
              reads=[C.d_h[ti]], writes=[d_t], semkey=f"hT{ti}")
        P.allgather_pairs(hT_g[ti], hT_own[ti], reads=[d_t], writes=[d_hg], semkey="cc1")
    emit_prenorm(P, C, dsts, 3, hyv(C), C.d_h, C.d_A, resident=True, after_tile=ship_tile)
    P.barrier()
    P.aoff = 0
    M = MixCtx(P)
    for r in range(2):
        for q in range(4):
            rs = slice(r * 2048 + q * 512, r * 2048 + (q + 1) * 512)
            P.dma(M.hs[:, q * 4:(q + 1) * 4, r * 128:(r + 1) * 128],
                  hT_g[2][rs, :].rearrange("(c p) t -> p c t", p=128), reads=[d_hg], writes=[M.d_hs], semkey="hs")
            for ti in range(2):
                c0 = 256 + r * 1024 + ti * 512
                P.dma(M.hs[:, q * 4:(q + 1) * 4, c0:c0 + 512],
                      hT_g[ti][rs, :].rearrange("(c p) t -> p c t", p=128), reads=[d_hg], writes=[M.d_hs],
                      semkey="hs")
    emit_mix_setup(P, M, None, ropeC[:], ropeS[:], perm[:], ident[:], maskf[:], maskb[:], lamT[:], dng[:], rng[:],
                   lbraw[:])
    d_mg = Dep("mg_own")
    d_mgg = Dep("mg_g")
    for i in range(3):
        load_w(P, M, i, w_att[0, i])
    for i in range(5):
        load_w(P, M, 3 + i, w_rec[0, i])
    for hd in range(4):
        mgdst = lambda kind, h: mg_own[(0 if kind == "att" else 2) + h // 2][(h % 2) * 128:(h % 2) * 128 + 128, :]
        emit_attention_head(P, M, hd, mgdst, d_mg)
        P.barrier()

        def prefetch(hd=hd):
            if hd + 1 < 4:
                for i in range(3):
                    load_w(P, M, i, w_att[hd + 1, i])
                for i in range(5):
                    load_w(P, M, 3 + i, w_rec[hd + 1, i])
        emit_rec_head(P, M, hd, mgdst, d_mg, after_burst=prefetch)
        P.barrier()
        if hd % 2 == 1:
            for q in (hd // 2, 2 + hd // 2):
                P.allgather_pairs(mg_g[q], mg_own[q], reads=[d_mg], writes=[d_mgg], semkey="cc2")
    if FUSE_STOP == 2:
        dbg = nc.dram_tensor("dbg", [2048, 2048], BF16, kind="ExternalOutput")
        for q in range(4):
            for r in range(2):
                base = r * 1024 + (q // 2) * 512 + (q % 2) * 256
                P.dma(dbg[base:base + 256, :], mg_g[q][r * 256:(r + 1) * 256, :], reads=[d_mgg],
                      writes=[Dep("dbg")], semkey="dbg")
        P.emit()
        return nc
    P.barrier()
    P.aoff = 0
    C = Ctx(P, [512, 512])
    selv = P.sbuf("selv", [128, 4], F32)
    d_sel = Dep("selv")
    st = [(P.sbuf(f"st{i}", [128, 514], F32), Dep(f"st{i}")) for i in range(4)]
    cw = P.sbuf("cw", [128, 16, 3], F32)
    d_cw = Dep("cw")
    hal = P.sbuf("hal", [128, 2, 16], F32)
    d_hal = Dep("hal")
    P.dma(selv[:], selin[:], writes=[d_sel], semkey="small5")
    P.dma(cw[:], cwT[:], writes=[d_cw], semkey="small5")
    P.dma(C.gT[:], gT0[:], writes=[C.d_gT], semkey="small3")
    P.dma(C.mT[:], mTd[0][:], reads=[d_mTd[0]], writes=[C.d_mT], semkey="small4")
    d_x2 = [Dep("x2a"), Dep("x2b")]
    d_x3 = [Dep("x3a"), Dep("x3b")]
    d_x4 = [Dep("x4a"), Dep("x4b")]
    d_x5 = [Dep("x5a"), Dep("x5b")]
    d_o = [Dep("oa"), Dep("ob")]
    emit_load_mg_sel(P, C, mg_g, d_mgg, selv, d_sel)
    emit_coefs(P, C, 4, 5, 2, 3, 1.0, [0])
    emit_down_residual(P, C, 16, w_out, tl(x1T, d_x1), tl(x2T, d_x2))
    emit_coefs(P, C, 7, 8, 4, 5, 0.5, [0])
    emit_ffn(P, C, tl(x2T, d_x2), tl(x3T, d_x3), W[1][0], W[1][1], W[1][2], 6, resident=True)
    P.dma(C.gT[:], gT1[:], writes=[C.d_gT], semkey="small3")
    P.dma(C.mT[:], mTd[1][:], reads=[d_mTd[1]], writes=[C.d_mT], semkey="small4")
    emit_coefs(P, C, 1, 2, 0, 1, 0.5, [0])
    emit_ffn(P, C, tl(x3T, d_x3), tl(x4T, d_x4), W[2][0], W[2][1], W[2][2], 0, resident=True)
    emit_coefs(P, C, 4, None, 2, None, 1.0, [0])
    emit_prenorm(P, C, tl(x4T, d_x4), 3, hyv(C), C.d_h, C.d_A, resident=True)
    d_cv = Dep("cvout")
    st512 = [(t[:, 0:512], d) for (t, d) in st]
    bnd_sb = P.sbuf("bnd_sb", [128, 2, 16], F32)
    d_bsb = Dep("bnd_sb")
    emit_conv_in(P, C, cw_in, bgT, cvT, d_cv, st512, bnd_sb, d_bsb)
    d_bo = Dep("bnd_own")
    P.dma(bnd_own[:, :], bnd_sb[:].rearrange("p a c -> p (a c)"), reads=[d_bsb], writes=[d_bo], semkey="bnd")
    d_bg = Dep("bnd_g")
    P.allgather_pairs(bnd_g, bnd_own, reads=[d_bo], writes=[d_bg], semkey="cc3")
    emit_conv_halo(P, C, bgT, cvT, d_cv, bnd_g, d_bg, selv, d_sel, cw, d_cw, st, hal, d_hal)
    emit_coefs(P, C, 4, 5, 2, 3, 1.0, [0])
    emit_down_residual(P, C, 16, cw_out, tl(x4T, d_x4), tl(x5T, d_x5))
    emit_coefs(P, C, 7, 8, 4, 5, 0.5, [0])
    emit_ffn(P, C, tl(x5T, d_x5), tl(outT, d_o), W[3][0], W[3][1], W[3][2], 6, resident=True)
    stuck = simulate_sync(P)
    if stuck:
        raise RuntimeError(f"sync deadlock: {stuck}")
    P.emit()
    return nc


def mix_consts():
    p = np.arange(128)
    d = p % 64
    i = d % 16
    freqs = (10000.0 ** (-np.arange(16, dtype=np.float32) / 16)).astype(np.float32)
    t = np.arange(2048)
    row = (t // 64).astype(np.float32)
    col = (t % 64).astype(np.float32)
    pos = np.where((d < 32)[:, None], row[None, :], col[None, :]).astype(np.float32)
    ang = (pos * freqs[i][:, None]).astype(np.float32)
    C = np.cos(ang).astype(np.float32)
    S = np.sin(ang).astype(np.float32)
    perm = np.zeros((128, 128), np.float32)
    for m in range(128):
        if (m % 32) < 16:
            perm[m + 16, m] = -1.0
        else:
            perm[m - 16, m] = 1.0
    ident = np.eye(128, dtype=np.float32)
    s = np.arange(128)[:, None]
    tt = np.arange(128)[None, :]
    return C, S, perm, ident, (s <= tt).astype(np.float32), (s >= tt).astype(np.float32)


_PROGS = {}


def _prog(name, fn):
    if name not in _PROGS:
        _PROGS[name] = fn()
    return _PROGS[name]


def _run(nc, in_maps):
    res = run_bass_kernel_spmd(nc, in_maps, core_ids=list(range(NCORES)))
    return res.results


def kernel(x, c, ctx, c_ctx, ada_w, ada_b, norm_g, ffn_w_gate, ffn_w_up, ffn_w_down, mix_w_in, mix_w_out,
           diff_lambda, diff_norm_g, rec_norm_g, rec_lb, conv_w_in, conv_w, conv_w_out):
    f32 = lambda a: np.ascontiguousarray(np.asarray(a, dtype=np.float32))
    x, c, ctx, c_ctx = f32(x), f32(c), f32(ctx), f32(c_ctx)
    ada_w, ada_b, norm_g = f32(ada_w), f32(ada_b), f32(norm_g)
    ffn_w_gate, ffn_w_up, ffn_w_down = f32(ffn_w_gate), f32(ffn_w_up), f32(ffn_w_down)
    mix_w_in, mix_w_out = f32(mix_w_in), f32(mix_w_out)
    conv_w_in, conv_w, conv_w_out = f32(conv_w_in), f32(conv_w), f32(conv_w_out)
    rec_lb = f32(rec_lb)
    gT = [np.ascontiguousarray(norm_g[l].reshape(6, 16, 128).transpose(2, 0, 1)) for l in range(2)]
    bT = [np.ascontiguousarray(ada_b[l].reshape(144, 128).T) for l in range(2)]
    cvec = [np.ascontiguousarray(np.stack([c[b], c_ctx], -1).reshape(16, 128, 2).transpose(1, 0, 2))
            for b in range(4)]
    Cc, Ss, perm, ident, maskf, maskb = mix_consts()
    w_in = mix_w_in[0]
    cwT = np.ascontiguousarray(conv_w[0].reshape(3, 16, 128).transpose(2, 1, 0))
    lamT = np.ascontiguousarray(f32(diff_lambda)[0].T)
    dng = f32(diff_norm_g)[0].reshape(128, 1).copy()
    rngv = f32(rec_norm_g)[0].reshape(128, 1).copy()
    heads = []
    for hh in range(2):
        w_att = np.empty((4, 3, 2048, 128), np.float32)
        w_rec = np.empty((4, 5, 2048, 128), np.float32)
        lbraw = np.empty((128, 2, 2, 4), np.float32)
        for hd in range(4):
            g = hh * 4 + hd
            for q in range(3):
                w_att[hd, q] = w_in[:, q * 1024 + g * 128: q * 1024 + (g + 1) * 128]
            for q in range(5):
                w_rec[hd, q] = w_in[:, 3072 + q * 1024 + g * 128: 3072 + q * 1024 + (g + 1) * 128]
            lbraw[:, :, :, hd] = rec_lb[:, :, g * 128:(g + 1) * 128].transpose(2, 0, 1)
        heads.append((w_att, w_rec, lbraw))
    ada_sl = [np.ascontiguousarray(ada_w[:, :, qq * 4608:(qq + 1) * 4608]) for qq in range(4)]
    cvec3 = [np.ascontiguousarray(np.stack([c[bl], c[bl + 2], c_ctx], -1).reshape(16, 128, 3).transpose(1, 0, 2))
             for bl in range(2)]
    selb = [np.ascontiguousarray(np.tile(np.eye(2, dtype=np.float32)[w][None, :], (128, 1))) for w in range(2)]
    in_maps = []
    for i in range(NCORES):
        b, h = i // 2, i % 2
        sel = np.zeros((128, 4), np.float32)
        sel[:, 0] = 1.0 if h == 0 else 0.0
        sel[:, 1] = 1.0 if h == 1 else 0.0
        sel[:, 2] = 1.0 if h == 1 else 0.0
        sel[:, 3] = 1.0 if h == 0 else 0.0
        m = {
            "xT": np.ascontiguousarray(x[b, h * 1024:(h + 1) * 1024].T),
            "ctxT": np.ascontiguousarray(ctx[b, h * 128:(h + 1) * 128].T),
            "ada_sl": ada_sl[(i % 2) + 2 * (i // 4)], "cvec3": cvec3[b % 2], "selb": selb[b // 2],
            "bT0": bT[0], "bT1": bT[1],
            "gT0": gT[0], "gT1": gT[1],
            "w_att": heads[h][0], "w_rec": heads[h][1], "lbraw": heads[h][2],
            "ropeCin": Cc, "ropeSin": Ss, "permin": perm, "identin": ident, "maskfin": maskf, "maskbin": maskb,
            "lamT": lamT, "dng": dng, "rng": rngv,
            "w_out": mix_w_out[0], "cw_in": conv_w_in[0], "cwT": cwT, "cw_out": conv_w_out[0], "selin": sel}
        for k, (l, j) in enumerate([(0, 0), (0, 1), (1, 0), (1, 1)]):
            m[f"wg{k}"] = ffn_w_gate[l, j]
            m[f"wu{k}"] = ffn_w_up[l, j]
            m[f"wd{k}"] = ffn_w_down[l, j]
        in_maps.append(m)
    rD = _run(_prog("F", build_fused), in_maps)
    if FUSE_STOP:
        return rD
    out = np.empty((4, 2048, 2048), np.float32)
    for i in range(NCORES):
        b, h = i // 2, i % 2
        out[b, h * 1024:(h + 1) * 1024] = np.asarray(rD[i]["outT"]).T
    return out
```

```python
import contextlib
import types
import os
import math
import numpy as np
import ml_dtypes
import concourse.bass as bass
import concourse.mybir as mybir
from concourse.bass_utils import run_bass_kernel_spmd


F32 = mybir.dt.float32
BF16 = mybir.dt.bfloat16
ALU = mybir.AluOpType
AF = mybir.ActivationFunctionType
AX = mybir.AxisListType


class Dep:
    __slots__ = ("name", "w", "r", "excl")

    def __init__(self, name, excl=False):
        self.name = name
        self.w = {}
        self.r = {}
        self.excl = excl


class Ins:
    __slots__ = ("eng", "fn", "deps", "signal", "semkey", "semval", "is_dma", "idx", "inc")
    _n = 0

    def __init__(self, eng, fn, is_dma=False, semkey=None):
        self.eng = eng
        self.fn = fn
        self.deps = []
        self.signal = False
        self.is_dma = is_dma
        self.semkey = semkey
        self.semval = None
        self.inc = 16
        Ins._n += 1
        self.idx = Ins._n


def _freeze(fn):
    if getattr(fn, "__closure__", None) is None:
        return fn
    cells = []
    for c in fn.__closure__:
        try:
            cells.append(types.CellType(c.cell_contents))
        except ValueError:
            cells.append(c)
    return types.FunctionType(fn.__code__, fn.__globals__, fn.__name__, fn.__defaults__, tuple(cells))


class Prog:
    ENGS = ("pe", "act", "dve", "pool", "sp")

    def __init__(self, nc):
        self.nc = nc
        self.streams = {e: [] for e in self.ENGS}
        self.stack = contextlib.ExitStack()
        self.dma_keys = {}
        self.n_sb = 0
        self.arena = None
        self.aoff = 0
        self.pending = {e: [] for e in self.ENGS}
        self.open_dmas = []
        self.ps = None
        self.d_ps = None

    def use_arena(self, nbytes):
        self.arena = self.stack.enter_context(self.nc.sbuf_tensor("arena", [128, nbytes], mybir.dt.uint8))
        self.asize = nbytes
        self.aoff = 0

    def shared_psum(self):
        if self.ps is None:
            self.ps = [self.psum(f"ps{i}", [128, 512]) for i in range(8)]
            self.d_ps = [Dep(f"ps{i}", excl=True) for i in range(8)]
        return self.ps, self.d_ps

    def barrier(self):
        lasts = []
        for e in ("pe", "act", "dve", "pool"):
            for ins in reversed(self.streams[e]):
                if not ins.is_dma:
                    lasts.append(ins)
                    break
        lasts += self.open_dmas
        self.open_dmas = []
        for d in lasts:
            d.signal = True
        for e in self.ENGS:
            self.pending[e] = list(lasts)

    def sbuf(self, name, shape, dtype):
        if self.arena is None:
            return self.stack.enter_context(self.nc.sbuf_tensor(name, list(shape), dtype))
        esz = {F32: 4, BF16: 2}[dtype]
        nel = 1
        for s_ in shape[1:]:
            nel *= s_
        nbytes = nel * esz
        off = (self.aoff + 63) // 64 * 64
        if off + nbytes > self.asize:
            raise MemoryError(f"SBUF arena overflow allocating {name}: {off}+{nbytes} > {self.asize}")
        self.aoff = off + nbytes
        v = self.arena[0:shape[0], off:off + nbytes].bitcast(dtype)
        if len(shape) == 3:
            v = v.rearrange("p (a b) -> p a b", a=shape[1])
        elif len(shape) == 4:
            v = v.rearrange("p (a b c) -> p a b c", a=shape[1], b=shape[2])
        return v

    def psum(self, name, shape, dtype=F32):
        return self.stack.enter_context(self.nc.psum_tensor(name, list(shape), dtype))

    def dram(self, name, shape, dtype, kind="Internal"):
        return self.nc.dram_tensor(name, list(shape), dtype, kind=kind)

    def op(self, eng, fn, reads=(), writes=(), is_dma=False, semkey=None, inc=16):
        ins = Ins(eng, _freeze(fn), is_dma, semkey)
        ins.inc = inc
        key = ("d", id(ins)) if is_dma else eng
        deps = {}
        for t in reads:
            for k, d in t.w.items():
                deps[id(d)] = d
            if t.excl:
                for k, d in t.r.items():
                    if not is_dma and k == eng:
                        continue
                    deps[id(d)] = d
        for t in writes:
            for k, d in t.r.items():
                if not is_dma and k == eng:
                    continue
                deps[id(d)] = d
            for k, d in t.w.items():
                if not is_dma and k == eng:
                    continue
                deps[id(d)] = d
        if self.pending[eng]:
            for d in self.pending[eng]:
                if d.is_dma or d.eng != eng:
                    deps[id(d)] = d
            self.pending[eng] = []
        for d in deps.values():
            d.signal = True
        ins.deps = list(deps.values())
        for t in reads:
            t.r[key] = ins
        for t in writes:
            if t.r:
                t.r = {}
                t.w = {}
            t.w[key] = ins
        if is_dma:
            ins.signal = True
            if semkey is None:
                raise ValueError("dma needs semkey")
            self.dma_keys.setdefault(semkey, 0)
            self.open_dmas.append(ins)
        self.streams[eng].append(ins)
        return ins

    def fence(self, srcs, dsts):
        for sd in srcs:
            for dd in dsts:
                for k, i in sd.w.items():
                    if k not in dd.w or dd.w[k].idx < i.idx:
                        dd.w[k] = i
                for k, i in sd.r.items():
                    if k not in dd.r or dd.r[k].idx < i.idx:
                        dd.r[k] = i

    def pe(self, fn, reads=(), writes=()):
        return self.op("pe", fn, reads, writes)

    def act(self, fn, reads=(), writes=()):
        return self.op("act", fn, reads, writes)

    def dve(self, fn, reads=(), writes=()):
        return self.op("dve", fn, reads, writes)

    def pool(self, fn, reads=(), writes=()):
        return self.op("pool", fn, reads, writes)

    def dma(self, out, in_, reads=(), writes=(), semkey=None, eng="sp", **kw):
        return self.op(eng, lambda e: e.dma_start(out=out, in_=in_, **kw), reads, writes,
                       is_dma=True, semkey=semkey)

    def allgather_pairs(self, out_t, in_t, reads=(), writes=(), semkey="cc"):
        return self.op("pool", lambda e: e.collective_compute(
            "AllGather", ALU.bypass, replica_groups=[[0, 1], [2, 3], [4, 5], [6, 7]],
            ins=[in_t.ap().opt()], outs=[out_t.ap().opt()]), reads, writes, is_dma=True, semkey=semkey, inc=1)

    def allgather_far(self, out_t, in_t, reads=(), writes=(), semkey="ccf"):
        return self.op("pool", lambda e: e.collective_compute(
            "AllGather", ALU.bypass, replica_groups=[[0, 4], [1, 5], [2, 6], [3, 7]],
            ins=[in_t.ap().opt()], outs=[out_t.ap().opt()]), reads, writes, is_dma=True, semkey=semkey, inc=1)

    def allreduce_all(self, out_t, in_t, reads=(), writes=(), semkey="ar"):
        return self.op("pool", lambda e: e.collective_compute(
            "AllReduce", ALU.add, replica_groups=[list(range(8))],
            ins=[in_t.ap().opt()], outs=[out_t.ap().opt()]), reads, writes, is_dma=True, semkey=semkey, inc=1)

    def emit(self):
        nc = self.nc
        st = self.stack
        sems = {}
        for e in ("pe", "act", "dve", "pool"):
            sems[e] = st.enter_context(nc.semaphore("s_" + e))
        for k in self.dma_keys:
            sems[("d", k)] = st.enter_context(nc.semaphore("d_" + str(k)))
        for e in self.ENGS:
            cnt = 0
            for ins in self.streams[e]:
                if ins.is_dma:
                    self.dma_keys[ins.semkey] += ins.inc
                    ins.semval = self.dma_keys[ins.semkey]
                elif ins.signal:
                    cnt += 1
                    ins.semval = cnt
        final_dma = dict(self.dma_keys)

        def run(eng_name, e):
            seen = {}
            for ins in self.streams[eng_name]:
                need = {}
                for d in ins.deps:
                    sk = ("d", d.semkey) if d.is_dma else d.eng
                    if d.semval > need.get(sk, 0):
                        need[sk] = d.semval
                for sk, v in need.items():
                    if seen.get(sk, 0) >= v:
                        continue
                    e.wait_ge(sems[sk], v)
                    seen[sk] = v
                bi = ins.fn(e)
                if ins.is_dma:
                    bi.then_inc(sems[("d", ins.semkey)], ins.inc)
                elif ins.signal:
                    bi.then_inc(sems[eng_name], 1)
            if eng_name == "sp":
                for k, v in final_dma.items():
                    if v > 0 and seen.get(("d", k), 0) < v:
                        e.wait_ge(sems[("d", k)], v)

        with nc.Block() as block:
            @block.tensor
            def _(e):
                run("pe", e)

            @block.scalar
            def _(e):
                run("act", e)

            @block.vector
            def _(e):
                run("dve", e)

            @block.gpsimd
            def _(e):
                run("pool", e)

            @block.sync
            def _(e):
                run("sp", e)
        st.close()


def simulate_sync(P):
    keys = dict.fromkeys(P.dma_keys, 0)
    for e in P.ENGS:
        cnt = 0
        for ins in P.streams[e]:
            if ins.is_dma:
                keys[ins.semkey] += ins.inc
                ins.semval = keys[ins.semkey]
            elif ins.signal:
                cnt += 1
                ins.semval = cnt
    sem = {}
    pc = {e: 0 for e in P.ENGS}
    progress = True
    while progress:
        progress = False
        for e in P.ENGS:
            st = P.streams[e]
            while pc[e] < len(st):
                ins = st[pc[e]]
                ok = True
                for d in ins.deps:
                    sk = ("d", d.semkey) if d.is_dma else d.eng
                    if sem.get(sk, 0) < d.semval:
                        ok = False
                        break
                if not ok:
                    break
                if ins.is_dma:
                    sem[("d", ins.semkey)] = sem.get(("d", ins.semkey), 0) + ins.inc
                elif ins.signal:
                    sem[e] = sem.get(e, 0) + 1
                pc[e] += 1
                progress = True
    stuck = {e: (pc[e], len(P.streams[e])) for e in P.ENGS if pc[e] < len(P.streams[e])}
    return stuck


EPS = 1e-6
D = 2048
DFF = 5504
NJ = 43
NC16 = 16


class Ctx:
    def __init__(self, P, ntiles):
        self.P = P
        self.tiles = []
        off = 0
        for n in ntiles:
            self.tiles.append((off, n))
            off += n
        self.NT = off
        NT = off
        self.hy = P.sbuf("hy", [128, 16, NT], BF16)
        self.A = P.sbuf("A", [128, NJ, NT], BF16)
        self.d_h = [Dep(f"h{i}") for i in range(len(ntiles))]
        self.d_A = [Dep(f"A{i}") for i in range(len(ntiles))]
        aflat = self.A[:].rearrange("p j t -> p (j t)")
        self.xt = []
        self.d_xt = []
        nx = min(3, (NJ * NT) // 16384)
        for i in range(nx):
            v = aflat[:, i * 16384:(i + 1) * 16384].bitcast(F32).rearrange("p (c t) -> p c t", c=16)
            self.xt.append(v)
            self.d_xt.append(Dep(f"xt{i}"))
        self.wgu = [P.sbuf(f"wgu{i}", [128, 2, 16, 256], BF16) for i in range(2)]
        self.d_wgu = [Dep(f"wgu{i}") for i in range(2)]
        self.wd = [P.sbuf(f"wd{i}", [128, NJ, 128], BF16) for i in range(2)]
        self.d_wd = [Dep(f"wd{i}") for i in range(2)]
        self.ps, self.d_ps = P.shared_psum()
        self.ones = P.sbuf("ones", [128, 128], BF16)
        self.d_ones = Dep("ones")
        P.dve(lambda e: e.memset(self.ones[:], 1.0), writes=[self.d_ones])
        self.sq = [P.sbuf(f"sq{i}", [128, 512], BF16) for i in range(4)]
        self.d_sq = [Dep(f"sq{i}") for i in range(4)]
        self.tmp = [P.sbuf(f"tmp{i}", [128, 512], F32) for i in range(4)]
        self.d_tmp = [Dep(f"tmp{i}") for i in range(4)]
        self.rstd = P.sbuf("rstd", [128, NT], F32)
        self.d_rstd = [Dep(f"rstd{i}") for i in range(len(ntiles))]
        self.mT = P.sbuf("mT", [128, 144, 2], F32)
        self.d_mT = Dep("mT")
        self.gT = P.sbuf("gT", [128, 6, 16], F32)
        self.d_gT = Dep("gT")
        self.coef = P.sbuf("coef", [128, 4, 16], F32)
        self.d_coef = Dep("coef")
        self.sqi = 0
        self.tmpi = 0
        self.psi = 0

    def next_sq(self):
        i = self.sqi % 4
        self.sqi += 1
        return self.sq[i], self.d_sq[i]

    def next_tmp(self):
        i = self.tmpi % 4
        self.tmpi += 1
        return self.tmp[i], self.d_tmp[i]


def emit_modulation(P, C, ada_w, ada_bT, cvecT, scr):
    cv = scr["cv"]
    sc = scr["sc"]
    bT = scr["bT"]
    d_cv, d_sc, d_bT = Dep("cv"), Dep("sc"), Dep("bT")
    P.dma(cv[:], cvecT, writes=[d_cv], semkey="small")
    P.dma(bT[:], ada_bT, writes=[d_bT], semkey="small2")
    P.act(lambda e: e.activation(out=sc[:], in_=cv[:], func=AF.Silu), reads=[d_cv], writes=[d_sc])
    mp = C.ps[7]
    d_mp = C.d_ps[7]
    mpv = mp[:, 0:288].rearrange("p (j r) -> p j r", r=2)
    for jb in range(36):
        s = jb % 2
        slot = C.wgu[s][:].rearrange("p a c n -> p c (a n)") if False else None
        sl = C.wgu[s][:].rearrange("p a c n -> p (a c n)").rearrange("p (c n) -> p c n", c=16)
        src = ada_w[:, jb * 512:(jb + 1) * 512].rearrange("(c p) n -> p c n", p=128)
        P.dma(sl, src, writes=[C.d_wgu[s]], semkey=f"wgu{s}", eng="pool")
        for j4 in range(4):
            j = jb * 4 + j4
            for k in range(16):
                P.pe(lambda e, sl=sl, j4=j4, k=k, j=j: e.matmul(
                    mpv[:, j, :], lhsT=sl[:, k, j4 * 128:(j4 + 1) * 128], rhs=sc[:, k, :],
                    start=(k == 0), stop=(k == 15)),
                    reads=[C.d_wgu[s], d_sc], writes=[d_mp])
    for r in range(2):
        P.dve(lambda e, r=r: e.tensor_tensor(out=C.mT[:, :, r], in0=mpv[:, :, r], in1=bT[:], op=ALU.add),
              reads=[d_mp, d_bT], writes=[C.d_mT])


def emit_coefs(P, C, sl_scale, sl_gate, gi_pre, gi_post, wres, cols, which="both"):
    for col in cols:
        if which in ("both", "pre"):
            P.dve(lambda e, col=col: e.scalar_tensor_tensor(
                out=C.coef[:, col, :], in0=C.mT[:, sl_scale * 16:(sl_scale + 1) * 16, col], scalar=1.0,
                in1=C.gT[:, gi_pre, :], op0=ALU.add, op1=ALU.mult),
                reads=[C.d_mT, C.d_gT], writes=[C.d_coef])
        if sl_gate is not None and which in ("both", "post"):
            P.dve(lambda e, col=col: e.scalar_tensor_tensor(
                out=C.coef[:, 2 + col, :], in0=C.mT[:, sl_gate * 16:(sl_gate + 1) * 16, col], scalar=float(wres),
                in1=C.gT[:, gi_post, :], op0=ALU.mult, op1=ALU.mult),
                reads=[C.d_mT, C.d_gT], writes=[C.d_coef])


def emit_prenorm(P, C, srcs, sl_shift, out_sb, d_out, arena_deps, resident=False, after_tile=None, tiles_only=None):
    nx = len(C.xt)
    for ti, (off, n) in enumerate(C.tiles):
        if tiles_only is not None and ti not in tiles_only:
            continue
        src, d_src, col = srcs[ti]
        xi = ti % nx
        xt = C.xt[xi][:, :, 0:n]
        if not resident:
            P.dma(xt, src.rearrange("(c p) t -> p c t", p=128), reads=[d_src],
                  writes=[C.d_xt[xi]] + arena_deps, semkey=f"xt{xi}")
        ssp = C.ps[6 + (ti % 2)]
        d_ssp = C.d_ps[6 + (ti % 2)]
        for c in range(16):
            sq, d_sq = C.next_sq()
            P.act(lambda e, sq=sq, c=c, xt=xt, n=n: e.activation(out=sq[:, 0:n], in_=xt[:, c, :], func=AF.Square),
                  reads=[C.d_xt[xi]], writes=[d_sq])
            P.pe(lambda e, sq=sq, c=c, ssp=ssp, n=n: e.matmul(ssp[:, 0:n], lhsT=C.ones[:], rhs=sq[:, 0:n],
                                                              start=(c == 0), stop=(c == 15)),
                 reads=[d_sq, C.d_ones], writes=[d_ssp])
        tmp, d_tmp = C.next_tmp()
        P.act(lambda e, tmp=tmp, ssp=ssp, n=n: e.activation(out=tmp[:, 0:n], in_=ssp[:, 0:n], func=AF.Sqrt,
                                                            scale=1.0 / D, bias=EPS),
              reads=[d_ssp], writes=[d_tmp])
        rs = C.rstd[:, off:off + n]
        P.dve(lambda e, tmp=tmp, rs=rs, n=n: e.reciprocal(out=rs, in_=tmp[:, 0:n]),
              reads=[d_tmp], writes=[C.d_rstd[ti]])
        dst = out_sb(ti)
        for c in range(16):
            tmp, d_tmp = C.next_tmp()
            P.dve(lambda e, tmp=tmp, c=c, xt=xt, rs=rs, n=n, col=col: e.scalar_tensor_tensor(
                out=tmp[:, 0:n], in0=xt[:, c, :], scalar=C.coef[:, col, c:c + 1], in1=rs,
                op0=ALU.mult, op1=ALU.mult),
                reads=[C.d_xt[xi], C.d_rstd[ti], C.d_coef], writes=[d_tmp])
            P.act(lambda e, tmp=tmp, c=c, dst=dst, n=n, col=col: e.activation(
                out=dst[:, c, :], in_=tmp[:, 0:n], func=AF.Identity,
                bias=C.mT[:, sl_shift * 16 + c, col:col + 1], scale=1.0),
                reads=[d_tmp, C.d_mT], writes=[d_out[ti]])
        if after_tile is not None:
            after_tile(ti, off, n)


def emit_ffn(P, C, srcs, dsts, wg, wu, wd, sl_shift, resident=False):
    nt = len(C.tiles)
    arena = C.d_A
    emit_prenorm(P, C, srcs, sl_shift, lambda ti: C.hy[:, :, C.tiles[ti][0]:C.tiles[ti][0] + C.tiles[ti][1]],
                 C.d_h, arena, resident)
    emit_gateup(P, C, wg, wu)
    emit_down_residual(P, C, NJ, wd, srcs, dsts)


def emit_gateup(P, C, wg, wu):
    it = 0
    for jj in range(22):
        s = jj % 2
        ncol = 256 if jj < 21 else 128
        for a, w in enumerate((wg, wu)):
            src = w[:, jj * 256:jj * 256 + ncol].rearrange("(c p) n -> p c n", p=128)
            P.dma(C.wgu[s][:, a, :, 0:ncol], src, writes=[C.d_wgu[s]], semkey=f"wgu{s}", eng="pool")
        for jl in range(ncol // 128):
            j = jj * 2 + jl
            for ti, (off, n) in enumerate(C.tiles):
                pg = (it % 4) * 2
                it += 1
                G, U = C.ps[pg], C.ps[pg + 1]
                for a, pt in enumerate((G, U)):
                    for k in range(16):
                        P.pe(lambda e, pt=pt, a=a, k=k, s=s, jl=jl, off=off, n=n: e.matmul(
                            pt[:, 0:n], lhsT=C.wgu[s][:, a, k, jl * 128:(jl + 1) * 128],
                            rhs=C.hy[:, k, off:off + n], start=(k == 0), stop=(k == 15)),
                            reads=[C.d_wgu[s], C.d_h[ti]], writes=[C.d_ps[pg + a]])
                tmp, d_tmp = C.next_tmp()
                P.act(lambda e, tmp=tmp, G=G, n=n: e.activation(out=tmp[:, 0:n], in_=G[:, 0:n], func=AF.Silu),
                      reads=[C.d_ps[pg]], writes=[d_tmp])
                P.dve(lambda e, tmp=tmp, U=U, j=j, off=off, n=n: e.tensor_tensor(
                    out=C.A[:, j, off:off + n], in0=tmp[:, 0:n], in1=U[:, 0:n], op=ALU.mult),
                    reads=[d_tmp, C.d_ps[pg + 1]], writes=[C.d_A[ti]] + C.d_xt)


def emit_down_residual(P, C, nch, wd, srcs, dsts, after_post=None):
    NJ = nch
    pend = []
    it = 0
    for dc in range(16):
        s = dc % 2
        src = wd[:, dc * 128:(dc + 1) * 128].rearrange("(j p) n -> p j n", p=128)
        P.dma(C.wd[s][:, 0:nch, :], src, writes=[C.d_wd[s]], semkey=f"wd{s}", eng="pool")
        for ti, (off, n) in enumerate(C.tiles):
            pi = it % 4
            it += 1
            Y = C.ps[pi]
            for j in range(NJ):
                P.pe(lambda e, Y=Y, j=j, s=s, off=off, n=n: e.matmul(
                    Y[:, 0:n], lhsT=C.wd[s][:, j, :], rhs=C.A[:, j, off:off + n],
                    start=(j == 0), stop=(j == NJ - 1)),
                    reads=[C.d_wd[s], C.d_A[ti]], writes=[C.d_ps[pi]])
            for f in pend:
                f()
            pend = []
            P.act(lambda e, Y=Y, dc=dc, off=off, n=n: e.activation(out=C.hy[:, dc, off:off + n], in_=Y[:, 0:n],
                                                                  func=AF.Copy),
                  reads=[C.d_ps[pi]], writes=[C.d_h[ti]])
            sq, d_sq = C.next_sq()
            P.act(lambda e, Y=Y, sq=sq, n=n: e.activation(out=sq[:, 0:n], in_=Y[:, 0:n], func=AF.Square),
                  reads=[C.d_ps[pi]], writes=[d_sq])
            SS = C.ps[4 + ti]

            def ssmm(sq=sq, d_sq=d_sq, SS=SS, ti=ti, dc=dc, n=n):
                P.pe(lambda e: e.matmul(SS[:, 0:n], lhsT=C.ones[:], rhs=sq[:, 0:n],
                                        start=(dc == 0), stop=(dc == 15)),
                     reads=[d_sq, C.d_ones], writes=[C.d_ps[4 + ti]])
            pend.append(ssmm)
    for f in pend:
        f()
    nx = len(C.xt)
    for ti, (off, n) in enumerate(C.tiles):
        if dsts[ti][0] is None:
            continue
        SS = C.ps[4 + ti]
        tmp, d_tmp = C.next_tmp()
        P.act(lambda e, tmp=tmp, SS=SS, n=n: e.activation(out=tmp[:, 0:n], in_=SS[:, 0:n], func=AF.Sqrt,
                                                          scale=1.0 / D, bias=EPS),
              reads=[C.d_ps[4 + ti]], writes=[d_tmp])
        rs = C.rstd[:, off:off + n]
        P.dve(lambda e, tmp=tmp, rs=rs, n=n: e.reciprocal(out=rs, in_=tmp[:, 0:n]),
              reads=[d_tmp], writes=[C.d_rstd[ti]])
    for ti, (off, n) in enumerate(C.tiles):
        src, d_src, col = srcs[ti]
        dst, d_dst, _ = dsts[ti]
        if dst is None:
            continue
        rs = C.rstd[:, off:off + n]
        xi = ti % nx
        xt = C.xt[xi][:, :, 0:n]
        P.dma(xt, src.rearrange("(c p) t -> p c t", p=128), reads=[d_src],
              writes=[C.d_xt[xi]] + C.d_A, semkey=f"xt{xi}")
        for c in range(16):
            tmp, d_tmp = C.next_tmp()
            P.dve(lambda e, tmp=tmp, c=c, rs=rs, off=off, n=n, col=col: e.scalar_tensor_tensor(
                out=tmp[:, 0:n], in0=C.hy[:, c, off:off + n], scalar=C.coef[:, 2 + col, c:c + 1], in1=rs,
                op0=ALU.mult, op1=ALU.mult),
                reads=[C.d_h[ti], C.d_rstd[ti], C.d_coef], writes=[d_tmp])
            P.dve(lambda e, tmp=tmp, c=c, xt=xt, n=n: e.tensor_tensor(
                out=xt[:, c, :], in0=xt[:, c, :], in1=tmp[:, 0:n], op=ALU.add),
                reads=[d_tmp, C.d_xt[xi]], writes=[C.d_xt[xi]])
        P.dma(dst.rearrange("(c p) t -> p c t", p=128), xt, reads=[C.d_xt[xi]], writes=[d_dst],
              semkey=f"xt{xi}")
        if after_post is not None:
            after_post(ti)


def emit_modulation_sharded(P, C, ada_sl, cvec3, selb, bT0, bT1, mp_own, mp_g1, mp_g2, mTd, d_mTd):
    cv = P.sbuf("mcv", [128, 16, 3], F32)
    sc = P.sbuf("msc", [128, 16, 3], BF16)
    sb = P.sbuf("mselb", [128, 2], F32)
    bT = P.sbuf("mbT", [128, 2, 144], F32)
    part = P.sbuf("mpart", [128, 216], F32)
    full = P.sbuf("mfull", [128, 4, 216], F32)
    fv = [full[:, g, :].rearrange("p (l j r) -> p l j r", l=2, j=36) for g in range(4)]
    mo = P.sbuf("mo", [128, 2, 144, 2], F32)
    d_cv, d_sc, d_bT, d_part, d_full, d_mo = [Dep(n) for n in ("mcv", "msc", "mbT", "mpart", "mfull", "mo")]
    P.dma(cv[:], cvec3, writes=[d_cv], semkey="small")
    P.dma(sb[:], selb, writes=[d_bT], semkey="small2")
    P.dma(bT[:, 0, :], bT0, writes=[d_bT], semkey="small2")
    P.dma(bT[:, 1, :], bT1, writes=[d_bT], semkey="small2")
    P.act(lambda e: e.activation(out=sc[:], in_=cv[:], func=AF.Silu), reads=[d_cv], writes=[d_sc])
    mp = C.ps[7]
    d_mp = C.d_ps[7]
    mpv = mp[:, 0:216].rearrange("p (l j r) -> p l j r", l=2, j=36)
    it = 0
    for l in range(2):
        for jb in range(9):
            s = it % 2
            it += 1
            sl = C.wgu[s][:].rearrange("p a c n -> p (a c n)").rearrange("p (c n) -> p c n", c=16)
            src = ada_sl[l, :, jb * 512:(jb + 1) * 512].rearrange("(c p) n -> p c n", p=128)
            P.dma(sl, src, writes=[C.d_wgu[s]], semkey=f"wgu{s}", eng="pool")
            for j4 in range(4):
                j = jb * 4 + j4
                for k in range(16):
                    P.pe(lambda e, sl=sl, j4=j4, k=k, j=j, l=l: e.matmul(
                        mpv[:, l, j, :], lhsT=sl[:, k, j4 * 128:(j4 + 1) * 128], rhs=sc[:, k, :],
                        start=(k == 0), stop=(k == 15)),
                        reads=[C.d_wgu[s], d_sc], writes=[d_mp])
    P.dve(lambda e: e.tensor_copy(out=part[:], in_=mp[:, 0:216]), reads=[d_mp], writes=[d_part])
    d_mi, d_m1, d_m2 = Dep("mp_own"), Dep("mp_g1"), Dep("mp_g2")
    P.dma(mp_own[:, :], part[:], reads=[d_part], writes=[d_mi], semkey="mred")
    P.allgather_pairs(mp_g1, mp_own, reads=[d_mi], writes=[d_m1], semkey="cc0")
    P.allgather_far(mp_g2, mp_g1, reads=[d_m1], writes=[d_m2], semkey="ccf")
    for g in range(4):
        P.dma(full[:, g, :], mp_g2[g * 128:(g + 1) * 128, :],
              reads=[d_m2], writes=[d_full], semkey="mred")
    for l in range(2):
        for g in range(4):
            dst = mo[:, l, g * 36:(g + 1) * 36, :]
            P.dve(lambda e, l=l, g=g, dst=dst: e.tensor_scalar(out=dst[:, :, 0], in0=fv[g][:, l, :, 0],
                                                               scalar1=sb[:, 0:1], scalar2=None, op0=ALU.mult),
                  reads=[d_full, d_bT], writes=[d_mo])
            P.dve(lambda e, l=l, g=g, dst=dst: e.scalar_tensor_tensor(out=dst[:, :, 0], in0=fv[g][:, l, :, 1],
                                                                      scalar=sb[:, 1:2], in1=dst[:, :, 0],
                                                                      op0=ALU.mult, op1=ALU.add),
                  reads=[d_full, d_bT, d_mo], writes=[d_mo])
            P.dve(lambda e, l=l, g=g, dst=dst: e.tensor_copy(out=dst[:, :, 1], in_=fv[g][:, l, :, 2]),
                  reads=[d_full], writes=[d_mo])
        for col in range(2):
            P.dve(lambda e, l=l, col=col: e.tensor_tensor(out=mo[:, l, :, col], in0=mo[:, l, :, col], in1=bT[:, l, :],
                                                          op=ALU.add), reads=[d_mo, d_bT], writes=[d_mo])
        P.dma(mTd[l][:], mo[:, l, :, :], reads=[d_mo], writes=[d_mTd[l]], semkey="mTo")


EPS = 1e-6
NTOK = 2304
NCTX = 256
NLAT = 2048
LAM_INIT0 = 0.8 - 0.6 * math.exp(-0.3 * 0)


DBG = {}


class MixCtx:
    def __init__(self, P):
        self.P = P
        self.hs = P.sbuf("hs", [128, 16, NTOK], BF16)
        self.d_hs = Dep("hs")
        self.ropeC = P.sbuf("ropeC", [128, NLAT], F32)
        self.ropeS = P.sbuf("ropeS", [128, NLAT], F32)
        self.d_rope = Dep("rope")
        self.wt = [P.sbuf(f"wt{i}", [128, 16, 128], BF16) for i in range(8)]
        self.d_wt = [Dep(f"wt{i}") for i in range(8)]
        self.ps, self.d_ps = P.shared_psum()
        self.d_ph = [[Dep(f"ph{i}_{h}") for h in range(2)] for i in range(8)]
        self.ones = P.sbuf("ones", [128, 128], BF16)
        self.onesf = P.sbuf("onesf", [128, 128], F32)
        self.d_ones = Dep("ones")
        P.dve(lambda e: e.memset(self.ones[:], 1.0), writes=[self.d_ones])
        P.dve(lambda e: e.memset(self.onesf[:], 1.0), writes=[self.d_ones])
        self.perm = P.sbuf("perm", [128, 128], BF16)
        self.ident = P.sbuf("ident", [128, 128], BF16)
        self.maskf = P.sbuf("maskf", [128, 128], F32)
        self.maskb = P.sbuf("maskb", [128, 128], F32)
        self.d_const = Dep("const")
        self.vec = P.sbuf("vec", [128, 32], F32)
        self.d_vec = Dep("vec")
        self.lbr = P.sbuf("lbr", [128, 2, 2, 4], F32)
        self.lb = P.sbuf("lb", [128, 2, 4], F32)
        self.oml = P.sbuf("oml", [128, 2, 4], F32)
        self.d_lb = Dep("lb")
        u0 = P.aoff
        self.qT = P.sbuf("qT", [128, NLAT], BF16)
        self.qT1 = P.sbuf("qT1", [128, NLAT], BF16)
        self.kT = P.sbuf("kT", [128, NTOK], BF16)
        self.V = P.sbuf("V", [128, 18, 128], BF16)
        self.d_qT, self.d_kT, self.d_V = Dep("qT"), Dep("kT"), Dep("V")
        self.E = [P.sbuf(f"E{i}", [128, 512], BF16) for i in range(4)]
        self.d_E = [Dep(f"E{i}") for i in range(4)]
        u1 = P.aoff
        if P.arena is not None:
            P.aoff = u0
        self.zf_sb = P.sbuf("zf_sb", [128, NTOK], F32)
        self.zb_sb = P.sbuf("zb_sb", [128, NTOK], F32)
        self.q_sb = P.sbuf("q_sb", [128, NTOK], BF16)
        self.i_sb = P.sbuf("i_sb", [128, 18, 128], BF16)
        self.d_rp = Dep("recproj")
        self.d_zf = Dep("zf_sb")
        if P.arena is not None:
            P.aoff = max(u1, P.aoff)
        self.qtF = P.sbuf("qtF", [128, NTOK], BF16)
        self.ktF = P.sbuf("ktF", [128, NTOK], BF16)
        self.zfb = self.zf_sb.bitcast(BF16)
        self.d_zfu = [Dep(f"zfu{u}") for u in range(9)]
        self.d_qk = [Dep("qkF"), Dep("qkB")]
        self.sg_sb = P.sbuf("sg_sb", [128, NLAT], BF16)
        self.svall = P.sbuf("svall", [128, 2, 18, 4], F32)
        self.d_sva = Dep("svall")
        self.d_svad = [Dep("svallF"), Dep("svallB")]
        self.d_tfh = [[Dep(f"tfh{i}_{h}") for h in range(2)] for i in range(6)]
        self.maskR = P.sbuf("maskR", [128, 512], F32)
        P.dve(lambda e: e.memset(self.maskR[:], 1.0), writes=[self.d_ones])
        for cc in range(4):
            P.dve(lambda e, cc=cc: e.memset(self.maskR[:, cc * 128:cc * 128 + 1], 0.0), writes=[self.d_ones])
        self.tf = [P.sbuf(f"tf{i}", [128, 512], F32) for i in range(6)]
        self.d_tf = [Dep(f"tf{i}") for i in range(6)]
        self.tb = [P.sbuf(f"tb{i}", [128, 512], BF16) for i in range(4)]
        self.d_tb = [Dep(f"tb{i}") for i in range(4)]
        self.tfi = 0
        self.tbi = 0
        self.Ei = 0
        self.S = P.sbuf("S", [128, 128], F32)
        self.d_S = Dep("S")
        self.S1 = P.sbuf("S1", [128, 128], F32)
        self.d_S1 = Dep("S1")
        self.ofw = P.sbuf("ofw", [128, NLAT], F32)
        self.d_ofw = Dep("ofw")
        self.obw = P.sbuf("obw", [128, NLAT], F32)
        self.d_obw = Dep("obw")
        self.rf = [P.sbuf(f"rf{i}", [128, 128], F32) for i in range(6)]
        self.d_rf = [Dep(f"rf{i}") for i in range(6)]
        self.rb = [P.sbuf(f"rb{i}", [128, 128], BF16) for i in range(12)]
        self.d_rb = [Dep(f"rb{i}") for i in range(12)]
        self.rfi = 0
        self.rbi = 0
        self.sv = [P.sbuf(f"sv{i}", [128, 8], F32) for i in range(8)]
        self.d_sv = [Dep(f"sv{i}") for i in range(8)]
        self.svi = 0

    def ntf(self):
        i = self.tfi % 6
        self.tfi += 1
        return self.tf[i], self.d_tf[i]

    def ntb(self):
        i = self.tbi % 4
        self.tbi += 1
        return self.tb[i], self.d_tb[i]

    def nE(self):
        i = self.Ei % 4
        self.Ei += 1
        return self.E[i], self.d_E[i]

    def nrf(self):
        i = self.rfi % 6
        self.rfi += 1
        return self.rf[i], self.d_rf[i]

    def nrb(self):
        i = self.rbi % 12
        self.rbi += 1
        return self.rb[i], self.d_rb[i]

    def nsv(self):
        i = self.svi % 8
        self.svi += 1
        return self.sv[i], self.d_sv[i]


def load_w(P, M, slot, src):
    P.dma(M.wt[slot][:], src.rearrange("(c p) n -> p c n", p=128), writes=[M.d_wt[slot]],
          semkey=f"wt{slot}", eng="pool")


def emit_mix_setup(P, M, hTf, ropeC, ropeS, perm, ident, maskf, maskb, lamT, dng, rng, lbraw):
    for q in range(4 if hTf is not None else 0):
        P.dma(M.hs[:, q * 4:(q + 1) * 4, :], hTf[q * 512:(q + 1) * 512, :].rearrange("(c p) t -> p c t", p=128),
              writes=[M.d_hs], semkey="hs")
    P.dma(M.ropeC[:], ropeC, writes=[M.d_rope], semkey="rope")
    P.dma(M.ropeS[:], ropeS, writes=[M.d_rope], semkey="rope")
    P.dma(M.perm[:], perm, writes=[M.d_const], semkey="cst", eng="pool")
    P.dma(M.ident[:], ident, writes=[M.d_const], semkey="cst", eng="pool")
    P.dma(M.maskf[:], maskf, writes=[M.d_const], semkey="cst2")
    P.dma(M.maskb[:], maskb, writes=[M.d_const], semkey="cst2")
    P.dma(M.vec[0:64, 0:4], lamT, writes=[M.d_vec], semkey="vec")
    P.dma(M.vec[:, 4:5], dng, writes=[M.d_vec], semkey="vec")
    P.dma(M.vec[:, 5:6], rng, writes=[M.d_vec], semkey="vec")
    P.dma(M.lbr[:], lbraw, writes=[M.d_lb], semkey="lb")
    P.dve(lambda e: e.tensor_tensor(out=M.vec[0:64, 6:7], in0=M.vec[0:64, 0:1], in1=M.vec[0:64, 1:2], op=ALU.mult),
          reads=[M.d_vec], writes=[M.d_vec])
    P.dve(lambda e: e.tensor_tensor(out=M.vec[0:64, 7:8], in0=M.vec[0:64, 2:3], in1=M.vec[0:64, 3:4], op=ALU.mult),
          reads=[M.d_vec], writes=[M.d_vec])
    lp = M.ps[7]
    P.pe(lambda e: e.matmul(lp[:, 0:2], lhsT=M.onesf[0:64, :], rhs=M.vec[0:64, 6:8], start=True, stop=True),
         reads=[M.d_vec, M.d_ones], writes=[M.d_ps[7]])
    P.act(lambda e: e.activation(out=M.vec[:, 8:10], in_=lp[:, 0:2], func=AF.Exp), reads=[M.d_ps[7]],
          writes=[M.d_vec])
    P.dve(lambda e: e.tensor_tensor(out=M.vec[:, 10:11], in0=M.vec[:, 9:10], in1=M.vec[:, 8:9], op=ALU.subtract),
          reads=[M.d_vec], writes=[M.d_vec])
    P.dve(lambda e: e.tensor_scalar(out=M.vec[:, 10:11], in0=M.vec[:, 10:11], scalar1=-LAM_INIT0, scalar2=None,
                                    op0=ALU.add), reads=[M.d_vec], writes=[M.d_vec])
    P.dve(lambda e: e.tensor_scalar(out=M.vec[:, 11:12], in0=M.vec[:, 4:5], scalar1=1.0 - LAM_INIT0, scalar2=None,
                                    op0=ALU.mult), reads=[M.d_vec], writes=[M.d_vec])
    P.dve(lambda e: e.tensor_tensor(out=M.lb[:], in0=M.lbr[:, :, 1, :], in1=M.lbr[:, :, 0, :], op=ALU.subtract),
          reads=[M.d_lb], writes=[M.d_lb])
    P.act(lambda e: e.activation(out=M.lb[:], in_=M.lb[:], func=AF.Exp), reads=[M.d_lb], writes=[M.d_lb])
    P.dve(lambda e: e.tensor_scalar(out=M.lb[:], in0=M.lb[:], scalar1=1.0, scalar2=None, op0=ALU.add),
          reads=[M.d_lb], writes=[M.d_lb])
    P.dve(lambda e: e.reciprocal(out=M.lb[:], in_=M.lb[:]), reads=[M.d_lb], writes=[M.d_lb])
    P.dve(lambda e: e.tensor_scalar(out=M.oml[:], in0=M.lb[:], scalar1=-1.0, scalar2=1.0, op0=ALU.mult, op1=ALU.add),
          reads=[M.d_lb], writes=[M.d_lb])


def proj_fm(P, M, out_ps, d_out, slot, off, n):
    for k in range(16):
        P.pe(lambda e, k=k: e.matmul(out_ps, lhsT=M.wt[slot][:, k, :], rhs=M.hs[:, k, off:off + n],
                                     start=(k == 0), stop=(k == 15)),
             reads=[M.d_wt[slot], M.d_hs], writes=[d_out])


def proj_tm(P, M, out_ps, d_out, slot, off):
    for k in range(16):
        P.pe(lambda e, k=k: e.matmul(out_ps, lhsT=M.hs[:, k, off:off + 128], rhs=M.wt[slot][:, k, :],
                                     start=(k == 0), stop=(k == 15)),
             reads=[M.d_wt[slot], M.d_hs], writes=[d_out])


def emit_rsqrt(P, M, out_sb, d_o, in_ps, d_in, n, inv_dim):
    P.act(lambda e: e.activation(out=out_sb, in_=in_ps, func=AF.Ln, scale=inv_dim, bias=EPS),
          reads=[d_in], writes=[d_o])
    P.act(lambda e: e.activation(out=out_sb, in_=out_sb, func=AF.Exp, scale=-0.5), reads=[d_o], writes=[d_o])


def emit_attention_head(P, M, hd, mergedT, d_merged):
    sq_, sk_, sv_ = 0, 1, 2
    P.dve(lambda e: e.memset(M.qT[64:128, :], 0.0), writes=[M.d_qT])
    P.dve(lambda e: e.memset(M.qT1[0:64, :], 0.0), writes=[M.d_qT])
    tiles = [(0, NCTX)] + [(NCTX + i * 512, 512) for i in range(4)]
    bi = 0
    import os
    NSUB = int(os.environ.get("ATT_SUB", "99"))
    for (off, n) in tiles:
        for which in ("k", "q"):
            if bi >= NSUB:
                continue
            if which == "q" and off < NCTX:
                continue
            slot = sk_ if which == "k" else sq_
            dstT = M.kT if which == "k" else M.qT
            d_dst = M.d_kT if which == "k" else M.d_qT
            doff = off if which == "k" else off - NCTX
            pb = bi % 2
            bi += 1
            pp, d_pp = M.ps[pb], M.d_ps[pb]
            proj_fm(P, M, pp[:, 0:n], d_pp, slot, off, n)
            if off < NCTX:
                P.act(lambda e, pp=pp, n=n, dstT=dstT, doff=doff: e.activation(
                    out=dstT[:, doff:doff + n], in_=pp[:, 0:n], func=AF.Copy), reads=[d_pp], writes=[d_dst])
                continue
            loff = off - NCTX
            sb, d_sb = M.ntb()
            P.act(lambda e, pp=pp, sb=sb, n=n: e.activation(out=sb[:, 0:n], in_=pp[:, 0:n], func=AF.Copy),
                  reads=[d_pp], writes=[d_sb])
            rp, d_rp = M.ps[2 + pb], M.d_ps[2 + pb]
            P.pe(lambda e, rp=rp, sb=sb, n=n: e.matmul(rp[:, 0:n], lhsT=M.perm[:], rhs=sb[:, 0:n], start=True,
                                                       stop=True), reads=[d_sb, M.d_const], writes=[d_rp])
            t1, d_t1 = M.ntf()
            P.dve(lambda e, t1=t1, pp=pp, n=n, loff=loff: e.tensor_tensor(
                out=t1[:, 0:n], in0=pp[:, 0:n], in1=M.ropeC[:, loff:loff + n], op=ALU.mult),
                reads=[d_pp, M.d_rope], writes=[d_t1])
            t2, d_t2 = M.ntf()
            P.dve(lambda e, t2=t2, rp=rp, n=n, loff=loff: e.tensor_tensor(
                out=t2[:, 0:n], in0=rp[:, 0:n], in1=M.ropeS[:, loff:loff + n], op=ALU.mult),
                reads=[d_rp, M.d_rope], writes=[d_t2])
            if which == "k":
                P.dve(lambda e, t1=t1, t2=t2, dstT=dstT, doff=doff, n=n: e.tensor_tensor(
                    out=dstT[:, doff:doff + n], in0=t1[:, 0:n], in1=t2[:, 0:n], op=ALU.add),
                    reads=[d_t1, d_t2], writes=[d_dst])
            else:
                P.dve(lambda e, t1=t1, t2=t2, doff=doff, n=n: e.tensor_tensor(
                    out=M.qT[0:64, doff:doff + n], in0=t1[0:64, 0:n], in1=t2[0:64, 0:n], op=ALU.add),
                    reads=[d_t1, d_t2], writes=[d_dst])
                P.dve(lambda e, t1=t1, t2=t2, doff=doff, n=n: e.tensor_tensor(
                    out=M.qT1[64:128, doff:doff + n], in0=t1[64:128, 0:n], in1=t2[64:128, 0:n], op=ALU.add),
                    reads=[d_t1, d_t2], writes=[d_dst])
    import os
    if DBG.get("qT") is not None and hd == 0:
        P.dma(DBG["qT"][:, :], M.qT[:], reads=[M.d_qT], writes=[Dep("dbgq")], semkey="dbg")
        P.dma(DBG["kT"][:, :], M.kT[:], reads=[M.d_kT], writes=[Dep("dbgk")], semkey="dbg")
    STG = int(os.environ.get("ATT_STAGE", "9"))
    if STG < 2:
        return
    for g4 in range(5):
        pb = 4 + (g4 % 2)
        vp, d_vp = M.ps[pb], M.d_ps[pb]
        nk = 4 if g4 < 4 else 2
        for i in range(nk):
            kt = g4 * 4 + i
            proj_tm(P, M, vp[:, i * 128:(i + 1) * 128], d_vp, sv_, kt * 128)
        P.act(lambda e, vp=vp, g4=g4, nk=nk: e.activation(
            out=M.V[:, g4 * 4:g4 * 4 + nk, :].rearrange("p a n -> p (a n)"), in_=vp[:, 0:nk * 128], func=AF.Copy),
            reads=[d_vp], writes=[M.d_V])
    if STG < 3:
        return
    for qt in [int(c) for c in os.environ.get("ATT_QT", "0123")]:
        qo = qt * 512
        steps = [(kt, m) for kt in range(18) for m in range(2)]
        pend = []
        for si, (kt, m) in enumerate(steps):
            sb_i = si % 4
            ST, d_ST = M.ps[sb_i], M.d_ps[sb_i]
            P.pe(lambda e, ST=ST, kt=kt, m=m: e.matmul(
                ST[:, :], lhsT=M.kT[:, kt * 128:(kt + 1) * 128],
                rhs=(M.qT if m == 0 else M.qT1)[:, qo:qo + 512], start=True, stop=True),
                reads=[M.d_kT, M.d_qT], writes=[d_ST])
            E, d_E = M.nE()
            P.act(lambda e, E=E, ST=ST: e.activation(out=E[:], in_=ST[:, :], func=AF.Exp, scale=0.125),
                  reads=[d_ST], writes=[d_E])
            if len(pend) >= 2:
                pend.pop(0)()

            def pv(E=E, d_E=d_E, kt=kt, m=m):
                P.pe(lambda e: e.matmul(M.ps[4 + m][:, :], lhsT=M.V[:, kt, :], rhs=E[:], start=(kt == 0),
                                        stop=(kt == 17)), reads=[M.d_V, d_E], writes=[M.d_ps[4 + m]])
                P.pe(lambda e: e.matmul(M.ps[6 + m][:, :], lhsT=M.ones[:], rhs=E[:], start=(kt == 0),
                                        stop=(kt == 17)), reads=[M.d_ones, d_E], writes=[M.d_ps[6 + m]])
            pend.append(pv)
        for f in pend:
            f()
        r0, d_r0 = M.ntf()
        r1, d_r1 = M.ntf()
        P.dve(lambda e, r0=r0: e.reciprocal(out=r0[:], in_=M.ps[6][:, :]), reads=[M.d_ps[6]], writes=[d_r0])
        P.dve(lambda e, r1=r1: e.reciprocal(out=r1[:], in_=M.ps[7][:, :]), reads=[M.d_ps[7]], writes=[d_r1])
        oa, d_oa = M.ntf()
        ob, d_ob = M.ntf()
        P.dve(lambda e, oa=oa, r0=r0: e.tensor_tensor(out=oa[:], in0=M.ps[4][:, :], in1=r0[:], op=ALU.mult),
              reads=[M.d_ps[4], d_r0], writes=[d_oa])
        P.dve(lambda e, ob=ob, r1=r1: e.tensor_tensor(out=ob[:], in0=M.ps[5][:, :], in1=r1[:], op=ALU.mult),
              reads=[M.d_ps[5], d_r1], writes=[d_ob])
        o, d_o = M.ntf()
        P.dve(lambda e, o=o, oa=oa, ob=ob: e.scalar_tensor_tensor(
            out=o[:], in0=ob[:], scalar=M.vec[:, 10:11], in1=oa[:], op0=ALU.mult, op1=ALU.add),
            reads=[d_oa, d_ob, M.d_vec], writes=[d_o])
        sq, d_sq = M.ntb()
        P.act(lambda e, sq=sq, o=o: e.activation(out=sq[:], in_=o[:], func=AF.Square), reads=[d_o], writes=[d_sq])
        P.pe(lambda e, sq=sq: e.matmul(M.ps[0][:, :], lhsT=M.ones[:], rhs=sq[:], start=True, stop=True),
             reads=[d_sq, M.d_ones], writes=[M.d_ps[0]])
        ri, d_ri = M.ntf()
        emit_rsqrt(P, M, ri[:], d_ri, M.ps[0][:, :], M.d_ps[0], 512, 1.0 / 128)
        ob16, d_ob16 = M.ntb()
        P.dve(lambda e, ob16=ob16, o=o, ri=ri: e.scalar_tensor_tensor(
            out=ob16[:], in0=o[:], scalar=M.vec[:, 11:12], in1=ri[:], op0=ALU.mult, op1=ALU.mult),
            reads=[d_o, d_ri, M.d_vec], writes=[d_ob16])
        rows = mergedT("att", hd) if callable(mergedT) else mergedT[hd * 128:(hd + 1) * 128, :]
        P.dma(rows[:, qo:qo + 512], ob16[:], reads=[d_ob16], writes=[d_merged], semkey="mg")


def emit_rec_head(P, M, r, mergedT, d_merged, after_burst=None):
    s_q, s_zf, s_zb, s_i, s_g = 3, 4, 5, 6, 7
    orders = [list(range(18)), [1, 0] + list(range(17, 1, -1))]
    Ss = [M.S, M.S1]
    dSs = [M.d_S, M.d_S1]
    obuf = [M.ofw, M.obw]
    d_obuf = [M.d_ofw, M.d_obw]
    for dr in range(2):
        P.dve(lambda e, dr=dr: e.memset(Ss[dr][:], 0.0), writes=[dSs[dr]])
    ptiles = [(0, NCTX)] + [(NCTX + i * 512, 512) for i in range(4)]
    bi = 0
    for (off, n) in ptiles:
        for (slot, dst, sc_) in ((s_zf, M.zf_sb, 1.0), (s_zb, M.zb_sb, 1.0), (s_q, M.q_sb, float(128 ** -0.5))):
            pb = bi % 4
            bi += 1
            proj_fm(P, M, M.ps[pb][:, 0:n], M.d_ps[pb], slot, off, n)
            if slot != s_q:
                P.act(lambda e, pb=pb, dst=dst, off=off, n=n: e.activation(
                    out=dst[:, off:off + n], in_=M.ps[pb][:, 0:n], func=AF.Sigmoid, scale=-1.0),
                    reads=[M.d_ps[pb]], writes=[M.d_rp])
            else:
                P.dve(lambda e, pb=pb, dst=dst, off=off, n=n, sc_=sc_: e.tensor_scalar(
                    out=dst[:, off:off + n], in0=M.ps[pb][:, 0:n], scalar1=sc_, scalar2=None, op0=ALU.mult),
                    reads=[M.d_ps[pb]], writes=[M.d_rp])
        if off >= NCTX:
            pb = bi % 4
            bi += 1
            proj_fm(P, M, M.ps[pb][:, 0:n], M.d_ps[pb], s_g, off, n)
            P.act(lambda e, pb=pb, off=off, n=n: e.activation(
                out=M.sg_sb[:, off - NCTX:off - NCTX + n], in_=M.ps[pb][:, 0:n], func=AF.Silu),
                reads=[M.d_ps[pb]], writes=[M.d_rp])
    for g4 in range(5):
        pb = 4 + (g4 % 2)
        nk = 4 if g4 < 4 else 2
        for i in range(nk):
            proj_tm(P, M, M.ps[pb][:, i * 128:(i + 1) * 128], M.d_ps[pb], s_i, (g4 * 4 + i) * 128)
        P.act(lambda e, pb=pb, g4=g4, nk=nk: e.activation(
            out=M.i_sb[:, g4 * 4:g4 * 4 + nk, :].rearrange("p a n -> p (a n)"), in_=M.ps[pb][:, 0:nk * 128],
            func=AF.Copy), reads=[M.d_ps[pb]], writes=[M.d_rp])

    if after_burst is not None:
        after_burst()
    RSTG = int(os.environ.get("REC_STAGE", "9"))
    if RSTG < 2:
        return
    def prep_gen(dr):
        zsb = M.zf_sb if dr == 0 else M.zb_sb
        T = [t[:, dr * 256:(dr + 1) * 256] for t in M.tf]
        dT = [M.d_tfh[i][dr] for i in range(6)]
        n = 256
        for off in range(0, NTOK, 256):
            c0 = off // 128
            u = off // 256
            if dr == 0:
                qdst, kdst = M.qtF[:, off:off + n], M.ktF[:, off:off + n]
                wdeps = [M.d_qk[0]]
                rdeps = [M.d_rp, M.d_zfu[u]]
            else:
                qdst, kdst = M.zfb[:, u * 512:u * 512 + 256], M.zfb[:, u * 512 + 256:u * 512 + 512]
                wdeps = [M.d_qk[1], M.d_zfu[u]]
                rdeps = [M.d_rp]
            P.dve(lambda e, off=off: e.tensor_scalar(out=T[2], in0=zsb[:, off:off + n], scalar1=M.oml[:, dr, r:r + 1],
                                                     scalar2=None, op0=ALU.mult), reads=rdeps + [M.d_lb],
                  writes=[dT[2]])
            yield
            P.act(lambda e: e.activation(out=T[3], in_=T[2], func=AF.Ln, scale=-1.0, bias=1.0), reads=[dT[2]],
                  writes=[dT[3]])
            yield
            P.dve(lambda e: e.tensor_tensor_scan(out=T[4], data0=M.maskR[:, 0:n], data1=T[3], initial=0.0,
                                                 op0=ALU.mult, op1=ALU.add), reads=[dT[3], M.d_ones], writes=[dT[4]])
            yield
            pfv = T[4].rearrange("p (c t) -> p c t", t=128)
            if dr == 1:
                P.dve(lambda e: e.tensor_tensor(out=T[0], in0=T[4], in1=T[3], op=ALU.subtract),
                      reads=[dT[4], dT[3]], writes=[dT[0]])
                yield
            for ci in range(2):
                src = T[0] if dr == 1 else T[4]
                P.dve(lambda e, ci=ci, src=src: e.tensor_scalar(
                    out=T[5][:, ci * 128:(ci + 1) * 128], in0=src[:, ci * 128:(ci + 1) * 128],
                    scalar1=T[4][:, ci * 128 + 63:ci * 128 + 64], scalar2=(1.0 if dr == 0 else -1.0),
                    op0=ALU.subtract, op1=ALU.mult), reads=[dT[0], dT[4]], writes=[dT[5]])
                yield
            P.act(lambda e: e.activation(out=T[1], in_=T[5], func=AF.Exp), reads=[dT[5]], writes=[dT[1]])
            yield
            P.act(lambda e: e.activation(out=T[3], in_=T[5], func=AF.Exp, scale=-1.0), reads=[dT[5]], writes=[dT[3]])
            yield
            P.dve(lambda e, off=off, qdst=qdst: e.tensor_tensor(out=qdst, in0=M.q_sb[:, off:off + n], in1=T[1],
                                                                op=ALU.mult), reads=[M.d_rp, dT[1]], writes=wdeps)
            yield
            P.dve(lambda e, kdst=kdst: e.tensor_tensor(out=kdst, in0=T[2], in1=T[3], op=ALU.mult),
                  reads=[dT[2], dT[3]], writes=wdeps)
            yield
            svv = M.svall[:, dr, c0:c0 + 2, :]
            d_sva = M.d_svad[dr]
            P.dve(lambda e, svv=svv, pfv=pfv: e.tensor_tensor(out=svv[:, :, 0], in0=pfv[:, :, 127], in1=pfv[:, :, 63],
                                                              op=ALU.subtract), reads=[dT[4]], writes=[d_sva])
            yield
            P.act(lambda e, svv=svv, pfv=pfv: e.activation(out=svv[:, :, 1], in_=pfv[:, :, 63], func=AF.Exp),
                  reads=[dT[4]], writes=[d_sva])
            yield
            P.act(lambda e, svv=svv: e.activation(out=svv[:, :, 2], in_=svv[:, :, 0], func=AF.Exp),
                  reads=[d_sva], writes=[d_sva])
            yield
            P.act(lambda e, svv=svv, pfv=pfv: e.activation(out=svv[:, :, 3], in_=pfv[:, :, 127], func=AF.Exp),
                  reads=[dT[4]], writes=[d_sva])
            yield

    import itertools
    P.fence(M.d_tf, [d for pair in M.d_tfh for d in pair])
    for _ in itertools.zip_longest(prep_gen(0), prep_gen(1)):
        pass
    P.fence([d for pair in M.d_tfh for d in pair], M.d_tf)

    if RSTG < 3:
        return

    def chunk_gen(dr, step):
        if True:
            c = orders[dr][step]
            mask = M.maskf if dr == 0 else M.maskb
            S_, d_S = Ss[dr], dSs[dr]
            a = c * 128
            lat = c >= 2
            la = a - NCTX
            bx = dr * 4 + (step % 2) * 2
            by = bx + 1
            if dr == 0:
                qt_, kt_ = M.qtF[:, a:a + 128], M.ktF[:, a:a + 128]
            else:
                ub = (c // 2) * 512 + (c % 2) * 128
                qt_, kt_ = M.zfb[:, ub:ub + 128], M.zfb[:, ub + 256:ub + 384]
            d_qt = d_kt = M.d_qk[dr]
            vt, d_vt = M.i_sb[:, c, :], M.d_rp
            sv, d_sv = M.svall[:, dr, c, :], M.d_svad[dr]
            c_e1 = 1 if dr == 0 else 2
            c_e2 = 2 if dr == 0 else 1
            ktp = M.ps[bx][:, 384:448].bitcast(BF16)
            d_ktp = M.d_ps[bx]
            P.pe(lambda e, ktp=ktp, kt_=kt_: e.transpose(ktp, kt_, M.ident[:]), reads=[d_kt, M.d_const],
                 writes=[d_ktp])
            yield
            ktok, d_ktok = M.nrb()
            P.act(lambda e, ktok=ktok, ktp=ktp: e.activation(out=ktok[:], in_=ktp, func=AF.Copy),
                  reads=[d_ktp], writes=[d_ktok])
            yield
            if lat:
                atp, d_atp = M.ps[by][:, 0:128], M.d_ps[by]
                P.pe(lambda e, atp=atp, kt_=kt_, qt_=qt_: e.matmul(atp, lhsT=kt_, rhs=qt_, start=True, stop=True),
                     reads=[d_kt, d_qt], writes=[d_atp])
                yield
                am, d_am = M.nrb()
                P.dve(lambda e, am=am, atp=atp: e.tensor_tensor(out=am[:], in0=atp, in1=mask[:], op=ALU.mult),
                      reads=[d_atp, M.d_const], writes=[d_am])
                yield
                sp, d_sp = M.nrb()
                P.dve(lambda e, sp=sp, sv=sv: e.tensor_scalar(out=sp[:], in0=S_[:], scalar1=sv[:, c_e1:c_e1 + 1],
                                                              scalar2=None, op0=ALU.mult),
                      reads=[d_S, d_sv], writes=[d_sp])
                yield
                op_, d_op = M.ps[by][:, 128:256], M.d_ps[by]
                P.pe(lambda e, op_=op_, sp=sp, qt_=qt_: e.matmul(op_, lhsT=sp[:], rhs=qt_, start=True, stop=False),
                     reads=[d_sp, d_qt], writes=[d_op])
                yield
                P.pe(lambda e, op_=op_, vt=vt, am=am: e.matmul(op_, lhsT=vt, rhs=am[:], start=False, stop=True),
                     reads=[d_vt, d_am], writes=[d_op])
                yield
                P.act(lambda e, op_=op_, la=la: e.activation(out=obuf[dr][:, la:la + 128], in_=op_, func=AF.Copy),
                      reads=[d_op], writes=[d_obuf[dr]])
                yield
            kvp, d_kvp = M.ps[by][:, 256:384], M.d_ps[by]
            P.pe(lambda e, kvp=kvp, ktok=ktok, vt=vt: e.matmul(kvp, lhsT=ktok[:], rhs=vt, start=True, stop=True),
                 reads=[d_ktok, d_vt], writes=[d_kvp])
            yield
            tk, d_tk = M.nrf()
            P.dve(lambda e, tk=tk, kvp=kvp, sv=sv: e.tensor_scalar(out=tk[:], in0=kvp, scalar1=sv[:, c_e2:c_e2 + 1],
                                                                    scalar2=None, op0=ALU.mult),
                  reads=[d_kvp, d_sv], writes=[d_tk])
            yield
            P.dve(lambda e, tk=tk, sv=sv: e.scalar_tensor_tensor(out=S_[:], in0=S_[:], scalar=sv[:, 3:4],
                                                                 in1=tk[:], op0=ALU.mult, op1=ALU.add),
                  reads=[d_tk, d_sv, d_S], writes=[d_S])
            yield
    import itertools
    for step in range(18):
        gens = [chunk_gen(0, step), chunk_gen(1, step)]
        for _ in itertools.zip_longest(*gens):
            pass
    if RSTG < 4:
        return
    for t in range(4):
        lo = t * 512
        o_, d_o = M.ntf()
        P.dve(lambda e, o_=o_: e.tensor_tensor(out=o_[:], in0=M.ofw[:, lo:lo + 512], in1=M.obw[:, lo:lo + 512],
                                               op=ALU.add), reads=[M.d_ofw, M.d_obw], writes=[d_o])
        sq, d_sq = M.ntb()
        P.act(lambda e, sq=sq, o_=o_: e.activation(out=sq[:], in_=o_[:], func=AF.Square), reads=[d_o], writes=[d_sq])
        P.pe(lambda e, sq=sq: e.matmul(M.ps[0][:, :], lhsT=M.ones[:], rhs=sq[:], start=True, stop=True),
             reads=[d_sq, M.d_ones], writes=[M.d_ps[0]])
        ri, d_ri = M.ntf()
        emit_rsqrt(P, M, ri[:], d_ri, M.ps[0][:, :], M.d_ps[0], 512, 1.0 / 128)
        o2, d_o2 = M.ntf()
        P.dve(lambda e, o2=o2, o_=o_, ri=ri: e.scalar_tensor_tensor(
            out=o2[:], in0=o_[:], scalar=M.vec[:, 5:6], in1=ri[:], op0=ALU.mult, op1=ALU.mult),
            reads=[d_o, d_ri, M.d_vec], writes=[d_o2])
        o3, d_o3 = M.ntb()
        P.dve(lambda e, o3=o3, o2=o2: e.tensor_tensor(out=o3[:], in0=o2[:], in1=M.sg_sb[:, lo:lo + 512], op=ALU.mult),
              reads=[d_o2, M.d_rp], writes=[d_o3])
        rows = mergedT("rec", r) if callable(mergedT) else mergedT[512 + r * 128:512 + (r + 1) * 128, :]
        P.dma(rows[:, lo:lo + 512], o3[:], reads=[d_o3], writes=[d_merged], semkey="mg")


def emit_load_act16(P, C, srcT, d_src):
    for q in range(4):
        P.dma(C.A[:, q * 4:(q + 1) * 4, :], srcT[q * 512:(q + 1) * 512, :].rearrange("(c p) t -> p c t", p=128),
              reads=[d_src], writes=C.d_A + C.d_xt, semkey="act16")


def emit_conv_in(P, C, w_in, bgT, cvT, d_out, st, bnd=None, d_bnd=None):
    it = 0
    si = 0
    for fc in range(16):
        s = fc % 2
        wv = C.wgu[s][:].rearrange("p a c n -> p (a c n)").rearrange("p (q c n) -> p q c n", q=4, c=16)
        for q in range(3):
            src = w_in[:, q * 2048 + fc * 128: q * 2048 + (fc + 1) * 128].rearrange("(c p) n -> p c n", p=128)
            P.dma(wv[:, q, :, :], src, writes=[C.d_wgu[s]], semkey=f"wgu{s}", eng="pool")
        for ti, (off, n) in enumerate(C.tiles):
            pb = (it % 2) * 3
            it += 1
            for q in range(3):
                pt = C.ps[pb + q]
                for k in range(16):
                    P.pe(lambda e, pt=pt, q=q, k=k, wv=wv, off=off, n=n: e.matmul(
                        pt[:, 0:n], lhsT=wv[:, q, k, :], rhs=C.hy[:, k, off:off + n],
                        start=(k == 0), stop=(k == 15)),
                        reads=[C.d_wgu[s], C.d_h[ti]], writes=[C.d_ps[pb + q]])
            sb, d_sb = st[si % len(st)]
            si += 1
            P.act(lambda e, sb=sb, pb=pb, n=n: e.activation(out=sb[:, 0:n], in_=C.ps[pb][:, 0:n], func=AF.Copy),
                  reads=[C.d_ps[pb]], writes=[d_sb])
            P.dma(bgT[fc * 128:(fc + 1) * 128, off:off + n], sb[:, 0:n], reads=[d_sb], writes=[d_out],
                  semkey="cvo")
            tmp, d_tmp = C.next_tmp()
            P.act(lambda e, tmp=tmp, pb=pb, n=n: e.activation(out=tmp[:, 0:n], in_=C.ps[pb + 1][:, 0:n],
                                                              func=AF.Copy),
                  reads=[C.d_ps[pb + 1]], writes=[d_tmp])
            sb2, d_sb2 = st[si % len(st)]
            si += 1
            P.dve(lambda e, sb2=sb2, tmp=tmp, pb=pb, n=n: e.tensor_tensor(
                out=sb2[:, 0:n], in0=tmp[:, 0:n], in1=C.ps[pb + 2][:, 0:n], op=ALU.mult),
                reads=[d_tmp, C.d_ps[pb + 2]], writes=[d_sb2])
            P.dma(cvT[fc * 128:(fc + 1) * 128, off:off + n], sb2[:, 0:n], reads=[d_sb2], writes=[d_out],
                  semkey="cvo")
            if bnd is not None and off == 0:
                P.act(lambda e, sb2=sb2, fc=fc: e.activation(out=bnd[:, 0, fc:fc + 1], in_=sb2[:, 0:1], func=AF.Copy),
                      reads=[d_sb2], writes=[d_bnd])
            if bnd is not None and off + n == C.NT:
                P.act(lambda e, sb2=sb2, fc=fc, n=n: e.activation(out=bnd[:, 1, fc:fc + 1], in_=sb2[:, n - 1:n],
                                                                  func=AF.Copy),
                      reads=[d_sb2], writes=[d_bnd])


def emit_conv(P, C, bgT, cvhT, d_in, cw, d_cw, st):
    si = 0
    for c in range(16):
        for ti, (off, n) in enumerate(C.tiles):
            i1 = si % len(st)
            si += 1
            i2 = si % len(st)
            si += 1
            cvt, d_cvt = st[i1]
            bt, d_bt = st[i2]
            P.dma(cvt[:, 0:n + 2], cvhT[c * 128:(c + 1) * 128, off:off + n + 2], reads=[d_in], writes=[d_cvt],
                  semkey=f"cvi{i1}")
            P.dma(bt[:, 0:n], bgT[c * 128:(c + 1) * 128, off:off + n], reads=[d_in], writes=[d_bt],
                  semkey=f"cvi{i2}")
            u, d_u = C.next_tmp()
            P.dve(lambda e, u=u, cvt=cvt, c=c, n=n: e.tensor_scalar(
                out=u[:, 0:n], in0=cvt[:, 0:n], scalar1=cw[:, c, 0:1], scalar2=None, op0=ALU.mult),
                reads=[d_cvt, d_cw], writes=[d_u])
            P.dve(lambda e, u=u, cvt=cvt, c=c, n=n: e.scalar_tensor_tensor(
                out=u[:, 0:n], in0=cvt[:, 1:n + 1], scalar=cw[:, c, 1:2], in1=u[:, 0:n], op0=ALU.mult, op1=ALU.add),
                reads=[d_cvt, d_cw, d_u], writes=[d_u])
            P.dve(lambda e, u=u, cvt=cvt, c=c, n=n: e.scalar_tensor_tensor(
                out=u[:, 0:n], in0=cvt[:, 2:n + 2], scalar=cw[:, c, 2:3], in1=u[:, 0:n], op0=ALU.mult, op1=ALU.add),
                reads=[d_cvt, d_cw, d_u], writes=[d_u])
            P.dve(lambda e, u=u, bt=bt, c=c, off=off, n=n: e.tensor_tensor(
                out=C.A[:, c, off:off + n], in0=u[:, 0:n], in1=bt[:, 0:n], op=ALU.mult),
                reads=[d_u, d_bt], writes=[C.d_A[ti]] + C.d_xt)


def emit_load_mg_sel(P, C, mg_g, d_src, selv, d_sel):
    for hc in range(2):
        for c in range(16):
            kind, rk, hd = c // 8, (c // 4) % 2, c % 4
            k = kind * 2 + hd // 2
            r0 = rk * 256 + (hd % 2) * 128
            P.dma(C.A[:, hc * 16 + c, :], mg_g[k][r0:r0 + 128, hc * 1024:(hc + 1) * 1024],
                  reads=[d_src], writes=C.d_A + C.d_xt, semkey="act16")
    for c in range(16):
        P.dve(lambda e, c=c: e.tensor_scalar(out=C.A[:, c, :], in0=C.A[:, c, :], scalar1=selv[:, 0:1], scalar2=None,
                                             op0=ALU.mult), reads=C.d_A + [d_sel], writes=C.d_A)
        P.dve(lambda e, c=c: e.scalar_tensor_tensor(out=C.A[:, c, :], in0=C.A[:, 16 + c, :], scalar=selv[:, 1:2],
                                                    in1=C.A[:, c, :], op0=ALU.mult, op1=ALU.add),
              reads=C.d_A + [d_sel], writes=C.d_A)


def emit_conv_halo(P, C, bgT, cvT, d_in, bnd_g, d_bnd, selv, d_sel, cw, d_cw, st, hal, d_hal):
    P.dma(hal[:, 0, :], bnd_g[0:128, 16:32], reads=[d_bnd], writes=[d_hal], semkey="hal")
    P.dma(hal[:, 1, :], bnd_g[128:256, 0:16], reads=[d_bnd], writes=[d_hal], semkey="hal")
    P.dve(lambda e: e.tensor_scalar(out=hal[:, 0, :], in0=hal[:, 0, :], scalar1=selv[:, 2:3], scalar2=None,
                                    op0=ALU.mult), reads=[d_hal, d_sel], writes=[d_hal])
    P.dve(lambda e: e.tensor_scalar(out=hal[:, 1, :], in0=hal[:, 1, :], scalar1=selv[:, 3:4], scalar2=None,
                                    op0=ALU.mult), reads=[d_hal, d_sel], writes=[d_hal])
    si = 0
    NT = C.NT
    for c in range(16):
        for ti, (off, n) in enumerate(C.tiles):
            i1 = si % len(st)
            si += 1
            i2 = si % len(st)
            si += 1
            cvt, d_cvt = st[i1]
            bt, d_bt = st[i2]
            lo = max(off - 1, 0)
            hi = min(off + n + 1, NT)
            dlo = lo - (off - 1)
            P.dma(cvt[:, dlo:dlo + (hi - lo)], cvT[c * 128:(c + 1) * 128, lo:hi], reads=[d_in], writes=[d_cvt],
                  semkey=f"cvi{i1}")
            if off == 0:
                P.dve(lambda e, cvt=cvt, c=c: e.tensor_copy(out=cvt[:, 0:1], in_=hal[:, 0, c:c + 1]),
                      reads=[d_hal], writes=[d_cvt])
            if off + n == NT:
                P.dve(lambda e, cvt=cvt, c=c, n=n: e.tensor_copy(out=cvt[:, n + 1:n + 2], in_=hal[:, 1, c:c + 1]),
                      reads=[d_hal], writes=[d_cvt])
            P.dma(bt[:, 0:n], bgT[c * 128:(c + 1) * 128, off:off + n], reads=[d_in], writes=[d_bt],
                  semkey=f"cvi{i2}")
            u, d_u = C.next_tmp()
            P.dve(lambda e, u=u, cvt=cvt, c=c, n=n: e.tensor_scalar(
                out=u[:, 0:n], in0=cvt[:, 0:n], scalar1=cw[:, c, 0:1], scalar2=None, op0=ALU.mult),
                reads=[d_cvt, d_cw], writes=[d_u])
            P.dve(lambda e, u=u, cvt=cvt, c=c, n=n: e.scalar_tensor_tensor(
                out=u[:, 0:n], in0=cvt[:, 1:n + 1], scalar=cw[:, c, 1:2], in1=u[:, 0:n], op0=ALU.mult, op1=ALU.add),
                reads=[d_cvt, d_cw, d_u], writes=[d_u])
            P.dve(lambda e, u=u, cvt=cvt, c=c, n=n: e.scalar_tensor_tensor(
                out=u[:, 0:n], in0=cvt[:, 2:n + 2], scalar=cw[:, c, 2:3], in1=u[:, 0:n], op0=ALU.mult, op1=ALU.add),
                reads=[d_cvt, d_cw, d_u], writes=[d_u])
            P.dve(lambda e, u=u, bt=bt, c=c, off=off, n=n: e.tensor_tensor(
                out=C.A[:, c, off:off + n], in0=u[:, 0:n], in1=bt[:, 0:n], op=ALU.mult),
                reads=[d_u, d_bt], writes=[C.d_A[ti]] + C.d_xt)


BF = ml_dtypes.bfloat16
NCORES = 8


def _ffn_w(nc, tag):
    wg = nc.dram_tensor("wg" + tag, [2048, 5504], F32, kind="ExternalInput")
    wu = nc.dram_tensor("wu" + tag, [2048, 5504], F32, kind="ExternalInput")
    wd = nc.dram_tensor("wd" + tag, [5504, 2048], F32, kind="ExternalInput")
    return wg, wu, wd


def build_A():
    nc = bass.Bass("TRN2", target_bir_lowering=False)
    P = Prog(nc)
    xT = nc.dram_tensor("xT", [2048, 1024], F32, kind="ExternalInput")
    ctxT = nc.dram_tensor("ctxT", [2048, 128], F32, kind="ExternalInput")
    cvecT = nc.dram_tensor("cvecT", [128, 16, 2], F32, kind="ExternalInput")
    ada_w = nc.dram_tensor("ada_w", [2048, 18432], F32, kind="ExternalInput")
    ada_bT = nc.dram_tensor("ada_bT", [128, 144], F32, kind="ExternalInput")
    gTin = nc.dram_tensor("gTin", [128, 6, 16], F32, kind="ExternalInput")
    wg, wu, wd = _ffn_w(nc, "")
    x1T = nc.dram_tensor("x1T", [2048, 1024], F32, kind="ExternalOutput")
    hT = nc.dram_tensor("hT", [2048, 1152], BF16, kind="ExternalOutput")
    mTo = nc.dram_tensor("mTo", [128, 144, 2], F32, kind="ExternalOutput")
    xc1T = nc.dram_tensor("xc1T", [2048, 128], F32, kind="Internal")
    C = Ctx(P, [512, 512, 128])
    scr = {"cv": P.sbuf("cv", [128, 16, 2], F32), "sc": P.sbuf("sc", [128, 16, 2], BF16),
           "bT": P.sbuf("bT", [128, 144], F32)}
    P.dma(C.gT[:], gTin[:], writes=[C.d_gT], semkey="small3")
    emit_modulation(P, C, ada_w, ada_bT[:], cvecT[:], scr)
    d_in = Dep("in")
    d_x1 = [Dep("x1a"), Dep("x1b"), Dep("xc1")]
    emit_coefs(P, C, 1, 2, 0, 1, 0.5, [0, 1])
    srcs = [(xT[:, 0:512], d_in, 0), (xT[:, 512:1024], d_in, 0), (ctxT[:, :], d_in, 1)]
    dsts = [(x1T[:, 0:512], d_x1[0], 0), (x1T[:, 512:1024], d_x1[1], 0), (xc1T[:, :], d_x1[2], 1)]
    emit_ffn(P, C, srcs, dsts, wg, wu, wd, 0)
    emit_coefs(P, C, 4, None, 2, None, 1.0, [0, 1])
    emit_prenorm(P, C, dsts, 3, lambda ti: C.hy[:, :, C.tiles[ti][0]:C.tiles[ti][0] + C.tiles[ti][1]],
                 C.d_h, C.d_A)
    d_hT = Dep("hT")
    for ti, (off, n) in enumerate(C.tiles):
        P.dma(hT[:, off:off + n].rearrange("(c p) t -> p c t", p=128), C.hy[:, :, off:off + n],
              reads=[C.d_h[ti]], writes=[d_hT], semkey="hT")
    P.dma(mTo[:], C.mT[:], reads=[C.d_mT], writes=[Dep("mTo")], semkey="mTo")
    P.emit()
    return nc


def build_B():
    nc = bass.Bass("TRN2", target_bir_lowering=False)
    P = Prog(nc)
    hTf = nc.dram_tensor("hTf", [2048, NTOK], BF16, kind="ExternalInput")
    w_att = nc.dram_tensor("w_att", [4, 3, 2048, 128], F32, kind="ExternalInput")
    w_rec = nc.dram_tensor("w_rec", [4, 5, 2048, 128], F32, kind="ExternalInput")
    ropeC = nc.dram_tensor("ropeCin", [128, 2048], F32, kind="ExternalInput")
    ropeS = nc.dram_tensor("ropeSin", [128, 2048], F32, kind="ExternalInput")
    perm = nc.dram_tensor("permin", [128, 128], F32, kind="ExternalInput")
    ident = nc.dram_tensor("identin", [128, 128], F32, kind="ExternalInput")
    maskf = nc.dram_tensor("maskfin", [128, 128], F32, kind="ExternalInput")
    maskb = nc.dram_tensor("maskbin", [128, 128], F32, kind="ExternalInput")
    lamT = nc.dram_tensor("lamT", [64, 4], F32, kind="ExternalInput")
    dng = nc.dram_tensor("dng", [128, 1], F32, kind="ExternalInput")
    rng = nc.dram_tensor("rng", [128, 1], F32, kind="ExternalInput")
    lbraw = nc.dram_tensor("lbraw", [128, 2, 2, 4], F32, kind="ExternalInput")
    mergedT = nc.dram_tensor("mergedT", [1024, 2048], BF16, kind="ExternalOutput")
    M = MixCtx(P)
    emit_mix_setup(P, M, hTf, ropeC[:], ropeS[:], perm[:], ident[:], maskf[:], maskb[:], lamT[:], dng[:], rng[:],
                   lbraw[:])
    d_merged = Dep("merged")
    for hd in range(4):
        for i in range(3):
            load_w(P, M, i, w_att[hd, i])
        for i in range(5):
            load_w(P, M, 3 + i, w_rec[hd, i])
        emit_attention_head(P, M, hd, mergedT, d_merged)
        emit_rec_head(P, M, hd, mergedT, d_merged)
    P.emit()
    return nc


def build_C():
    nc = bass.Bass("TRN2", target_bir_lowering=False)
    P = Prog(nc)
    x1T = nc.dram_tensor("x1T", [2048, 1024], F32, kind="ExternalInput")
    mgT = nc.dram_tensor("mgT", [2048, 1024], BF16, kind="ExternalInput")
    w_out = nc.dram_tensor("w_out", [2048, 2048], F32, kind="ExternalInput")
    mT0 = nc.dram_tensor("mT0", [128, 144, 2], F32, kind="ExternalInput")
    gT0 = nc.dram_tensor("gT0", [128, 6, 16], F32, kind="ExternalInput")
    gT1 = nc.dram_tensor("gT1", [128, 6, 16], F32, kind="ExternalInput")
    cvecT = nc.dram_tensor("cvecT", [128, 16, 2], F32, kind="ExternalInput")
    ada_w = nc.dram_tensor("ada_w", [2048, 18432], F32, kind="ExternalInput")
    ada_bT = nc.dram_tensor("ada_bT", [128, 144], F32, kind="ExternalInput")
    wgA, wuA, wdA = _ffn_w(nc, "A")
    wgB, wuB, wdB = _ffn_w(nc, "B")
    cw_in = nc.dram_tensor("cw_in", [2048, 6144], F32, kind="ExternalInput")
    x4T = nc.dram_tensor("x4T", [2048, 1024], F32, kind="ExternalOutput")
    bgT = nc.dram_tensor("bgT", [2048, 1024], F32, kind="ExternalOutput")
    cvT = nc.dram_tensor("cvT", [2048, 1024], F32, kind="ExternalOutput")
    mT1 = nc.dram_tensor("mT1", [128, 144, 2], F32, kind="ExternalOutput")
    x2T = nc.dram_tensor("x2T", [2048, 1024], F32, kind="Internal")
    x3T = nc.dram_tensor("x3T", [2048, 1024], F32, kind="Internal")
    C = Ctx(P, [512, 512])
    scr = {"cv": P.sbuf("cv", [128, 16, 2], F32), "sc": P.sbuf("sc", [128, 16, 2], BF16),
           "bT": P.sbuf("bT", [128, 144], F32)}
    st = [(P.sbuf(f"st{i}", [128, 512], F32), Dep(f"st{i}")) for i in range(4)]
    d_in = Dep("in")

    def tl(t, d):
        return [(t[:, 0:512], d[0], 0), (t[:, 512:1024], d[1], 0)]
    d_x1 = [d_in, d_in]
    d_x2 = [Dep("x2a"), Dep("x2b")]
    d_x3 = [Dep("x3a"), Dep("x3b")]
    d_x4 = [Dep("x4a"), Dep("x4b")]
    P.dma(C.gT[:], gT0[:], writes=[C.d_gT], semkey="small3")
    P.dma(C.mT[:], mT0[:], writes=[C.d_mT], semkey="small4")
    emit_load_act16(P, C, mgT, d_in)
    emit_coefs(P, C, 4, 5, 2, 3, 1.0, [0])
    emit_down_residual(P, C, 16, w_out, tl(x1T, d_x1), tl(x2T, d_x2))
    emit_coefs(P, C, 7, 8, 4, 5, 0.5, [0])
    emit_ffn(P, C, tl(x2T, d_x2), tl(x3T, d_x3), wgA, wuA, wdA, 6)
    P.dma(C.gT[:], gT1[:], writes=[C.d_gT], semkey="small3")
    emit_modulation(P, C, ada_w, ada_bT[:], cvecT[:], scr)
    emit_coefs(P, C, 1, 2, 0, 1, 0.5, [0])
    emit_ffn(P, C, tl(x3T, d_x3), tl(x4T, d_x4), wgB, wuB, wdB, 0)
    emit_coefs(P, C, 4, None, 2, None, 1.0, [0])
    emit_prenorm(P, C, tl(x4T, d_x4), 3, lambda ti: C.hy[:, :, C.tiles[ti][0]:C.tiles[ti][0] + C.tiles[ti][1]],
                 C.d_h, C.d_A)
    emit_conv_in(P, C, cw_in, bgT, cvT, Dep("cvout"), st)
    P.dma(mT1[:], C.mT[:], reads=[C.d_mT], writes=[Dep("mT1o")], semkey="mTo")
    P.emit()
    return nc


def build_D():
    nc = bass.Bass("TRN2", target_bir_lowering=False)
    P = Prog(nc)
    x4T = nc.dram_tensor("x4T", [2048, 1024], F32, kind="ExternalInput")
    bgT = nc.dram_tensor("bgT", [2048, 1024], F32, kind="ExternalInput")
    cvhT = nc.dram_tensor("cvhT", [2048, 1026], F32, kind="ExternalInput")
    cwT = nc.dram_tensor("cwT", [128, 16, 3], F32, kind="ExternalInput")
    cw_out = nc.dram_tensor("cw_out", [2048, 2048], F32, kind="ExternalInput")
    mT1 = nc.dram_tensor("mT1", [128, 144, 2], F32, kind="ExternalInput")
    gT1 = nc.dram_tensor("gT1", [128, 6, 16], F32, kind="ExternalInput")
    wg, wu, wd = _ffn_w(nc, "")
    outT = nc.dram_tensor("outT", [2048, 1024], F32, kind="ExternalOutput")
    x5T = nc.dram_tensor("x5T", [2048, 1024], F32, kind="Internal")
    C = Ctx(P, [512, 512])
    st = [(P.sbuf(f"st{i}", [128, 514], F32), Dep(f"st{i}")) for i in range(4)]
    cw = P.sbuf("cw", [128, 16, 3], F32)
    d_cw = Dep("cw")
    d_in = Dep("in")

    def tl(t, d):
        return [(t[:, 0:512], d[0], 0), (t[:, 512:1024], d[1], 0)]
    d_x5 = [Dep("x5a"), Dep("x5b")]
    d_o = [Dep("oa"), Dep("ob")]
    P.dma(C.gT[:], gT1[:], writes=[C.d_gT], semkey="small3")
    P.dma(C.mT[:], mT1[:], writes=[C.d_mT], semkey="small4")
    P.dma(cw[:], cwT[:], writes=[d_cw], semkey="small5")
    emit_conv(P, C, bgT, cvhT, d_in, cw, d_cw, st)
    emit_coefs(P, C, 4, 5, 2, 3, 1.0, [0])
    emit_down_residual(P, C, 16, cw_out, tl(x4T, [d_in, d_in]), tl(x5T, d_x5))
    emit_coefs(P, C, 7, 8, 4, 5, 0.5, [0])
    emit_ffn(P, C, tl(x5T, d_x5), tl(outT, d_o), wg, wu, wd, 6)
    P.emit()
    return nc


ARENA_BYTES = 207 * 1024
FUSE_STOP = int(os.environ.get("FUSE_STOP", "0"))


def build_fused():
    nc = bass.Bass("TRN2", target_bir_lowering=False)
    P = Prog(nc)
    P.use_arena(ARENA_BYTES)
    inp = lambda n, sh, dt=F32: nc.dram_tensor(n, sh, dt, kind="ExternalInput")
    xT = inp("xT", [2048, 1024])
    ctxT = inp("ctxT", [2048, 128])
    ada_sl = inp("ada_sl", [2, 2048, 4608])
    cvec3 = inp("cvec3", [128, 16, 3])
    selb = inp("selb", [128, 2])
    bT0 = inp("bT0", [128, 144])
    bT1 = inp("bT1", [128, 144])
    gT0 = inp("gT0", [128, 6, 16])
    gT1 = inp("gT1", [128, 6, 16])
    W = [_ffn_w(nc, str(i)) for i in range(4)]
    w_att = inp("w_att", [4, 3, 2048, 128])
    w_rec = inp("w_rec", [4, 5, 2048, 128])
    ropeC = inp("ropeCin", [128, 2048])
    ropeS = inp("ropeSin", [128, 2048])
    perm = inp("permin", [128, 128])
    ident = inp("identin", [128, 128])
    maskf = inp("maskfin", [128, 128])
    maskb = inp("maskbin", [128, 128])
    lamT = inp("lamT", [64, 4])
    dng = inp("dng", [128, 1])
    rng = inp("rng", [128, 1])
    lbraw = inp("lbraw", [128, 2, 2, 4])
    w_out = inp("w_out", [2048, 2048])
    cw_in = inp("cw_in", [2048, 6144])
    cwT = inp("cwT", [128, 16, 3])
    cw_out = inp("cw_out", [2048, 2048])
    selin = inp("selin", [128, 4])
    outT = nc.dram_tensor("outT", [2048, 1024], F32, kind="ExternalOutput")
    itn = lambda n, sh, dt=F32: nc.dram_tensor(n, sh, dt, kind="Internal")
    x1T, x2T, x3T, x4T, x5T = [itn(f"x{i}T", [2048, 1024]) for i in (1, 2, 3, 4, 5)]
    xc1T = itn("xc1T", [2048, 128])
    hT_own = [itn(f"hT_own{t}", [2048, n_], BF16) for t, n_ in enumerate((512, 512, 128))]
    hT_g = [itn(f"hT_g{t}", [4096, n_], BF16) for t, n_ in enumerate((512, 512, 128))]
    mg_own = [itn(f"mg_own{q}", [256, 2048], BF16) for q in range(4)]
    mg_g = [itn(f"mg_g{q}", [512, 2048], BF16) for q in range(4)]
    bgT = itn("bgT", [2048, 1024])
    cvT = itn("cvT", [2048, 1024])
    bnd_own = itn("bnd_own", [128, 32])
    bnd_g = itn("bnd_g", [256, 32])
    d_in = Dep("in")

    def tl(t, d):
        return [(t[:, 0:512], d[0], 0), (t[:, 512:1024], d[1], 0)]

    def mk_scr():
        return {"cv": P.sbuf("cv", [128, 16, 2], F32), "sc": P.sbuf("sc", [128, 16, 2], BF16),
                "bT": P.sbuf("bT", [128, 144], F32)}
    hyv = lambda C: (lambda ti: C.hy[:, :, C.tiles[ti][0]:C.tiles[ti][0] + C.tiles[ti][1]])

    mp_own = itn("mp_own", [128, 216])
    mp_g1 = itn("mp_g1", [256, 216])
    mp_g2 = itn("mp_g2", [512, 216])
    mTd = [itn("mT0d", [128, 144, 2]), itn("mT1d", [128, 144, 2])]
    d_mTd = [Dep("mT0d"), Dep("mT1d")]
    C = Ctx(P, [512, 512])
    emit_modulation_sharded(P, C, ada_sl, cvec3[:], selb[:], bT0[:], bT1[:], mp_own, mp_g1, mp_g2, mTd, d_mTd)
    P.barrier()
    P.aoff = 0
    C = Ctx(P, [512, 512, 128])
    P.dma(C.gT[:], gT0[:], writes=[C.d_gT], semkey="small3")
    P.dma(C.mT[:], mTd[0][:], reads=[d_mTd[0]], writes=[C.d_mT], semkey="small4")
    d_x1 = [Dep("x1a"), Dep("x1b"), Dep("xc1")]
    emit_coefs(P, C, 1, 2, 0, 1, 0.5, [0, 1])
    srcs = [(xT[:, 0:512], d_in, 0), (xT[:, 512:1024], d_in, 0), (ctxT[:, :], d_in, 1)]
    dsts = [(x1T[:, 0:512], d_x1[0], 0), (x1T[:, 512:1024], d_x1[1], 0), (xc1T[:, :], d_x1[2], 1)]
    d_hg = Dep("hT_g")

    def ship_tile(ti, off, n):
        d_t = Dep(f"hT_own{ti}")
        P.dma(hT_own[ti][:, :].rearrange("(c p) t -> p c t", p=128), C.hy[:, :, off:off + n],
              reads=[C.d_h[ti]], writes=[d_t], semkey=f"hT{ti}")
        P.allgather_pairs(hT_g[ti], hT_own[ti], reads=[d_t], writes=[d_hg], semkey="cc1")
    emit_prenorm(P, C, srcs, 0, hyv(C), C.d_h, C.d_A)
    emit_gateup(P, C, W[0][0], W[0][1])
    emit_coefs(P, C, 4, None, 2, None, 1.0, [0, 1], which="pre")
    emit_down_residual(P, C, NJ, W[0][2], srcs, dsts, after_post=lambda ti: emit_prenorm(
        P, C, dsts, 3, hyv(C), C.d_h, C.d_A, resident=True, after_tile=ship_tile, tiles_only=[ti]))
    P.barrier()
    P.aoff = 0
    M = MixCtx(P)
    for r in range(2):
        for q in range(4):
            rs = slice(r * 2048 + q * 512, r * 2048 + (q + 1) * 512)
            P.dma(M.hs[:, q * 4:(q + 1) * 4, r * 128:(r + 1) * 128],
                  hT_g[2][rs, :].rearrange("(c p) t -> p c t", p=128), reads=[d_hg], writes=[M.d_hs], semkey="hs")
            for ti in range(2):
                c0 = 256 + r * 1024 + ti * 512
                P.dma(M.hs[:, q * 4:(q + 1) * 4, c0:c0 + 512],
                      hT_g[ti][rs, :].rearrange("(c p) t -> p c t", p=128), reads=[d_hg], writes=[M.d_hs],
                      semkey="hs")
    emit_mix_setup(P, M, None, ropeC[:], ropeS[:], perm[:], ident[:], maskf[:], maskb[:], lamT[:], dng[:], rng[:],
                   lbraw[:])
    d_mg = Dep("mg_own")
    d_mgg = Dep("mg_g")
    for i in range(3):
        load_w(P, M, i, w_att[0, i])
    for i in range(5):
        load_w(P, M, 3 + i, w_rec[0, i])
    for hd in range(4):
        mgdst = lambda kind, h: mg_own[(0 if kind == "att" else 2) + h // 2][(h % 2) * 128:(h % 2) * 128 + 128, :]
        emit_attention_head(P, M, hd, mgdst, d_mg)
        P.barrier()

        def prefetch(hd=hd):
            if hd + 1 < 4:
                for i in range(3):
                    load_w(P, M, i, w_att[hd + 1, i])
                for i in range(5):
                    load_w(P, M, 3 + i, w_rec[hd + 1, i])
        emit_rec_head(P, M, hd, mgdst, d_mg, after_burst=prefetch)
        P.barrier()
        if hd % 2 == 1:
            for q in (hd // 2, 2 + hd // 2):
                P.allgather_pairs(mg_g[q], mg_own[q], reads=[d_mg], writes=[d_mgg], semkey="cc2")
    if FUSE_STOP == 2:
        dbg = nc.dram_tensor("dbg", [2048, 2048], BF16, kind="ExternalOutput")
        for q in range(4):
            for r in range(2):
                base = r * 1024 + (q // 2) * 512 + (q % 2) * 256
                P.dma(dbg[base:base + 256, :], mg_g[q][r * 256:(r + 1) * 256, :], reads=[d_mgg],
                      writes=[Dep("dbg")], semkey="dbg")
        P.emit()
        return nc
    P.barrier()
    P.aoff = 0
    C = Ctx(P, [512, 512])
    selv = P.sbuf("selv", [128, 4], F32)
    d_sel = Dep("selv")
    st = [(P.sbuf(f"st{i}", [128, 514], F32), Dep(f"st{i}")) for i in range(4)]
    cw = P.sbuf("cw", [128, 16, 3], F32)
    d_cw = Dep("cw")
    hal = P.sbuf("hal", [128, 2, 16], F32)
    d_hal = Dep("hal")
    P.dma(selv[:], selin[:], writes=[d_sel], semkey="small5")
    P.dma(cw[:], cwT[:], writes=[d_cw], semkey="small5")
    P.dma(C.gT[:], gT0[:], writes=[C.d_gT], semkey="small3")
    P.dma(C.mT[:], mTd[0][:], reads=[d_mTd[0]], writes=[C.d_mT], semkey="small4")
    d_x2 = [Dep("x2a"), Dep("x2b")]
    d_x3 = [Dep("x3a"), Dep("x3b")]
    d_x4 = [Dep("x4a"), Dep("x4b")]
    d_x5 = [Dep("x5a"), Dep("x5b")]
    d_o = [Dep("oa"), Dep("ob")]
    emit_load_mg_sel(P, C, mg_g, d_mgg, selv, d_sel)

    def pre_next(C, srcs, sl_shift):
        return lambda ti: emit_prenorm(P, C, srcs, sl_shift, hyv(C), C.d_h, C.d_A, resident=True, tiles_only=[ti])
    emit_coefs(P, C, 4, 5, 2, 3, 1.0, [0], which="post")
    emit_coefs(P, C, 7, None, 4, None, 1.0, [0], which="pre")
    emit_down_residual(P, C, 16, w_out, tl(x1T, d_x1), tl(x2T, d_x2), after_post=pre_next(C, tl(x2T, d_x2), 6))
    emit_gateup(P, C, W[1][0], W[1][1])
    emit_coefs(P, C, 7, 8, 4, 5, 0.5, [0], which="post")
    P.dma(C.gT[:], gT1[:], writes=[C.d_gT], semkey="small3")
    P.dma(C.mT[:], mTd[1][:], reads=[d_mTd[1]], writes=[C.d_mT], semkey="small4")
    emit_coefs(P, C, 1, None, 0, None, 1.0, [0], which="pre")
    emit_down_residual(P, C, NJ, W[1][2], tl(x2T, d_x2), tl(x3T, d_x3), after_post=pre_next(C, tl(x3T, d_x3), 0))
    emit_gateup(P, C, W[2][0], W[2][1])
    emit_coefs(P, C, 1, 2, 0, 1, 0.5, [0], which="post")
    emit_coefs(P, C, 4, None, 2, None, 1.0, [0], which="pre")
    emit_down_residual(P, C, NJ, W[2][2], tl(x3T, d_x3), tl(x4T, d_x4), after_post=pre_next(C, tl(x4T, d_x4), 3))
    d_cv = Dep("cvout")
    st512 = [(t[:, 0:512], d) for (t, d) in st]
    bnd_sb = P.sbuf("bnd_sb", [128, 2, 16], F32)
    d_bsb = Dep("bnd_sb")
    emit_conv_in(P, C, cw_in, bgT, cvT, d_cv, st512, bnd_sb, d_bsb)
    d_bo = Dep("bnd_own")
    P.dma(bnd_own[:, :], bnd_sb[:].rearrange("p a c -> p (a c)"), reads=[d_bsb], writes=[d_bo], semkey="bnd")
    d_bg = Dep("bnd_g")
    P.allgather_pairs(bnd_g, bnd_own, reads=[d_bo], writes=[d_bg], semkey="cc3")
    emit_conv_halo(P, C, bgT, cvT, d_cv, bnd_g, d_bg, selv, d_sel, cw, d_cw, st, hal, d_hal)
    emit_coefs(P, C, 4, 5, 2, 3, 1.0, [0], which="post")
    emit_coefs(P, C, 7, None, 4, None, 1.0, [0], which="pre")
    emit_down_residual(P, C, 16, cw_out, tl(x4T, d_x4), tl(x5T, d_x5), after_post=pre_next(C, tl(x5T, d_x5), 6))
    emit_gateup(P, C, W[3][0], W[3][1])
    emit_coefs(P, C, 7, 8, 4, 5, 0.5, [0], which="post")
    emit_down_residual(P, C, NJ, W[3][2], tl(x5T, d_x5), tl(outT, d_o))
    stuck = simulate_sync(P)
    if stuck:
        raise RuntimeError(f"sync deadlock: {stuck}")
    P.emit()
    return nc


def mix_consts():
    p = np.arange(128)
    d = p % 64
    i = d % 16
    freqs = (10000.0 ** (-np.arange(16, dtype=np.float32) / 16)).astype(np.float32)
    t = np.arange(2048)
    row = (t // 64).astype(np.float32)
    col = (t % 64).astype(np.float32)
    pos = np.where((d < 32)[:, None], row[None, :], col[None, :]).astype(np.float32)
    ang = (pos * freqs[i][:, None]).astype(np.float32)
    C = np.cos(ang).astype(np.float32)
    S = np.sin(ang).astype(np.float32)
    perm = np.zeros((128, 128), np.float32)
    for m in range(128):
        if (m % 32) < 16:
            perm[m + 16, m] = -1.0
        else:
            perm[m - 16, m] = 1.0
    ident = np.eye(128, dtype=np.float32)
    s = np.arange(128)[:, None]
    tt = np.arange(128)[None, :]
    return C, S, perm, ident, (s <= tt).astype(np.float32), (s >= tt).astype(np.float32)


_PROGS = {}


def _prog(name, fn):
    if name not in _PROGS:
        _PROGS[name] = fn()
    return _PROGS[name]


def _run(nc, in_maps):
    res = run_bass_kernel_spmd(nc, in_maps, core_ids=list(range(NCORES)))
    return res.results


def kernel(x, c, ctx, c_ctx, ada_w, ada_b, norm_g, ffn_w_gate, ffn_w_up, ffn_w_down, mix_w_in, mix_w_out,
           diff_lambda, diff_norm_g, rec_norm_g, rec_lb, conv_w_in, conv_w, conv_w_out):
    f32 = lambda a: np.ascontiguousarray(np.asarray(a, dtype=np.float32))
    x, c, ctx, c_ctx = f32(x), f32(c), f32(ctx), f32(c_ctx)
    ada_w, ada_b, norm_g = f32(ada_w), f32(ada_b), f32(norm_g)
    ffn_w_gate, ffn_w_up, ffn_w_down = f32(ffn_w_gate), f32(ffn_w_up), f32(ffn_w_down)
    mix_w_in, mix_w_out = f32(mix_w_in), f32(mix_w_out)
    conv_w_in, conv_w, conv_w_out = f32(conv_w_in), f32(conv_w), f32(conv_w_out)
    rec_lb = f32(rec_lb)
    gT = [np.ascontiguousarray(norm_g[l].reshape(6, 16, 128).transpose(2, 0, 1)) for l in range(2)]
    bT = [np.ascontiguousarray(ada_b[l].reshape(144, 128).T) for l in range(2)]
    cvec = [np.ascontiguousarray(np.stack([c[b], c_ctx], -1).reshape(16, 128, 2).transpose(1, 0, 2))
            for b in range(4)]
    Cc, Ss, perm, ident, maskf, maskb = mix_consts()
    w_in = mix_w_in[0]
    cwT = np.ascontiguousarray(conv_w[0].reshape(3, 16, 128).transpose(2, 1, 0))
    lamT = np.ascontiguousarray(f32(diff_lambda)[0].T)
    dng = f32(diff_norm_g)[0].reshape(128, 1).copy()
    rngv = f32(rec_norm_g)[0].reshape(128, 1).copy()
    heads = []
    for hh in range(2):
        w_att = np.empty((4, 3, 2048, 128), np.float32)
        w_rec = np.empty((4, 5, 2048, 128), np.float32)
        lbraw = np.empty((128, 2, 2, 4), np.float32)
        for hd in range(4):
            g = hh * 4 + hd
            for q in range(3):
                w_att[hd, q] = w_in[:, q * 1024 + g * 128: q * 1024 + (g + 1) * 128]
            for q in range(5):
                w_rec[hd, q] = w_in[:, 3072 + q * 1024 + g * 128: 3072 + q * 1024 + (g + 1) * 128]
            lbraw[:, :, :, hd] = rec_lb[:, :, g * 128:(g + 1) * 128].transpose(2, 0, 1)
        heads.append((w_att, w_rec, lbraw))
    ada_sl = [np.ascontiguousarray(ada_w[:, :, qq * 4608:(qq + 1) * 4608]) for qq in range(4)]
    cvec3 = [np.ascontiguousarray(np.stack([c[bl], c[bl + 2], c_ctx], -1).reshape(16, 128, 3).transpose(1, 0, 2))
             for bl in range(2)]
    selb = [np.ascontiguousarray(np.tile(np.eye(2, dtype=np.float32)[w][None, :], (128, 1))) for w in range(2)]
    in_maps = []
    for i in range(NCORES):
        b, h = i // 2, i % 2
        sel = np.zeros((128, 4), np.float32)
        sel[:, 0] = 1.0 if h == 0 else 0.0
        sel[:, 1] = 1.0 if h == 1 else 0.0
        sel[:, 2] = 1.0 if h == 1 else 0.0
        sel[:, 3] = 1.0 if h == 0 else 0.0
        m = {
            "xT": np.ascontiguousarray(x[b, h * 1024:(h + 1) * 1024].T),
            "ctxT": np.ascontiguousarray(ctx[b, h * 128:(h + 1) * 128].T),
            "ada_sl": ada_sl[(i % 2) + 2 * (i // 4)], "cvec3": cvec3[b % 2], "selb": selb[b // 2],
            "bT0": bT[0], "bT1": bT[1],
            "gT0": gT[0], "gT1": gT[1],
            "w_att": heads[h][0], "w_rec": heads[h][1], "lbraw": heads[h][2],
            "ropeCin": Cc, "ropeSin": Ss, "permin": perm, "identin": ident, "maskfin": maskf, "maskbin": maskb,
            "lamT": lamT, "dng": dng, "rng": rngv,
            "w_out": mix_w_out[0], "cw_in": conv_w_in[0], "cwT": cwT, "cw_out": conv_w_out[0], "selin": sel}
        for k, (l, j) in enumerate([(0, 0), (0, 1), (1, 0), (1, 1)]):
            m[f"wg{k}"] = ffn_w_gate[l, j]
            m[f"wu{k}"] = ffn_w_up[l, j]
            m[f"wd{k}"] = ffn_w_down[l, j]
        in_maps.append(m)
    rD = _run(_prog("F", build_fused), in_maps)
    if FUSE_STOP:
        return rD
    out = np.empty((4, 2048, 2048), np.float32)
    for i in range(NCORES):
        b, h = i // 2, i % 2
        out[b, h * 1024:(h + 1) * 1024] = np.asarray(rD[i]["outT"]).T
    return out
```

```python
import contextlib
import types
import os
import math
import numpy as np
import ml_dtypes
import concourse.bass as bass
import concourse.mybir as mybir
from concourse.bass_utils import run_bass_kernel_spmd


F32 = mybir.dt.float32
BF16 = mybir.dt.bfloat16
ALU = mybir.AluOpType
AF = mybir.ActivationFunctionType
AX = mybir.AxisListType


class Dep:
    __slots__ = ("name", "w", "r", "excl")

    def __init__(self, name, excl=False):
        self.name = name
        self.w = {}
        self.r = {}
        self.excl = excl


class Ins:
    __slots__ = ("eng", "fn", "deps", "signal", "semkey", "semval", "is_dma", "idx", "inc")
    _n = 0

    def __init__(self, eng, fn, is_dma=False, semkey=None):
        self.eng = eng
        self.fn = fn
        self.deps = []
        self.signal = False
        self.is_dma = is_dma
        self.semkey = semkey
        self.semval = None
        self.inc = 16
        Ins._n += 1
        self.idx = Ins._n


def _freeze(fn):
    if getattr(fn, "__closure__", None) is None:
        return fn
    cells = []
    for c in fn.__closure__:
        try:
            cells.append(types.CellType(c.cell_contents))
        except ValueError:
            cells.append(c)
    return types.FunctionType(fn.__code__, fn.__globals__, fn.__name__, fn.__defaults__, tuple(cells))


class Prog:
    ENGS = ("pe", "act", "dve", "pool", "sp")

    def __init__(self, nc):
        self.nc = nc
        self.streams = {e: [] for e in self.ENGS}
        self.stack = contextlib.ExitStack()
        self.dma_keys = {}
        self.n_sb = 0
        self.arena = None
        self.aoff = 0
        self.pending = {e: [] for e in self.ENGS}
        self.open_dmas = []
        self.ps = None
        self.d_ps = None

    def use_arena(self, nbytes):
        self.arena = self.stack.enter_context(self.nc.sbuf_tensor("arena", [128, nbytes], mybir.dt.uint8))
        self.asize = nbytes
        self.aoff = 0

    def shared_psum(self):
        if self.ps is None:
            self.ps = [self.psum(f"ps{i}", [128, 512]) for i in range(8)]
            self.d_ps = [Dep(f"ps{i}", excl=True) for i in range(8)]
        return self.ps, self.d_ps

    def barrier(self):
        lasts = []
        for e in ("pe", "act", "dve", "pool"):
            for ins in reversed(self.streams[e]):
                if not ins.is_dma:
                    lasts.append(ins)
                    break
        lasts += self.open_dmas
        self.open_dmas = []
        for d in lasts:
            d.signal = True
        for e in self.ENGS:
            self.pending[e] = list(lasts)

    def sbuf(self, name, shape, dtype):
        if self.arena is None:
            return self.stack.enter_context(self.nc.sbuf_tensor(name, list(shape), dtype))
        esz = {F32: 4, BF16: 2}[dtype]
        nel = 1
        for s_ in shape[1:]:
            nel *= s_
        nbytes = nel * esz
        off = (self.aoff + 63) // 64 * 64
        if off + nbytes > self.asize:
            raise MemoryError(f"SBUF arena overflow allocating {name}: {off}+{nbytes} > {self.asize}")
        self.aoff = off + nbytes
        v = self.arena[0:shape[0], off:off + nbytes].bitcast(dtype)
        if len(shape) == 3:
            v = v.rearrange("p (a b) -> p a b", a=shape[1])
        elif len(shape) == 4:
            v = v.rearrange("p (a b c) -> p a b c", a=shape[1], b=shape[2])
        return v

    def psum(self, name, shape, dtype=F32):
        return self.stack.enter_context(self.nc.psum_tensor(name, list(shape), dtype))

    def dram(self, name, shape, dtype, kind="Internal"):
        return self.nc.dram_tensor(name, list(shape), dtype, kind=kind)

    def op(self, eng, fn, reads=(), writes=(), is_dma=False, semkey=None, inc=16):
        ins = Ins(eng, _freeze(fn), is_dma, semkey)
        ins.inc = inc
        key = ("d", id(ins)) if is_dma else eng
        deps = {}
        for t in reads:
            for k, d in t.w.items():
                deps[id(d)] = d
            if t.excl:
                for k, d in t.r.items():
                    if not is_dma and k == eng:
                        continue
                    deps[id(d)] = d
        for t in writes:
            for k, d in t.r.items():
                if not is_dma and k == eng:
                    continue
                deps[id(d)] = d
            for k, d in t.w.items():
                if not is_dma and k == eng:
                    continue
                deps[id(d)] = d
        if self.pending[eng]:
            for d in self.pending[eng]:
                if d.is_dma or d.eng != eng:
                    deps[id(d)] = d
            self.pending[eng] = []
        for d in deps.values():
            d.signal = True
        ins.deps = list(deps.values())
        for t in reads:
            t.r[key] = ins
        for t in writes:
            if t.r:
                t.r = {}
                t.w = {}
            t.w[key] = ins
        if is_dma:
            ins.signal = True
            if semkey is None:
                raise ValueError("dma needs semkey")
            self.dma_keys.setdefault(semkey, 0)
            self.open_dmas.append(ins)
        self.streams[eng].append(ins)
        return ins

    def fence(self, srcs, dsts):
        for sd in srcs:
            for dd in dsts:
                for k, i in sd.w.items():
                    if k not in dd.w or dd.w[k].idx < i.idx:
                        dd.w[k] = i
                for k, i in sd.r.items():
                    if k not in dd.r or dd.r[k].idx < i.idx:
                        dd.r[k] = i

    def pe(self, fn, reads=(), writes=()):
        return self.op("pe", fn, reads, writes)

    def act(self, fn, reads=(), writes=()):
        return self.op("act", fn, reads, writes)

    def dve(self, fn, reads=(), writes=()):
        return self.op("dve", fn, reads, writes)

    def pool(self, fn, reads=(), writes=()):
        return self.op("pool", fn, reads, writes)

    def dma(self, out, in_, reads=(), writes=(), semkey=None, eng="sp", **kw):
        return self.op(eng, lambda e: e.dma_start(out=out, in_=in_, **kw), reads, writes,
                       is_dma=True, semkey=semkey)

    def allgather_pairs(self, out_t, in_t, reads=(), writes=(), semkey="cc"):
        return self.op("pool", lambda e: e.collective_compute(
            "AllGather", ALU.bypass, replica_groups=[[0, 1], [2, 3], [4, 5], [6, 7]],
            ins=[in_t.ap().opt()], outs=[out_t.ap().opt()]), reads, writes, is_dma=True, semkey=semkey, inc=1)

    def allgather_far(self, out_t, in_t, reads=(), writes=(), semkey="ccf"):
        return self.op("pool", lambda e: e.collective_compute(
            "AllGather", ALU.bypass, replica_groups=[[0, 4], [1, 5], [2, 6], [3, 7]],
            ins=[in_t.ap().opt()], outs=[out_t.ap().opt()]), reads, writes, is_dma=True, semkey=semkey, inc=1)

    def allreduce_all(self, out_t, in_t, reads=(), writes=(), semkey="ar"):
        return self.op("pool", lambda e: e.collective_compute(
            "AllReduce", ALU.add, replica_groups=[list(range(8))],
            ins=[in_t.ap().opt()], outs=[out_t.ap().opt()]), reads, writes, is_dma=True, semkey=semkey, inc=1)

    def emit(self):
        nc = self.nc
        st = self.stack
        sems = {}
        for e in ("pe", "act", "dve", "pool"):
            sems[e] = st.enter_context(nc.semaphore("s_" + e))
        for k in self.dma_keys:
            sems[("d", k)] = st.enter_context(nc.semaphore("d_" + str(k)))
        for e in self.ENGS:
            cnt = 0
            for ins in self.streams[e]:
                if ins.is_dma:
                    self.dma_keys[ins.semkey] += ins.inc
                    ins.semval = self.dma_keys[ins.semkey]
                elif ins.signal:
                    cnt += 1
                    ins.semval = cnt
        final_dma = dict(self.dma_keys)

        def run(eng_name, e):
            seen = {}
            for ins in self.streams[eng_name]:
                need = {}
                for d in ins.deps:
                    sk = ("d", d.semkey) if d.is_dma else d.eng
                    if d.semval > need.get(sk, 0):
                        need[sk] = d.semval
                for sk, v in need.items():
                    if seen.get(sk, 0) >= v:
                        continue
                    e.wait_ge(sems[sk], v)
                    seen[sk] = v
                bi = ins.fn(e)
                if ins.is_dma:
                    bi.then_inc(sems[("d", ins.semkey)], ins.inc)
                elif ins.signal:
                    bi.then_inc(sems[eng_name], 1)
            if eng_name == "sp":
                for k, v in final_dma.items():
                    if v > 0 and seen.get(("d", k), 0) < v:
                        e.wait_ge(sems[("d", k)], v)

        with nc.Block() as block:
            @block.tensor
            def _(e):
                run("pe", e)

            @block.scalar
            def _(e):
                run("act", e)

            @block.vector
            def _(e):
                run("dve", e)

            @block.gpsimd
            def _(e):
                run("pool", e)

            @block.sync
            def _(e):
                run("sp", e)
        st.close()


def simulate_sync(P):
    keys = dict.fromkeys(P.dma_keys, 0)
    for e in P.ENGS:
        cnt = 0
        for ins in P.streams[e]:
            if ins.is_dma:
                keys[ins.semkey] += ins.inc
                ins.semval = keys[ins.semkey]
            elif ins.signal:
                cnt += 1
                ins.semval = cnt
    sem = {}
    pc = {e: 0 for e in P.ENGS}
    progress = True
    while progress:
        progress = False
        for e in P.ENGS:
            st = P.streams[e]
            while pc[e] < len(st):
                ins = st[pc[e]]
                ok = True
                for d in ins.deps:
                    sk = ("d", d.semkey) if d.is_dma else d.eng
                    if sem.get(sk, 0) < d.semval:
                        ok = False
                        break
                if not ok:
                    break
                if ins.is_dma:
                    sem[("d", ins.semkey)] = sem.get(("d", ins.semkey), 0) + ins.inc
                elif ins.signal:
                    sem[e] = sem.get(e, 0) + 1
                pc[e] += 1
                progress = True
    stuck = {e: (pc[e], len(P.streams[e])) for e in P.ENGS if pc[e] < len(P.streams[e])}
    return stuck


EPS = 1e-6
D = 2048
DFF = 5504
NJ = 43
NC16 = 16


class Ctx:
    def __init__(self, P, ntiles):
        self.P = P
        self.tiles = []
        off = 0
        for n in ntiles:
            self.tiles.append((off, n))
            off += n
        self.NT = off
        NT = off
        self.hy = P.sbuf("hy", [128, 16, NT], BF16)
        self.A = P.sbuf("A", [128, NJ, NT], BF16)
        self.d_h = [Dep(f"h{i}") for i in range(len(ntiles))]
        self.d_A = [Dep(f"A{i}") for i in range(len(ntiles))]
        aflat = self.A[:].rearrange("p j t -> p (j t)")
        self.xt = []
        self.d_xt = []
        nx = min(3, (NJ * NT) // 16384)
        for i in range(nx):
            v = aflat[:, i * 16384:(i + 1) * 16384].bitcast(F32).rearrange("p (c t) -> p c t", c=16)
            self.xt.append(v)
            self.d_xt.append(Dep(f"xt{i}"))
        self.wgu = [P.sbuf(f"wgu{i}", [128, 2, 16, 256], BF16) for i in range(2)]
        self.d_wgu = [Dep(f"wgu{i}") for i in range(2)]
        self.wd = [P.sbuf(f"wd{i}", [128, NJ, 128], BF16) for i in range(2)]
        self.d_wd = [Dep(f"wd{i}") for i in range(2)]
        self.ps, self.d_ps = P.shared_psum()
        self.ones = P.sbuf("ones", [128, 128], BF16)
        self.d_ones = Dep("ones")
        P.dve(lambda e: e.memset(self.ones[:], 1.0), writes=[self.d_ones])
        self.sq = [P.sbuf(f"sq{i}", [128, 512], BF16) for i in range(4)]
        self.d_sq = [Dep(f"sq{i}") for i in range(4)]
        self.tmp = [P.sbuf(f"tmp{i}", [128, 512], F32) for i in range(4)]
        self.d_tmp = [Dep(f"tmp{i}") for i in range(4)]
        self.rstd = P.sbuf("rstd", [128, NT], F32)
        self.d_rstd = [Dep(f"rstd{i}") for i in range(len(ntiles))]
        self.mT = P.sbuf("mT", [128, 144, 2], F32)
        self.d_mT = Dep("mT")
        self.gT = P.sbuf("gT", [128, 6, 16], F32)
        self.d_gT = Dep("gT")
        self.coef = P.sbuf("coef", [128, 4, 16], F32)
        self.d_coef = Dep("coef")
        self.sqi = 0
        self.tmpi = 0
        self.psi = 0

    def next_sq(self):
        i = self.sqi % 4
        self.sqi += 1
        return self.sq[i], self.d_sq[i]

    def next_tmp(self):
        i = self.tmpi % 4
        self.tmpi += 1
        return self.tmp[i], self.d_tmp[i]


def emit_modulation(P, C, ada_w, ada_bT, cvecT, scr):
    cv = scr["cv"]
    sc = scr["sc"]
    bT = scr["bT"]
    d_cv, d_sc, d_bT = Dep("cv"), Dep("sc"), Dep("bT")
    P.dma(cv[:], cvecT, writes=[d_cv], semkey="small")
    P.dma(bT[:], ada_bT, writes=[d_bT], semkey="small2")
    P.act(lambda e: e.activation(out=sc[:], in_=cv[:], func=AF.Silu), reads=[d_cv], writes=[d_sc])
    mp = C.ps[7]
    d_mp = C.d_ps[7]
    mpv = mp[:, 0:288].rearrange("p (j r) -> p j r", r=2)
    for jb in range(36):
        s = jb % 2
        slot = C.wgu[s][:].rearrange("p a c n -> p c (a n)") if False else None
        sl = C.wgu[s][:].rearrange("p a c n -> p (a c n)").rearrange("p (c n) -> p c n", c=16)
        src = ada_w[:, jb * 512:(jb + 1) * 512].rearrange("(c p) n -> p c n", p=128)
        P.dma(sl, src, writes=[C.d_wgu[s]], semkey=f"wgu{s}", eng="pool")
        for j4 in range(4):
            j = jb * 4 + j4
            for k in range(16):
                P.pe(lambda e, sl=sl, j4=j4, k=k, j=j: e.matmul(
                    mpv[:, j, :], lhsT=sl[:, k, j4 * 128:(j4 + 1) * 128], rhs=sc[:, k, :],
                    start=(k == 0), stop=(k == 15)),
                    reads=[C.d_wgu[s], d_sc], writes=[d_mp])
    for r in range(2):
        P.dve(lambda e, r=r: e.tensor_tensor(out=C.mT[:, :, r], in0=mpv[:, :, r], in1=bT[:], op=ALU.add),
              reads=[d_mp, d_bT], writes=[C.d_mT])


def emit_coefs(P, C, sl_scale, sl_gate, gi_pre, gi_post, wres, cols, which="both"):
    for col in cols:
        if which in ("both", "pre"):
            P.dve(lambda e, col=col: e.scalar_tensor_tensor(
                out=C.coef[:, col, :], in0=C.mT[:, sl_scale * 16:(sl_scale + 1) * 16, col], scalar=1.0,
                in1=C.gT[:, gi_pre, :], op0=ALU.add, op1=ALU.mult),
                reads=[C.d_mT, C.d_gT], writes=[C.d_coef])
        if sl_gate is not None and which in ("both", "post"):
            P.dve(lambda e, col=col: e.scalar_tensor_tensor(
                out=C.coef[:, 2 + col, :], in0=C.mT[:, sl_gate * 16:(sl_gate + 1) * 16, col], scalar=float(wres),
                in1=C.gT[:, gi_post, :], op0=ALU.mult, op1=ALU.mult),
                reads=[C.d_mT, C.d_gT], writes=[C.d_coef])


def emit_prenorm(P, C, srcs, sl_shift, out_sb, d_out, arena_deps, resident=False, after_tile=None, tiles_only=None):
    nx = len(C.xt)
    for ti, (off, n) in enumerate(C.tiles):
        if tiles_only is not None and ti not in tiles_only:
            continue
        src, d_src, col = srcs[ti]
        xi = ti % nx
        xt = C.xt[xi][:, :, 0:n]
        if not resident:
            P.dma(xt, src.rearrange("(c p) t -> p c t", p=128), reads=[d_src],
                  writes=[C.d_xt[xi]] + arena_deps, semkey=f"xt{xi}")
        ssp = C.ps[6 + (ti % 2)]
        d_ssp = C.d_ps[6 + (ti % 2)]
        for c in range(16):
            sq, d_sq = C.next_sq()
            P.act(lambda e, sq=sq, c=c, xt=xt, n=n: e.activation(out=sq[:, 0:n], in_=xt[:, c, :], func=AF.Square),
                  reads=[C.d_xt[xi]], writes=[d_sq])
            P.pe(lambda e, sq=sq, c=c, ssp=ssp, n=n: e.matmul(ssp[:, 0:n], lhsT=C.ones[:], rhs=sq[:, 0:n],
                                                              start=(c == 0), stop=(c == 15)),
                 reads=[d_sq, C.d_ones], writes=[d_ssp])
        tmp, d_tmp = C.next_tmp()
        P.act(lambda e, tmp=tmp, ssp=ssp, n=n: e.activation(out=tmp[:, 0:n], in_=ssp[:, 0:n], func=AF.Sqrt,
                                                            scale=1.0 / D, bias=EPS),
              reads=[d_ssp], writes=[d_tmp])
        rs = C.rstd[:, off:off + n]
        P.dve(lambda e, tmp=tmp, rs=rs, n=n: e.reciprocal(out=rs, in_=tmp[:, 0:n]),
              reads=[d_tmp], writes=[C.d_rstd[ti]])
        dst = out_sb(ti)
        for c in range(16):
            tmp, d_tmp = C.next_tmp()
            P.dve(lambda e, tmp=tmp, c=c, xt=xt, rs=rs, n=n, col=col: e.scalar_tensor_tensor(
                out=tmp[:, 0:n], in0=xt[:, c, :], scalar=C.coef[:, col, c:c + 1], in1=rs,
                op0=ALU.mult, op1=ALU.mult),
                reads=[C.d_xt[xi], C.d_rstd[ti], C.d_coef], writes=[d_tmp])
            P.act(lambda e, tmp=tmp, c=c, dst=dst, n=n, col=col: e.activation(
                out=dst[:, c, :], in_=tmp[:, 0:n], func=AF.Identity,
                bias=C.mT[:, sl_shift * 16 + c, col:col + 1], scale=1.0),
                reads=[d_tmp, C.d_mT], writes=[d_out[ti]])
        if after_tile is not None:
            after_tile(ti, off, n)


def emit_ffn(P, C, srcs, dsts, wg, wu, wd, sl_shift, resident=False):
    nt = len(C.tiles)
    arena = C.d_A
    emit_prenorm(P, C, srcs, sl_shift, lambda ti: C.hy[:, :, C.tiles[ti][0]:C.tiles[ti][0] + C.tiles[ti][1]],
                 C.d_h, arena, resident)
    emit_gateup(P, C, wg, wu)
    emit_down_residual(P, C, NJ, wd, srcs, dsts)


def emit_gateup(P, C, wg, wu):
    it = 0
    for jj in range(22):
        s = jj % 2
        ncol = 256 if jj < 21 else 128
        for a, w in enumerate((wg, wu)):
            src = w[:, jj * 256:jj * 256 + ncol].rearrange("(c p) n -> p c n", p=128)
            P.dma(C.wgu[s][:, a, :, 0:ncol], src, writes=[C.d_wgu[s]], semkey=f"wgu{s}", eng="pool")
        for jl in range(ncol // 128):
            j = jj * 2 + jl
            for ti, (off, n) in enumerate(C.tiles):
                pg = (it % 4) * 2
                it += 1
                G, U = C.ps[pg], C.ps[pg + 1]
                for a, pt in enumerate((G, U)):
                    for k in range(16):
                        P.pe(lambda e, pt=pt, a=a, k=k, s=s, jl=jl, off=off, n=n: e.matmul(
                            pt[:, 0:n], lhsT=C.wgu[s][:, a, k, jl * 128:(jl + 1) * 128],
                            rhs=C.hy[:, k, off:off + n], start=(k == 0), stop=(k == 15)),
                            reads=[C.d_wgu[s], C.d_h[ti]], writes=[C.d_ps[pg + a]])
                tmp, d_tmp = C.next_tmp()
                P.act(lambda e, tmp=tmp, G=G, n=n: e.activation(out=tmp[:, 0:n], in_=G[:, 0:n], func=AF.Silu),
                      reads=[C.d_ps[pg]], writes=[d_tmp])
                P.dve(lambda e, tmp=tmp, U=U, j=j, off=off, n=n: e.tensor_tensor(
                    out=C.A[:, j, off:off + n], in0=tmp[:, 0:n], in1=U[:, 0:n], op=ALU.mult),
                    reads=[d_tmp, C.d_ps[pg + 1]], writes=[C.d_A[ti]] + C.d_xt)


def emit_down_residual(P, C, nch, wd, srcs, dsts, after_post=None):
    NJ = nch
    pend = []
    it = 0
    for dc in range(16):
        s = dc % 2
        src = wd[:, dc * 128:(dc + 1) * 128].rearrange("(j p) n -> p j n", p=128)
        P.dma(C.wd[s][:, 0:nch, :], src, writes=[C.d_wd[s]], semkey=f"wd{s}", eng="pool")
        for ti, (off, n) in enumerate(C.tiles):
            pi = it % 4
            it += 1
            Y = C.ps[pi]
            for j in range(NJ):
                P.pe(lambda e, Y=Y, j=j, s=s, off=off, n=n: e.matmul(
                    Y[:, 0:n], lhsT=C.wd[s][:, j, :], rhs=C.A[:, j, off:off + n],
                    start=(j == 0), stop=(j == NJ - 1)),
                    reads=[C.d_wd[s], C.d_A[ti]], writes=[C.d_ps[pi]])
            for f in pend:
                f()
            pend = []
            P.act(lambda e, Y=Y, dc=dc, off=off, n=n: e.activation(out=C.hy[:, dc, off:off + n], in_=Y[:, 0:n],
                                                                  func=AF.Copy),
                  reads=[C.d_ps[pi]], writes=[C.d_h[ti]])
            sq, d_sq = C.next_sq()
            P.act(lambda e, Y=Y, sq=sq, n=n: e.activation(out=sq[:, 0:n], in_=Y[:, 0:n], func=AF.Square),
                  reads=[C.d_ps[pi]], writes=[d_sq])
            SS = C.ps[4 + ti]

            def ssmm(sq=sq, d_sq=d_sq, SS=SS, ti=ti, dc=dc, n=n):
                P.pe(lambda e: e.matmul(SS[:, 0:n], lhsT=C.ones[:], rhs=sq[:, 0:n],
                                        start=(dc == 0), stop=(dc == 15)),
                     reads=[d_sq, C.d_ones], writes=[C.d_ps[4 + ti]])
            pend.append(ssmm)
    for f in pend:
        f()
    nx = len(C.xt)
    for ti, (off, n) in enumerate(C.tiles):
        if dsts[ti][0] is None:
            continue
        SS = C.ps[4 + ti]
        tmp, d_tmp = C.next_tmp()
        P.act(lambda e, tmp=tmp, SS=SS, n=n: e.activation(out=tmp[:, 0:n], in_=SS[:, 0:n], func=AF.Sqrt,
                                                          scale=1.0 / D, bias=EPS),
              reads=[C.d_ps[4 + ti]], writes=[d_tmp])
        rs = C.rstd[:, off:off + n]
        P.dve(lambda e, tmp=tmp, rs=rs, n=n: e.reciprocal(out=rs, in_=tmp[:, 0:n]),
              reads=[d_tmp], writes=[C.d_rstd[ti]])
    for ti, (off, n) in enumerate(C.tiles):
        src, d_src, col = srcs[ti]
        dst, d_dst, _ = dsts[ti]
        if dst is None:
            continue
        rs = C.rstd[:, off:off + n]
        xi = ti % nx
        xt = C.xt[xi][:, :, 0:n]
        P.dma(xt, src.rearrange("(c p) t -> p c t", p=128), reads=[d_src],
              writes=[C.d_xt[xi]] + C.d_A, semkey=f"xt{xi}")
        prev = None
        for c in range(16):
            tmp, d_tmp = C.next_tmp()
            P.dve(lambda e, tmp=tmp, c=c, rs=rs, off=off, n=n, col=col: e.scalar_tensor_tensor(
                out=tmp[:, 0:n], in0=C.hy[:, c, off:off + n], scalar=C.coef[:, 2 + col, c:c + 1], in1=rs,
                op0=ALU.mult, op1=ALU.mult),
                reads=[C.d_h[ti], C.d_rstd[ti], C.d_coef], writes=[d_tmp])
            if prev is not None:
                prev()

            def addc(tmp=tmp, d_tmp=d_tmp, c=c, xt=xt, n=n, xi=xi):
                P.dve(lambda e: e.tensor_tensor(out=xt[:, c, :], in0=xt[:, c, :], in1=tmp[:, 0:n], op=ALU.add),
                      reads=[d_tmp, C.d_xt[xi]], writes=[C.d_xt[xi]])
            prev = addc
        prev()
        P.dma(dst.rearrange("(c p) t -> p c t", p=128), xt, reads=[C.d_xt[xi]], writes=[d_dst],
              semkey=f"xt{xi}")
        if after_post is not None:
            after_post(ti)


def emit_modulation_sharded(P, C, ada_sl, cvec3, selb, bT0, bT1, mp_own, mp_g1, mp_g2, mTd, d_mTd):
    cv = P.sbuf("mcv", [128, 16, 3], F32)
    sc = P.sbuf("msc", [128, 16, 3], BF16)
    sb = P.sbuf("mselb", [128, 2], F32)
    bT = P.sbuf("mbT", [128, 2, 144], F32)
    part = P.sbuf("mpart", [128, 216], F32)
    full = P.sbuf("mfull", [128, 4, 216], F32)
    fv = [full[:, g, :].rearrange("p (l j r) -> p l j r", l=2, j=36) for g in range(4)]
    mo = P.sbuf("mo", [128, 2, 144, 2], F32)
    d_cv, d_sc, d_bT, d_part, d_full, d_mo = [Dep(n) for n in ("mcv", "msc", "mbT", "mpart", "mfull", "mo")]
    P.dma(cv[:], cvec3, writes=[d_cv], semkey="small")
    P.dma(sb[:], selb, writes=[d_bT], semkey="small2")
    P.dma(bT[:, 0, :], bT0, writes=[d_bT], semkey="small2")
    P.dma(bT[:, 1, :], bT1, writes=[d_bT], semkey="small2")
    P.act(lambda e: e.activation(out=sc[:], in_=cv[:], func=AF.Silu), reads=[d_cv], writes=[d_sc])
    mp = C.ps[7]
    d_mp = C.d_ps[7]
    mpv = mp[:, 0:216].rearrange("p (l j r) -> p l j r", l=2, j=36)
    it = 0
    for l in range(2):
        for jb in range(9):
            s = it % 2
            it += 1
            sl = C.wgu[s][:].rearrange("p a c n -> p (a c n)").rearrange("p (c n) -> p c n", c=16)
            src = ada_sl[l, :, jb * 512:(jb + 1) * 512].rearrange("(c p) n -> p c n", p=128)
            P.dma(sl, src, writes=[C.d_wgu[s]], semkey=f"wgu{s}", eng="pool")
            for j4 in range(4):
                j = jb * 4 + j4
                for k in range(16):
                    P.pe(lambda e, sl=sl, j4=j4, k=k, j=j, l=l: e.matmul(
                        mpv[:, l, j, :], lhsT=sl[:, k, j4 * 128:(j4 + 1) * 128], rhs=sc[:, k, :],
                        start=(k == 0), stop=(k == 15)),
                        reads=[C.d_wgu[s], d_sc], writes=[d_mp])
    P.dve(lambda e: e.tensor_copy(out=part[:], in_=mp[:, 0:216]), reads=[d_mp], writes=[d_part])
    d_mi, d_m1, d_m2 = Dep("mp_own"), Dep("mp_g1"), Dep("mp_g2")
    P.dma(mp_own[:, :], part[:], reads=[d_part], writes=[d_mi], semkey="mred")
    P.allgather_pairs(mp_g1, mp_own, reads=[d_mi], writes=[d_m1], semkey="cc0")
    P.allgather_far(mp_g2, mp_g1, reads=[d_m1], writes=[d_m2], semkey="ccf")
    for g in range(4):
        P.dma(full[:, g, :], mp_g2[g * 128:(g + 1) * 128, :],
              reads=[d_m2], writes=[d_full], semkey="mred")
    for l in range(2):
        for g in range(4):
            dst = mo[:, l, g * 36:(g + 1) * 36, :]
            P.dve(lambda e, l=l, g=g, dst=dst: e.tensor_scalar(out=dst[:, :, 0], in0=fv[g][:, l, :, 0],
                                                               scalar1=sb[:, 0:1], scalar2=None, op0=ALU.mult),
                  reads=[d_full, d_bT], writes=[d_mo])
            P.dve(lambda e, l=l, g=g, dst=dst: e.scalar_tensor_tensor(out=dst[:, :, 0], in0=fv[g][:, l, :, 1],
                                                                      scalar=sb[:, 1:2], in1=dst[:, :, 0],
                                                                      op0=ALU.mult, op1=ALU.add),
                  reads=[d_full, d_bT, d_mo], writes=[d_mo])
            P.dve(lambda e, l=l, g=g, dst=dst: e.tensor_copy(out=dst[:, :, 1], in_=fv[g][:, l, :, 2]),
                  reads=[d_full], writes=[d_mo])
        for col in range(2):
            P.dve(lambda e, l=l, col=col: e.tensor_tensor(out=mo[:, l, :, col], in0=mo[:, l, :, col], in1=bT[:, l, :],
                                                          op=ALU.add), reads=[d_mo, d_bT], writes=[d_mo])
        P.dma(mTd[l][:], mo[:, l, :, :], reads=[d_mo], writes=[d_mTd[l]], semkey="mTo")


EPS = 1e-6
NTOK = 2304
NCTX = 256
NLAT = 2048
LAM_INIT0 = 0.8 - 0.6 * math.exp(-0.3 * 0)


DBG = {}


class MixCtx:
    def __init__(self, P):
        self.P = P
        self.hs = P.sbuf("hs", [128, 16, NTOK], BF16)
        self.d_hs = Dep("hs")
        self.ropeC = P.sbuf("ropeC", [128, NLAT], F32)
        self.ropeS = P.sbuf("ropeS", [128, NLAT], F32)
        self.d_rope = Dep("rope")
        self.wt = [P.sbuf(f"wt{i}", [128, 16, 128], BF16) for i in range(8)]
        self.d_wt = [Dep(f"wt{i}") for i in range(8)]
        self.ps, self.d_ps = P.shared_psum()
        self.d_ph = [[Dep(f"ph{i}_{h}") for h in range(2)] for i in range(8)]
        self.ones = P.sbuf("ones", [128, 128], BF16)
        self.onesf = P.sbuf("onesf", [128, 128], F32)
        self.d_ones = Dep("ones")
        P.dve(lambda e: e.memset(self.ones[:], 1.0), writes=[self.d_ones])
        P.dve(lambda e: e.memset(self.onesf[:], 1.0), writes=[self.d_ones])
        self.perm = P.sbuf("perm", [128, 128], BF16)
        self.ident = P.sbuf("ident", [128, 128], BF16)
        self.maskf = P.sbuf("maskf", [128, 128], F32)
        self.maskb = P.sbuf("maskb", [128, 128], F32)
        self.d_const = Dep("const")
        self.vec = P.sbuf("vec", [128, 32], F32)
        self.d_vec = Dep("vec")
        self.lbr = P.sbuf("lbr", [128, 2, 2, 4], F32)
        self.lb = P.sbuf("lb", [128, 2, 4], F32)
        self.oml = P.sbuf("oml", [128, 2, 4], F32)
        self.d_lb = Dep("lb")
        u0 = P.aoff
        self.qT = P.sbuf("qT", [128, NLAT], BF16)
        self.qT1 = P.sbuf("qT1", [128, NLAT], BF16)
        self.kT = P.sbuf("kT", [128, NTOK], BF16)
        self.V = P.sbuf("V", [128, 18, 128], BF16)
        self.d_qT, self.d_kT, self.d_V = Dep("qT"), Dep("kT"), Dep("V")
        self.E = [P.sbuf(f"E{i}", [128, 512], BF16) for i in range(4)]
        self.d_E = [Dep(f"E{i}") for i in range(4)]
        u1 = P.aoff
        if P.arena is not None:
            P.aoff = u0
        self.zf_sb = P.sbuf("zf_sb", [128, NTOK], F32)
        self.zb_sb = P.sbuf("zb_sb", [128, NTOK], F32)
        self.q_sb = P.sbuf("q_sb", [128, NTOK], BF16)
        self.i_sb = P.sbuf("i_sb", [128, 18, 128], BF16)
        self.d_rp = Dep("recproj")
        self.d_zf = Dep("zf_sb")
        if P.arena is not None:
            P.aoff = max(u1, P.aoff)
        self.qtF = P.sbuf("qtF", [128, NTOK], BF16)
        self.ktF = P.sbuf("ktF", [128, NTOK], BF16)
        self.zfb = self.zf_sb.bitcast(BF16)
        self.d_zfu = [Dep(f"zfu{u}") for u in range(9)]
        self.d_qk = [Dep("qkF"), Dep("qkB")]
        self.sg_sb = P.sbuf("sg_sb", [128, NLAT], BF16)
        self.svall = P.sbuf("svall", [128, 2, 18, 4], F32)
        self.d_sva = Dep("svall")
        self.d_svad = [Dep("svallF"), Dep("svallB")]
        self.d_tfh = [[Dep(f"tfh{i}_{h}") for h in range(2)] for i in range(6)]
        self.maskR = P.sbuf("maskR", [128, 512], F32)
        P.dve(lambda e: e.memset(self.maskR[:], 1.0), writes=[self.d_ones])
        for cc in range(4):
            P.dve(lambda e, cc=cc: e.memset(self.maskR[:, cc * 128:cc * 128 + 1], 0.0), writes=[self.d_ones])
        self.tf = [P.sbuf(f"tf{i}", [128, 512], F32) for i in range(6)]
        self.d_tf = [Dep(f"tf{i}") for i in range(6)]
        self.tb = [P.sbuf(f"tb{i}", [128, 512], BF16) for i in range(4)]
        self.d_tb = [Dep(f"tb{i}") for i in range(4)]
        self.tfi = 0
        self.tbi = 0
        self.Ei = 0
        self.S = P.sbuf("S", [128, 128], F32)
        self.d_S = Dep("S")
        self.S1 = P.sbuf("S1", [128, 128], F32)
        self.d_S1 = Dep("S1")
        self.ofw = P.sbuf("ofw", [128, NLAT], F32)
        self.d_ofw = Dep("ofw")
        self.obw = P.sbuf("obw", [128, NLAT], F32)
        self.d_obw = Dep("obw")
        self.rf = [P.sbuf(f"rf{i}", [128, 128], F32) for i in range(6)]
        self.d_rf = [Dep(f"rf{i}") for i in range(6)]
        self.rb = [P.sbuf(f"rb{i}", [128, 128], BF16) for i in range(12)]
        self.d_rb = [Dep(f"rb{i}") for i in range(12)]
        self.rfi = 0
        self.rbi = 0
        self.sv = [P.sbuf(f"sv{i}", [128, 8], F32) for i in range(8)]
        self.d_sv = [Dep(f"sv{i}") for i in range(8)]
        self.svi = 0

    def ntf(self):
        i = self.tfi % 6
        self.tfi += 1
        return self.tf[i], self.d_tf[i]

    def ntb(self):
        i = self.tbi % 4
        self.tbi += 1
        return self.tb[i], self.d_tb[i]

    def nE(self):
        i = self.Ei % 4
        self.Ei += 1
        return self.E[i], self.d_E[i]

    def nrf(self):
        i = self.rfi % 6
        self.rfi += 1
        return self.rf[i], self.d_rf[i]

    def nrb(self):
        i = self.rbi % 12
        self.rbi += 1
        return self.rb[i], self.d_rb[i]

    def nsv(self):
        i = self.svi % 8
        self.svi += 1
        return self.sv[i], self.d_sv[i]


def load_w(P, M, slot, src):
    P.dma(M.wt[slot][:], src.rearrange("(c p) n -> p c n", p=128), writes=[M.d_wt[slot]],
          semkey=f"wt{slot}", eng="pool")


def emit_mix_setup(P, M, hTf, ropeC, ropeS, perm, ident, maskf, maskb, lamT, dng, rng, lbraw):
    for q in range(4 if hTf is not None else 0):
        P.dma(M.hs[:, q * 4:(q + 1) * 4, :], hTf[q * 512:(q + 1) * 512, :].rearrange("(c p) t -> p c t", p=128),
              writes=[M.d_hs], semkey="hs")
    P.dma(M.ropeC[:], ropeC, writes=[M.d_rope], semkey="rope")
    P.dma(M.ropeS[:], ropeS, writes=[M.d_rope], semkey="rope")
    P.dma(M.perm[:], perm, writes=[M.d_const], semkey="cst", eng="pool")
    P.dma(M.ident[:], ident, writes=[M.d_const], semkey="cst", eng="pool")
    P.dma(M.maskf[:], maskf, writes=[M.d_const], semkey="cst2")
    P.dma(M.maskb[:], maskb, writes=[M.d_const], semkey="cst2")
    P.dma(M.vec[0:64, 0:4], lamT, writes=[M.d_vec], semkey="vec")
    P.dma(M.vec[:, 4:5], dng, writes=[M.d_vec], semkey="vec")
    P.dma(M.vec[:, 5:6], rng, writes=[M.d_vec], semkey="vec")
    P.dma(M.lbr[:], lbraw, writes=[M.d_lb], semkey="lb")
    P.dve(lambda e: e.tensor_tensor(out=M.vec[0:64, 6:7], in0=M.vec[0:64, 0:1], in1=M.vec[0:64, 1:2], op=ALU.mult),
          reads=[M.d_vec], writes=[M.d_vec])
    P.dve(lambda e: e.tensor_tensor(out=M.vec[0:64, 7:8], in0=M.vec[0:64, 2:3], in1=M.vec[0:64, 3:4], op=ALU.mult),
          reads=[M.d_vec], writes=[M.d_vec])
    lp = M.ps[7]
    P.pe(lambda e: e.matmul(lp[:, 0:2], lhsT=M.onesf[0:64, :], rhs=M.vec[0:64, 6:8], start=True, stop=True),
         reads=[M.d_vec, M.d_ones], writes=[M.d_ps[7]])
    P.act(lambda e: e.activation(out=M.vec[:, 8:10], in_=lp[:, 0:2], func=AF.Exp), reads=[M.d_ps[7]],
          writes=[M.d_vec])
    P.dve(lambda e: e.tensor_tensor(out=M.vec[:, 10:11], in0=M.vec[:, 9:10], in1=M.vec[:, 8:9], op=ALU.subtract),
          reads=[M.d_vec], writes=[M.d_vec])
    P.dve(lambda e: e.tensor_scalar(out=M.vec[:, 10:11], in0=M.vec[:, 10:11], scalar1=-LAM_INIT0, scalar2=None,
                                    op0=ALU.add), reads=[M.d_vec], writes=[M.d_vec])
    P.dve(lambda e: e.tensor_scalar(out=M.vec[:, 11:12], in0=M.vec[:, 4:5], scalar1=1.0 - LAM_INIT0, scalar2=None,
                                    op0=ALU.mult), reads=[M.d_vec], writes=[M.d_vec])
    P.dve(lambda e: e.tensor_tensor(out=M.lb[:], in0=M.lbr[:, :, 1, :], in1=M.lbr[:, :, 0, :], op=ALU.subtract),
          reads=[M.d_lb], writes=[M.d_lb])
    P.act(lambda e: e.activation(out=M.lb[:], in_=M.lb[:], func=AF.Exp), reads=[M.d_lb], writes=[M.d_lb])
    P.dve(lambda e: e.tensor_scalar(out=M.lb[:], in0=M.lb[:], scalar1=1.0, scalar2=None, op0=ALU.add),
          reads=[M.d_lb], writes=[M.d_lb])
    P.dve(lambda e: e.reciprocal(out=M.lb[:], in_=M.lb[:]), reads=[M.d_lb], writes=[M.d_lb])
    P.dve(lambda e: e.tensor_scalar(out=M.oml[:], in0=M.lb[:], scalar1=-1.0, scalar2=1.0, op0=ALU.mult, op1=ALU.add),
          reads=[M.d_lb], writes=[M.d_lb])


def proj_fm(P, M, out_ps, d_out, slot, off, n):
    for k in range(16):
        P.pe(lambda e, k=k: e.matmul(out_ps, lhsT=M.wt[slot][:, k, :], rhs=M.hs[:, k, off:off + n],
                                     start=(k == 0), stop=(k == 15)),
             reads=[M.d_wt[slot], M.d_hs], writes=[d_out])


def proj_tm(P, M, out_ps, d_out, slot, off):
    for k in range(16):
        P.pe(lambda e, k=k: e.matmul(out_ps, lhsT=M.hs[:, k, off:off + 128], rhs=M.wt[slot][:, k, :],
                                     start=(k == 0), stop=(k == 15)),
             reads=[M.d_wt[slot], M.d_hs], writes=[d_out])


def emit_rsqrt(P, M, out_sb, d_o, in_ps, d_in, n, inv_dim):
    P.act(lambda e: e.activation(out=out_sb, in_=in_ps, func=AF.Ln, scale=inv_dim, bias=EPS),
          reads=[d_in], writes=[d_o])
    P.act(lambda e: e.activation(out=out_sb, in_=out_sb, func=AF.Exp, scale=-0.5), reads=[d_o], writes=[d_o])


def emit_attention_head(P, M, hd, mergedT, d_merged):
    sq_, sk_, sv_ = 0, 1, 2
    P.dve(lambda e: e.memset(M.qT[64:128, :], 0.0), writes=[M.d_qT])
    P.dve(lambda e: e.memset(M.qT1[0:64, :], 0.0), writes=[M.d_qT])
    tiles = [(0, NCTX)] + [(NCTX + i * 512, 512) for i in range(4)]
    bi = 0
    import os
    NSUB = int(os.environ.get("ATT_SUB", "99"))
    for (off, n) in tiles:
        for which in ("k", "q"):
            if bi >= NSUB:
                continue
            if which == "q" and off < NCTX:
                continue
            slot = sk_ if which == "k" else sq_
            dstT = M.kT if which == "k" else M.qT
            d_dst = M.d_kT if which == "k" else M.d_qT
            doff = off if which == "k" else off - NCTX
            pb = bi % 2
            bi += 1
            pp, d_pp = M.ps[pb], M.d_ps[pb]
            proj_fm(P, M, pp[:, 0:n], d_pp, slot, off, n)
            if off < NCTX:
                P.act(lambda e, pp=pp, n=n, dstT=dstT, doff=doff: e.activation(
                    out=dstT[:, doff:doff + n], in_=pp[:, 0:n], func=AF.Copy), reads=[d_pp], writes=[d_dst])
                continue
            loff = off - NCTX
            sb, d_sb = M.ntb()
            P.act(lambda e, pp=pp, sb=sb, n=n: e.activation(out=sb[:, 0:n], in_=pp[:, 0:n], func=AF.Copy),
                  reads=[d_pp], writes=[d_sb])
            rp, d_rp = M.ps[2 + pb], M.d_ps[2 + pb]
            P.pe(lambda e, rp=rp, sb=sb, n=n: e.matmul(rp[:, 0:n], lhsT=M.perm[:], rhs=sb[:, 0:n], start=True,
                                                       stop=True), reads=[d_sb, M.d_const], writes=[d_rp])
            t1, d_t1 = M.ntf()
            P.dve(lambda e, t1=t1, pp=pp, n=n, loff=loff: e.tensor_tensor(
                out=t1[:, 0:n], in0=pp[:, 0:n], in1=M.ropeC[:, loff:loff + n], op=ALU.mult),
                reads=[d_pp, M.d_rope], writes=[d_t1])
            t2, d_t2 = M.ntf()
            P.dve(lambda e, t2=t2, rp=rp, n=n, loff=loff: e.tensor_tensor(
                out=t2[:, 0:n], in0=rp[:, 0:n], in1=M.ropeS[:, loff:loff + n], op=ALU.mult),
                reads=[d_rp, M.d_rope], writes=[d_t2])
            if which == "k":
                P.dve(lambda e, t1=t1, t2=t2, dstT=dstT, doff=doff, n=n: e.tensor_tensor(
                    out=dstT[:, doff:doff + n], in0=t1[:, 0:n], in1=t2[:, 0:n], op=ALU.add),
                    reads=[d_t1, d_t2], writes=[d_dst])
            else:
                P.dve(lambda e, t1=t1, t2=t2, doff=doff, n=n: e.tensor_tensor(
                    out=M.qT[0:64, doff:doff + n], in0=t1[0:64, 0:n], in1=t2[0:64, 0:n], op=ALU.add),
                    reads=[d_t1, d_t2], writes=[d_dst])
                P.dve(lambda e, t1=t1, t2=t2, doff=doff, n=n: e.tensor_tensor(
                    out=M.qT1[64:128, doff:doff + n], in0=t1[64:128, 0:n], in1=t2[64:128, 0:n], op=ALU.add),
                    reads=[d_t1, d_t2], writes=[d_dst])
    import os
    if DBG.get("qT") is not None and hd == 0:
        P.dma(DBG["qT"][:, :], M.qT[:], reads=[M.d_qT], writes=[Dep("dbgq")], semkey="dbg")
        P.dma(DBG["kT"][:, :], M.kT[:], reads=[M.d_kT], writes=[Dep("dbgk")], semkey="dbg")
    STG = int(os.environ.get("ATT_STAGE", "9"))
    if STG < 2:
        return
    for g4 in range(5):
        pb = 4 + (g4 % 2)
        vp, d_vp = M.ps[pb], M.d_ps[pb]
        nk = 4 if g4 < 4 else 2
        for i in range(nk):
            kt = g4 * 4 + i
            proj_tm(P, M, vp[:, i * 128:(i + 1) * 128], d_vp, sv_, kt * 128)
        P.act(lambda e, vp=vp, g4=g4, nk=nk: e.activation(
            out=M.V[:, g4 * 4:g4 * 4 + nk, :].rearrange("p a n -> p (a n)"), in_=vp[:, 0:nk * 128], func=AF.Copy),
            reads=[d_vp], writes=[M.d_V])
    if STG < 3:
        return
    for qt in [int(c) for c in os.environ.get("ATT_QT", "0123")]:
        qo = qt * 512
        steps = [(kt, m) for kt in range(18) for m in range(2)]
        pend = []
        for si, (kt, m) in enumerate(steps):
            sb_i = si % 4
            ST, d_ST = M.ps[sb_i], M.d_ps[sb_i]
            P.pe(lambda e, ST=ST, kt=kt, m=m: e.matmul(
                ST[:, :], lhsT=M.kT[:, kt * 128:(kt + 1) * 128],
                rhs=(M.qT if m == 0 else M.qT1)[:, qo:qo + 512], start=True, stop=True),
                reads=[M.d_kT, M.d_qT], writes=[d_ST])
            E, d_E = M.nE()
            P.act(lambda e, E=E, ST=ST: e.activation(out=E[:], in_=ST[:, :], func=AF.Exp, scale=0.125),
                  reads=[d_ST], writes=[d_E])
            if len(pend) >= 2:
                pend.pop(0)()

            def pv(E=E, d_E=d_E, kt=kt, m=m):
                P.pe(lambda e: e.matmul(M.ps[4 + m][:, :], lhsT=M.V[:, kt, :], rhs=E[:], start=(kt == 0),
                                        stop=(kt == 17)), reads=[M.d_V, d_E], writes=[M.d_ps[4 + m]])
                P.pe(lambda e: e.matmul(M.ps[6 + m][:, :], lhsT=M.ones[:], rhs=E[:], start=(kt == 0),
                                        stop=(kt == 17)), reads=[M.d_ones, d_E], writes=[M.d_ps[6 + m]])
            pend.append(pv)
        for f in pend:
            f()
        r0, d_r0 = M.ntf()
        r1, d_r1 = M.ntf()
        P.dve(lambda e, r0=r0: e.reciprocal(out=r0[:], in_=M.ps[6][:, :]), reads=[M.d_ps[6]], writes=[d_r0])
        P.dve(lambda e, r1=r1: e.reciprocal(out=r1[:], in_=M.ps[7][:, :]), reads=[M.d_ps[7]], writes=[d_r1])
        oa, d_oa = M.ntf()
        ob, d_ob = M.ntf()
        P.dve(lambda e, oa=oa, r0=r0: e.tensor_tensor(out=oa[:], in0=M.ps[4][:, :], in1=r0[:], op=ALU.mult),
              reads=[M.d_ps[4], d_r0], writes=[d_oa])
        P.dve(lambda e, ob=ob, r1=r1: e.tensor_tensor(out=ob[:], in0=M.ps[5][:, :], in1=r1[:], op=ALU.mult),
              reads=[M.d_ps[5], d_r1], writes=[d_ob])
        o, d_o = M.ntf()
        P.dve(lambda e, o=o, oa=oa, ob=ob: e.scalar_tensor_tensor(
            out=o[:], in0=ob[:], scalar=M.vec[:, 10:11], in1=oa[:], op0=ALU.mult, op1=ALU.add),
            reads=[d_oa, d_ob, M.d_vec], writes=[d_o])
        sq, d_sq = M.ntb()
        P.act(lambda e, sq=sq, o=o: e.activation(out=sq[:], in_=o[:], func=AF.Square), reads=[d_o], writes=[d_sq])
        P.pe(lambda e, sq=sq: e.matmul(M.ps[0][:, :], lhsT=M.ones[:], rhs=sq[:], start=True, stop=True),
             reads=[d_sq, M.d_ones], writes=[M.d_ps[0]])
        ri, d_ri = M.ntf()
        emit_rsqrt(P, M, ri[:], d_ri, M.ps[0][:, :], M.d_ps[0], 512, 1.0 / 128)
        ob16, d_ob16 = M.ntb()
        P.dve(lambda e, ob16=ob16, o=o, ri=ri: e.scalar_tensor_tensor(
            out=ob16[:], in0=o[:], scalar=M.vec[:, 11:12], in1=ri[:], op0=ALU.mult, op1=ALU.mult),
            reads=[d_o, d_ri, M.d_vec], writes=[d_ob16])
        rows = mergedT("att", hd) if callable(mergedT) else mergedT[hd * 128:(hd + 1) * 128, :]
        P.dma(rows[:, qo:qo + 512], ob16[:], reads=[d_ob16], writes=[d_merged], semkey="mg")


def emit_rec_head(P, M, r, mergedT, d_merged, after_burst=None):
    s_q, s_zf, s_zb, s_i, s_g = 3, 4, 5, 6, 7
    orders = [list(range(18)), [1, 0] + list(range(17, 1, -1))]
    Ss = [M.S, M.S1]
    dSs = [M.d_S, M.d_S1]
    obuf = [M.ofw, M.obw]
    d_obuf = [M.d_ofw, M.d_obw]
    for dr in range(2):
        P.dve(lambda e, dr=dr: e.memset(Ss[dr][:], 0.0), writes=[dSs[dr]])
    ptiles = [(0, NCTX)] + [(NCTX + i * 512, 512) for i in range(4)]
    bi = 0
    for (off, n) in ptiles:
        for (slot, dst, sc_) in ((s_zf, M.zf_sb, 1.0), (s_zb, M.zb_sb, 1.0), (s_q, M.q_sb, float(128 ** -0.5))):
            pb = bi % 4
            bi += 1
            proj_fm(P, M, M.ps[pb][:, 0:n], M.d_ps[pb], slot, off, n)
            if slot != s_q:
                P.act(lambda e, pb=pb, dst=dst, off=off, n=n: e.activation(
                    out=dst[:, off:off + n], in_=M.ps[pb][:, 0:n], func=AF.Sigmoid, scale=-1.0),
                    reads=[M.d_ps[pb]], writes=[M.d_rp])
            else:
                P.dve(lambda e, pb=pb, dst=dst, off=off, n=n, sc_=sc_: e.tensor_scalar(
                    out=dst[:, off:off + n], in0=M.ps[pb][:, 0:n], scalar1=sc_, scalar2=None, op0=ALU.mult),
                    reads=[M.d_ps[pb]], writes=[M.d_rp])
        if off >= NCTX:
            pb = bi % 4
            bi += 1
            proj_fm(P, M, M.ps[pb][:, 0:n], M.d_ps[pb], s_g, off, n)
            P.act(lambda e, pb=pb, off=off, n=n: e.activation(
                out=M.sg_sb[:, off - NCTX:off - NCTX + n], in_=M.ps[pb][:, 0:n], func=AF.Silu),
                reads=[M.d_ps[pb]], writes=[M.d_rp])
    for g4 in range(5):
        pb = 4 + (g4 % 2)
        nk = 4 if g4 < 4 else 2
        for i in range(nk):
            proj_tm(P, M, M.ps[pb][:, i * 128:(i + 1) * 128], M.d_ps[pb], s_i, (g4 * 4 + i) * 128)
        P.act(lambda e, pb=pb, g4=g4, nk=nk: e.activation(
            out=M.i_sb[:, g4 * 4:g4 * 4 + nk, :].rearrange("p a n -> p (a n)"), in_=M.ps[pb][:, 0:nk * 128],
            func=AF.Copy), reads=[M.d_ps[pb]], writes=[M.d_rp])

    if after_burst is not None:
        after_burst()
    RSTG = int(os.environ.get("REC_STAGE", "9"))
    if RSTG < 2:
        return
    def prep_gen(dr):
        zsb = M.zf_sb if dr == 0 else M.zb_sb
        T = [t[:, dr * 256:(dr + 1) * 256] for t in M.tf]
        dT = [M.d_tfh[i][dr] for i in range(6)]
        n = 256
        for off in range(0, NTOK, 256):
            c0 = off // 128
            u = off // 256
            if dr == 0:
                qdst, kdst = M.qtF[:, off:off + n], M.ktF[:, off:off + n]
                wdeps = [M.d_qk[0]]
                rdeps = [M.d_rp, M.d_zfu[u]]
            else:
                qdst, kdst = M.zfb[:, u * 512:u * 512 + 256], M.zfb[:, u * 512 + 256:u * 512 + 512]
                wdeps = [M.d_qk[1], M.d_zfu[u]]
                rdeps = [M.d_rp]
            P.dve(lambda e, off=off: e.tensor_scalar(out=T[2], in0=zsb[:, off:off + n], scalar1=M.oml[:, dr, r:r + 1],
                                                     scalar2=None, op0=ALU.mult), reads=rdeps + [M.d_lb],
                  writes=[dT[2]])
            yield
            P.act(lambda e: e.activation(out=T[3], in_=T[2], func=AF.Ln, scale=-1.0, bias=1.0), reads=[dT[2]],
                  writes=[dT[3]])
            yield
            P.dve(lambda e: e.tensor_tensor_scan(out=T[4], data0=M.maskR[:, 0:n], data1=T[3], initial=0.0,
                                                 op0=ALU.mult, op1=ALU.add), reads=[dT[3], M.d_ones], writes=[dT[4]])
            yield
            pfv = T[4].rearrange("p (c t) -> p c t", t=128)
            if dr == 1:
                P.dve(lambda e: e.tensor_tensor(out=T[0], in0=T[4], in1=T[3], op=ALU.subtract),
                      reads=[dT[4], dT[3]], writes=[dT[0]])
                yield
            for ci in range(2):
                src = T[0] if dr == 1 else T[4]
                P.dve(lambda e, ci=ci, src=src: e.tensor_scalar(
                    out=T[5][:, ci * 128:(ci + 1) * 128], in0=src[:, ci * 128:(ci + 1) * 128],
                    scalar1=T[4][:, ci * 128 + 63:ci * 128 + 64], scalar2=(1.0 if dr == 0 else -1.0),
                    op0=ALU.subtract, op1=ALU.mult), reads=[dT[0], dT[4]], writes=[dT[5]])
                yield
            P.act(lambda e: e.activation(out=T[1], in_=T[5], func=AF.Exp), reads=[dT[5]], writes=[dT[1]])
            yield
            P.act(lambda e: e.activation(out=T[3], in_=T[5], func=AF.Exp, scale=-1.0), reads=[dT[5]], writes=[dT[3]])
            yield
            P.dve(lambda e, off=off, qdst=qdst: e.tensor_tensor(out=qdst, in0=M.q_sb[:, off:off + n], in1=T[1],
                                                                op=ALU.mult), reads=[M.d_rp, dT[1]], writes=wdeps)
            yield
            P.dve(lambda e, kdst=kdst: e.tensor_tensor(out=kdst, in0=T[2], in1=T[3], op=ALU.mult),
                  reads=[dT[2], dT[3]], writes=wdeps)
            yield
            svv = M.svall[:, dr, c0:c0 + 2, :]
            d_sva = M.d_svad[dr]
            P.dve(lambda e, svv=svv, pfv=pfv: e.tensor_tensor(out=svv[:, :, 0], in0=pfv[:, :, 127], in1=pfv[:, :, 63],
                                                              op=ALU.subtract), reads=[dT[4]], writes=[d_sva])
            yield
            P.act(lambda e, svv=svv, pfv=pfv: e.activation(out=svv[:, :, 1], in_=pfv[:, :, 63], func=AF.Exp),
                  reads=[dT[4]], writes=[d_sva])
            yield
            P.act(lambda e, svv=svv: e.activation(out=svv[:, :, 2], in_=svv[:, :, 0], func=AF.Exp),
                  reads=[d_sva], writes=[d_sva])
            yield
            P.act(lambda e, svv=svv, pfv=pfv: e.activation(out=svv[:, :, 3], in_=pfv[:, :, 127], func=AF.Exp),
                  reads=[dT[4]], writes=[d_sva])
            yield

    import itertools
    P.fence(M.d_tf, [d for pair in M.d_tfh for d in pair])
    for _ in itertools.zip_longest(prep_gen(0), prep_gen(1)):
        pass
    P.fence([d for pair in M.d_tfh for d in pair], M.d_tf)

    if RSTG < 3:
        return

    def chunk_gen(dr, step):
        if True:
            c = orders[dr][step]
            mask = M.maskf if dr == 0 else M.maskb
            S_, d_S = Ss[dr], dSs[dr]
            a = c * 128
            lat = c >= 2
            la = a - NCTX
            bx = dr * 4 + (step % 2) * 2
            by = bx + 1
            if dr == 0:
                qt_, kt_ = M.qtF[:, a:a + 128], M.ktF[:, a:a + 128]
            else:
                ub = (c // 2) * 512 + (c % 2) * 128
                qt_, kt_ = M.zfb[:, ub:ub + 128], M.zfb[:, ub + 256:ub + 384]
            d_qt = d_kt = M.d_qk[dr]
            vt, d_vt = M.i_sb[:, c, :], M.d_rp
            sv, d_sv = M.svall[:, dr, c, :], M.d_svad[dr]
            c_e1 = 1 if dr == 0 else 2
            c_e2 = 2 if dr == 0 else 1
            ktp = M.ps[bx][:, 384:448].bitcast(BF16)
            d_ktp = M.d_ps[bx]
            P.pe(lambda e, ktp=ktp, kt_=kt_: e.transpose(ktp, kt_, M.ident[:]), reads=[d_kt, M.d_const],
                 writes=[d_ktp])
            yield
            ktok, d_ktok = M.nrb()
            P.act(lambda e, ktok=ktok, ktp=ktp: e.activation(out=ktok[:], in_=ktp, func=AF.Copy),
                  reads=[d_ktp], writes=[d_ktok])
            yield
            if lat:
                atp, d_atp = M.ps[by][:, 0:128], M.d_ps[by]
                P.pe(lambda e, atp=atp, kt_=kt_, qt_=qt_: e.matmul(atp, lhsT=kt_, rhs=qt_, start=True, stop=True),
                     reads=[d_kt, d_qt], writes=[d_atp])
                yield
                am, d_am = M.nrb()
                P.dve(lambda e, am=am, atp=atp: e.tensor_tensor(out=am[:], in0=atp, in1=mask[:], op=ALU.mult),
                      reads=[d_atp, M.d_const], writes=[d_am])
                yield
                sp, d_sp = M.nrb()
                P.dve(lambda e, sp=sp, sv=sv: e.tensor_scalar(out=sp[:], in0=S_[:], scalar1=sv[:, c_e1:c_e1 + 1],
                                                              scalar2=None, op0=ALU.mult),
                      reads=[d_S, d_sv], writes=[d_sp])
                yield
                op_, d_op = M.ps[by][:, 128:256], M.d_ps[by]
                P.pe(lambda e, op_=op_, sp=sp, qt_=qt_: e.matmul(op_, lhsT=sp[:], rhs=qt_, start=True, stop=False),
                     reads=[d_sp, d_qt], writes=[d_op])
                yield
                P.pe(lambda e, op_=op_, vt=vt, am=am: e.matmul(op_, lhsT=vt, rhs=am[:], start=False, stop=True),
                     reads=[d_vt, d_am], writes=[d_op])
                yield
                P.act(lambda e, op_=op_, la=la: e.activation(out=obuf[dr][:, la:la + 128], in_=op_, func=AF.Copy),
                      reads=[d_op], writes=[d_obuf[dr]])
                yield
            kvp, d_kvp = M.ps[by][:, 256:384], M.d_ps[by]
            P.pe(lambda e, kvp=kvp, ktok=ktok, vt=vt: e.matmul(kvp, lhsT=ktok[:], rhs=vt, start=True, stop=True),
                 reads=[d_ktok, d_vt], writes=[d_kvp])
            yield
            tk, d_tk = M.nrf()
            P.dve(lambda e, tk=tk, kvp=kvp, sv=sv: e.tensor_scalar(out=tk[:], in0=kvp, scalar1=sv[:, c_e2:c_e2 + 1],
                                                                    scalar2=None, op0=ALU.mult),
                  reads=[d_kvp, d_sv], writes=[d_tk])
            yield
            P.dve(lambda e, tk=tk, sv=sv: e.scalar_tensor_tensor(out=S_[:], in0=S_[:], scalar=sv[:, 3:4],
                                                                 in1=tk[:], op0=ALU.mult, op1=ALU.add),
                  reads=[d_tk, d_sv, d_S], writes=[d_S])
            yield
    import itertools
    for step in range(18):
        gens = [chunk_gen(0, step), chunk_gen(1, step)]
        for _ in itertools.zip_longest(*gens):
            pass
    if RSTG < 4:
        return
    for t in range(4):
        lo = t * 512
        o_, d_o = M.ntf()
        P.dve(lambda e, o_=o_: e.tensor_tensor(out=o_[:], in0=M.ofw[:, lo:lo + 512], in1=M.obw[:, lo:lo + 512],
                                               op=ALU.add), reads=[M.d_ofw, M.d_obw], writes=[d_o])
        sq, d_sq = M.ntb()
        P.act(lambda e, sq=sq, o_=o_: e.activation(out=sq[:], in_=o_[:], func=AF.Square), reads=[d_o], writes=[d_sq])
        P.pe(lambda e, sq=sq: e.matmul(M.ps[0][:, :], lhsT=M.ones[:], rhs=sq[:], start=True, stop=True),
             reads=[d_sq, M.d_ones], writes=[M.d_ps[0]])
        ri, d_ri = M.ntf()
        emit_rsqrt(P, M, ri[:], d_ri, M.ps[0][:, :], M.d_ps[0], 512, 1.0 / 128)
        o2, d_o2 = M.ntf()
        P.dve(lambda e, o2=o2, o_=o_, ri=ri: e.scalar_tensor_tensor(
            out=o2[:], in0=o_[:], scalar=M.vec[:, 5:6], in1=ri[:], op0=ALU.mult, op1=ALU.mult),
            reads=[d_o, d_ri, M.d_vec], writes=[d_o2])
        o3, d_o3 = M.ntb()
        P.dve(lambda e, o3=o3, o2=o2: e.tensor_tensor(out=o3[:], in0=o2[:], in1=M.sg_sb[:, lo:lo + 512], op=ALU.mult),
              reads=[d_o2, M.d_rp], writes=[d_o3])
        rows = mergedT("rec", r) if callable(mergedT) else mergedT[512 + r * 128:512 + (r + 1) * 128, :]
        P.dma(rows[:, lo:lo + 512], o3[:], reads=[d_o3], writes=[d_merged], semkey="mg")


def emit_load_act16(P, C, srcT, d_src):
    for q in range(4):
        P.dma(C.A[:, q * 4:(q + 1) * 4, :], srcT[q * 512:(q + 1) * 512, :].rearrange("(c p) t -> p c t", p=128),
              reads=[d_src], writes=C.d_A + C.d_xt, semkey="act16")


def emit_conv_in(P, C, w_in, bgT, cvT, d_out, st, bnd=None, d_bnd=None):
    it = 0
    si = 0
    for fc in range(16):
        s = fc % 2
        wv = C.wgu[s][:].rearrange("p a c n -> p (a c n)").rearrange("p (q c n) -> p q c n", q=4, c=16)
        for q in range(3):
            src = w_in[:, q * 2048 + fc * 128: q * 2048 + (fc + 1) * 128].rearrange("(c p) n -> p c n", p=128)
            P.dma(wv[:, q, :, :], src, writes=[C.d_wgu[s]], semkey=f"wgu{s}", eng="pool")
        for ti, (off, n) in enumerate(C.tiles):
            pb = (it % 2) * 3
            it += 1
            for q in range(3):
                pt = C.ps[pb + q]
                for k in range(16):
                    P.pe(lambda e, pt=pt, q=q, k=k, wv=wv, off=off, n=n: e.matmul(
                        pt[:, 0:n], lhsT=wv[:, q, k, :], rhs=C.hy[:, k, off:off + n],
                        start=(k == 0), stop=(k == 15)),
                        reads=[C.d_wgu[s], C.d_h[ti]], writes=[C.d_ps[pb + q]])
            sb, d_sb = st[si % len(st)]
            si += 1
            P.act(lambda e, sb=sb, pb=pb, n=n: e.activation(out=sb[:, 0:n], in_=C.ps[pb][:, 0:n], func=AF.Copy),
                  reads=[C.d_ps[pb]], writes=[d_sb])
            P.dma(bgT[fc * 128:(fc + 1) * 128, off:off + n], sb[:, 0:n], reads=[d_sb], writes=[d_out],
                  semkey="cvo")
            tmp, d_tmp = C.next_tmp()
            P.act(lambda e, tmp=tmp, pb=pb, n=n: e.activation(out=tmp[:, 0:n], in_=C.ps[pb + 1][:, 0:n],
                                                              func=AF.Copy),
                  reads=[C.d_ps[pb + 1]], writes=[d_tmp])
            sb2, d_sb2 = st[si % len(st)]
            si += 1
            P.dve(lambda e, sb2=sb2, tmp=tmp, pb=pb, n=n: e.tensor_tensor(
                out=sb2[:, 0:n], in0=tmp[:, 0:n], in1=C.ps[pb + 2][:, 0:n], op=ALU.mult),
                reads=[d_tmp, C.d_ps[pb + 2]], writes=[d_sb2])
            P.dma(cvT[fc * 128:(fc + 1) * 128, off:off + n], sb2[:, 0:n], reads=[d_sb2], writes=[d_out],
                  semkey="cvo")
            if bnd is not None and off == 0:
                P.act(lambda e, sb2=sb2, fc=fc: e.activation(out=bnd[:, 0, fc:fc + 1], in_=sb2[:, 0:1], func=AF.Copy),
                      reads=[d_sb2], writes=[d_bnd])
            if bnd is not None and off + n == C.NT:
                P.act(lambda e, sb2=sb2, fc=fc, n=n: e.activation(out=bnd[:, 1, fc:fc + 1], in_=sb2[:, n - 1:n],
                                                                  func=AF.Copy),
                      reads=[d_sb2], writes=[d_bnd])


def emit_conv(P, C, bgT, cvhT, d_in, cw, d_cw, st):
    si = 0
    for c in range(16):
        for ti, (off, n) in enumerate(C.tiles):
            i1 = si % len(st)
            si += 1
            i2 = si % len(st)
            si += 1
            cvt, d_cvt = st[i1]
            bt, d_bt = st[i2]
            P.dma(cvt[:, 0:n + 2], cvhT[c * 128:(c + 1) * 128, off:off + n + 2], reads=[d_in], writes=[d_cvt],
                  semkey=f"cvi{i1}")
            P.dma(bt[:, 0:n], bgT[c * 128:(c + 1) * 128, off:off + n], reads=[d_in], writes=[d_bt],
                  semkey=f"cvi{i2}")
            u, d_u = C.next_tmp()
            P.dve(lambda e, u=u, cvt=cvt, c=c, n=n: e.tensor_scalar(
                out=u[:, 0:n], in0=cvt[:, 0:n], scalar1=cw[:, c, 0:1], scalar2=None, op0=ALU.mult),
                reads=[d_cvt, d_cw], writes=[d_u])
            P.dve(lambda e, u=u, cvt=cvt, c=c, n=n: e.scalar_tensor_tensor(
                out=u[:, 0:n], in0=cvt[:, 1:n + 1], scalar=cw[:, c, 1:2], in1=u[:, 0:n], op0=ALU.mult, op1=ALU.add),
                reads=[d_cvt, d_cw, d_u], writes=[d_u])
            P.dve(lambda e, u=u, cvt=cvt, c=c, n=n: e.scalar_tensor_tensor(
                out=u[:, 0:n], in0=cvt[:, 2:n + 2], scalar=cw[:, c, 2:3], in1=u[:, 0:n], op0=ALU.mult, op1=ALU.add),
                reads=[d_cvt, d_cw, d_u], writes=[d_u])
            P.dve(lambda e, u=u, bt=bt, c=c, off=off, n=n: e.tensor_tensor(
                out=C.A[:, c, off:off + n], in0=u[:, 0:n], in1=bt[:, 0:n], op=ALU.mult),
                reads=[d_u, d_bt], writes=[C.d_A[ti]] + C.d_xt)


def emit_load_mg_sel(P, C, mg_g, d_src, selv, d_sel):
    for hc in range(2):
        for c in range(16):
            kind, rk, hd = c // 8, (c // 4) % 2, c % 4
            k = kind * 2 + hd // 2
            r0 = rk * 256 + (hd % 2) * 128
            P.dma(C.A[:, hc * 16 + c, :], mg_g[k][r0:r0 + 128, hc * 1024:(hc + 1) * 1024],
                  reads=[d_src], writes=C.d_A + C.d_xt, semkey="act16")
    for c in range(16):
        P.dve(lambda e, c=c: e.tensor_scalar(out=C.A[:, c, :], in0=C.A[:, c, :], scalar1=selv[:, 0:1], scalar2=None,
                                             op0=ALU.mult), reads=C.d_A + [d_sel], writes=C.d_A)
        P.dve(lambda e, c=c: e.scalar_tensor_tensor(out=C.A[:, c, :], in0=C.A[:, 16 + c, :], scalar=selv[:, 1:2],
                                                    in1=C.A[:, c, :], op0=ALU.mult, op1=ALU.add),
              reads=C.d_A + [d_sel], writes=C.d_A)


def emit_conv_halo(P, C, bgT, cvT, d_in, bnd_g, d_bnd, selv, d_sel, cw, d_cw, st, hal, d_hal):
    P.dma(hal[:, 0, :], bnd_g[0:128, 16:32], reads=[d_bnd], writes=[d_hal], semkey="hal")
    P.dma(hal[:, 1, :], bnd_g[128:256, 0:16], reads=[d_bnd], writes=[d_hal], semkey="hal")
    P.dve(lambda e: e.tensor_scalar(out=hal[:, 0, :], in0=hal[:, 0, :], scalar1=selv[:, 2:3], scalar2=None,
                                    op0=ALU.mult), reads=[d_hal, d_sel], writes=[d_hal])
    P.dve(lambda e: e.tensor_scalar(out=hal[:, 1, :], in0=hal[:, 1, :], scalar1=selv[:, 3:4], scalar2=None,
                                    op0=ALU.mult), reads=[d_hal, d_sel], writes=[d_hal])
    si = 0
    NT = C.NT
    for c in range(16):
        for ti, (off, n) in enumerate(C.tiles):
            i1 = si % len(st)
            si += 1
            i2 = si % len(st)
            si += 1
            cvt, d_cvt = st[i1]
            bt, d_bt = st[i2]
            lo = max(off - 1, 0)
            hi = min(off + n + 1, NT)
            dlo = lo - (off - 1)
            P.dma(cvt[:, dlo:dlo + (hi - lo)], cvT[c * 128:(c + 1) * 128, lo:hi], reads=[d_in], writes=[d_cvt],
                  semkey=f"cvi{i1}")
            if off == 0:
                P.dve(lambda e, cvt=cvt, c=c: e.tensor_copy(out=cvt[:, 0:1], in_=hal[:, 0, c:c + 1]),
                      reads=[d_hal], writes=[d_cvt])
            if off + n == NT:
                P.dve(lambda e, cvt=cvt, c=c, n=n: e.tensor_copy(out=cvt[:, n + 1:n + 2], in_=hal[:, 1, c:c + 1]),
                      reads=[d_hal], writes=[d_cvt])
            P.dma(bt[:, 0:n], bgT[c * 128:(c + 1) * 128, off:off + n], reads=[d_in], writes=[d_bt],
                  semkey=f"cvi{i2}")
            u, d_u = C.next_tmp()
            P.dve(lambda e, u=u, cvt=cvt, c=c, n=n: e.tensor_scalar(
                out=u[:, 0:n], in0=cvt[:, 0:n], scalar1=cw[:, c, 0:1], scalar2=None, op0=ALU.mult),
                reads=[d_cvt, d_cw], writes=[d_u])
            P.dve(lambda e, u=u, cvt=cvt, c=c, n=n: e.scalar_tensor_tensor(
                out=u[:, 0:n], in0=cvt[:, 1:n + 1], scalar=cw[:, c, 1:2], in1=u[:, 0:n], op0=ALU.mult, op1=ALU.add),
                reads=[d_cvt, d_cw, d_u], writes=[d_u])
            P.dve(lambda e, u=u, cvt=cvt, c=c, n=n: e.scalar_tensor_tensor(
                out=u[:, 0:n], in0=cvt[:, 2:n + 2], scalar=cw[:, c, 2:3], in1=u[:, 0:n], op0=ALU.mult, op1=ALU.add),
                reads=[d_cvt, d_cw, d_u], writes=[d_u])
            P.dve(lambda e, u=u, bt=bt, c=c, off=off, n=n: e.tensor_tensor(
                out=C.A[:, c, off:off + n], in0=u[:, 0:n], in1=bt[:, 0:n], op=ALU.mult),
                reads=[d_u, d_bt], writes=[C.d_A[ti]] + C.d_xt)


BF = ml_dtypes.bfloat16
NCORES = 8


def _ffn_w(nc, tag):
    wg = nc.dram_tensor("wg" + tag, [2048, 5504], F32, kind="ExternalInput")
    wu = nc.dram_tensor("wu" + tag, [2048, 5504], F32, kind="ExternalInput")
    wd = nc.dram_tensor("wd" + tag, [5504, 2048], F32, kind="ExternalInput")
    return wg, wu, wd


def build_A():
    nc = bass.Bass("TRN2", target_bir_lowering=False)
    P = Prog(nc)
    xT = nc.dram_tensor("xT", [2048, 1024], F32, kind="ExternalInput")
    ctxT = nc.dram_tensor("ctxT", [2048, 128], F32, kind="ExternalInput")
    cvecT = nc.dram_tensor("cvecT", [128, 16, 2], F32, kind="ExternalInput")
    ada_w = nc.dram_tensor("ada_w", [2048, 18432], F32, kind="ExternalInput")
    ada_bT = nc.dram_tensor("ada_bT", [128, 144], F32, kind="ExternalInput")
    gTin = nc.dram_tensor("gTin", [128, 6, 16], F32, kind="ExternalInput")
    wg, wu, wd = _ffn_w(nc, "")
    x1T = nc.dram_tensor("x1T", [2048, 1024], F32, kind="ExternalOutput")
    hT = nc.dram_tensor("hT", [2048, 1152], BF16, kind="ExternalOutput")
    mTo = nc.dram_tensor("mTo", [128, 144, 2], F32, kind="ExternalOutput")
    xc1T = nc.dram_tensor("xc1T", [2048, 128], F32, kind="Internal")
    C = Ctx(P, [512, 512, 128])
    scr = {"cv": P.sbuf("cv", [128, 16, 2], F32), "sc": P.sbuf("sc", [128, 16, 2], BF16),
           "bT": P.sbuf("bT", [128, 144], F32)}
    P.dma(C.gT[:], gTin[:], writes=[C.d_gT], semkey="small3")
    emit_modulation(P, C, ada_w, ada_bT[:], cvecT[:], scr)
    d_in = Dep("in")
    d_x1 = [Dep("x1a"), Dep("x1b"), Dep("xc1")]
    emit_coefs(P, C, 1, 2, 0, 1, 0.5, [0, 1])
    srcs = [(xT[:, 0:512], d_in, 0), (xT[:, 512:1024], d_in, 0), (ctxT[:, :], d_in, 1)]
    dsts = [(x1T[:, 0:512], d_x1[0], 0), (x1T[:, 512:1024], d_x1[1], 0), (xc1T[:, :], d_x1[2], 1)]
    emit_ffn(P, C, srcs, dsts, wg, wu, wd, 0)
    emit_coefs(P, C, 4, None, 2, None, 1.0, [0, 1])
    emit_prenorm(P, C, dsts, 3, lambda ti: C.hy[:, :, C.tiles[ti][0]:C.tiles[ti][0] + C.tiles[ti][1]],
                 C.d_h, C.d_A)
    d_hT = Dep("hT")
    for ti, (off, n) in enumerate(C.tiles):
        P.dma(hT[:, off:off + n].rearrange("(c p) t -> p c t", p=128), C.hy[:, :, off:off + n],
              reads=[C.d_h[ti]], writes=[d_hT], semkey="hT")
    P.dma(mTo[:], C.mT[:], reads=[C.d_mT], writes=[Dep("mTo")], semkey="mTo")
    P.emit()
    return nc


def build_B():
    nc = bass.Bass("TRN2", target_bir_lowering=False)
    P = Prog(nc)
    hTf = nc.dram_tensor("hTf", [2048, NTOK], BF16, kind="ExternalInput")
    w_att = nc.dram_tensor("w_att", [4, 3, 2048, 128], F32, kind="ExternalInput")
    w_rec = nc.dram_tensor("w_rec", [4, 5, 2048, 128], F32, kind="ExternalInput")
    ropeC = nc.dram_tensor("ropeCin", [128, 2048], F32, kind="ExternalInput")
    ropeS = nc.dram_tensor("ropeSin", [128, 2048], F32, kind="ExternalInput")
    perm = nc.dram_tensor("permin", [128, 128], F32, kind="ExternalInput")
    ident = nc.dram_tensor("identin", [128, 128], F32, kind="ExternalInput")
    maskf = nc.dram_tensor("maskfin", [128, 128], F32, kind="ExternalInput")
    maskb = nc.dram_tensor("maskbin", [128, 128], F32, kind="ExternalInput")
    lamT = nc.dram_tensor("lamT", [64, 4], F32, kind="ExternalInput")
    dng = nc.dram_tensor("dng", [128, 1], F32, kind="ExternalInput")
    rng = nc.dram_tensor("rng", [128, 1], F32, kind="ExternalInput")
    lbraw = nc.dram_tensor("lbraw", [128, 2, 2, 4], F32, kind="ExternalInput")
    mergedT = nc.dram_tensor("mergedT", [1024, 2048], BF16, kind="ExternalOutput")
    M = MixCtx(P)
    emit_mix_setup(P, M, hTf, ropeC[:], ropeS[:], perm[:], ident[:], maskf[:], maskb[:], lamT[:], dng[:], rng[:],
                   lbraw[:])
    d_merged = Dep("merged")
    for hd in range(4):
        for i in range(3):
            load_w(P, M, i, w_att[hd, i])
        for i in range(5):
            load_w(P, M, 3 + i, w_rec[hd, i])
        emit_attention_head(P, M, hd, mergedT, d_merged)
        emit_rec_head(P, M, hd, mergedT, d_merged)
    P.emit()
    return nc


def build_C():
    nc = bass.Bass("TRN2", target_bir_lowering=False)
    P = Prog(nc)
    x1T = nc.dram_tensor("x1T", [2048, 1024], F32, kind="ExternalInput")
    mgT = nc.dram_tensor("mgT", [2048, 1024], BF16, kind="ExternalInput")
    w_out = nc.dram_tensor("w_out", [2048, 2048], F32, kind="ExternalInput")
    mT0 = nc.dram_tensor("mT0", [128, 144, 2], F32, kind="ExternalInput")
    gT0 = nc.dram_tensor("gT0", [128, 6, 16], F32, kind="ExternalInput")
    gT1 = nc.dram_tensor("gT1", [128, 6, 16], F32, kind="ExternalInput")
    cvecT = nc.dram_tensor("cvecT", [128, 16, 2], F32, kind="ExternalInput")
    ada_w = nc.dram_tensor("ada_w", [2048, 18432], F32, kind="ExternalInput")
    ada_bT = nc.dram_tensor("ada_bT", [128, 144], F32, kind="ExternalInput")
    wgA, wuA, wdA = _ffn_w(nc, "A")
    wgB, wuB, wdB = _ffn_w(nc, "B")
    cw_in = nc.dram_tensor("cw_in", [2048, 6144], F32, kind="ExternalInput")
    x4T = nc.dram_tensor("x4T", [2048, 1024], F32, kind="ExternalOutput")
    bgT = nc.dram_tensor("bgT", [2048, 1024], F32, kind="ExternalOutput")
    cvT = nc.dram_tensor("cvT", [2048, 1024], F32, kind="ExternalOutput")
    mT1 = nc.dram_tensor("mT1", [128, 144, 2], F32, kind="ExternalOutput")
    x2T = nc.dram_tensor("x2T", [2048, 1024], F32, kind="Internal")
    x3T = nc.dram_tensor("x3T", [2048, 1024], F32, kind="Internal")
    C = Ctx(P, [512, 512])
    scr = {"cv": P.sbuf("cv", [128, 16, 2], F32), "sc": P.sbuf("sc", [128, 16, 2], BF16),
           "bT": P.sbuf("bT", [128, 144], F32)}
    st = [(P.sbuf(f"st{i}", [128, 512], F32), Dep(f"st{i}")) for i in range(4)]
    d_in = Dep("in")

    def tl(t, d):
        return [(t[:, 0:512], d[0], 0), (t[:, 512:1024], d[1], 0)]
    d_x1 = [d_in, d_in]
    d_x2 = [Dep("x2a"), Dep("x2b")]
    d_x3 = [Dep("x3a"), Dep("x3b")]
    d_x4 = [Dep("x4a"), Dep("x4b")]
    P.dma(C.gT[:], gT0[:], writes=[C.d_gT], semkey="small3")
    P.dma(C.mT[:], mT0[:], writes=[C.d_mT], semkey="small4")
    emit_load_act16(P, C, mgT, d_in)
    emit_coefs(P, C, 4, 5, 2, 3, 1.0, [0])
    emit_down_residual(P, C, 16, w_out, tl(x1T, d_x1), tl(x2T, d_x2))
    emit_coefs(P, C, 7, 8, 4, 5, 0.5, [0])
    emit_ffn(P, C, tl(x2T, d_x2), tl(x3T, d_x3), wgA, wuA, wdA, 6)
    P.dma(C.gT[:], gT1[:], writes=[C.d_gT], semkey="small3")
    emit_modulation(P, C, ada_w, ada_bT[:], cvecT[:], scr)
    emit_coefs(P, C, 1, 2, 0, 1, 0.5, [0])
    emit_ffn(P, C, tl(x3T, d_x3), tl(x4T, d_x4), wgB, wuB, wdB, 0)
    emit_coefs(P, C, 4, None, 2, None, 1.0, [0])
    emit_prenorm(P, C, tl(x4T, d_x4), 3, lambda ti: C.hy[:, :, C.tiles[ti][0]:C.tiles[ti][0] + C.tiles[ti][1]],
                 C.d_h, C.d_A)
    emit_conv_in(P, C, cw_in, bgT, cvT, Dep("cvout"), st)
    P.dma(mT1[:], C.mT[:], reads=[C.d_mT], writes=[Dep("mT1o")], semkey="mTo")
    P.emit()
    return nc


def build_D():
    nc = bass.Bass("TRN2", target_bir_lowering=False)
    P = Prog(nc)
    x4T = nc.dram_tensor("x4T", [2048, 1024], F32, kind="ExternalInput")
    bgT = nc.dram_tensor("bgT", [2048, 1024], F32, kind="ExternalInput")
    cvhT = nc.dram_tensor("cvhT", [2048, 1026], F32, kind="ExternalInput")
    cwT = nc.dram_tensor("cwT", [128, 16, 3], F32, kind="ExternalInput")
    cw_out = nc.dram_tensor("cw_out", [2048, 2048], F32, kind="ExternalInput")
    mT1 = nc.dram_tensor("mT1", [128, 144, 2], F32, kind="ExternalInput")
    gT1 = nc.dram_tensor("gT1", [128, 6, 16], F32, kind="ExternalInput")
    wg, wu, wd = _ffn_w(nc, "")
    outT = nc.dram_tensor("outT", [2048, 1024], F32, kind="ExternalOutput")
    x5T = nc.dram_tensor("x5T", [2048, 1024], F32, kind="Internal")
    C = Ctx(P, [512, 512])
    st = [(P.sbuf(f"st{i}", [128, 514], F32), Dep(f"st{i}")) for i in range(4)]
    cw = P.sbuf("cw", [128, 16, 3], F32)
    d_cw = Dep("cw")
    d_in = Dep("in")

    def tl(t, d):
        return [(t[:, 0:512], d[0], 0), (t[:, 512:1024], d[1], 0)]
    d_x5 = [Dep("x5a"), Dep("x5b")]
    d_o = [Dep("oa"), Dep("ob")]
    P.dma(C.gT[:], gT1[:], writes=[C.d_gT], semkey="small3")
    P.dma(C.mT[:], mT1[:], writes=[C.d_mT], semkey="small4")
    P.dma(cw[:], cwT[:], writes=[d_cw], semkey="small5")
    emit_conv(P, C, bgT, cvhT, d_in, cw, d_cw, st)
    emit_coefs(P, C, 4, 5, 2, 3, 1.0, [0])
    emit_down_residual(P, C, 16, cw_out, tl(x4T, [d_in, d_in]), tl(x5T, d_x5))
    emit_coefs(P, C, 7, 8, 4, 5, 0.5, [0])
    emit_ffn(P, C, tl(x5T, d_x5), tl(outT, d_o), wg, wu, wd, 6)
    P.emit()
    return nc


ARENA_BYTES = 207 * 1024
FUSE_STOP = int(os.environ.get("FUSE_STOP", "0"))


def build_fused():
    nc = bass.Bass("TRN2", target_bir_lowering=False)
    P = Prog(nc)
    P.use_arena(ARENA_BYTES)
    inp = lambda n, sh, dt=F32: nc.dram_tensor(n, sh, dt, kind="ExternalInput")
    xT = inp("xT", [2048, 1024])
    ctxT = inp("ctxT", [2048, 128])
    ada_sl = inp("ada_sl", [2, 2048, 4608])
    cvec3 = inp("cvec3", [128, 16, 3])
    selb = inp("selb", [128, 2])
    bT0 = inp("bT0", [128, 144])
    bT1 = inp("bT1", [128, 144])
    gT0 = inp("gT0", [128, 6, 16])
    gT1 = inp("gT1", [128, 6, 16])
    W = [_ffn_w(nc, str(i)) for i in range(4)]
    w_att = inp("w_att", [4, 3, 2048, 128])
    w_rec = inp("w_rec", [4, 5, 2048, 128])
    ropeC = inp("ropeCin", [128, 2048])
    ropeS = inp("ropeSin", [128, 2048])
    perm = inp("permin", [128, 128])
    ident = inp("identin", [128, 128])
    maskf = inp("maskfin", [128, 128])
    maskb = inp("maskbin", [128, 128])
    lamT = inp("lamT", [64, 4])
    dng = inp("dng", [128, 1])
    rng = inp("rng", [128, 1])
    lbraw = inp("lbraw", [128, 2, 2, 4])
    w_out = inp("w_out", [2048, 2048])
    cw_in = inp("cw_in", [2048, 6144])
    cwT = inp("cwT", [128, 16, 3])
    cw_out = inp("cw_out", [2048, 2048])
    selin = inp("selin", [128, 4])
    outT = nc.dram_tensor("outT", [2048, 1024], F32, kind="ExternalOutput")
    itn = lambda n, sh, dt=F32: nc.dram_tensor(n, sh, dt, kind="Internal")
    x1T, x2T, x3T, x4T, x5T = [itn(f"x{i}T", [2048, 1024]) for i in (1, 2, 3, 4, 5)]
    xc1T = itn("xc1T", [2048, 128])
    hT_own = [itn(f"hT_own{t}", [2048, n_], BF16) for t, n_ in enumerate((512, 512, 128))]
    hT_g = [itn(f"hT_g{t}", [4096, n_], BF16) for t, n_ in enumerate((512, 512, 128))]
    mg_own = [itn(f"mg_own{q}", [256, 2048], BF16) for q in range(4)]
    mg_g = [itn(f"mg_g{q}", [512, 2048], BF16) for q in range(4)]
    bgT = itn("bgT", [2048, 1024])
    cvT = itn("cvT", [2048, 1024])
    bnd_own = itn("bnd_own", [128, 32])
    bnd_g = itn("bnd_g", [256, 32])
    d_in = Dep("in")

    def tl(t, d):
        return [(t[:, 0:512], d[0], 0), (t[:, 512:1024], d[1], 0)]

    def mk_scr():
        return {"cv": P.sbuf("cv", [128, 16, 2], F32), "sc": P.sbuf("sc", [128, 16, 2], BF16),
                "bT": P.sbuf("bT", [128, 144], F32)}
    hyv = lambda C: (lambda ti: C.hy[:, :, C.tiles[ti][0]:C.tiles[ti][0] + C.tiles[ti][1]])

    mp_own = itn("mp_own", [128, 216])
    mp_g1 = itn("mp_g1", [256, 216])
    mp_g2 = itn("mp_g2", [512, 216])
    mTd = [itn("mT0d", [128, 144, 2]), itn("mT1d", [128, 144, 2])]
    d_mTd = [Dep("mT0d"), Dep("mT1d")]
    C = Ctx(P, [512, 512])
    emit_modulation_sharded(P, C, ada_sl, cvec3[:], selb[:], bT0[:], bT1[:], mp_own, mp_g1, mp_g2, mTd, d_mTd)
    P.barrier()
    P.aoff = 0
    C = Ctx(P, [512, 512, 128])
    P.dma(C.gT[:], gT0[:], writes=[C.d_gT], semkey="small3")
    P.dma(C.mT[:], mTd[0][:], reads=[d_mTd[0]], writes=[C.d_mT], semkey="small4")
    d_x1 = [Dep("x1a"), Dep("x1b"), Dep("xc1")]
    emit_coefs(P, C, 1, 2, 0, 1, 0.5, [0, 1])
    srcs = [(xT[:, 0:512], d_in, 0), (xT[:, 512:1024], d_in, 0), (ctxT[:, :], d_in, 1)]
    dsts = [(x1T[:, 0:512], d_x1[0], 0), (x1T[:, 512:1024], d_x1[1], 0), (xc1T[:, :], d_x1[2], 1)]
    d_hg = Dep("hT_g")

    def ship_tile(ti, off, n):
        d_t = Dep(f"hT_own{ti}")
        P.dma(hT_own[ti][:, :].rearrange("(c p) t -> p c t", p=128), C.hy[:, :, off:off + n],
              reads=[C.d_h[ti]], writes=[d_t], semkey=f"hT{ti}")
        P.allgather_pairs(hT_g[ti], hT_own[ti], reads=[d_t], writes=[d_hg], semkey="cc1")
    emit_prenorm(P, C, srcs, 0, hyv(C), C.d_h, C.d_A)
    emit_gateup(P, C, W[0][0], W[0][1])
    emit_coefs(P, C, 4, None, 2, None, 1.0, [0, 1], which="pre")
    emit_down_residual(P, C, NJ, W[0][2], srcs, dsts, after_post=lambda ti: emit_prenorm(
        P, C, dsts, 3, hyv(C), C.d_h, C.d_A, resident=True, after_tile=ship_tile, tiles_only=[ti]))
    P.barrier()
    P.aoff = 0
    M = MixCtx(P)
    for r in range(2):
        for q in range(4):
            rs = slice(r * 2048 + q * 512, r * 2048 + (q + 1) * 512)
            P.dma(M.hs[:, q * 4:(q + 1) * 4, r * 128:(r + 1) * 128],
                  hT_g[2][rs, :].rearrange("(c p) t -> p c t", p=128), reads=[d_hg], writes=[M.d_hs], semkey="hs")
            for ti in range(2):
                c0 = 256 + r * 1024 + ti * 512
                P.dma(M.hs[:, q * 4:(q + 1) * 4, c0:c0 + 512],
                      hT_g[ti][rs, :].rearrange("(c p) t -> p c t", p=128), reads=[d_hg], writes=[M.d_hs],
                      semkey="hs")
    emit_mix_setup(P, M, None, ropeC[:], ropeS[:], perm[:], ident[:], maskf[:], maskb[:], lamT[:], dng[:], rng[:],
                   lbraw[:])
    d_mg = Dep("mg_own")
    d_mgg = Dep("mg_g")
    for i in range(3):
        load_w(P, M, i, w_att[0, i])
    for i in range(5):
        load_w(P, M, 3 + i, w_rec[0, i])
    for hd in range(4):
        mgdst = lambda kind, h: mg_own[(0 if kind == "att" else 2) + h // 2][(h % 2) * 128:(h % 2) * 128 + 128, :]
        emit_attention_head(P, M, hd, mgdst, d_mg)
        P.barrier()

        def prefetch(hd=hd):
            if hd + 1 < 4:
                for i in range(3):
                    load_w(P, M, i, w_att[hd + 1, i])
                for i in range(5):
                    load_w(P, M, 3 + i, w_rec[hd + 1, i])
        emit_rec_head(P, M, hd, mgdst, d_mg, after_burst=prefetch)
        P.barrier()
        if hd % 2 == 1:
            for q in (hd // 2, 2 + hd // 2):
                P.allgather_pairs(mg_g[q], mg_own[q], reads=[d_mg], writes=[d_mgg], semkey="cc2")
    if FUSE_STOP == 2:
        dbg = nc.dram_tensor("dbg", [2048, 2048], BF16, kind="ExternalOutput")
        for q in range(4):
            for r in range(2):
                base = r * 1024 + (q // 2) * 512 + (q % 2) * 256
                P.dma(dbg[base:base + 256, :], mg_g[q][r * 256:(r + 1) * 256, :], reads=[d_mgg],
                      writes=[Dep("dbg")], semkey="dbg")
        P.emit()
        return nc
    P.barrier()
    P.aoff = 0
    C = Ctx(P, [512, 512])
    selv = P.sbuf("selv", [128, 4], F32)
    d_sel = Dep("selv")
    st = [(P.sbuf(f"st{i}", [128, 514], F32), Dep(f"st{i}")) for i in range(4)]
    cw = P.sbuf("cw", [128, 16, 3], F32)
    d_cw = Dep("cw")
    hal = P.sbuf("hal", [128, 2, 16], F32)
    d_hal = Dep("hal")
    P.dma(selv[:], selin[:], writes=[d_sel], semkey="small5")
    P.dma(cw[:], cwT[:], writes=[d_cw], semkey="small5")
    P.dma(C.gT[:], gT0[:], writes=[C.d_gT], semkey="small3")
    P.dma(C.mT[:], mTd[0][:], reads=[d_mTd[0]], writes=[C.d_mT], semkey="small4")
    d_x2 = [Dep("x2a"), Dep("x2b")]
    d_x3 = [Dep("x3a"), Dep("x3b")]
    d_x4 = [Dep("x4a"), Dep("x4b")]
    d_x5 = [Dep("x5a"), Dep("x5b")]
    d_o = [Dep("oa"), Dep("ob")]
    emit_load_mg_sel(P, C, mg_g, d_mgg, selv, d_sel)

    def pre_next(C, srcs, sl_shift):
        return lambda ti: emit_prenorm(P, C, srcs, sl_shift, hyv(C), C.d_h, C.d_A, resident=True, tiles_only=[ti])
    emit_coefs(P, C, 4, 5, 2, 3, 1.0, [0], which="post")
    emit_coefs(P, C, 7, None, 4, None, 1.0, [0], which="pre")
    emit_down_residual(P, C, 16, w_out, tl(x1T, d_x1), tl(x2T, d_x2), after_post=pre_next(C, tl(x2T, d_x2), 6))
    emit_gateup(P, C, W[1][0], W[1][1])
    emit_coefs(P, C, 7, 8, 4, 5, 0.5, [0], which="post")
    P.dma(C.gT[:], gT1[:], writes=[C.d_gT], semkey="small3")
    P.dma(C.mT[:], mTd[1][:], reads=[d_mTd[1]], writes=[C.d_mT], semkey="small4")
    emit_coefs(P, C, 1, None, 0, None, 1.0, [0], which="pre")
    emit_down_residual(P, C, NJ, W[1][2], tl(x2T, d_x2), tl(x3T, d_x3), after_post=pre_next(C, tl(x3T, d_x3), 0))
    emit_gateup(P, C, W[2][0], W[2][1])
    emit_coefs(P, C, 1, 2, 0, 1, 0.5, [0], which="post")
    emit_coefs(P, C, 4, None, 2, None, 1.0, [0], which="pre")
    emit_down_residual(P, C, NJ, W[2][2], tl(x3T, d_x3), tl(x4T, d_x4), after_post=pre_next(C, tl(x4T, d_x4), 3))
    d_cv = Dep("cvout")
    st512 = [(t[:, 0:512], d) for (t, d) in st]
    bnd_sb = P.sbuf("bnd_sb", [128, 2, 16], F32)
    d_bsb = Dep("bnd_sb")
    emit_conv_in(P, C, cw_in, bgT, cvT, d_cv, st512, bnd_sb, d_bsb)
    d_bo = Dep("bnd_own")
    P.dma(bnd_own[:, :], bnd_sb[:].rearrange("p a c -> p (a c)"), reads=[d_bsb], writes=[d_bo], semkey="bnd")
    d_bg = Dep("bnd_g")
    P.allgather_pairs(bnd_g, bnd_own, reads=[d_bo], writes=[d_bg], semkey="cc3")
    emit_conv_halo(P, C, bgT, cvT, d_cv, bnd_g, d_bg, selv, d_sel, cw, d_cw, st, hal, d_hal)
    emit_coefs(P, C, 4, 5, 2, 3, 1.0, [0], which="post")
    emit_coefs(P, C, 7, None, 4, None, 1.0, [0], which="pre")
    emit_down_residual(P, C, 16, cw_out, tl(x4T, d_x4), tl(x5T, d_x5), after_post=pre_next(C, tl(x5T, d_x5), 6))
    emit_gateup(P, C, W[3][0], W[3][1])
    emit_coefs(P, C, 7, 8, 4, 5, 0.5, [0], which="post")
    emit_down_residual(P, C, NJ, W[3][2], tl(x5T, d_x5), tl(outT, d_o))
    stuck = simulate_sync(P)
    if stuck:
        raise RuntimeError(f"sync deadlock: {stuck}")
    P.emit()
    return nc


def mix_consts():
    p = np.arange(128)
    d = p % 64
    i = d % 16
    freqs = (10000.0 ** (-np.arange(16, dtype=np.float32) / 16)).astype(np.float32)
    t = np.arange(2048)
    row = (t // 64).astype(np.float32)
    col = (t % 64).astype(np.float32)
    pos = np.where((d < 32)[:, None], row[None, :], col[None, :]).astype(np.float32)
    ang = (pos * freqs[i][:, None]).astype(np.float32)
    C = np.cos(ang).astype(np.float32)
    S = np.sin(ang).astype(np.float32)
    perm = np.zeros((128, 128), np.float32)
    for m in range(128):
        if (m % 32) < 16:
            perm[m + 16, m] = -1.0
        else:
            perm[m - 16, m] = 1.0
    ident = np.eye(128, dtype=np.float32)
    s = np.arange(128)[:, None]
    tt = np.arange(128)[None, :]
    return C, S, perm, ident, (s <= tt).astype(np.float32), (s >= tt).astype(np.float32)


_PROGS = {}


def _prog(name, fn):
    if name not in _PROGS:
        _PROGS[name] = fn()
    return _PROGS[name]


def _run(nc, in_maps):
    res = run_bass_kernel_spmd(nc, in_maps, core_ids=list(range(NCORES)))
    return res.results


def kernel(x, c, ctx, c_ctx, ada_w, ada_b, norm_g, ffn_w_gate, ffn_w_up, ffn_w_down, mix_w_in, mix_w_out,
           diff_lambda, diff_norm_g, rec_norm_g, rec_lb, conv_w_in, conv_w, conv_w_out):
    f32 = lambda a: np.ascontiguousarray(np.asarray(a, dtype=np.float32))
    x, c, ctx, c_ctx = f32(x), f32(c), f32(ctx), f32(c_ctx)
    ada_w, ada_b, norm_g = f32(ada_w), f32(ada_b), f32(norm_g)
    ffn_w_gate, ffn_w_up, ffn_w_down = f32(ffn_w_gate), f32(ffn_w_up), f32(ffn_w_down)
    mix_w_in, mix_w_out = f32(mix_w_in), f32(mix_w_out)
    conv_w_in, conv_w, conv_w_out = f32(conv_w_in), f32(conv_w), f32(conv_w_out)
    rec_lb = f32(rec_lb)
    gT = [np.ascontiguousarray(norm_g[l].reshape(6, 16, 128).transpose(2, 0, 1)) for l in range(2)]
    bT = [np.ascontiguousarray(ada_b[l].reshape(144, 128).T) for l in range(2)]
    cvec = [np.ascontiguousarray(np.stack([c[b], c_ctx], -1).reshape(16, 128, 2).transpose(1, 0, 2))
            for b in range(4)]
    Cc, Ss, perm, ident, maskf, maskb = mix_consts()
    w_in = mix_w_in[0]
    cwT = np.ascontiguousarray(conv_w[0].reshape(3, 16, 128).transpose(2, 1, 0))
    lamT = np.ascontiguousarray(f32(diff_lambda)[0].T)
    dng = f32(diff_norm_g)[0].reshape(128, 1).copy()
    rngv = f32(rec_norm_g)[0].reshape(128, 1).copy()
    heads = []
    for hh in range(2):
        w_att = np.empty((4, 3, 2048, 128), np.float32)
        w_rec = np.empty((4, 5, 2048, 128), np.float32)
        lbraw = np.empty((128, 2, 2, 4), np.float32)
        for hd in range(4):
            g = hh * 4 + hd
            for q in range(3):
                w_att[hd, q] = w_in[:, q * 1024 + g * 128: q * 1024 + (g + 1) * 128]
            for q in range(5):
                w_rec[hd, q] = w_in[:, 3072 + q * 1024 + g * 128: 3072 + q * 1024 + (g + 1) * 128]
            lbraw[:, :, :, hd] = rec_lb[:, :, g * 128:(g + 1) * 128].transpose(2, 0, 1)
        heads.append((w_att, w_rec, lbraw))
    ada_sl = [np.ascontiguousarray(ada_w[:, :, qq * 4608:(qq + 1) * 4608]) for qq in range(4)]
    cvec3 = [np.ascontiguousarray(np.stack([c[bl], c[bl + 2], c_ctx], -1).reshape(16, 128, 3).transpose(1, 0, 2))
             for bl in range(2)]
    selb = [np.ascontiguousarray(np.tile(np.eye(2, dtype=np.float32)[w][None, :], (128, 1))) for w in range(2)]
    in_maps = []
    for i in range(NCORES):
        b, h = i // 2, i % 2
        sel = np.zeros((128, 4), np.float32)
        sel[:, 0] = 1.0 if h == 0 else 0.0
        sel[:, 1] = 1.0 if h == 1 else 0.0
        sel[:, 2] = 1.0 if h == 1 else 0.0
        sel[:, 3] = 1.0 if h == 0 else 0.0
        m = {
            "xT": np.ascontiguousarray(x[b, h * 1024:(h + 1) * 1024].T),
            "ctxT": np.ascontiguousarray(ctx[b, h * 128:(h + 1) * 128].T),
            "ada_sl": ada_sl[(i % 2) + 2 * (i // 4)], "cvec3": cvec3[b % 2], "selb": selb[b // 2],
            "bT0": bT[0], "bT1": bT[1],
            "gT0": gT[0], "gT1": gT[1],
            "w_att": heads[h][0], "w_rec": heads[h][1], "lbraw": heads[h][2],
            "ropeCin": Cc, "ropeSin": Ss, "permin": perm, "identin": ident, "maskfin": maskf, "maskbin": maskb,
            "lamT": lamT, "dng": dng, "rng": rngv,
            "w_out": mix_w_out[0], "cw_in": conv_w_in[0], "cwT": cwT, "cw_out": conv_w_out[0], "selin": sel}
        for k, (l, j) in enumerate([(0, 0), (0, 1), (1, 0), (1, 1)]):
            m[f"wg{k}"] = ffn_w_gate[l, j]
            m[f"wu{k}"] = ffn_w_up[l, j]
            m[f"wd{k}"] = ffn_w_down[l, j]
        in_maps.append(m)
    rD = _run(_prog("F", build_fused), in_maps)
    if FUSE_STOP:
        return rD
    out = np.empty((4, 2048, 2048), np.float32)
    for i in range(NCORES):
        b, h = i // 2, i % 2
        out[b, h * 1024:(h + 1) * 1024] = np.asarray(rD[i]["outT"]).T
    return out
```
